# Optimizing a Trainium2 kernel written in Bass

```python
import jax, jax.numpy as jnp
from jax import lax
import numpy as np

D_MODEL = 1024
BATCH = 4
SEQ = 4096
DEPTH = 2
DEC_BATCH = 32
DEC_SEQ = 8
PAST_LEN = 8192
PAGE_SIZE = 128

N_MIXERS = 2
N_SB_LAYERS = (DEPTH + 1) // 2
N_RW_LAYERS = DEPTH // 2
SB_HEAD_DIM = 64
SB_HEADS = D_MODEL // SB_HEAD_DIM
Q_BLOCK = 128
RW_HEAD_DIM = 64
RW_HEADS = D_MODEL // RW_HEAD_DIM
DECAY_LORA = 64
AAA_LORA = 64
GATE_LORA = 160
D_FF = 4 * D_MODEL
NORM_EPS = 1e-6
GN_EPS = 1e-5 * RW_HEAD_DIM

kernel_name = 'stickbreak_rwkv7_hybrid_step'


def rmsnorm(x, g):
    x32 = x.astype(jnp.float32)
    y = x32 * lax.rsqrt(jnp.mean(x32 * x32, axis=-1, keepdims=True) + NORM_EPS)
    return (y * g.astype(jnp.float32)).astype(x.dtype)


def ada_norm(x, c, g, w_ada, b_ada):
    mod = jax.nn.silu(c) @ w_ada + b_ada
    shift, scale, gate = jnp.split(mod[:, None, :], 3, axis=-1)
    return rmsnorm(x, g) * (1 + scale) + shift, gate


def sqrelu_mlp(h, w_up, w_down):
    u = jax.nn.relu(h @ w_up)
    return (u * u) @ w_down


def sb_attend(q, k, v, q_pos, k_pos, bias):
    z = (jnp.einsum('bqhd,bkhd->bhqk', q, k).astype(jnp.float32) * (SB_HEAD_DIM ** -0.5)
         + bias.astype(jnp.float32)[None, :, None, None])
    mask = k_pos[None, :] < q_pos[:, None]
    log_beta = jax.nn.log_sigmoid(z)
    log_1mb = jnp.where(mask, log_beta - z, 0.0)
    tail = lax.cumsum(log_1mb, axis=3, reverse=True) - log_1mb
    a = jnp.where(mask, jnp.exp(log_beta + tail), 0.0)
    return jnp.einsum('bhqk,bkhd->bqhd', a.astype(v.dtype), v)


def sb_qkv(h, w_qkv):
    B, T, _ = h.shape
    qkv = (h @ w_qkv).reshape(B, T, 3, SB_HEADS, SB_HEAD_DIM)
    return qkv[:, :, 0], qkv[:, :, 1], qkv[:, :, 2]


def sb_mixer_prompt(h, w_qkv, w_o, bias):
    B, S, D = h.shape
    q, k, v = sb_qkv(h, w_qkv)
    nb = S // Q_BLOCK
    qb = jnp.moveaxis(q.reshape(B, nb, Q_BLOCK, SB_HEADS, SB_HEAD_DIM), 1, 0)
    k_pos = jnp.arange(S)

    def one_block(args):
        bi, q_blk = args
        q_pos = bi * Q_BLOCK + jnp.arange(Q_BLOCK)
        return sb_attend(q_blk, k, v, q_pos, k_pos, bias)

    o = lax.map(one_block, (jnp.arange(nb), qb))
    o = jnp.moveaxis(o, 0, 1).reshape(B, S, D)
    return o @ w_o, k, v


def sb_mixer_sample(h, cache_k, cache_v, layer, page_table, w_qkv, w_o, bias):
    B, T, D = h.shape
    q, k, v = sb_qkv(h, w_qkv)
    past_k = cache_k[layer][page_table]
    past_v = cache_v[layer][page_table]
    past_len = past_k.shape[1] * past_k.shape[2]
    past_k = past_k.reshape(B, past_len, SB_HEADS, SB_HEAD_DIM)
    past_v = past_v.reshape(B, past_len, SB_HEADS, SB_HEAD_DIM)
    k_all = jnp.concatenate([past_k, k.astype(past_k.dtype)], axis=1)
    v_all = jnp.concatenate([past_v, v.astype(past_v.dtype)], axis=1)
    q_pos = past_len + jnp.arange(T)
    k_pos = jnp.arange(past_len + T)
    o = sb_attend(q, k_all.astype(q.dtype), v_all.astype(q.dtype), q_pos, k_pos,
                  bias).reshape(B, T, D)
    return o @ w_o, k, v


def rwkv7_mixer(h, h_last, S0, mu, w_r, w_k, w_v, w_o, w0, w1, w2, a0, a1, a2,
                g1, g2, k_k, k_a, r_k, ln_g, ln_b):
    B, T, D = h.shape
    prev = jnp.concatenate([h_last[:, None, :].astype(h.dtype), h[:, :-1]], axis=1)
    xx = prev - h
    xr, xw, xk, xv, xa, xg = (h + xx * mu[n] for n in range(6))
    r = xr @ w_r
    k = xk @ w_k
    v = xv @ w_v
    w = -jax.nn.softplus(-(w0 + jnp.tanh(xw @ w1) @ w2)) - 0.5
    decay = jnp.exp(-jnp.exp(w.astype(jnp.float32)))
    a = jax.nn.sigmoid(a0 + (xa @ a1) @ a2)
    g = jax.nn.sigmoid(xg @ g1) @ g2

    def heads(t):
        return t.reshape(B, T, RW_HEADS, RW_HEAD_DIM).astype(jnp.float32)

    kk = heads(k * k_k)
    kk = kk / jnp.maximum(jnp.sqrt(jnp.sum(kk * kk, axis=-1, keepdims=True)), 1e-12)
    k_h = heads(k * (1 + (a - 1) * k_a))
    r_h, v_h, a_h, w_h = heads(r), heads(v), heads(a), heads(decay)

    def step(S, inp):
        r_t, w_t, k_t, v_t, kk_t, b_t = inp
        sa = jnp.einsum('bhij,bhj->bhi', S, kk_t)
        S = (S * w_t[:, :, None, :] - sa[..., None] * b_t[:, :, None, :]
             + v_t[..., None] * k_t[:, :, None, :])
        return S, jnp.einsum('bhij,bhj->bhi', S, r_t)

    xs = tuple(jnp.moveaxis(t, 1, 0) for t in (r_h, w_h, k_h, v_h, kk, kk * a_h))
    S_T, y = lax.scan(step, S0.astype(jnp.float32), xs)
    y = jnp.moveaxis(y, 0, 1)
    mean = jnp.mean(y, axis=-1, keepdims=True)
    var = jnp.mean(jnp.square(y - mean), axis=-1, keepdims=True)
    y = ((y - mean) * lax.rsqrt(var + GN_EPS)).reshape(B, T, D) * ln_g + ln_b
    bonus = jnp.sum(r_h * k_h * r_k, axis=-1, keepdims=True) * v_h
    y = (y + bonus.reshape(B, T, D)).astype(h.dtype)
    return (y * g) @ w_o, S_T.astype(S0.dtype), h[:, -1]


def setup_inputs(seed: int = 0) -> dict:
    key = jax.random.key(seed)
    ks = iter(jax.random.split(key, 48))
    f32 = jnp.float32
    D = D_MODEL

    def nrm(shape, scale=1.0):
        return scale * jax.random.normal(next(ks), shape, f32)

    def gain(shape):
        return 1.0 + 0.05 * jax.random.normal(next(ks), shape, f32)

    n_pages = PAST_LEN // PAGE_SIZE
    n_used = DEC_BATCH * n_pages
    n_phys = n_used + max(n_used // 4, 1)
    x_prompt = nrm((BATCH, SEQ, D))
    x_sample = nrm((DEC_BATCH, DEC_SEQ, D))
    cache_k = nrm((N_SB_LAYERS, n_phys, PAGE_SIZE, SB_HEADS, SB_HEAD_DIM))
    cache_v = nrm((N_SB_LAYERS, n_phys, PAGE_SIZE, SB_HEADS, SB_HEAD_DIM))
    state_wkv = nrm((N_RW_LAYERS, DEC_BATCH, RW_HEADS, RW_HEAD_DIM, RW_HEAD_DIM), 0.3)
    state_shift = nrm((N_RW_LAYERS, DEC_BATCH, D))
    page_table = jax.random.permutation(next(ks), n_phys)[:n_used].reshape(
        DEC_BATCH, n_pages).astype(jnp.int32)
    c_prompt = nrm((BATCH, D))
    c_sample = nrm((DEC_BATCH, D))
    sb_bias = (-jnp.linspace(3.0, 9.0, SB_HEADS, dtype=f32)[None, :]
               + nrm((N_SB_LAYERS, SB_HEADS), 0.1))
    return {
        'x_prompt': x_prompt, 'x_sample': x_sample,
        'cache_k': cache_k, 'cache_v': cache_v,
        'state_wkv': state_wkv, 'state_shift': state_shift,
        'page_table': page_table,
        'c_prompt': c_prompt, 'c_sample': c_sample,
        'norm_mix': gain((DEPTH, D)),
        'ada_w_mix': nrm((DEPTH, D, 3 * D), D ** -0.5),
        'ada_b_mix': nrm((DEPTH, 3 * D), 0.02),
        'norm_ffn': gain((DEPTH, D)),
        'ada_w_ffn': nrm((DEPTH, D, 3 * D), D ** -0.5),
        'ada_b_ffn': nrm((DEPTH, 3 * D), 0.02),
        'w_up': nrm((DEPTH, D, D_FF), D ** -0.5),
        'w_down': nrm((DEPTH, D_FF, D), D_FF ** -0.5),
        'sb_w_qkv': nrm((N_SB_LAYERS, D, 3 * D), D ** -0.5),
        'sb_w_o': nrm((N_SB_LAYERS, D, D), D ** -0.5),
        'sb_bias': sb_bias,
        'rw_mu': jax.random.uniform(next(ks), (N_RW_LAYERS, 6, D), f32),
        'rw_w_r': nrm((N_RW_LAYERS, D, D), D ** -0.5),
        'rw_w_k': nrm((N_RW_LAYERS, D, D), D ** -0.5),
        'rw_w_v': nrm((N_RW_LAYERS, D, D), D ** -0.5),
        'rw_w_o': nrm((N_RW_LAYERS, D, D), D ** -0.5),
        'rw_w0': nrm((N_RW_LAYERS, D), 0.5),
        'rw_w1': nrm((N_RW_LAYERS, D, DECAY_LORA), D ** -0.5),
        'rw_w2': nrm((N_RW_LAYERS, DECAY_LORA, D), DECAY_LORA ** -0.5),
        'rw_a0': nrm((N_RW_LAYERS, D), 0.1),
        'rw_a1': nrm((N_RW_LAYERS, D, AAA_LORA), D ** -0.5),
        'rw_a2': nrm((N_RW_LAYERS, AAA_LORA, D), AAA_LORA ** -0.5),
        'rw_g1': nrm((N_RW_LAYERS, D, GATE_LORA), D ** -0.5),
        'rw_g2': nrm((N_RW_LAYERS, GATE_LORA, D), GATE_LORA ** -0.5),
        'rw_k_k': 0.85 + nrm((N_RW_LAYERS, D), 0.05),
        'rw_k_a': 1.0 + nrm((N_RW_LAYERS, D), 0.05),
        'rw_r_k': nrm((N_RW_LAYERS, RW_HEADS, RW_HEAD_DIM), 0.1),
        'rw_ln_g': gain((N_RW_LAYERS, D)),
        'rw_ln_b': nrm((N_RW_LAYERS, D), 0.02),
        'final_norm': gain((D,)),
    }


def reference(x_prompt, x_sample, cache_k, cache_v, state_wkv, state_shift, page_table,
              c_prompt, c_sample, norm_mix, ada_w_mix, ada_b_mix, norm_ffn, ada_w_ffn,
              ada_b_ffn, w_up, w_down, sb_w_qkv, sb_w_o, sb_bias, rw_mu, rw_w_r, rw_w_k,
              rw_w_v, rw_w_o, rw_w0, rw_w1, rw_w2, rw_a0, rw_a1, rw_a2, rw_g1, rw_g2,
              rw_k_k, rw_k_a, rw_r_k, rw_ln_g, rw_ln_b, final_norm):
    xp, xs = x_prompt, x_sample
    kp_l, vp_l, ks_l, vs_l = [], [], [], []
    wp_l, sp_l, ws_l, ss_l = [], [], [], []
    for i in range(DEPTH):
        j = i // N_MIXERS
        hp, gp = ada_norm(xp, c_prompt, norm_mix[i], ada_w_mix[i], ada_b_mix[i])
        hs, gs = ada_norm(xs, c_sample, norm_mix[i], ada_w_mix[i], ada_b_mix[i])
        if i % N_MIXERS == 0:
            op, kp, vp = sb_mixer_prompt(hp, sb_w_qkv[j], sb_w_o[j], sb_bias[j])
            os_, ks, vs = sb_mixer_sample(hs, cache_k, cache_v, j, page_table,
                                          sb_w_qkv[j], sb_w_o[j], sb_bias[j])
            kp_l.append(kp); vp_l.append(vp); ks_l.append(ks); vs_l.append(vs)
        else:
            rw = (rw_mu[j], rw_w_r[j], rw_w_k[j], rw_w_v[j], rw_w_o[j], rw_w0[j], rw_w1[j],
                  rw_w2[j], rw_a0[j], rw_a1[j], rw_a2[j], rw_g1[j], rw_g2[j], rw_k_k[j],
                  rw_k_a[j], rw_r_k[j], rw_ln_g[j], rw_ln_b[j])
            zero_shift = jnp.zeros_like(hp[:, 0])
            zero_state = jnp.zeros((hp.shape[0],) + state_wkv.shape[2:], state_wkv.dtype)
            op, wp, sp = rwkv7_mixer(hp, zero_shift, zero_state, *rw)
            os_, ws, ss = rwkv7_mixer(hs, state_shift[j], state_wkv[j], *rw)
            wp_l.append(wp); sp_l.append(sp); ws_l.append(ws); ss_l.append(ss)
        xp = xp + gp * op
        xs = xs + gs * os_
        hp, gp = ada_norm(xp, c_prompt, norm_ffn[i], ada_w_ffn[i], ada_b_ffn[i])
        hs, gs = ada_norm(xs, c_sample, norm_ffn[i], ada_w_ffn[i], ada_b_ffn[i])
        xp = xp + gp * sqrelu_mlp(hp, w_up[i], w_down[i])
        xs = xs + gs * sqrelu_mlp(hs, w_up[i], w_down[i])
    y_prompt = rmsnorm(xp, final_norm)
    y_sample = rmsnorm(xs, final_norm)
    return (y_prompt, y_sample, jnp.stack(kp_l), jnp.stack(vp_l), jnp.stack(ks_l),
            jnp.stack(vs_l), jnp.stack(wp_l), jnp.stack(sp_l), jnp.stack(ws_l),
            jnp.stack(ss_l))
```

```python
import numpy as np
from contextlib import ExitStack
import concourse.bass as bass
import concourse.mybir as mybir
from concourse.bass_utils import run_bass_kernel_spmd

F32 = mybir.dt.float32
BF16 = mybir.dt.bfloat16
I32 = mybir.dt.int32
AF = mybir.ActivationFunctionType
ALU = mybir.AluOpType

D = 1024
NCORES = 8
NS = 4
TS = 8
NST = NS * TS
NSEQ = 1 + NS
EPS = 1e-6


class Cfg:
    def __init__(self, SEQ=4096, NPG=64, NPHYS=2560, debug=False, upto=99, split=False):
        self.SEQ, self.NPG, self.NPHYS, self.debug, self.upto = SEQ, NPG, NPHYS, debug, upto
        self.split = split
        import os as _os
        self.dbg = set(_os.environ.get('KDBG', '').split(','))
        self.kstop = int(_os.environ.get('KSTOP', '99'))
        self.nsmax = int(_os.environ.get('NSMAX', '99'))


class Prog:
    def __init__(self, nc, stack):
        self.nc = nc
        self.stack = stack
        self.eng = {"pe": nc.tensor, "dve": nc.vector, "act": nc.scalar,
                    "pool": nc.gpsimd, "sp": nc.sync}
        self.sem = {k: stack.enter_context(nc.semaphore("e_" + k)) for k in self.eng}
        self.cnt = {k: 0 for k in self.eng}
        self.seen = {k: {} for k in self.eng}
        self.lastw = {}
        self.readers = {}
        self.dsem = {}
        self.nsem = 0
        self.excl = set()

    def _wait(self, e, dep):
        sem, val, src = dep
        if src == e and e == "pe":
            return
        name = id(sem)
        if self.seen[e].get(name, 0) >= val:
            return
        self.eng[e].wait_ge(sem, val)
        self.seen[e][name] = val

    def _deps(self, e, reads, writes):
        for k in reads:
            if k in self.lastw:
                self._wait(e, self.lastw[k])
        for k in writes:
            if k in self.lastw:
                self._wait(e, self.lastw[k])
            for d in self.readers.get(k, {}).values():
                self._wait(e, d)

    def _record(self, dep, reads, writes):
        for k in writes:
            self.lastw[k] = dep
            self.readers[k] = {}
        for k in reads:
            if k in writes:
                continue
            self.readers.setdefault(k, {})[id(dep[0])] = dep

    def op(self, e, fn, reads=(), writes=()):
        xr = [k for k in reads if k in self.excl]
        if xr:
            reads = [k for k in reads if k not in self.excl]
            writes = list(writes) + [k for k in xr if k not in writes]
        self._deps(e, reads, writes)
        ins = fn(self.eng[e])
        self.cnt[e] += 1
        ins.then_inc(self.sem[e], 1)
        self._record((self.sem[e], self.cnt[e], e), reads, writes)
        return ins

    def dma(self, e, out, in_, reads=(), writes=(), semkey=None, indirect=None, **kw):
        self._deps(e, reads, writes)
        key = semkey if semkey is not None else (writes[0] if writes else reads[0])
        if key not in self.dsem:
            self.dsem[key] = [self.stack.enter_context(self.nc.semaphore("d%d" % self.nsem)), 0]
            self.nsem += 1
        ds = self.dsem[key]
        if indirect is not None:
            ins = self.eng[e].indirect_dma_start(out=out, out_offset=None, in_=in_,
                                                 in_offset=indirect, **kw)
        else:
            ins = self.eng[e].dma_start(out=out, in_=in_, **kw)
        ds[1] += 16
        ins.then_inc(ds[0], 16)
        dep = (ds[0], ds[1], "dma")
        self._record(dep, reads, writes)
        return dep

    def wait_all(self, e, deps):
        for d in deps:
            self._wait(e, d)

    def barrier(self):
        deps = [(self.sem[k], self.cnt[k], k) for k in self.eng if self.cnt[k] > 0]
        deps += [(ds[0], ds[1], "dma") for ds in self.dsem.values()]
        for e in self.eng:
            for d in deps:
                if d[2] == e:
                    continue
                self._wait(e, d)


def build_nc(cfg):
    SEQ, NPG, NPHYS = cfg.SEQ, cfg.NPG, cfg.NPHYS
    NT = SEQ // 128
    NQ = SEQ // 512
    nc = bass.Bass("TRN2", target_bir_lowering=False)

    def din(name, shape, dt=F32):
        return nc.dram_tensor(name, list(shape), dt, kind="ExternalInput").ap()

    def dout(name, shape, dt=F32):
        return nc.dram_tensor(name, list(shape), dt, kind="ExternalOutput").ap()

    def dscr(name, shape, dt=BF16):
        return nc.dram_tensor(name, list(shape), dt, kind="Internal").ap()

    xp = din("xp", [SEQ, D])
    xs = din("xs", [NST, D])
    cvec = din("cvec", [NSEQ, D])
    norm_mix = din("norm_mix", [2, D])
    norm_ffn = din("norm_ffn", [2, D])
    ada_w_mix = din("ada_w_mix", [2, D, 3 * D])
    ada_b_mix = din("ada_b_mix", [2, 3 * D])
    ada_w_ffn = din("ada_w_ffn", [2, D, 3 * D])
    ada_b_ffn = din("ada_b_ffn", [2, 3 * D])
    w_up = din("w_up", [2, D, 4 * D])
    w_down = din("w_down", [2, 4 * D, D])
    w_qkv = din("w_qkv", [D, 3 * D])
    w_o = din("w_o", [D, D])
    sb_bias = din("sb_bias", [16])
    if not cfg.split:
        cache_k = din("cache_k", [NPHYS * 128, D])
        cache_v = din("cache_v", [NPHYS * 128, D])
        ptab = din("ptab", [NS * NPG], I32)
    else:
        oTs_in = din("oTs_in", [128, 8, NST])
    state_wkv = din("state_wkv", [NS, 16, 64, 64])
    state_shift = din("state_shift", [NS, D])
    rw_mu = din("rw_mu", [6, D])
    rw_w_r = din("rw_w_r", [D, D])
    rw_w_k = din("rw_w_k", [D, D])
    rw_w_v = din("rw_w_v", [D, D])
    rw_w_o = din("rw_w_o", [D, D])
    rw_w0 = din("rw_w0", [D])
    rw_w1 = din("rw_w1", [D, 64])
    rw_w2 = din("rw_w2", [64, D])
    rw_a0 = din("rw_a0", [D])
    rw_a1 = din("rw_a1", [D, 64])
    rw_a2 = din("rw_a2", [64, D])
    rw_g1 = din("rw_g1", [D, 160])
    rw_g2 = din("rw_g2", [160, D])
    rw_k_k = din("rw_k_k", [D])
    rw_k_a = din("rw_k_a", [D])
    rw_r_k = din("rw_r_k", [D])
    rw_ln_g = din("rw_ln_g", [D])
    rw_ln_b = din("rw_ln_b", [D])
    final_norm = din("final_norm", [D])

    kp = dout("kp", [SEQ, D])
    vp = dout("vp", [SEQ, D])
    if not cfg.split:
        ks = dout("ks", [NST, D])
        vs = dout("vs", [NST, D])
    if cfg.debug:
        xl0p = dout("xl0p", [SEQ, D])
        xl0s = dout("xl0s", [NST, D])
    else:
        xl0p = dscr("xl0p", [SEQ, D], F32)
        xl0s = dscr("xl0s", [NST, D], F32)
    yp = dout("yp", [SEQ, D])
    ys = dout("ys", [NST, D])
    wkvp = dout("wkvp", [16, 64, 64])
    shp = dout("shp", [D])
    wkvs = dout("wkvs", [NS, 16, 64, 64])
    shs = dout("shs", [NS, D])

    Wqkv_b = dscr("Wqkv_b", [D, 3 * D])
    Wo_b = dscr("Wo_b", [D, D])
    Wup_b = [dscr("Wup_b%d" % i, [D, 4 * D]) for i in range(2)]
    Wdn_b = [dscr("Wdn_b%d" % i, [4 * D, D]) for i in range(2)]
    Wam_b = [dscr("Wam_b%d" % i, [D, 3 * D]) for i in range(2)]
    Waf_b = [dscr("Waf_b%d" % i, [D, 3 * D]) for i in range(2)]
    qT_d = dscr("qT_d", [128, 8, SEQ])
    kT_d = dscr("kT_d", [128, 8, SEQ])
    V_d = dscr("V_d", [SEQ, D])
    oT_d = dscr("oT_d", [128, 8, SEQ])
    ygT_d = dscr("ygT_d", [128, 8, SEQ])
    Wr_b = dscr("Wr_b", [D, D])
    Wk_b = dscr("Wk_b", [D, D])
    Wv_b = dscr("Wv_b", [D, D])
    Wrwo_b = dscr("Wrwo_b", [D, D])

    with ExitStack() as st:
        P = Prog(nc, st)

        uniq = [0]

        def sbt(stack, name, shape, dt=F32):
            uniq[0] += 1
            return stack.enter_context(nc.sbuf_tensor("%s_u%d" % (name, uniq[0]), list(shape), dt))

        def pst(stack, name, shape, dt=F32):
            uniq[0] += 1
            P.excl.add(name)
            esz = 4 if dt == F32 else 2
            n = int(np.prod(shape[1:]))
            per_bank = 2048 // esz
            npad = ((n + per_bank - 1) // per_bank) * per_bank
            t_ = stack.enter_context(nc.psum_tensor("%s_u%d" % (name, uniq[0]), [128, npad], dt))
            v = t_[:, 0:n]
            if len(shape) == 3:
                v = v.rearrange("p (a b) -> p a b", b=shape[2])
            elif len(shape) == 4:
                v = v.rearrange("p (a b c) -> p a b c", b=shape[2], c=shape[3])
            return v

        out_deps = []

        ident = sbt(st, "ident", [128, 128], F32)
        identb = sbt(st, "identb", [128, 128], BF16)
        negU = sbt(st, "negU", [128, 128], BF16)
        negO = sbt(st, "negO", [128, 128], BF16)
        biasb = sbt(st, "biasb", [128, 16], F32)
        cT = sbt(st, "cT", [128, 8, NSEQ], F32)
        scT = sbt(st, "scT", [128, 8, NSEQ], BF16)
        modT = [sbt(st, "modT%d" % i, [128, 16, NSEQ], F32) for i in range(4)]
        Acoef = [sbt(st, "Acoef%d" % i, [128, 8, NSEQ], F32) for i in range(4)]
        gate_bc = [sbt(st, "gate%d" % (i % 2), [128, NSEQ, D], BF16) for i in range(2)] * 2
        gsm = [sbt(st, "gsm%d" % (i % 2), [NST, D], BF16) for i in range(2)] * 2

        qTs = sbt(st, "qTs", [128, 8, NST], BF16)
        oTs = sbt(st, "oTs", [128, 8, NST], BF16)
        ygs = sbt(st, "ygs", [128, 8, NST], BF16)
        P.op("pool", lambda e: e.memset(ident[:], 0.0), writes=["ident"])
        P.op("pool", lambda e: e.affine_select(
            out=ident[:], in_=ident[:], pattern=[[-1, 128]], compare_op=ALU.not_equal,
            fill=1.0, base=0, channel_multiplier=1), reads=["ident"], writes=["ident"])
        P.op("pool", lambda e: e.tensor_copy(out=identb[:], in_=ident[:]), reads=["ident"], writes=["identb"])
        P.op("pool", lambda e: e.memset(negO[:], -1.0), writes=["negO"])
        P.op("pool", lambda e: e.affine_select(
            out=negU[:], in_=negO[:], pattern=[[-1, 128]], compare_op=ALU.is_ge,
            fill=0.0, base=0, channel_multiplier=1), reads=["negO"], writes=["negU"])
        P.dma("sp", biasb[:], sb_bias.partition_broadcast(128), writes=["biasb"])
        with nc.allow_non_contiguous_dma(reason="tiny transposed parameter loads"):
            for s_ in range(NSEQ):
                P.dma("sp", cT[:, :, s_], cvec[s_].rearrange("(k p) -> p k", p=128), writes=["cT"])
        P.op("act", lambda e: e.activation(out=scT[:], in_=cT[:], func=AF.Silu),
             reads=["cT"], writes=["scT"])

        with ExitStack() as ph:
            NB = 4
            bnc = [sbt(ph, "bnc%d" % i, [128, 2048], BF16) for i in range(NB)]
            cnt = [0]

            def cast_w(src, dst, rows, cols):
                cw = 2048 if cols % 2048 == 0 else 1536 if cols % 1536 == 0 else 1024
                for r in range(rows // 128):
                    for c in range(cols // cw):
                        b = cnt[0] % NB
                        cnt[0] += 1
                        key = "bnc%d" % b
                        P.dma("pool", bnc[b][:, 0:cw], src[r * 128:(r + 1) * 128, c * cw:(c + 1) * cw],
                              writes=[key], max_dma_last_dim=4096)
                        P.dma("sp", dst[r * 128:(r + 1) * 128, c * cw:(c + 1) * cw], bnc[b][:, 0:cw],
                              reads=[key], semkey=key + "o")

            cast_w(w_qkv, Wqkv_b, D, 3 * D)
            cast_w(w_o, Wo_b, D, D)
            cast_w(rw_w_r, Wr_b, D, D)
            cast_w(rw_w_k, Wk_b, D, D)
            cast_w(rw_w_v, Wv_b, D, D)
            cast_w(rw_w_o, Wrwo_b, D, D)
            for i in range(2):
                cast_w(ada_w_mix[i], Wam_b[i], D, 3 * D)
                cast_w(ada_w_ffn[i], Waf_b[i], D, 3 * D)
                cast_w(w_up[i], Wup_b[i], D, 4 * D)
                cast_w(w_down[i], Wdn_b[i], 4 * D, D)
            P.barrier()

        def compute_ada(idx, Wb, bvec, gvec):
            with ExitStack() as ph:
                wada = sbt(ph, "wada", [128, 8, 3 * D], BF16)
                screp = sbt(ph, "screp", [128, 8, NSEQ, 128], BF16)
                P.op("dve", lambda e: e.tensor_copy(
                    out=screp[:], in_=scT[:].unsqueeze(3).to_broadcast([128, 8, NSEQ, 128])),
                    reads=["scT"], writes=["screp"])
                adabT = sbt(ph, "adabT", [128, 24], F32)
                gT = sbt(ph, "gT", [128, 8], F32)
                adab_bc = sbt(ph, "adab_bc", [128, D], F32)
                psm = pst(ph, "psm", [128, 16, 8], F32)
                psg = [pst(ph, "psg%d" % i, [128, 512], F32) for i in range(2)]
                with nc.allow_non_contiguous_dma(reason="tiny transposed parameter loads"):
                    P.dma("sp", gT[:], gvec.rearrange("(k p) -> p k", p=128), writes=["gT"])
                    P.dma("sp", adabT[:], bvec.rearrange("(k p) -> p k", p=128), writes=["adabT"])
                P.dma("sp", adab_bc[:], bvec[2 * D:3 * D].partition_broadcast(128), writes=["adab_bc"])
                for k in range(8):
                    P.dma("sp", wada[:, k, :], Wb[k * 128:(k + 1) * 128, :], writes=["wada"])
                M, A, G = "modT%d" % idx, "Acoef%d" % idx, "gate%d" % (idx % 2)
                for oc in range(16):
                    for k in range(8):
                        P.op("pe", lambda e: e.matmul(
                            psm[:, oc, 0:NSEQ], lhsT=wada[:, k, oc * 128:(oc + 1) * 128],
                            rhs=scT[:, k, :], start=(k == 0), stop=(k == 7)),
                            reads=["wada", "scT"], writes=["psm"])
                P.op("dve", lambda e: e.tensor_tensor(
                    out=modT[idx][:], in0=psm[:, :, 0:NSEQ],
                    in1=adabT[:, 0:16].unsqueeze(2).to_broadcast([128, 16, NSEQ]), op=ALU.add),
                    reads=["psm", "adabT"], writes=[M])
                P.op("dve", lambda e: e.tensor_scalar(
                    out=Acoef[idx][:], in0=modT[idx][:, 8:16, :], scalar1=1.0, scalar2=None, op0=ALU.add),
                    reads=[M], writes=[A])
                P.op("dve", lambda e: e.tensor_tensor(
                    out=Acoef[idx][:], in0=Acoef[idx][:],
                    in1=gT[:].unsqueeze(2).to_broadcast([128, 8, NSEQ]), op=ALU.mult),
                    reads=[A, "gT"], writes=[A])
                for s_ in range(NSEQ):
                    for n in range(2):
                        pg = psg[n]
                        PG = "psg%d" % n
                        for k in range(8):
                            P.op("pe", lambda e: e.matmul(
                                pg[:], lhsT=screp[:, k, s_, :],
                                rhs=wada[:, k, 2 * D + n * 512:2 * D + (n + 1) * 512],
                                start=(k == 0), stop=(k == 7)),
                                reads=["wada", "screp"], writes=[PG])
                        P.op("dve", lambda e: e.tensor_tensor(
                            out=gate_bc[idx][:, s_, n * 512:(n + 1) * 512], in0=pg[:],
                            in1=adab_bc[:, n * 512:(n + 1) * 512], op=ALU.add),
                            reads=[PG, "adab_bc"], writes=[G])
                for s_ in range(NS):
                    P.dma("sp", gsm[idx][s_ * TS:(s_ + 1) * TS, :], gate_bc[idx][0:TS, 1 + s_, :],
                          reads=[G], writes=["gsm%d" % (idx % 2)])
                P.barrier()

        if cfg.upto >= 1:
            compute_ada(0, Wam_b[0], ada_b_mix[0], norm_mix[0])
            compute_ada(1, Waf_b[0], ada_b_ffn[0], norm_ffn[0])

        def norm_to_hT(tl, X, np_, idx, groups, hT_ap_fn, HT, pstr):
            b = tl["b"]
            J, SS, RS, XN = "junk%d" % b, "ssq%d" % b, "rstd%d" % b, "xn%d" % b
            P.op("act", lambda e: e.activation(out=tl["junk"][0:np_, :], in_=tl["x"][0:np_, :],
                                               func=AF.Square, accum_out=tl["ssq"][0:np_, :]),
                 reads=[X], writes=[J, SS])
            P.op("act", lambda e: e.activation(out=tl["rstd"][0:np_, :], in_=tl["ssq"][0:np_, :],
                                               func=AF.Sqrt, scale=1.0 / D, bias=EPS),
                 reads=[SS], writes=[RS])
            P.op("dve", lambda e: e.reciprocal(out=tl["rstd"][0:np_, :], in_=tl["rstd"][0:np_, :]),
                 reads=[RS], writes=[RS])
            P.op("dve", lambda e: e.tensor_scalar(out=tl["xn"][0:np_, :], in0=tl["x"][0:np_, :],
                                                  scalar1=tl["rstd"][0:np_, :], scalar2=None,
                                                  op0=ALU.mult),
                 reads=[X, RS], writes=[XN])
            for k in range(8 if cfg.kstop >= 3 else 0):
                pt, PT = pstr[k % 2]
                P.op("pe", lambda e: e.transpose(out=pt[:, 0:np_], in_=tl["xn"][0:np_, k * 128:(k + 1) * 128],
                                                 identity=ident[0:np_, 0:np_]),
                     reads=[XN, "ident"], writes=[PT])
                for (sq, c0, c1) in groups:
                    P.op("dve", lambda e: e.tensor_scalar(
                        out=hT_ap_fn(k, c0, c1), in0=pt[:, c0:c1],
                        scalar1=Acoef[idx][:, k, sq:sq + 1], scalar2=modT[idx][:, k, sq:sq + 1],
                        op0=ALU.mult, op1=ALU.add),
                        reads=[PT, "Acoef%d" % idx, "modT%d" % idx], writes=[HT])

        stiles = [("p", q, 512) for q in range(NQ)] + [("s", 0, NST)]
        sgroups = [(1 + s, s * TS, (s + 1) * TS) for s in range(NS)]

        st_att = ExitStack()
        kTs = sbt(st_att, "kTs", [128, 8, NS, 128], BF16)
        Vs = sbt(st_att, "Vs", [128, NS, D], BF16)
        if cfg.upto >= 2:
            with ExitStack() as ph:
                wqkv = sbt(ph, "wqkv", [128, 8, 3 * D], BF16)
                NBUF = 2
                tls = [dict(b=i, x=sbt(ph, "xt%d" % i, [128, D]), junk=sbt(ph, "junk%d" % i, [128, D], F32 if "junk32" in cfg.dbg else BF16),
                            ssq=sbt(ph, "ssq%d" % i, [128, 1]), rstd=sbt(ph, "rstd%d" % i, [128, 1]),
                            xn=sbt(ph, "xn%d" % i, [128, D])) for i in range(NBUF)]
                hT = [sbt(ph, "hT%d" % i, [128, 8, 512], BF16) for i in range(2)]
                qk_st = [sbt(ph, "qkst%d" % i, [128, 512], BF16) for i in range(4)]
                kvo = [sbt(ph, "kvo%d" % i, [128, 2 * D], F32) for i in range(2)]
                vbf = [sbt(ph, "vbf%d" % i, [128, D], BF16) for i in range(2)]
                ptr_ = [pst(ph, "ptr%d" % i, [128, 512], F32) for i in range(2)]
                pqk = [pst(ph, "pqk%d" % i, [128, 512], F32) for i in range(2)]
                pkv = [pst(ph, "pkv%d" % i, [128, 512], F32) for i in range(4)]
                pstr = [(ptr_[i], "ptr%d" % i) for i in range(2)]
                for k in range(8):
                    P.dma("sp", wqkv[:, k, :], Wqkv_b[k * 128:(k + 1) * 128, :], writes=["wqkv"])
                if "nomemset" not in cfg.dbg:
                    P.op("pool", lambda e: e.memset(kTs[:], 0.0), writes=["kTs"])
                    P.op("pool", lambda e: e.memset(Vs[:], 0.0), writes=["Vs"])
                it = 0
                nq = 0
                for si, (kind, qi, ntok) in enumerate(stiles):
                    if kind == "s" and ("nosample" in cfg.dbg or cfg.split):
                        continue
                    hb = si % 2
                    HT = "hT%d" % hb
                    nsub = (ntok + 127) // 128
                    for sub in range(nsub):
                        tl = tls[it % NBUF]
                        it += 1
                        b = tl["b"]
                        X = "xt%d" % b
                        np_ = min(128, ntok)
                        if kind == "p":
                            t = qi * 4 + sub
                            P.dma("sp", tl["x"][0:np_, :], xp[t * 128:(t + 1) * 128, :], writes=[X])
                            groups = [(0, 0, 128)]
                        else:
                            P.dma("sp", tl["x"][0:np_, :], xs, writes=[X])
                            groups = sgroups
                        if cfg.kstop < 2:
                            continue
                        norm_to_hT(tl, X, np_, 0, groups,
                                   lambda k, c0, c1: hT[hb][:, k, sub * 128 + c0:sub * 128 + c1], HT, pstr)
                        if cfg.kstop < 4:
                            continue
                        kb_ = (it - 1) % 2
                        KV, VB = "kvo%d" % kb_, "vbf%d" % kb_
                        for n in range(4):
                            po, PO = pkv[n], "pkv%d" % n
                            for k in range(8):
                                P.op("pe", lambda e: e.matmul(
                                    po[0:np_, :], lhsT=hT[hb][:, k, sub * 128:sub * 128 + np_],
                                    rhs=wqkv[:, k, D + n * 512:D + (n + 1) * 512],
                                    start=(k == 0), stop=(k == 7)),
                                    reads=[HT, "wqkv"], writes=[PO])
                            P.op("act", lambda e: e.copy(out=kvo[kb_][0:np_, n * 512:(n + 1) * 512], in_=po[0:np_, :]),
                                 reads=[PO], writes=[KV])
                            if n >= 2 and "novbf" not in cfg.dbg:
                                if "vbfpsum" in cfg.dbg:
                                    P.op("dve", lambda e: e.tensor_copy(
                                        out=vbf[kb_][0:np_, (n - 2) * 512:(n - 1) * 512], in_=po[0:np_, :]),
                                        reads=[PO, KV], writes=[VB])
                                else:
                                    P.op("pool", lambda e: e.tensor_copy(
                                        out=vbf[kb_][0:np_, (n - 2) * 512:(n - 1) * 512],
                                        in_=kvo[kb_][0:np_, n * 512:(n + 1) * 512]),
                                        reads=[KV], writes=[VB])
                        if cfg.kstop < 5:
                            continue
                        if kind == "p":
                            out_deps.append(P.dma("sp", kp[t * 128:(t + 1) * 128, :], kvo[kb_][:, 0:D],
                                                  reads=[KV], semkey="ok%d" % kb_))
                            out_deps.append(P.dma("sp", vp[t * 128:(t + 1) * 128, :], kvo[kb_][:, D:2 * D],
                                                  reads=[KV], semkey="ov%d" % kb_))
                            if "novd" not in cfg.dbg:
                                P.dma("sp", V_d[t * 128:(t + 1) * 128, :], vbf[kb_][:], reads=[VB], semkey="ovb%d" % kb_)
                        else:
                            out_deps.append(P.dma("sp", ks, kvo[kb_][0:NST, 0:D], reads=[KV], semkey="ok%d" % kb_))
                            out_deps.append(P.dma("sp", vs, kvo[kb_][0:NST, D:2 * D], reads=[KV], semkey="ov%d" % kb_))
                            for s_ in range(NS):
                                P.dma("sp", Vs[0:TS, s_, :], vbf[kb_][s_ * TS:(s_ + 1) * TS, :],
                                      reads=[VB], writes=["Vs"], semkey="vs_mv")
                    for fc in range(0 if "noqk" in cfg.dbg else 16):
                        pq, PQ = pqk[fc % 2], "pqk%d" % (fc % 2)
                        for k in range(8):
                            P.op("pe", lambda e: e.matmul(
                                pq[:, 0:ntok], lhsT=wqkv[:, k, fc * 128:(fc + 1) * 128],
                                rhs=hT[hb][:, k, 0:ntok], start=(k == 0), stop=(k == 7)),
                                reads=[HT, "wqkv"], writes=[PQ])
                        if kind == "p":
                            sb_ = nq % 4
                            nq += 1
                            QS = "qkst%d" % sb_
                            if fc < 8:
                                P.op("act", lambda e: e.activation(out=qk_st[sb_][:], in_=pq[:], func=AF.Copy, scale=0.125),
                                     reads=[PQ], writes=[QS])
                                P.dma("sp", qT_d[:, fc, qi * 512:(qi + 1) * 512], qk_st[sb_][:], reads=[QS], semkey=QS + "o")
                            else:
                                P.op("dve", lambda e: e.tensor_copy(out=qk_st[sb_][:], in_=pq[:]),
                                     reads=[PQ], writes=[QS])
                                P.dma("sp", kT_d[:, fc - 8, qi * 512:(qi + 1) * 512], qk_st[sb_][:], reads=[QS], semkey=QS + "o")
                        else:
                            if fc < 8:
                                P.op("act", lambda e: e.activation(out=qTs[:, fc, :], in_=pq[:, 0:NST], func=AF.Copy, scale=0.125),
                                     reads=[PQ], writes=["qTs"])
                            else:
                                P.op("dve", lambda e: e.tensor_copy(
                                    out=kTs[:, fc - 8, :, 0:TS],
                                    in_=pq[:, 0:NST].rearrange("p (s t) -> p s t", t=TS)),
                                    reads=[PQ], writes=["kTs"])
                P.barrier()

        def attn_pipeline(tmp, nblocks, emit_z, emit_av, mask_of, bias_ap_of, ncols=512):
            raise NotImplementedError

        if cfg.upto >= 3:
            with ExitStack() as ph:
                kT_all = sbt(ph, "kT_all", [128, 4, SEQ], BF16)
                V_all = sbt(ph, "V_all", [128, NT, 512], BF16)
                qT_t = [sbt(ph, "qT_t%d" % i, [128, 4, 512], BF16) for i in range(2)]
                masks = sbt(ph, "masks", [128, 4, 512], BF16)
                ones_m = sbt(ph, "ones_m", [128, 512], BF16)
                e_t = [sbt(ph, "e_t%d" % i, [128, 512], F32) for i in range(2)]
                sp_t = [[sbt(ph, "sp_t%d_%d" % (i, j), [128, 512], BF16) for j in range(2)] for i in range(2)]
                a_t = [[sbt(ph, "a_t%d_%d" % (i, j), [128, 512], BF16) for j in range(2)] for i in range(2)]
                SP32 = [sbt(ph, "SP32_%d" % i, [128, 512], F32) for i in range(2)]
                SPbf = [sbt(ph, "SPbf_%d" % i, [128, 512], BF16) for i in range(2)]
                o_st = [sbt(ph, "o_st%d" % i, [128, 512], BF16) for i in range(2)]
                Pz = [pst(ph, "Pz%d" % i, [128, 512], F32) for i in range(2)]
                Pa = [pst(ph, "Pa%d" % i, [128, 512], F32) for i in range(2)]
                Po = [pst(ph, "Po%d" % i, [128, 512], F32) for i in range(2)]

                P.op("pool", lambda e: e.memset(ones_m[:], 1.0), writes=["ones_m"])
                for j in range(4):
                    P.op("pool", lambda e: e.affine_select(
                        out=masks[:, j, :], in_=ones_m[:], pattern=[[1, 512]], compare_op=ALU.is_gt,
                        fill=0.0, base=-128 * j, channel_multiplier=-1), reads=["ones_m"], writes=["masks"])

                for gh, qi in [(a_, b_) for a_ in range(2) for b_ in range(NQ)]:
                    if qi == 0:
                        for gl in range(4):
                            P.dma("sp", kT_all[:, gl, :], kT_d[:, gh * 4 + gl, :], writes=["kT_all"])
                        for t in range(NT):
                            P.dma("sp", V_all[:, t, :], V_d[t * 128:(t + 1) * 128, gh * 512:(gh + 1) * 512],
                                  writes=["V_all"])
                    qb = (gh * NQ + qi) % 2
                    QT = "qT_t%d" % qb
                    for gl in range(4):
                        P.dma("sp", qT_t[qb][:, gl, :], qT_d[:, gh * 4 + gl, qi * 512:(qi + 1) * 512], writes=[QT])
                    nblk = 4 * qi + 4
                    for gl in range(4):
                        g = gh * 4 + gl
                        po, PO = Po[g % 2], "Po%d" % (g % 2)
                        for h2 in range(2):
                            P.op("pool", lambda e: e.memset(SP32[h2][:], 0.0), writes=["SP32_%d" % h2])
                        for bi, kb in enumerate(range(nblk - 1, -1, -1)):
                            j = kb - 4 * qi
                            first, last = (bi == 0), (kb == 0)
                            pr = slice(None)
                            for h2 in range(2):
                                pr = slice(h2 * 64, (h2 + 1) * 64)
                                P.op("pe", lambda e: e.matmul(
                                    Pz[h2][:], lhsT=kT_all[pr, gl, kb * 128:(kb + 1) * 128],
                                    rhs=qT_t[qb][pr, gl, :], start=True, stop=True),
                                    reads=["kT_all", QT], writes=["Pz%d" % h2])
                            for h2 in range(2):
                                h = 2 * g + h2
                                S = "sp_t%d_%d" % (h2, bi % 2)
                                spt = sp_t[h2][bi % 2]
                                P.op("act", lambda e: e.activation(out=e_t[h2][:], in_=Pz[h2][:], func=AF.Exp,
                                                                   bias=biasb[:, h:h + 1]),
                                     reads=["Pz%d" % h2, "biasb"], writes=["e_t%d" % h2])
                                P.op("act", lambda e: e.activation(out=spt[:], in_=e_t[h2][:], func=AF.Ln, bias=1.0),
                                     reads=["e_t%d" % h2], writes=[S])
                                if j >= 0:
                                    P.op("dve", lambda e: e.tensor_tensor(out=spt[:], in0=spt[:], in1=masks[:, j, :],
                                                                          op=ALU.mult),
                                         reads=[S, "masks"], writes=[S])
                            for h2 in range(2):
                                pr = slice(h2 * 64, (h2 + 1) * 64)
                                S = "sp_t%d_%d" % (h2, bi % 2)
                                spt = sp_t[h2][bi % 2]
                                PA = "Pa%d" % h2
                                P.op("pe", lambda e: e.matmul(
                                    Pa[h2][:], lhsT=kT_all[pr, gl, kb * 128:(kb + 1) * 128],
                                    rhs=qT_t[qb][pr, gl, :], start=True, stop=False),
                                    reads=["kT_all", QT], writes=[PA])
                                P.op("pe", lambda e: e.matmul(
                                    Pa[h2][:], lhsT=negU[:], rhs=spt[:], start=False, stop=first),
                                    reads=["negU", S], writes=[PA])
                                if not first:
                                    P.op("pe", lambda e: e.matmul(
                                        Pa[h2][:], lhsT=negO[:], rhs=SPbf[h2][:], start=False, stop=True),
                                        reads=["negO", "SPbf_%d" % h2], writes=[PA])
                            for h2 in range(2):
                                h = 2 * g + h2
                                A = "a_t%d_%d" % (h2, bi % 2)
                                at = a_t[h2][bi % 2]
                                P.op("act", lambda e: e.activation(out=at[:], in_=Pa[h2][:], func=AF.Exp,
                                                                   bias=biasb[:, h:h + 1]),
                                     reads=["Pa%d" % h2, "biasb"], writes=[A])
                                if j >= 0:
                                    P.op("dve", lambda e: e.tensor_tensor(out=at[:], in0=at[:], in1=masks[:, j, :],
                                                                          op=ALU.mult),
                                         reads=[A, "masks"], writes=[A])
                            for h2 in range(2):
                                h = 2 * g + h2
                                at = a_t[h2][bi % 2]
                                P.op("pe", lambda e: e.matmul(
                                    po[h2 * 64:(h2 + 1) * 64, :], lhsT=V_all[:, kb, (h - 8 * gh) * 64:(h - 8 * gh + 1) * 64],
                                    rhs=at[:], start=first, stop=last),
                                    reads=["V_all", "a_t%d_%d" % (h2, bi % 2)], writes=[PO])
                            if not last:
                                for h2 in range(2):
                                    S = "sp_t%d_%d" % (h2, bi % 2)
                                    P.op("pool", lambda e: e.tensor_tensor(out=SP32[h2][:], in0=SP32[h2][:],
                                                                           in1=sp_t[h2][bi % 2][:], op=ALU.add),
                                         reads=["SP32_%d" % h2, S], writes=["SP32_%d" % h2])
                                    P.op("pool", lambda e: e.tensor_copy(out=SPbf[h2][:], in_=SP32[h2][:]),
                                         reads=["SP32_%d" % h2], writes=["SPbf_%d" % h2])
                        OS = "o_st%d" % (g % 2)
                        P.op("dve", lambda e: e.tensor_copy(out=o_st[g % 2][:], in_=po[:]), reads=[PO], writes=[OS])
                        P.dma("sp", oT_d[:, g, qi * 512:(qi + 1) * 512], o_st[g % 2][:], reads=[OS], semkey=OS + "o")
                P.barrier()

        if cfg.split:
            P.dma("pool", oTs[:], oTs_in, writes=["oTs"])
        if cfg.upto >= 4 and not cfg.split:
            with ExitStack() as ph:
                NCOL = NS * 16 * TS
                pt_i = sbt(ph, "pt_i", [128, NS * NPG], I32)
                pt_f = sbt(ph, "pt_f", [128, NS * NPG], F32)
                iota_f = sbt(ph, "iota_f", [128, 1], F32)
                iota_i = sbt(ph, "iota_i", [128, 1], I32)
                idx = sbt(ph, "idx", [128, NS * NPG], I32)
                NG = 2
                Kpg = [[sbt(ph, "Kpg%d_%d" % (i, s), [128, D], BF16) for s in range(NS)] for i in range(NG)]
                Vpg = [[sbt(ph, "Vpg%d_%d" % (i, s), [128, D], BF16) for s in range(NS)] for i in range(NG)]
                kTp = [sbt(ph, "kTp%d" % i, [128, NS, 8, 128], BF16) for i in range(NG)]
                maskn = sbt(ph, "maskn", [128, NS * 16, TS], BF16)
                ones_m = sbt(ph, "ones_m2", [128, NCOL], BF16)
                bias_bc = sbt(ph, "bias_bc", [128, NS, 16, TS], F32)
                zsb = sbt(ph, "zsb", [128, NCOL], F32)
                e_t = sbt(ph, "es_t", [128, NCOL], F32)
                sp_t = [sbt(ph, "sps_t%d" % i, [128, NCOL], BF16) for i in range(2)]
                t1 = sbt(ph, "t1s", [128, NCOL], F32)
                a_t = [sbt(ph, "as_t%d" % i, [128, NCOL], BF16) for i in range(2)]
                SP32 = sbt(ph, "SPs32", [128, NCOL], F32)
                SPbf = sbt(ph, "SPsbf", [128, NCOL], BF16)
                zero_l = sbt(ph, "zero_l", [128, 128], BF16)
                Pz = [pst(ph, "Pzs%d" % i, [128, NCOL], F32) for i in range(2)]
                Pa = [pst(ph, "Pas%d" % i, [128, NCOL], F32) for i in range(2)]
                Po = pst(ph, "Pos", [128, NS * 8 * TS], F32)
                Ptr = [pst(ph, "Ptrs%d" % i, [128, 8, 128], BF16) for i in range(2)]

                P.dma("sp", pt_i[:], ptab.partition_broadcast(128), writes=["pt_i"])
                P.op("dve", lambda e: e.tensor_copy(out=pt_f[:], in_=pt_i[:]), reads=["pt_i"], writes=["pt_f"])
                P.op("pool", lambda e: e.iota(iota_i[:], pattern=[[0, 1]], base=0, channel_multiplier=1),
                     writes=["iota_i"])
                P.op("dve", lambda e: e.tensor_copy(out=iota_f[:], in_=iota_i[:]), reads=["iota_i"], writes=["iota_f"])
                P.op("dve", lambda e: e.tensor_scalar(out=pt_f[:], in0=pt_f[:], scalar1=128.0, scalar2=iota_f[:, 0:1],
                                                      op0=ALU.mult, op1=ALU.add),
                     reads=["pt_f", "iota_f"], writes=["pt_f"])
                P.op("dve", lambda e: e.tensor_copy(out=idx[:], in_=pt_f[:]), reads=["pt_f"], writes=["idx"])
                P.op("pool", lambda e: e.memset(ones_m[:], 1.0), writes=["ones_m2"])
                P.op("pool", lambda e: e.memset(zero_l[:], 0.0), writes=["zero_l"])
                P.op("pool", lambda e: e.memset(SP32[:], 0.0), writes=["SPs32"])
                P.op("pool", lambda e: e.affine_select(
                    out=maskn[:], in_=ones_m[:].rearrange("p (c t) -> p c t", t=TS),
                    pattern=[[0, NS * 16], [1, TS]], compare_op=ALU.is_gt,
                    fill=0.0, base=0, channel_multiplier=-1), reads=["ones_m2"], writes=["maskn"])
                P.op("dve", lambda e: e.tensor_copy(
                    out=bias_bc[:, 0, :, :], in_=biasb[:].unsqueeze(2).to_broadcast([128, 16, TS])),
                    reads=["biasb"], writes=["bias_bc"])
                for s_ in range(1, NS):
                    P.op("dve", lambda e: e.tensor_copy(out=bias_bc[:, s_, :, :], in_=bias_bc[:, 0, :, :]),
                         reads=["bias_bc"], writes=["bias_bc"])
                P.op("pe", lambda e: e.matmul(Po[:], lhsT=zero_l[:], rhs=ones_m[:, 0:NS * 8 * TS], start=True, stop=False),
                     reads=["zero_l", "ones_m2"], writes=["Pos"])

                def col(s_, g, h2):
                    return ((s_ * 8 + g) * 2 + h2) * TS

                blocks = ["new"] + list(range(NPG - 1, -1, -1))
                for bi, blk in enumerate(blocks):
                    gb = bi % NG
                    first, last = (bi == 0), (bi == len(blocks) - 1)
                    if blk == "new":
                        def k_lhsT(s_, g, h2):
                            return kTs[h2 * 64:(h2 + 1) * 64, g, s_, :]

                        def v_lhsT(s_, h):
                            return Vs[:, s_, h * 64:(h + 1) * 64]
                        KK, VK = ["kTs"], ["Vs"]
                    else:
                        KT = "kTp%d" % gb
                        for s_ in range(NS):
                            c_ = s_ * NPG + blk
                            P.dma("pool", Kpg[gb][s_][:], cache_k, writes=["Kpg%d_%d" % (gb, s_)],
                                  indirect=bass.IndirectOffsetOnAxis(ap=idx[:, c_:c_ + 1], axis=0), reads=["idx"])
                            P.dma("pool", Vpg[gb][s_][:], cache_v, writes=["Vpg%d_%d" % (gb, s_)],
                                  indirect=bass.IndirectOffsetOnAxis(ap=idx[:, c_:c_ + 1], axis=0), reads=["idx"])
                        for s_ in range(NS):
                            ptb, PTB = Ptr[s_ % 2], "Ptrs%d" % (s_ % 2)
                            for g in range(8):
                                P.op("pe", lambda e: e.transpose(out=ptb[:, g, :], in_=Kpg[gb][s_][:, g * 128:(g + 1) * 128],
                                                                 identity=identb[:]),
                                     reads=["Kpg%d_%d" % (gb, s_), "identb"], writes=[PTB])
                            if s_ % 2 == 0:
                                P.op("dve", lambda e: e.tensor_copy(out=kTp[gb][:, s_, :, :], in_=ptb[:]),
                                     reads=[PTB], writes=[KT])
                            else:
                                P.op("act", lambda e: e.copy(out=kTp[gb][:, s_, :, :], in_=ptb[:]),
                                     reads=[PTB], writes=[KT])

                        def k_lhsT(s_, g, h2):
                            return kTp[gb][h2 * 64:(h2 + 1) * 64, s_, g, :]

                        def v_lhsT(s_, h):
                            return Vpg[gb][s_][:, h * 64:(h + 1) * 64]
                        KK = [KT]
                        VK = ["Vpg%d_%d" % (gb, s_) for s_ in range(NS)]
                    pz, PZ = Pz[bi % 2], "Pzs%d" % (bi % 2)
                    pa, PA = Pa[bi % 2], "Pas%d" % (bi % 2)
                    for s_ in range(NS):
                        for g in range(8):
                            for h2 in range(2):
                                c0 = col(s_, g, h2)
                                P.op("pe", lambda e: e.matmul(
                                    pz[:, c0:c0 + TS], lhsT=k_lhsT(s_, g, h2),
                                    rhs=qTs[h2 * 64:(h2 + 1) * 64, g, s_ * TS:(s_ + 1) * TS],
                                    start=True, stop=True, skip_group_check=True),
                                    reads=KK + ["qTs"], writes=[PZ])
                    spt, S = sp_t[bi % 2], "sps_t%d" % (bi % 2)
                    at, A = a_t[bi % 2], "as_t%d" % (bi % 2)
                    P.op("dve", lambda e: e.tensor_tensor(out=zsb[:], in0=pz[:],
                                                          in1=bias_bc[:].rearrange("p s h t -> p (s h t)"), op=ALU.add),
                         reads=[PZ, "bias_bc"], writes=["zsb"])
                    P.op("act", lambda e: e.activation(out=e_t[:], in_=zsb[:], func=AF.Exp),
                         reads=["zsb"], writes=["es_t"])
                    P.op("act", lambda e: e.activation(out=spt[:], in_=e_t[:], func=AF.Ln, bias=1.0),
                         reads=["es_t"], writes=[S])
                    if blk == "new":
                        P.op("dve", lambda e: e.tensor_tensor(out=spt[:], in0=spt[:],
                                                              in1=maskn[:].rearrange("p c t -> p (c t)"), op=ALU.mult),
                             reads=[S, "maskn"], writes=[S])
                    P.op("pe", lambda e: e.matmul(pa[:], lhsT=negU[:], rhs=spt[:], start=True, stop=first),
                         reads=["negU", S], writes=[PA])
                    if not first:
                        P.op("pe", lambda e: e.matmul(pa[:], lhsT=negO[:], rhs=SPbf[:], start=False, stop=True),
                             reads=["negO", "SPsbf"], writes=[PA])
                    P.op("dve", lambda e: e.tensor_tensor(out=t1[:], in0=pa[:], in1=zsb[:], op=ALU.add),
                         reads=[PA, "zsb"], writes=["t1s"])
                    P.op("act", lambda e: e.activation(out=at[:], in_=t1[:], func=AF.Exp),
                         reads=["t1s"], writes=[A])
                    if blk == "new":
                        P.op("dve", lambda e: e.tensor_tensor(out=at[:], in0=at[:],
                                                              in1=maskn[:].rearrange("p c t -> p (c t)"), op=ALU.mult),
                             reads=[A, "maskn"], writes=[A])
                    for s_ in range(NS):
                        for g in range(8):
                            for h2 in range(2):
                                c0 = col(s_, g, h2)
                                oc = (s_ * 8 + g) * TS
                                P.op("pe", lambda e: e.matmul(
                                    Po[h2 * 64:(h2 + 1) * 64, oc:oc + TS], lhsT=v_lhsT(s_, 2 * g + h2),
                                    rhs=at[:, c0:c0 + TS], start=False, stop=False, skip_group_check=True),
                                    reads=VK + [A], writes=["Pos"])
                    if not last:
                        P.op("pool", lambda e: e.tensor_tensor(out=SP32[:], in0=SP32[:], in1=spt[:], op=ALU.add),
                             reads=["SPs32", S], writes=["SPs32"])
                        P.op("pool", lambda e: e.tensor_copy(out=SPbf[:], in_=SP32[:]),
                             reads=["SPs32"], writes=["SPsbf"])
                P.op("pe", lambda e: e.matmul(Po[:], lhsT=zero_l[:], rhs=ones_m[:, 0:NS * 8 * TS], start=False, stop=True),
                     reads=["zero_l", "ones_m2"], writes=["Pos"])
                P.op("dve", lambda e: e.tensor_copy(
                    out=oTs[:].rearrange("p g (s t) -> p s g t", t=TS),
                    in_=Po[:].rearrange("p (s g t) -> p s g t", g=8, t=TS)),
                    reads=["Pos"], writes=["oTs"])
                P.barrier()

        st_att.close()
        def ffn_phase(li, mix_idx, ffn_idx, Wo_src, get_oT, x_src_fn, x_dst_fn, final_out):
            with ExitStack() as ph:
                wo = sbt(ph, "wo", [128, 8, D], BF16)
                NBUF = 2
                tls = [dict(b=i, x=sbt(ph, "xt%d" % i, [128, D]), junk=sbt(ph, "junk%d" % i, [128, D], F32 if "junk32" in cfg.dbg else BF16),
                            ssq=sbt(ph, "ssq%d" % i, [128, 1]), rstd=sbt(ph, "rstd%d" % i, [128, 1]),
                            xn=sbt(ph, "xn%d" % i, [128, D])) for i in range(NBUF)]
                x1 = [sbt(ph, "x1_%d" % i, [128, D]) for i in range(4)]
                oT = [sbt(ph, "oT0", [128, 8, 512], BF16)] * 2
                h2T = [sbt(ph, "h2T0", [128, 8, 512], BF16)] * 2
                uT = sbt(ph, "uT", [128, 32, 512], BF16)
                rl = [sbt(ph, "rl%d" % i, [128, 512], BF16) for i in range(2)]
                NWB = 2
                wup = [sbt(ph, "wup%d" % i, [128, 8, 1024], BF16) for i in range(NWB)]
                wdn = [sbt(ph, "wdn%d" % i, [128, 8, 512], BF16) for i in range(NWB)]
                jts = [sbt(ph, "jt%d" % i, [128, 512], F32) for i in range(2)]
                ptr_ = [pst(ph, "ptr%d" % i, [128, 512], F32) for i in range(2)]
                pstr = [(ptr_[i], "ptr%d" % i) for i in range(2)]
                pmm = [pst(ph, "pmm%d" % i, [128, 512], F32) for i in range(4)]
                for k in range(8):
                    P.dma("sp", wo[:, k, :], Wo_src[k * 128:(k + 1) * 128, :], writes=["wo"])
                wcount = [0, 0]
                it = 0
                for si, (kind, qi, ntok) in enumerate(stiles):
                    hb = si % 2
                    OT, H2 = "oT0", "h2T0"
                    get_oT(kind, qi, ntok, oT[hb], OT)
                    nsub = (ntok + 127) // 128
                    np_ = min(128, ntok)
                    for sub in range(nsub):
                        tl = tls[it % NBUF]
                        it += 1
                        b = tl["b"]
                        X = "xt%d" % b
                        X1 = "x1_%d" % sub
                        x_src_fn(kind, qi, sub, tl["x"][0:np_, :], X)
                        for n in range(2):
                            pm, PM = pmm[n], "pmm%d" % n
                            for k in range(8):
                                P.op("pe", lambda e: e.matmul(
                                    pm[0:np_, :], lhsT=oT[hb][:, k, sub * 128:sub * 128 + np_],
                                    rhs=wo[:, k, n * 512:(n + 1) * 512], start=(k == 0), stop=(k == 7)),
                                    reads=[OT, "wo"], writes=[PM])
                            if kind == "p":
                                grp = [(0, 0, 128)]
                            else:
                                grp = sgroups
                            gmix = (gate_bc[mix_idx][0:np_, 0, n * 512:(n + 1) * 512] if kind == "p"
                                    else gsm[mix_idx][0:np_, n * 512:(n + 1) * 512])
                            P.op("dve", lambda e: e.tensor_tensor(
                                out=x1[sub][0:np_, n * 512:(n + 1) * 512], in0=pm[0:np_, :],
                                in1=gmix, op=ALU.mult),
                                reads=[PM, "gate%d" % (mix_idx % 2), "gsm%d" % (mix_idx % 2)], writes=[X1])
                            P.op("pool", lambda e: e.tensor_tensor(
                                out=x1[sub][0:np_, n * 512:(n + 1) * 512], in0=x1[sub][0:np_, n * 512:(n + 1) * 512],
                                in1=tl["x"][0:np_, n * 512:(n + 1) * 512], op=ALU.add),
                                reads=[X1, X], writes=[X1])
                        tl2 = dict(tl)
                        tl2["x"] = x1[sub]
                        norm_to_hT(tl2, X1, np_, ffn_idx, grp,
                                   lambda k, c0, c1: h2T[hb][:, k, sub * 128 + c0:sub * 128 + c1], H2, pstr)
                    for fg in range(4):
                        wb = wcount[0] % NWB
                        wcount[0] += 1
                        WU = "wup%d" % wb
                        for k in range(8):
                            P.dma("sp", wup[wb][:, k, :], Wup_b[li][k * 128:(k + 1) * 128, fg * 1024:(fg + 1) * 1024],
                                  writes=[WU])
                        for fl in range(8):
                            fc = fg * 8 + fl
                            pm, PM = pmm[fc % 4], "pmm%d" % (fc % 4)
                            for k in range(8):
                                P.op("pe", lambda e: e.matmul(
                                    pm[:, 0:ntok], lhsT=wup[wb][:, k, fl * 128:(fl + 1) * 128],
                                    rhs=h2T[hb][:, k, 0:ntok], start=(k == 0), stop=(k == 7)),
                                    reads=[WU, H2], writes=[PM])
                            R = "rl%d" % (fc % 2)
                            P.op("act", lambda e: e.activation(out=rl[fc % 2][:, 0:ntok], in_=pm[:, 0:ntok], func=AF.Relu),
                                 reads=[PM], writes=[R])
                            P.op("pool", lambda e: e.tensor_tensor(out=uT[:, fc, 0:ntok], in0=rl[fc % 2][:, 0:ntok],
                                                                   in1=rl[fc % 2][:, 0:ntok], op=ALU.mult),
                                 reads=[R], writes=["uT"])
                    for n in range(2):
                        for sub in range(nsub):
                            pass
                    for sub in range(nsub):
                        pass
                    for n in range(2):
                        accs = [(pmm[s4], "pmm%d" % s4) for s4 in range(nsub)]
                        for fg in range(4):
                            wb = wcount[1] % NWB
                            wcount[1] += 1
                            WD = "wdn%d" % wb
                            for fl in range(8):
                                fc = fg * 8 + fl
                                P.dma("sp", wdn[wb][:, fl, 0:512],
                                      Wdn_b[li][fc * 128:(fc + 1) * 128, n * 512:(n + 1) * 512], writes=[WD])
                            for sub in range(nsub):
                                pm, PM = accs[sub]
                                for fl in range(8):
                                    fc = fg * 8 + fl
                                    P.op("pe", lambda e: e.matmul(
                                        pm[0:np_, :], lhsT=uT[:, fc, sub * 128:sub * 128 + np_],
                                        rhs=wdn[wb][:, fl, 0:512], start=(fc == 0), stop=(fc == 31)),
                                        reads=["uT", WD], writes=[PM])
                        for sub in range(nsub):
                            pm, PM = accs[sub]
                            X1 = "x1_%d" % sub
                            J = "jt%d" % (sub % 2)
                            jt = jts[sub % 2]
                            gffn = (gate_bc[ffn_idx][0:np_, 0, n * 512:(n + 1) * 512] if kind == "p"
                                    else gsm[ffn_idx][0:np_, n * 512:(n + 1) * 512])
                            P.op("dve", lambda e: e.tensor_tensor(
                                out=jt[0:np_, 0:512], in0=pm[0:np_, :], in1=gffn, op=ALU.mult),
                                reads=[PM, "gate%d" % (ffn_idx % 2), "gsm%d" % (ffn_idx % 2)], writes=[J])
                            P.op("pool", lambda e: e.tensor_tensor(
                                out=x1[sub][0:np_, n * 512:(n + 1) * 512], in0=x1[sub][0:np_, n * 512:(n + 1) * 512],
                                in1=jt[0:np_, 0:512], op=ALU.add),
                                reads=[X1, J], writes=[X1])
                    for sub in range(nsub):
                        x_dst_fn(kind, qi, sub, x1[sub][0:np_, :], "x1_%d" % sub)
                P.barrier()

        def get_oT0(kind, qi, ntok, oT_tile, OT):
            if kind == "p":
                for g in range(8):
                    P.dma("sp", oT_tile[:, g, :], oT_d[:, g, qi * 512:(qi + 1) * 512], writes=[OT])
            else:
                P.op("dve", lambda e: e.tensor_copy(out=oT_tile[:, :, 0:NST], in_=oTs[:]), reads=["oTs"], writes=[OT])

        def x_src0(kind, qi, sub, dst, X):
            if kind == "p":
                t = qi * 4 + sub
                P.dma("sp", dst, xp[t * 128:(t + 1) * 128, :], writes=[X])
            else:
                P.dma("sp", dst, xs, writes=[X])

        def x_dst0(kind, qi, sub, src, X1):
            if kind == "p":
                t = qi * 4 + sub
                out_deps.append(P.dma("sp", xl0p[t * 128:(t + 1) * 128, :], src, reads=[X1], semkey="xo%d" % sub))
            else:
                out_deps.append(P.dma("sp", xl0s, src, reads=[X1], semkey="xo%d" % sub))

        if cfg.upto >= 5:
            ffn_phase(0, 0, 1, Wo_b, get_oT0, x_src0, x_dst0, False)

        if cfg.upto >= 6:
            compute_ada(2, Wam_b[1], ada_b_mix[1], norm_mix[1])
            compute_ada(3, Waf_b[1], ada_b_ffn[1], norm_ffn[1])
        if cfg.upto >= 6:
            with ExitStack() as ph:
                TT = 256
                CL = 64
                GN_EPS = 1e-5 * 64
                dtiles = [("p", q, TT) for q in range(SEQ // TT)] + [("s", 0, NST)]
                Wst = sbt(ph, "Wst", [128, 8, D], BF16)
                Wr = Wk = Wv = Wst

                def load_W(src):
                    for k in range(8):
                        P.dma("sp", Wst[:, k, :], src[k * 128:(k + 1) * 128, :], writes=["Wst"])
                w1b = sbt(ph, "w1b", [128, 8, 64], BF16)
                a1b = sbt(ph, "a1b", [128, 8, 64], BF16)
                g1b = sbt(ph, "g1b", [128, 8, 160], BF16)
                w2b = sbt(ph, "w2b", [64, D], BF16)
                a2b = sbt(ph, "a2b", [64, D], BF16)
                g2b0 = sbt(ph, "g2b0", [128, D], BF16)
                g2b1 = sbt(ph, "g2b1", [32, D], BF16)
                muT = sbt(ph, "muT", [128, 6, 8], F32)
                w0n = sbt(ph, "w0n", [128, 8], F32)
                a0T = sbt(ph, "a0T", [128, 8], F32)
                kkT_ = sbt(ph, "kkT_", [128, 8], F32)
                kaT_ = sbt(ph, "kaT_", [128, 8], F32)
                rkT_ = sbt(ph, "rkT_", [128, 8], F32)
                lngT = sbt(ph, "lngT", [128, 8], F32)
                lnbT = sbt(ph, "lnbT", [128, 8], F32)
                sshT = sbt(ph, "sshT", [128, 8, NS], F32)
                blkb = sbt(ph, "blkb", [128, 128], BF16)
                blk32 = sbt(ph, "blk32", [128, 128], F32)
                MK = sbt(ph, "MK", [128, 128], BF16)
                MB = sbt(ph, "MB", [128, 128], BF16)
                MN = sbt(ph, "MN", [128, 64], BF16)
                I64 = sbt(ph, "I64", [128, 64], BF16)
                onesb = sbt(ph, "onesb", [128, 128], BF16)
                rmask = [sbt(ph, "rmask0", [128, TT], F32), sbt(ph, "rmask1", [128, NST], F32)]
                hTe = sbt(ph, "hTe", [128, 8, TT + 1], F32)
                lastcol = sbt(ph, "lastcol", [128, 8, 1], F32)
                prevS = sbt(ph, "prevS", [128, 8, NST], F32)
                xx = sbt(ph, "xx", [128, 8, TT], F32)
                xnb = [sbt(ph, "xnb%d" % i, [128, 8, TT], BF16) for i in range(2)]
                rT = sbt(ph, "rT", [128, 8, TT], BF16)
                kT = sbt(ph, "kT", [128, 8, TT], BF16)
                vT = sbt(ph, "vT", [128, 8, TT], BF16)
                aT = sbt(ph, "aT", [128, 8, TT], BF16)
                gT_ = sbt(ph, "gT_", [128, 8, TT], BF16)
                ld = sbt(ph, "ld", [128, 8, TT], F32)
                lw_t = sbt(ph, "lw_t", [128, TT], BF16)
                lg2_t = sbt(ph, "lg2_t", [32, TT], BF16)
                NCH = TT // CL
                Vtok = sbt(ph, "Vtok", [128, NCH, 512], BF16)
                KR = sbt(ph, "KR", [128, 8, NCH, 2, CL], BF16)
                BK = sbt(ph, "BK", [128, 8, NCH, 2, CL], BF16)
                gam = sbt(ph, "gam", [128, 8, NCH], F32)
                bon = sbt(ph, "bon", [128, 8, TT], BF16)
                Ysb = sbt(ph, "Ysb", [128, 8, NCH, CL], F32)
                H32 = sbt(ph, "H32", [128, 8, 64], F32)
                Hbf = sbt(ph, "Hbf", [128, 8, 64], BF16)
                Htmp = sbt(ph, "Htmp", [128, 8, 64], F32)
                tA = [sbt(ph, "tA%d" % i, [128, TT], F32) for i in range(6)]
                tB = [sbt(ph, "tB%d" % i, [128, TT], BF16) for i in range(3)]
                GBm = sbt(ph, "GBm", [128, 8, 128], BF16)
                GKm = sbt(ph, "GKm", [128, 8, 128], BF16)
                Xl = [sbt(ph, "Xl%d" % i, [128, 8, 64], F32) for i in range(2)]
                Yl = [sbt(ph, "Yl%d" % i, [128, 8, 64], F32) for i in range(2)]
                Pl = [sbt(ph, "Pl%d" % i, [128, 8, 64], F32) for i in range(2)]
                Wsb = sbt(ph, "Wsb", [128, 8, 64], F32)
                Usb = sbt(ph, "Usb", [128, 8, 64], BF16)
                Ktok = sbt(ph, "Ktok", [128, 8, 64], BF16)
                nBtok = sbt(ph, "nBtok", [128, 8, 64], BF16)
                stT = sbt(ph, "stT", [64, 8, 128], F32)
                NBUF = 1
                tls = [dict(b=i, x=sbt(ph, "xt%d" % i, [128, D]), junk=sbt(ph, "junk%d" % i, [128, D], BF16),
                            ssq=sbt(ph, "ssq%d" % i, [128, 1]), rstd=sbt(ph, "rstd%d" % i, [128, 1]),
                            xn=sbt(ph, "xn%d" % i, [128, D])) for i in range(NBUF)]
                ptr_ = [pst(ph, "ptr0", [128, 512], F32)] * 2
                pstr = [(ptr_[i], "ptr0") for i in range(2)]
                pm_ = [pst(ph, "pmd%d" % i, [128, 512], F32) for i in range(2)]
                Pg = pst(ph, "Pg", [128, 8, 128], F32)
                Pn = [pst(ph, "Pn%d" % i, [128, 8, 64], F32) for i in range(2)]
                Ptk = pst(ph, "Ptk", [128, 2, 8, 64], BF16)

                for k in range(8):
                    P.dma("pool", w1b[:, k, :], rw_w1[k * 128:(k + 1) * 128, :], writes=["w1b"])
                    P.dma("pool", a1b[:, k, :], rw_a1[k * 128:(k + 1) * 128, :], writes=["a1b"])
                    P.dma("pool", g1b[:, k, :], rw_g1[k * 128:(k + 1) * 128, :], writes=["g1b"])
                P.dma("pool", w2b[:], rw_w2, writes=["w2b"], max_dma_last_dim=4096)
                P.dma("pool", a2b[:], rw_a2, writes=["a2b"], max_dma_last_dim=4096)
                P.dma("pool", g2b0[:], rw_g2[0:128, :], writes=["g2b0"], max_dma_last_dim=4096)
                P.dma("pool", g2b1[:], rw_g2[128:160, :], writes=["g2b1"], max_dma_last_dim=4096)
                with nc.allow_non_contiguous_dma(reason="tiny transposed parameter loads"):
                    for n in range(6):
                        P.dma("sp", muT[:, n, :], rw_mu[n].rearrange("(k p) -> p k", p=128), writes=["muT"])
                    for (tile_, src_, key_) in [(w0n, rw_w0, "w0n"), (a0T, rw_a0, "a0T"), (kkT_, rw_k_k, "kkT_"),
                                                (kaT_, rw_k_a, "kaT_"), (rkT_, rw_r_k, "rkT_"),
                                                (lngT, rw_ln_g, "lngT"), (lnbT, rw_ln_b, "lnbT")]:
                        P.dma("sp", tile_[:], src_.rearrange("(k p) -> p k", p=128), writes=[key_])
                    for s_ in range(NS):
                        P.dma("sp", sshT[:, :, s_], state_shift[s_].rearrange("(k p) -> p k", p=128), writes=["sshT"])
                P.op("dve", lambda e: e.tensor_scalar(out=w0n[:], in0=w0n[:], scalar1=-1.0, scalar2=None, op0=ALU.mult),
                     reads=["w0n"], writes=["w0n"])
                P.op("pool", lambda e: e.memset(onesb[:], 1.0), writes=["onesb"])
                P.op("pool", lambda e: e.memset(blkb[:], 0.0), writes=["blkb"])
                for h2 in range(2):
                    pr = slice(h2 * 64, (h2 + 1) * 64)
                    P.op("pool", lambda e: e.memset(blkb[pr, h2 * 64:(h2 + 1) * 64], 1.0), reads=["blkb"], writes=["blkb"])
                    P.op("pool", lambda e: e.affine_select(
                        out=MK[pr, 0:64], in_=onesb[pr, 0:64], pattern=[[1, 64]], compare_op=ALU.is_gt,
                        fill=0.0, base=0, channel_multiplier=-1), reads=["onesb"], writes=["MK"])
                    P.op("pool", lambda e: e.affine_select(
                        out=MK[pr, 64:128], in_=onesb[pr, 0:64], pattern=[[1, 64]], compare_op=ALU.is_ge,
                        fill=0.0, base=0, channel_multiplier=-1), reads=["onesb"], writes=["MK"])
                    P.op("pool", lambda e: e.affine_select(
                        out=MN[pr, :], in_=negO[pr, 0:64], pattern=[[-1, 64]], compare_op=ALU.is_gt,
                        fill=0.0, base=0, channel_multiplier=1), reads=["negO"], writes=["MN"])
                    P.op("pool", lambda e: e.tensor_copy(out=I64[pr, :], in_=identb[pr, h2 * 64:(h2 + 1) * 64]),
                         reads=["identb"], writes=["I64"])
                P.op("pool", lambda e: e.tensor_copy(out=blk32[:], in_=blkb[:]), reads=["blkb"], writes=["blk32"])
                P.op("dve", lambda e: e.tensor_scalar(out=MB[:], in0=MK[:], scalar1=-1.0, scalar2=None, op0=ALU.mult),
                     reads=["MK"], writes=["MB"])
                P.op("pool", lambda e: e.memset(rmask[0][:], 1.0), writes=["rmask0"])
                P.op("pool", lambda e: e.memset(rmask[0][:].rearrange("p (c t) -> p c t", t=CL)[:, :, 0:1], 0.0),
                     reads=["rmask0"], writes=["rmask0"])
                P.op("pool", lambda e: e.memset(rmask[1][:], 1.0), writes=["rmask1"])
                P.op("pool", lambda e: e.memset(rmask[1][:].rearrange("p (c t) -> p c t", t=TS)[:, :, 0:1], 0.0),
                     reads=["rmask1"], writes=["rmask1"])
                P.op("pool", lambda e: e.memset(H32[:], 0.0), writes=["H32"])
                P.op("pool", lambda e: e.memset(Hbf[:], 0.0), writes=["Hbf"])

                def hd(t, g, h2):
                    return t[h2 * 64:(h2 + 1) * 64, g, :]

                def store_state(dst):
                    for g in range(8):
                        P.op("pe", lambda e: e.transpose(out=Pg[0:64, g, :], in_=H32[:, g, :], identity=ident[:]),
                             reads=["H32", "ident"], writes=["Pg"])
                    P.op("dve", lambda e: e.tensor_copy(out=stT[:], in_=Pg[0:64, :, :]), reads=["Pg"], writes=["stT"])
                    return P.dma("sp", dst.rearrange("(g h) i j -> i g h j", h=2),
                                 stT[:].rearrange("i g (h j) -> i g h j", h=2), reads=["stT"], semkey="stTo")

                def load_state(src):
                    P.dma("sp", stT[:].rearrange("i g (h j) -> i g h j", h=2),
                          src.rearrange("(g h) i j -> i g h j", h=2), writes=["stT"])
                    for g in range(8):
                        P.op("pe", lambda e: e.transpose(out=Pn[0][:, g, :], in_=stT[:, g, :], identity=ident[0:64, 0:64]),
                             reads=["stT", "ident"], writes=["Pn0"])
                    P.op("dve", lambda e: e.tensor_copy(out=H32[:], in_=Pn[0][:]), reads=["Pn0"], writes=["H32"])
                    P.op("pool", lambda e: e.tensor_copy(out=Hbf[:], in_=H32[:]), reads=["H32"], writes=["Hbf"])

                it = 0
                for di, (kind, qi, ntok) in enumerate(dtiles):
                    cl = CL if kind == "p" else TS
                    nch = ntok // cl
                    rm = rmask[0] if kind == "p" else rmask[1]
                    RM = "rmask0" if kind == "p" else "rmask1"
                    nsub = (ntok + 127) // 128
                    np_ = min(128, ntok)
                    if kind == "p":
                        if qi == 0:
                            P.op("pool", lambda e: e.memset(hTe[:, :, 0:1], 0.0), writes=["hTe"])
                        else:
                            P.op("pool", lambda e: e.tensor_copy(out=hTe[:, :, 0:1], in_=lastcol[:]),
                                 reads=["lastcol"], writes=["hTe"])
                    for sub in range(nsub):
                        tl = tls[it % NBUF]
                        it += 1
                        X = "xt%d" % tl["b"]
                        if kind == "p":
                            t = qi * (TT // 128) + sub
                            P.dma("sp", tl["x"][0:np_, :], xl0p[t * 128:(t + 1) * 128, :], writes=[X])
                            groups = [(0, 0, 128)]
                        else:
                            P.dma("sp", tl["x"][0:np_, :], xl0s, writes=[X])
                            groups = sgroups
                        norm_to_hT(tl, X, np_, 2, groups,
                                   lambda k, c0, c1: hTe[:, k, 1 + sub * 128 + c0:1 + sub * 128 + c1], "hTe", pstr)
                    hcur = hTe[:, :, 1:1 + ntok]
                    if kind == "p":
                        hprev = hTe[:, :, 0:ntok]
                        PV = "hTe"
                    else:
                        P.op("pool", lambda e: e.tensor_copy(
                            out=prevS[:].rearrange("p k (s t) -> p k s t", t=TS)[:, :, :, 1:TS],
                            in_=hTe[:, :, 1:1 + NST].rearrange("p k (s t) -> p k s t", t=TS)[:, :, :, 0:TS - 1]),
                            reads=["hTe"], writes=["prevS"])
                        P.op("pool", lambda e: e.tensor_copy(
                            out=prevS[:].rearrange("p k (s t) -> p k s t", t=TS)[:, :, :, 0],
                            in_=sshT[:]), reads=["sshT", "prevS"], writes=["prevS"])
                        hprev = prevS[:]
                        PV = "prevS"
                    P.op("dve", lambda e: e.tensor_tensor(out=xx[:, :, 0:ntok], in0=hprev, in1=hcur, op=ALU.subtract),
                         reads=["hTe", PV], writes=["xx"])
                    with nc.allow_non_contiguous_dma(reason="tiny transposed outputs"):
                        if kind == "p" and qi == SEQ // TT - 1:
                            out_deps.append(P.dma("sp", shp.rearrange("(k p) -> p k", p=128), hTe[:, :, TT],
                                                  reads=["hTe"], semkey="sho"))
                        if kind == "s":
                            for s_ in range(NS):
                                out_deps.append(P.dma("sp", shs[s_].rearrange("(k p) -> p k", p=128),
                                                      hTe[:, :, 1 + s_ * TS + TS - 1], reads=["hTe"], semkey="sho"))

                    def make_xn(n, nb):
                        XN = "xnb%d" % nb
                        for k in range(8):
                            P.op("dve", lambda e: e.scalar_tensor_tensor(
                                out=xnb[nb][:, k, 0:ntok], in0=xx[:, k, 0:ntok], scalar=muT[:, n, k:k + 1],
                                in1=hTe[:, k, 1:1 + ntok], op0=ALU.mult, op1=ALU.add),
                                reads=["xx", "muT", "hTe"], writes=[XN])
                        return xnb[nb], XN

                    def proj_fm(W, WK, xn_, XN, fc, pm, PM, m0=0, m1=128, wcols=None):
                        for k in range(8):
                            P.op("pe", lambda e: e.matmul(
                                pm[0:m1 - m0, 0:ntok],
                                lhsT=(W[:, k, fc * 128 + m0:fc * 128 + m1] if wcols is None else W[:, k, wcols[0]:wcols[1]]),
                                rhs=xn_[:, k, 0:ntok], start=(k == 0), stop=(k == 7)),
                                reads=[WK, XN], writes=[PM])

                    xn_, XN = make_xn(0, 0)
                    load_W(Wr_b)
                    for fc in range(8):
                        pm, PM = pm_[fc % 2], "pmd%d" % (fc % 2)
                        proj_fm(Wr, "Wst", xn_, XN, fc, pm, PM)
                        P.op("act", lambda e: e.copy(out=rT[:, fc, 0:ntok], in_=pm[:, 0:ntok]), reads=[PM], writes=["rT"])
                    xn_, XN = make_xn(1, 1)
                    pm, PM = pm_[0], "pmd0"
                    proj_fm(w1b, "w1b", xn_, XN, 0, pm, PM, wcols=(0, 64), m1=64)
                    P.op("act", lambda e: e.activation(out=lw_t[0:64, 0:ntok], in_=pm[0:64, 0:ntok], func=AF.Tanh),
                         reads=[PM], writes=["lw_t"])
                    for fc in range(8):
                        pm, PM = pm_[(fc + 1) % 2], "pmd%d" % ((fc + 1) % 2)
                        P.op("pe", lambda e: e.matmul(pm[:, 0:ntok], lhsT=w2b[0:64, fc * 128:(fc + 1) * 128],
                                                      rhs=lw_t[0:64, 0:ntok], start=True, stop=True),
                             reads=["w2b", "lw_t"], writes=[PM])
                        P.op("act", lambda e: e.activation(out=tA[0][:, 0:ntok], in_=pm[:, 0:ntok], func=AF.Exp,
                                                           scale=-1.0, bias=w0n[:, fc:fc + 1]),
                             reads=[PM, "w0n"], writes=["tA0"])
                        P.op("act", lambda e: e.activation(out=tA[0][:, 0:ntok], in_=tA[0][:, 0:ntok], func=AF.Ln, bias=1.0),
                             reads=["tA0"], writes=["tA0"])
                        P.op("act", lambda e: e.activation(out=tA[0][:, 0:ntok], in_=tA[0][:, 0:ntok], func=AF.Exp,
                                                           scale=-1.0, bias=-0.5),
                             reads=["tA0"], writes=["tA0"])
                        P.op("dve", lambda e: e.tensor_scalar(out=ld[:, fc, 0:ntok], in0=tA[0][:, 0:ntok], scalar1=-1.0,
                                                              scalar2=None, op0=ALU.mult),
                             reads=["tA0"], writes=["ld"])
                    xn_, XN = make_xn(2, 0)
                    load_W(Wk_b)
                    for fc in range(8):
                        pm, PM = pm_[fc % 2], "pmd%d" % (fc % 2)
                        proj_fm(Wk, "Wst", xn_, XN, fc, pm, PM)
                        P.op("act", lambda e: e.copy(out=kT[:, fc, 0:ntok], in_=pm[:, 0:ntok]), reads=[PM], writes=["kT"])
                    xn_, XN = make_xn(3, 1)
                    load_W(Wv_b)
                    for fc in range(8):
                        pm, PM = pm_[fc % 2], "pmd%d" % (fc % 2)
                        proj_fm(Wv, "Wst", xn_, XN, fc, pm, PM)
                        P.op("act", lambda e: e.copy(out=vT[:, fc, 0:ntok], in_=pm[:, 0:ntok]), reads=[PM], writes=["vT"])
                    if kind == "s":
                        P.op("pool", lambda e: e.memset(Vtok[:], 0.0), writes=["Vtok"])
                    Wv4 = Wv[:].rearrange("p k (g h d) -> p k g h d", h=2, d=64)
                    for c in range(nch):
                        pm, PM = pm_[c % 2], "pmd%d" % (c % 2)
                        for h2 in range(2):
                            for k in range(8):
                                P.op("pe", lambda e: e.matmul(
                                    pm[h2 * 64:h2 * 64 + cl, :], lhsT=xn_[:, k, c * cl:(c + 1) * cl],
                                    rhs=Wv4[:, k, :, h2, :], start=(k == 0), stop=(k == 7)),
                                    reads=["Wst", XN], writes=[PM])
                        if cl == 64:
                            P.op("dve", lambda e: e.tensor_copy(out=Vtok[:, c, :], in_=pm[:]), reads=[PM], writes=["Vtok"])
                        else:
                            for h2 in range(2):
                                P.op("dve", lambda e: e.tensor_copy(out=Vtok[h2 * 64:h2 * 64 + cl, c, :],
                                                                    in_=pm[h2 * 64:h2 * 64 + cl, :]),
                                     reads=[PM], writes=["Vtok"])
                    xn_, XN = make_xn(4, 0)
                    pm, PM = pm_[0], "pmd0"
                    proj_fm(a1b, "a1b", xn_, XN, 0, pm, PM, wcols=(0, 64), m1=64)
                    P.op("act", lambda e: e.copy(out=lw_t[0:64, 0:ntok], in_=pm[0:64, 0:ntok]), reads=[PM], writes=["lw_t"])
                    for fc in range(8):
                        pm, PM = pm_[(fc + 1) % 2], "pmd%d" % ((fc + 1) % 2)
                        P.op("pe", lambda e: e.matmul(pm[:, 0:ntok], lhsT=a2b[0:64, fc * 128:(fc + 1) * 128],
                                                      rhs=lw_t[0:64, 0:ntok], start=True, stop=True),
                             reads=["a2b", "lw_t"], writes=[PM])
                        P.op("act", lambda e: e.activation(out=aT[:, fc, 0:ntok], in_=pm[:, 0:ntok], func=AF.Sigmoid,
                                                           bias=a0T[:, fc:fc + 1]),
                             reads=[PM, "a0T"], writes=["aT"])
                    xn_, XN = make_xn(5, 1)
                    pm, PM = pm_[0], "pmd0"
                    proj_fm(g1b, "g1b", xn_, XN, 0, pm, PM, wcols=(0, 128))
                    P.op("act", lambda e: e.activation(out=lw_t[:, 0:ntok], in_=pm[:, 0:ntok], func=AF.Sigmoid),
                         reads=[PM], writes=["lw_t"])
                    pm, PM = pm_[1], "pmd1"
                    proj_fm(g1b, "g1b", xn_, XN, 0, pm, PM, wcols=(128, 160), m1=32)
                    P.op("act", lambda e: e.activation(out=lg2_t[0:32, 0:ntok], in_=pm[0:32, 0:ntok], func=AF.Sigmoid),
                         reads=[PM], writes=["lg2_t"])
                    for fc in range(8):
                        pm, PM = pm_[fc % 2], "pmd%d" % (fc % 2)
                        P.op("pe", lambda e: e.matmul(pm[:, 0:ntok], lhsT=g2b0[:, fc * 128:(fc + 1) * 128],
                                                      rhs=lw_t[:, 0:ntok], start=True, stop=False),
                             reads=["g2b0", "lw_t"], writes=[PM])
                        P.op("pe", lambda e: e.matmul(pm[:, 0:ntok], lhsT=g2b1[0:32, fc * 128:(fc + 1) * 128],
                                                      rhs=lg2_t[0:32, 0:ntok], start=False, stop=True),
                             reads=["g2b1", "lg2_t"], writes=[PM])
                        P.op("act", lambda e: e.copy(out=gT_[:, fc, 0:ntok], in_=pm[:, 0:ntok]), reads=[PM], writes=["gT_"])

                    if kind == "s":
                        P.op("pool", lambda e: e.memset(KR[:], 0.0), writes=["KR"])
                        P.op("pool", lambda e: e.memset(BK[:], 0.0), writes=["BK"])
                    nt = slice(0, ntok)
                    for g in range(8):
                        pm, PM = pm_[g % 2], "pmd%d" % (g % 2)
                        kk, kk2, rn, Linc, Lexc, E1 = tA[0], tB[0], tA[1], tA[2], tA[3], tA[4]
                        P.op("dve", lambda e: e.tensor_scalar(out=kk[:, nt], in0=kT[:, g, nt], scalar1=kkT_[:, g:g + 1],
                                                              scalar2=None, op0=ALU.mult),
                             reads=["kT", "kkT_"], writes=["tA0"])
                        P.op("act", lambda e: e.activation(out=kk2[:, nt], in_=kk[:, nt], func=AF.Square),
                             reads=["tA0"], writes=["tB0"])
                        P.op("pe", lambda e: e.matmul(pm[:, nt], lhsT=blkb[:], rhs=kk2[:, nt], start=True, stop=True),
                             reads=["blkb", "tB0"], writes=[PM])
                        P.op("act", lambda e: e.activation(out=rn[:, nt], in_=pm[:, nt], func=AF.Sqrt),
                             reads=[PM], writes=["tA1"])
                        P.op("dve", lambda e: e.tensor_scalar(out=rn[:, nt], in0=rn[:, nt], scalar1=1e-12, scalar2=None,
                                                              op0=ALU.max), reads=["tA1"], writes=["tA1"])
                        P.op("dve", lambda e: e.reciprocal(out=rn[:, nt], in_=rn[:, nt]), reads=["tA1"], writes=["tA1"])
                        P.op("dve", lambda e: e.tensor_tensor(out=kk[:, nt], in0=kk[:, nt], in1=rn[:, nt], op=ALU.mult),
                             reads=["tA0", "tA1"], writes=["tA0"])
                        kh = tA[5]
                        P.op("dve", lambda e: e.tensor_scalar(out=kh[:, nt], in0=aT[:, g, nt], scalar1=kaT_[:, g:g + 1],
                                                              scalar2=kaT_[:, g:g + 1], op0=ALU.mult, op1=ALU.subtract),
                             reads=["aT", "kaT_"], writes=["tA5"])
                        P.op("dve", lambda e: e.scalar_tensor_tensor(out=kh[:, nt], in0=kh[:, nt], scalar=1.0,
                                                                     in1=kT[:, g, nt], op0=ALU.add, op1=ALU.mult),
                             reads=["tA5", "kT"], writes=["tA5"])
                        P.op("dve", lambda e: e.tensor_tensor(out=rn[:, nt], in0=kk[:, nt], in1=aT[:, g, nt], op=ALU.mult),
                             reads=["tA0", "aT"], writes=["tA1"])
                        P.op("dve", lambda e: e.scalar_tensor_tensor(out=tB[1][:, nt], in0=rT[:, g, nt],
                                                                     scalar=rkT_[:, g:g + 1], in1=kh[:, nt],
                                                                     op0=ALU.mult, op1=ALU.mult),
                             reads=["rT", "rkT_", "tA5"], writes=["tB1"])
                        pm2, PM2 = pm_[(g + 1) % 2], "pmd%d" % ((g + 1) % 2)
                        P.op("pe", lambda e: e.matmul(pm2[:, nt], lhsT=blkb[:], rhs=tB[1][:, nt], start=True, stop=True),
                             reads=["blkb", "tB1"], writes=[PM2])
                        P.op("dve", lambda e: e.tensor_tensor(out=bon[:, g, nt], in0=pm2[:, nt], in1=vT[:, g, nt], op=ALU.mult),
                             reads=[PM2, "vT"], writes=["bon"])
                        P.op("dve", lambda e: e.tensor_tensor_scan(out=Linc[:, nt], data0=rm[:, nt], data1=ld[:, g, nt],
                                                                   initial=0.0, op0=ALU.mult, op1=ALU.add),
                             reads=[RM, "ld"], writes=["tA2"])
                        P.op("pool", lambda e: e.tensor_tensor(out=Lexc[:, nt], in0=Linc[:, nt], in1=ld[:, g, nt], op=ALU.subtract),
                             reads=["tA2", "ld"], writes=["tA3"])
                        c3 = lambda ap: ap.rearrange("p (c t) -> p c t", t=cl)
                        P.op("act", lambda e: e.activation(out=E1[:, nt], in_=Linc[:, nt], func=AF.Exp),
                             reads=["tA2"], writes=["tA4"])
                        P.op("dve", lambda e: e.tensor_tensor(out=KR[:, g, 0:nch, 1, 0:cl], in0=c3(rT[:, g, nt]),
                                                              in1=c3(E1[:, nt]), op=ALU.mult),
                             reads=["rT", "tA4"], writes=["KR"])
                        P.op("pool", lambda e: e.tensor_copy(out=gam[:, g, 0:nch], in_=c3(E1[:, nt])[:, :, cl - 1]),
                             reads=["tA4"], writes=["gam"])
                        P.op("act", lambda e: e.activation(out=E1[:, nt], in_=Lexc[:, nt], func=AF.Exp),
                             reads=["tA3", "gam"], writes=["tA4"])
                        P.op("dve", lambda e: e.tensor_tensor(out=KR[:, g, 0:nch, 0, 0:cl], in0=c3(kk[:, nt]),
                                                              in1=c3(E1[:, nt]), op=ALU.mult),
                             reads=["tA0", "tA4"], writes=["KR"])
                        P.op("act", lambda e: e.activation(out=E1[:, nt], in_=Linc[:, nt], func=AF.Exp, scale=-1.0),
                             reads=["tA2", "KR"], writes=["tA4"])
                        P.op("dve", lambda e: e.tensor_tensor(out=BK[:, g, 0:nch, 0, 0:cl], in0=c3(rn[:, nt]),
                                                              in1=c3(E1[:, nt]), op=ALU.mult),
                             reads=["tA1", "tA4"], writes=["BK"])
                        P.op("dve", lambda e: e.tensor_tensor(out=BK[:, g, 0:nch, 1, 0:cl], in0=c3(kh[:, nt]),
                                                              in1=c3(E1[:, nt]), op=ALU.mult),
                             reads=["tA5", "tA4"], writes=["BK"])

                    for c in range(nch):
                        if kind == "s":
                            load_state(state_wkv[c])
                        for which, (li_, Gm_, GM, Mm, MM) in enumerate([(0, GBm, "GBm", MB, "MB"), (1, GKm, "GKm", MK, "MK")]):
                            for g in range(8):
                                for h2 in range(2):
                                    pr = slice(h2 * 64, (h2 + 1) * 64)
                                    P.op("pe", lambda e: e.matmul(
                                        Pg[pr, g, :], lhsT=BK[pr, g, c, li_, :],
                                        rhs=KR[pr, g, c, :, :], start=True, stop=True, skip_group_check=True),
                                        reads=["BK", "KR"], writes=["Pg"])
                            P.op("dve", lambda e: e.tensor_tensor(
                                out=Gm_[:], in0=Pg[:], in1=Mm[:].unsqueeze(1).to_broadcast([128, 8, 128]), op=ALU.mult),
                                reads=["Pg", MM], writes=[GM])
                            if which == 0:
                                P.op("dve", lambda e: e.tensor_tensor(
                                    out=Xl[0][:], in0=Pg[:, :, 0:64],
                                    in1=MB[:, 0:64].unsqueeze(1).to_broadcast([128, 8, 64]), op=ALU.mult),
                                    reads=["Pg", "MB"], writes=["Xl0"])
                        for g in range(8):
                            for h2 in range(2):
                                pr = slice(h2 * 64, (h2 + 1) * 64)
                                P.op("pe", lambda e: e.matmul(
                                    Pn[0][pr, g, :], lhsT=KR[pr, g, c, 0, :], rhs=BK[pr, g, c, 0, :],
                                    start=True, stop=True, skip_group_check=True),
                                    reads=["BK", "KR"], writes=["Pn0"])
                        P.op("dve", lambda e: e.tensor_tensor(
                            out=Yl[0][:], in0=Pn[0][:], in1=MN[:].unsqueeze(1).to_broadcast([128, 8, 64]), op=ALU.mult),
                            reads=["Pn0", "MN"], writes=["Yl0"])
                        P.op("pool", lambda e: e.tensor_tensor(
                            out=Pl[0][:], in0=Xl[0][:], in1=I64[:].unsqueeze(1).to_broadcast([128, 8, 64]), op=ALU.add),
                            reads=["Xl0", "I64"], writes=["Pl0"])
                        cur = 0
                        for lev in range(5):
                            nx = 1 - cur
                            XC, YC, PC = "Xl%d" % cur, "Yl%d" % cur, "Pl%d" % cur
                            XN_, YN_, PN_ = "Xl%d" % nx, "Yl%d" % nx, "Pl%d" % nx
                            for g in range(8):
                                for h2 in range(2):
                                    pr = slice(h2 * 64, (h2 + 1) * 64)
                                    P.op("pe", lambda e: e.matmul(Pn[1][pr, g, :], lhsT=Xl[cur][pr, g, :], rhs=Yl[cur][pr, g, :],
                                                                  start=True, stop=True, skip_group_check=True),
                                         reads=[XC, YC], writes=["Pn1"])
                            P.op("act", lambda e: e.copy(out=Yl[nx][:], in_=Pn[1][:]), reads=["Pn1"], writes=[YN_])
                            if lev < 4:
                                for g in range(8):
                                    for h2 in range(2):
                                        pr = slice(h2 * 64, (h2 + 1) * 64)
                                        P.op("pe", lambda e: e.matmul(Pn[0][pr, g, :], lhsT=Yl[cur][pr, g, :], rhs=Xl[cur][pr, g, :],
                                                                      start=True, stop=True, skip_group_check=True),
                                             reads=[XC, YC], writes=["Pn0"])
                                P.op("dve", lambda e: e.tensor_copy(out=Xl[nx][:], in_=Pn[0][:]), reads=["Pn0"], writes=[XN_])
                            for g in range(8):
                                for h2 in range(2):
                                    pr = slice(h2 * 64, (h2 + 1) * 64)
                                    P.op("pe", lambda e: e.matmul(Pg[pr, g, 0:64], lhsT=Yl[nx][pr, g, :], rhs=Pl[cur][pr, g, :],
                                                                  start=True, stop=True, skip_group_check=True),
                                         reads=[YN_, PC], writes=["Pg"])
                            P.op("dve", lambda e: e.tensor_tensor(out=Pl[nx][:], in0=Pg[:, :, 0:64], in1=Pl[cur][:], op=ALU.add),
                                 reads=["Pg", PC], writes=[PN_])
                            cur = nx
                        Tt, TT_ = Pl[cur], "Pl%d" % cur
                        for g in range(8):
                            for h2 in range(2):
                                pr = slice(h2 * 64, (h2 + 1) * 64)
                                P.op("pe", lambda e: e.matmul(Pn[0][pr, g, :], lhsT=KR[pr, g, c, 0, :], rhs=Hbf[pr, g, :],
                                                              start=True, stop=False, skip_group_check=True),
                                     reads=["KR", "Hbf"], writes=["Pn0"])
                                P.op("pe", lambda e: e.matmul(Pn[0][pr, g, :], lhsT=GKm[pr, g, 0:64],
                                                              rhs=Vtok[pr, c, g * 64:(g + 1) * 64],
                                                              start=False, stop=True, skip_group_check=True),
                                     reads=["GKm", "Vtok"], writes=["Pn0"])
                        P.op("act", lambda e: e.copy(out=Wsb[:], in_=Pn[0][:]), reads=["Pn0"], writes=["Wsb"])
                        for g in range(8):
                            for h2 in range(2):
                                pr = slice(h2 * 64, (h2 + 1) * 64)
                                P.op("pe", lambda e: e.matmul(Pn[1][pr, g, :], lhsT=Tt[pr, g, :], rhs=Wsb[pr, g, :],
                                                              start=True, stop=True, skip_group_check=True),
                                     reads=[TT_, "Wsb"], writes=["Pn1"])
                        P.op("act", lambda e: e.copy(out=Usb[:], in_=Pn[1][:]), reads=["Pn1"], writes=["Usb"])
                        for g in range(8):
                            for h2 in range(2):
                                pr = slice(h2 * 64, (h2 + 1) * 64)
                                P.op("pe", lambda e: e.transpose(out=Ptk[pr, 0, g, :], in_=BK[pr, g, c, 1, :],
                                                                 identity=identb[pr, h2 * 64:(h2 + 1) * 64]),
                                     reads=["BK", "identb"], writes=["Ptk"])
                                P.op("pe", lambda e: e.transpose(out=Ptk[pr, 1, g, :], in_=BK[pr, g, c, 0, :],
                                                                 identity=identb[pr, h2 * 64:(h2 + 1) * 64]),
                                     reads=["BK", "identb"], writes=["Ptk"])
                        P.op("dve", lambda e: e.tensor_copy(out=Ktok[:], in_=Ptk[:, 0, :, :]), reads=["Ptk"], writes=["Ktok"])
                        P.op("dve", lambda e: e.tensor_scalar(out=nBtok[:], in0=Ptk[:, 1, :, :], scalar1=-1.0, scalar2=None,
                                                              op0=ALU.mult), reads=["Ptk"], writes=["nBtok"])
                        for g in range(8):
                            for h2 in range(2):
                                pr = slice(h2 * 64, (h2 + 1) * 64)
                                P.op("pe", lambda e: e.matmul(Pg[pr, g, 0:64], lhsT=Hbf[pr, g, :], rhs=KR[pr, g, c, 1, :],
                                                              start=True, stop=False, skip_group_check=True),
                                     reads=["Hbf", "KR"], writes=["Pg"])
                                P.op("pe", lambda e: e.matmul(Pg[pr, g, 0:64], lhsT=Vtok[pr, c, g * 64:(g + 1) * 64],
                                                              rhs=GKm[pr, g, 64:128], start=False, stop=False, skip_group_check=True),
                                     reads=["Vtok", "GKm"], writes=["Pg"])
                                P.op("pe", lambda e: e.matmul(Pg[pr, g, 0:64], lhsT=Usb[pr, g, :], rhs=GBm[pr, g, 64:128],
                                                              start=False, stop=True, skip_group_check=True),
                                     reads=["Usb", "GBm"], writes=["Pg"])
                        P.op("act", lambda e: e.copy(out=Ysb[:, :, c, :], in_=Pg[:, :, 0:64]), reads=["Pg"], writes=["Ysb"])
                        for g in range(8):
                            for h2 in range(2):
                                pr = slice(h2 * 64, (h2 + 1) * 64)
                                P.op("pe", lambda e: e.matmul(Pn[0][pr, g, :], lhsT=Ktok[pr, g, :],
                                                              rhs=Vtok[pr, c, g * 64:(g + 1) * 64],
                                                              start=True, stop=False, skip_group_check=True),
                                     reads=["Ktok", "Vtok"], writes=["Pn0"])
                                P.op("pe", lambda e: e.matmul(Pn[0][pr, g, :], lhsT=nBtok[pr, g, :], rhs=Usb[pr, g, :],
                                                              start=False, stop=True, skip_group_check=True),
                                     reads=["nBtok", "Usb"], writes=["Pn0"])
                        P.op("dve", lambda e: e.tensor_tensor(out=Htmp[:], in0=Pn[0][:], in1=H32[:], op=ALU.add),
                             reads=["Pn0", "H32"], writes=["Htmp"])
                        P.op("pool", lambda e: e.tensor_tensor(
                            out=H32[:], in0=Htmp[:], in1=gam[:, :, c:c + 1].to_broadcast([128, 8, 64]), op=ALU.mult),
                            reads=["Htmp", "gam"], writes=["H32"])
                        P.op("act", lambda e: e.copy(out=Hbf[:], in_=H32[:]), reads=["H32"], writes=["Hbf"])
                        if kind == "s":
                            out_deps.append(store_state(wkvs[c]))
                    if kind == "p" and qi == SEQ // TT - 1:
                        out_deps.append(store_state(wkvp))

                    for g in range(8):
                        pm, PM = pm_[0], "pmd0"
                        pm2, PM2 = pm_[1], "pmd1"
                        Yg = Ysb[:, g, 0:nch, 0:cl]
                        m_, ysq, var_, yn = tA[0], tA[1], tA[2], tA[3]
                        c3 = lambda ap: ap.rearrange("p (c t) -> p c t", t=cl)
                        P.op("pool", lambda e: e.tensor_copy(out=c3(yn[:, nt]), in_=Yg), reads=["Ysb"], writes=["tA3"])
                        P.op("pe", lambda e: e.matmul(pm[:, nt], lhsT=blk32[:], rhs=yn[:, nt], start=True, stop=True),
                             reads=["blk32", "tA3"], writes=[PM])
                        P.op("pool", lambda e: e.tensor_tensor(out=ysq[:, nt], in0=yn[:, nt], in1=yn[:, nt], op=ALU.mult),
                             reads=["tA3"], writes=["tA1"])
                        P.op("pe", lambda e: e.matmul(pm2[:, nt], lhsT=blk32[:], rhs=ysq[:, nt], start=True, stop=True),
                             reads=["blk32", "tA1"], writes=[PM2])
                        P.op("act", lambda e: e.activation(out=m_[:, nt], in_=pm[:, nt], func=AF.Copy, scale=1.0 / 64),
                             reads=[PM], writes=["tA0"])
                        P.op("pool", lambda e: e.tensor_tensor(out=ysq[:, nt], in0=m_[:, nt], in1=m_[:, nt], op=ALU.mult),
                             reads=["tA0", "tA1"], writes=["tA1"])
                        P.op("dve", lambda e: e.scalar_tensor_tensor(out=var_[:, nt], in0=pm2[:, nt], scalar=1.0 / 64,
                                                                     in1=ysq[:, nt], op0=ALU.mult, op1=ALU.subtract),
                             reads=[PM2, "tA1"], writes=["tA2"])
                        P.op("act", lambda e: e.activation(out=var_[:, nt], in_=var_[:, nt], func=AF.Sqrt, bias=GN_EPS),
                             reads=["tA2"], writes=["tA2"])
                        P.op("dve", lambda e: e.reciprocal(out=var_[:, nt], in_=var_[:, nt]), reads=["tA2"], writes=["tA2"])
                        P.op("dve", lambda e: e.tensor_tensor(out=yn[:, nt], in0=yn[:, nt], in1=m_[:, nt], op=ALU.subtract),
                             reads=["tA3", "tA0"], writes=["tA3"])
                        P.op("dve", lambda e: e.tensor_tensor(out=yn[:, nt], in0=yn[:, nt], in1=var_[:, nt], op=ALU.mult),
                             reads=["tA3", "tA2"], writes=["tA3"])
                        P.op("dve", lambda e: e.tensor_scalar(out=yn[:, nt], in0=yn[:, nt], scalar1=lngT[:, g:g + 1],
                                                              scalar2=lnbT[:, g:g + 1], op0=ALU.mult, op1=ALU.add),
                             reads=["tA3", "lngT", "lnbT"], writes=["tA3"])
                        P.op("pool", lambda e: e.tensor_tensor(out=yn[:, nt], in0=yn[:, nt], in1=bon[:, g, nt], op=ALU.add),
                             reads=["tA3", "bon"], writes=["tA3"])
                        if kind == "p":
                            yb, YB = tB[2], "tB2"
                            P.op("dve", lambda e: e.tensor_tensor(out=yb[:, nt], in0=yn[:, nt], in1=gT_[:, g, nt], op=ALU.mult),
                                 reads=["tA3", "gT_"], writes=[YB])
                            P.dma("sp", ygT_d[:, g, qi * TT:(qi + 1) * TT], yb[:, nt], reads=[YB], semkey="ygo")
                        else:
                            P.op("dve", lambda e: e.tensor_tensor(out=ygs[:, g, :], in0=yn[:, nt], in1=gT_[:, g, nt], op=ALU.mult),
                                 reads=["tA3", "gT_"], writes=["ygs"])
                    if kind == "p":
                        P.op("pool", lambda e: e.tensor_copy(out=lastcol[:], in_=hTe[:, :, TT:TT + 1]),
                             reads=["hTe"], writes=["lastcol"])
                P.barrier()

        if cfg.upto >= 7:
            gfin = sbt(st, "gfin", [128, D], F32)
            P.dma("sp", gfin[:], final_norm.partition_broadcast(128), writes=["gfin"])
            fin = dict(ssq=sbt(st, "fssq", [128, 1]), rstd=sbt(st, "frstd", [128, 1]),
                       junk=sbt(st, "fjunk", [128, D], BF16), y=[sbt(st, "fy%d" % i, [128, D]) for i in range(2)])
            fcnt = [0]

            def get_oT1(kind, qi, ntok, oT_tile, OT):
                if kind == "p":
                    for g in range(8):
                        P.dma("sp", oT_tile[:, g, :], ygT_d[:, g, qi * 512:(qi + 1) * 512], writes=[OT])
                else:
                    P.op("dve", lambda e: e.tensor_copy(out=oT_tile[:, :, 0:NST], in_=ygs[:]), reads=["ygs"], writes=[OT])

            def x_src1(kind, qi, sub, dst, X):
                if kind == "p":
                    t = qi * 4 + sub
                    P.dma("sp", dst, xl0p[t * 128:(t + 1) * 128, :], writes=[X])
                else:
                    P.dma("sp", dst, xl0s, writes=[X])

            def x_dst1(kind, qi, sub, src, X1):
                np_ = src.shape[0]
                fb = fcnt[0] % 2
                fcnt[0] += 1
                FY = "fy%d" % fb
                P.op("act", lambda e: e.activation(out=fin["junk"][0:np_, :], in_=src, func=AF.Square,
                                                   accum_out=fin["ssq"][0:np_, :]),
                     reads=[X1], writes=["fjunk", "fssq"])
                P.op("act", lambda e: e.activation(out=fin["rstd"][0:np_, :], in_=fin["ssq"][0:np_, :], func=AF.Sqrt,
                                                   scale=1.0 / D, bias=EPS), reads=["fssq"], writes=["frstd"])
                P.op("dve", lambda e: e.reciprocal(out=fin["rstd"][0:np_, :], in_=fin["rstd"][0:np_, :]),
                     reads=["frstd"], writes=["frstd"])
                P.op("dve", lambda e: e.scalar_tensor_tensor(out=fin["y"][fb][0:np_, :], in0=src, scalar=fin["rstd"][0:np_, :],
                                                             in1=gfin[0:np_, :], op0=ALU.mult, op1=ALU.mult),
                     reads=[X1, "frstd", "gfin"], writes=[FY])
                if kind == "p":
                    t = qi * 4 + sub
                    out_deps.append(P.dma("sp", yp[t * 128:(t + 1) * 128, :], fin["y"][fb][0:np_, :], reads=[FY], semkey=FY + "o"))
                else:
                    out_deps.append(P.dma("sp", ys, fin["y"][fb][0:np_, :], reads=[FY], semkey=FY + "o"))

            ffn_phase(1, 2, 3, Wrwo_b, get_oT1, x_src1, x_dst1, True)

        P.wait_all("sp", out_deps)
        build_nc.last_counts = dict(P.cnt)
    return nc


def build_satt(cfg):
    NPG, NPHYS = cfg.NPG, cfg.NPHYS
    NSA = 32
    NTK = NSA * TS
    NCOL = NSA * 2 * TS
    nc = bass.Bass("TRN2", target_bir_lowering=False)

    def din(name, shape, dt=F32):
        return nc.dram_tensor(name, list(shape), dt, kind="ExternalInput").ap()

    def dout(name, shape, dt=F32):
        return nc.dram_tensor(name, list(shape), dt, kind="ExternalOutput").ap()

    xs = din("xs", [NTK, D])
    cvec = din("cvec", [NSA, D])
    g_mix0 = din("g_mix0", [D])
    adaw = din("adaw", [D, 2 * D])
    adab = din("adab", [2 * D])
    w_hs = din("w_hs", [D, 384])
    bias2 = din("bias2", [2])
    cache_k = din("cache_k", [NPHYS * 128, 128])
    cache_v = din("cache_v", [NPHYS * 128, 128])
    ptab = din("ptab", [NSA * NPG], I32)
    Kg_d = nc.dram_tensor("Kg_d", [NPG + 1, 128, NSA, 128], F32, kind="Internal").ap()
    Vg_d = nc.dram_tensor("Vg_d", [NPG + 1, 128, NSA, 128], F32, kind="Internal").ap()
    oT_o = dout("oT_o", [128, NTK])
    ks_o = dout("ks_o", [NTK, 128])
    vs_o = dout("vs_o", [NTK, 128])
    idx_o = dout("idx_o", [128, NSA * NPG], I32) if cfg.debug else None

    with ExitStack() as st:
        P = Prog(nc, st)
        uniq = [0]

        def sbt(stack, name, shape, dt=F32):
            uniq[0] += 1
            return stack.enter_context(nc.sbuf_tensor("%s_u%d" % (name, uniq[0]), list(shape), dt))

        def pst(stack, name, shape, dt=F32):
            uniq[0] += 1
            P.excl.add(name)
            esz = 4 if dt == F32 else 2
            n = int(np.prod(shape[1:]))
            per_bank = 2048 // esz
            npad = ((n + per_bank - 1) // per_bank) * per_bank
            t_ = stack.enter_context(nc.psum_tensor("%s_u%d" % (name, uniq[0]), [128, npad], dt))
            v = t_[:, 0:n]
            if len(shape) == 3:
                v = v.rearrange("p (a b) -> p a b", b=shape[2])
            elif len(shape) == 4:
                v = v.rearrange("p (a b c) -> p a b c", b=shape[2], c=shape[3])
            return v

        out_deps = []
        if "noguard" not in cfg.dbg:
            guard = sbt(st, "guard", [128, 4096], F32)
        ident = sbt(st, "ident", [128, 128], F32)
        identb = sbt(st, "identb", [128, 128], BF16)
        negU = sbt(st, "negU", [128, 128], BF16)
        negO = sbt(st, "negO", [128, 128], BF16)
        P.op("pool", lambda e: e.memset(ident[:], 0.0), writes=["ident"])
        P.op("pool", lambda e: e.affine_select(
            out=ident[:], in_=ident[:], pattern=[[-1, 128]], compare_op=ALU.not_equal,
            fill=1.0, base=0, channel_multiplier=1), reads=["ident"], writes=["ident"])
        P.op("pool", lambda e: e.tensor_copy(out=identb[:], in_=ident[:]), reads=["ident"], writes=["identb"])
        P.op("pool", lambda e: e.memset(negO[:], -1.0), writes=["negO"])
        P.op("pool", lambda e: e.affine_select(
            out=negU[:], in_=negO[:], pattern=[[-1, 128]], compare_op=ALU.is_ge,
            fill=0.0, base=0, channel_multiplier=1), reads=["negO"], writes=["negU"])

        with ExitStack() as ph:
            pt_i = sbt(ph, "pt_i", [128, NSA * NPG], I32)
            pt_f = sbt(ph, "pt_f", [128, NSA * NPG], F32)
            iota_f = sbt(ph, "iota_f", [128, 1], F32)
            iota_i = sbt(ph, "iota_i", [128, 1], I32)
            idx = sbt(ph, "idx", [128, NSA * NPG], I32)
            NGB = 8
            G32 = [sbt(ph, "G32_%d" % i, [128, 128], F32) for i in range(NGB)]
            P.dma("sp", pt_i[:], ptab.partition_broadcast(128), writes=["pt_i"])
            P.op("dve", lambda e: e.tensor_copy(out=pt_f[:], in_=pt_i[:]), reads=["pt_i"], writes=["pt_f"])
            P.op("pool", lambda e: e.iota(iota_i[:], pattern=[[0, 1]], base=0, channel_multiplier=1), writes=["iota_i"])
            P.op("dve", lambda e: e.tensor_copy(out=iota_f[:], in_=iota_i[:]), reads=["iota_i"], writes=["iota_f"])
            P.op("dve", lambda e: e.tensor_scalar(out=pt_f[:], in0=pt_f[:], scalar1=128.0, scalar2=iota_f[:, 0:1],
                                                  op0=ALU.mult, op1=ALU.add), reads=["pt_f", "iota_f"], writes=["pt_f"])
            P.op("dve", lambda e: e.tensor_copy(out=idx[:], in_=pt_f[:]), reads=["pt_f"], writes=["idx"])
            if cfg.debug:
                out_deps.append(P.dma("sp", idx_o, idx[:], reads=["idx"], semkey="idxo"))
            gc = 0
            for blk in range(NPG):
                for s_ in range(NSA):
                    c_ = s_ * NPG + blk
                    for (src_, dst_) in ((cache_k, Kg_d), (cache_v, Vg_d)):
                        gi = gc % NGB
                        gc += 1
                        GK_ = "G32_%d" % gi
                        P.dma("pool", G32[gi][:], src_, writes=[GK_],
                              indirect=bass.IndirectOffsetOnAxis(ap=idx[:, c_:c_ + 1], axis=0), reads=["idx"])
                        P.dma("sp", dst_[blk, :, s_, :], G32[gi][:], reads=[GK_], semkey=GK_ + "o")
            P.barrier()

        qTs = sbt(st, "qTs", [128, NTK], BF16)
        kTs = sbt(st, "kTs", [128, NSA, 128], BF16)
        Vs = sbt(st, "Vs", [128, NSA, 128], BF16)
        P.op("pool", lambda e: e.memset(kTs[:], 0.0), writes=["kTs"])
        P.op("pool", lambda e: e.memset(Vs[:], 0.0), writes=["Vs"])

        for _once in ([] if "nophase1" in cfg.dbg else [0]):
          with ExitStack() as ph:
              cT = sbt(ph, "cT", [128, 8, NSA], F32)
              scT = sbt(ph, "scT", [128, 8, NSA], BF16)
              wada = sbt(ph, "wada", [128, 8, 2 * D], BF16)
              whs = sbt(ph, "whs", [128, 8, 384], BF16)
              adabT = sbt(ph, "adabT", [128, 16], F32)
              gT = sbt(ph, "gT", [128, 8], F32)
              modT = sbt(ph, "modT", [128, 16, NSA], F32)
              Acoef = sbt(ph, "Acoef", [128, 8, NSA], F32)
              A_full = sbt(ph, "A_full", [128, 8, NTK], F32)
              B_full = sbt(ph, "B_full", [128, 8, NTK], F32)
              hT = sbt(ph, "hT", [128, 8, NTK], BF16)
              xt = [sbt(ph, "xt%d" % i, [128, D]) for i in range(2)]
              junk = sbt(ph, "junk", [128, D], BF16)
              ssq = [sbt(ph, "ssq%d" % i, [128, 1]) for i in range(2)]
              rstd = [sbt(ph, "rstd%d" % i, [128, 1]) for i in range(2)]
              xn = [sbt(ph, "xn%d" % i, [128, D]) for i in range(2)]
              tmpf = sbt(ph, "tmpf", [128, 128], F32)
              kvo = [sbt(ph, "kvo%d" % i, [128, 256], F32) for i in range(2)]
              vbf = [sbt(ph, "vbf%d" % i, [128, 128], BF16) for i in range(2)]
              psm = pst(ph, "psm", [128, 16, NSA], F32)
              ptr_ = [pst(ph, "ptr%d" % i, [128, 512], F32) for i in range(2)]
              pq = [pst(ph, "pq%d" % i, [128, 512], F32) for i in range(2)]
              with nc.allow_non_contiguous_dma(reason="tiny transposed parameter loads"):
                  for s_ in range(NSA):
                      P.dma("sp", cT[:, :, s_], cvec[s_].rearrange("(k p) -> p k", p=128), writes=["cT"])
                  P.dma("sp", gT[:], g_mix0.rearrange("(k p) -> p k", p=128), writes=["gT"])
                  P.dma("sp", adabT[:], adab.rearrange("(k p) -> p k", p=128), writes=["adabT"])
              for k in range(8):
                  for hc in range(2):
                      P.dma("pool", wada[:, k, hc * D:(hc + 1) * D], adaw[k * 128:(k + 1) * 128, hc * D:(hc + 1) * D],
                            writes=["wada"], max_dma_last_dim=4096)
                  P.dma("pool", whs[:, k, :], w_hs[k * 128:(k + 1) * 128, :], writes=["whs"])
              P.op("act", lambda e: e.activation(out=scT[:], in_=cT[:], func=AF.Silu), reads=["cT"], writes=["scT"])
              for oc in range(16):
                  for k in range(8):
                      P.op("pe", lambda e: e.matmul(psm[:, oc, :], lhsT=wada[:, k, oc * 128:(oc + 1) * 128],
                                                    rhs=scT[:, k, :], start=(k == 0), stop=(k == 7)),
                           reads=["wada", "scT"], writes=["psm"])
              P.op("dve", lambda e: e.tensor_tensor(
                  out=modT[:], in0=psm[:], in1=adabT[:].unsqueeze(2).to_broadcast([128, 16, NSA]), op=ALU.add),
                  reads=["psm", "adabT"], writes=["modT"])
              P.op("dve", lambda e: e.tensor_scalar(out=Acoef[:], in0=modT[:, 8:16, :], scalar1=1.0, scalar2=None, op0=ALU.add),
                   reads=["modT"], writes=["Acoef"])
              P.op("dve", lambda e: e.tensor_tensor(
                  out=Acoef[:], in0=Acoef[:], in1=gT[:].unsqueeze(2).to_broadcast([128, 8, NSA]), op=ALU.mult),
                  reads=["Acoef", "gT"], writes=["Acoef"])
              for k in range(8):
                  P.op("dve", lambda e: e.tensor_copy(
                      out=A_full[:, k, :].rearrange("p (s t) -> p s t", t=TS),
                      in_=Acoef[:, k, :].unsqueeze(2).to_broadcast([128, NSA, TS])),
                      reads=["Acoef"], writes=["A_full"])
                  P.op("pool", lambda e: e.tensor_copy(
                      out=B_full[:, k, :].rearrange("p (s t) -> p s t", t=TS),
                      in_=modT[:, k, :].unsqueeze(2).to_broadcast([128, NSA, TS])),
                      reads=["modT"], writes=["B_full"])
              for ti in range(2):
                  X, SS, RS, XN = "xt%d" % ti, "ssq%d" % ti, "rstd%d" % ti, "xn%d" % ti
                  P.dma("sp", xt[ti][:], xs[ti * 128:(ti + 1) * 128, :], writes=[X])
                  P.op("act", lambda e: e.activation(out=junk[:], in_=xt[ti][:], func=AF.Square, accum_out=ssq[ti][:]),
                       reads=[X], writes=["junk", SS])
                  P.op("act", lambda e: e.activation(out=rstd[ti][:], in_=ssq[ti][:], func=AF.Sqrt, scale=1.0 / D, bias=EPS),
                       reads=[SS], writes=[RS])
                  P.op("dve", lambda e: e.reciprocal(out=rstd[ti][:], in_=rstd[ti][:]), reads=[RS], writes=[RS])
                  P.op("dve", lambda e: e.tensor_scalar(out=xn[ti][:], in0=xt[ti][:], scalar1=rstd[ti][:], scalar2=None,
                                                        op0=ALU.mult), reads=[X, RS], writes=[XN])
                  for k in range(8):
                      pt, PT = ptr_[k % 2], "ptr%d" % (k % 2)
                      P.op("pe", lambda e: e.transpose(out=pt[:, 0:128], in_=xn[ti][:, k * 128:(k + 1) * 128], identity=ident[:]),
                           reads=[XN, "ident"], writes=[PT])
                      P.op("dve", lambda e: e.tensor_tensor(out=tmpf[:], in0=pt[:, 0:128],
                                                            in1=A_full[:, k, ti * 128:(ti + 1) * 128], op=ALU.mult),
                           reads=[PT, "A_full"], writes=["tmpf"])
                      P.op("pool", lambda e: e.tensor_tensor(out=hT[:, k, ti * 128:(ti + 1) * 128], in0=tmpf[:],
                                                             in1=B_full[:, k, ti * 128:(ti + 1) * 128], op=ALU.add),
                           reads=["tmpf", "B_full"], writes=["hT"])
              for which in range(1):
                  p_, PQ = pq[which], "pq%d" % which
                  for k in range(8):
                      P.op("pe", lambda e: e.matmul(p_[:, 0:NTK], lhsT=whs[:, k, which * 128:(which + 1) * 128],
                                                    rhs=hT[:, k, :], start=(k == 0), stop=(k == 7)),
                           reads=["whs", "hT"], writes=[PQ])
                  if which == 0:
                      P.op("act", lambda e: e.activation(out=qTs[:], in_=p_[:, 0:NTK], func=AF.Copy, scale=0.125),
                           reads=[PQ], writes=["qTs"])
                  else:
                      pass
              zt = sbt(ph, "zt", [128, NSA * 128], F32)
              P.op("pool", lambda e: e.memset(zt[:], 0.0), writes=["zt"])
              P.dma("sp", Kg_d[NPG].rearrange("p s d -> p (s d)"), zt[:], reads=["zt"], writes=["KgN"], semkey="KgN")
              P.dma("sp", Vg_d[NPG].rearrange("p s d -> p (s d)"), zt[:], reads=["zt"], writes=["VgN"], semkey="VgN")
              for ti in range(2):
                  p_, PQ = pq[ti], "pq%d" % ti
                  KV, VB = "kvo%d" % ti, "vbf%d" % ti
                  for k in range(8):
                      P.op("pe", lambda e: e.matmul(p_[:, 0:256], lhsT=hT[:, k, ti * 128:(ti + 1) * 128],
                                                    rhs=whs[:, k, 128:384], start=(k == 0), stop=(k == 7)),
                           reads=["whs", "hT"], writes=[PQ])
                  P.op("act", lambda e: e.copy(out=kvo[ti][:], in_=p_[:, 0:256]), reads=[PQ], writes=[KV])
                  P.op("pool", lambda e: e.tensor_copy(out=vbf[ti][:], in_=kvo[ti][:, 128:256]), reads=[KV], writes=[VB])
                  out_deps.append(P.dma("sp", ks_o[ti * 128:(ti + 1) * 128, :], kvo[ti][:, 0:128], reads=[KV], semkey="ok%d" % ti))
                  out_deps.append(P.dma("sp", vs_o[ti * 128:(ti + 1) * 128, :], kvo[ti][:, 128:256], reads=[KV], semkey="ov%d" % ti))
                  for sl in range(16):
                      s_ = ti * 16 + sl
                      P.dma("sp", Kg_d[NPG, 0:TS, s_, :], kvo[ti][sl * TS:(sl + 1) * TS, 0:128], reads=[KV], writes=["KgN"], semkey="KgN")
                      P.dma("sp", Vg_d[NPG, 0:TS, s_, :], kvo[ti][sl * TS:(sl + 1) * TS, 128:256], reads=[KV], writes=["VgN"], semkey="VgN")
              P.barrier()

        with ExitStack() as ph:
            NG = 2
            K32 = [sbt(ph, "K32_%d" % i, [128, NSA, 128], F32) for i in range(NG)]
            V32 = [sbt(ph, "V32_%d" % i, [128, NSA, 128], F32) for i in range(NG)]
            Kpg = [sbt(ph, "Kpg%d" % i, [128, NSA, 128], BF16) for i in range(NG)]
            Vpg = [sbt(ph, "Vpg%d" % i, [128, NSA, 128], BF16) for i in range(NG)]
            kTp = [sbt(ph, "kTp%d" % i, [128, NSA, 128], BF16) for i in range(NG)]
            maskn = sbt(ph, "maskn", [128, NSA * 2, TS], BF16)
            ones_m = sbt(ph, "ones_m2", [128, NCOL], BF16)
            bias2b = sbt(ph, "bias2b", [128, 2], F32)
            bias_bc = sbt(ph, "bias_bc", [128, NSA, 2, TS], F32)
            zsb = sbt(ph, "zsb", [128, NCOL], F32)
            e_t = sbt(ph, "es_t", [128, NCOL], F32)
            sp_t = [sbt(ph, "sps_t%d" % i, [128, NCOL], BF16) for i in range(2)]
            t1 = sbt(ph, "t1s", [128, NCOL], F32)
            a_t = [sbt(ph, "as_t%d" % i, [128, NCOL], BF16) for i in range(2)]
            SP32 = sbt(ph, "SPs32", [128, NCOL], F32)
            SPbf = sbt(ph, "SPsbf", [128, NCOL], BF16)
            zero_l = sbt(ph, "zero_l", [128, 128], BF16)
            zsb_n = sbt(ph, "zsb_n", [128, NCOL], F32)
            sp_n = sbt(ph, "sp_n", [128, NCOL], BF16)
            o_sb = sbt(ph, "o_sb", [128, NTK], F32)
            Pz = [pst(ph, "Pzs%d" % i, [128, NCOL], F32) for i in range(2)]
            Pa = [pst(ph, "Pas%d" % i, [128, NCOL], F32) for i in range(2)]
            Po = pst(ph, "Pos", [128, NTK], F32)
            Ptr = [pst(ph, "Ptrs%d" % i, [128, 8, 128], BF16) for i in range(2)]

            P.dma("sp", bias2b[:], bias2.partition_broadcast(128), writes=["bias2b"])
            P.op("pool", lambda e: e.memset(ones_m[:], 1.0), writes=["ones_m2"])
            P.op("pool", lambda e: e.memset(zero_l[:], 0.0), writes=["zero_l"])
            P.op("pool", lambda e: e.memset(SP32[:], 0.0), writes=["SPs32"])
            P.op("pool", lambda e: e.affine_select(
                out=maskn[:], in_=ones_m[:].rearrange("p (c t) -> p c t", t=TS),
                pattern=[[0, NSA * 2], [1, TS]], compare_op=ALU.is_gt,
                fill=0.0, base=0, channel_multiplier=-1), reads=["ones_m2"], writes=["maskn"])
            for s_ in range(NSA):
                P.op("dve" if s_ % 2 == 0 else "pool", lambda e: e.tensor_copy(
                    out=bias_bc[:, s_, :, :], in_=bias2b[:].unsqueeze(2).to_broadcast([128, 2, TS])),
                    reads=["bias2b"], writes=["bias_bc"])
            P.op("pe", lambda e: e.matmul(Po[:], lhsT=zero_l[:], rhs=ones_m[:, 0:NTK], start=True, stop=False),
                 reads=["zero_l", "ones_m2"], writes=["Pos"])

            def col(s_, h2):
                return (s_ * 2 + h2) * TS

            blocks = list(range(NPG, -1, -1))
            if cfg.kstop <= 4:
                blocks = blocks[:max(0, cfg.kstop - 2)]

            gdeps = []
            for bi, blk in enumerate(blocks):
                gb = bi % NG
                first, last = (bi == 0), (bi == len(blocks) - 1)
                if True:
                    KT, KP, VP = "kTp%d" % gb, "Kpg%d" % gb, "Vpg%d" % gb
                    P.dma("sp", K32[gb][:], Kg_d[blk], reads=["KgN"], writes=["K32_%d" % gb])
                    P.dma("sp", V32[gb][:], Vg_d[blk], reads=["VgN"], writes=["V32_%d" % gb])
                    P.op("pool", lambda e: e.tensor_copy(out=Kpg[gb][:], in_=K32[gb][:]), reads=["K32_%d" % gb], writes=[KP])
                    P.op("pool", lambda e: e.tensor_copy(out=Vpg[gb][:], in_=V32[gb][:]), reads=["V32_%d" % gb], writes=[VP])
                    if "gatheronly" in cfg.dbg:
                        continue
                    for q4 in range(NSA // 8):
                        ptb, PTB = Ptr[q4 % 2], "Ptrs%d" % (q4 % 2)
                        for sl in range(8):
                            s_ = q4 * 8 + sl
                            P.op("pe", lambda e: e.transpose(out=ptb[:, sl, :], in_=Kpg[gb][:, s_, :], identity=identb[:]),
                                 reads=[KP, "identb"], writes=[PTB])
                        P.op("dve" if q4 % 2 == 0 else "act",
                             (lambda e: e.tensor_copy(out=kTp[gb][:, q4 * 8:(q4 + 1) * 8, :], in_=ptb[:])) if q4 % 2 == 0
                             else (lambda e: e.copy(out=kTp[gb][:, q4 * 8:(q4 + 1) * 8, :], in_=ptb[:])),
                             reads=[PTB], writes=[KT])
                    kt_, vt_ = kTp[gb], None
                    KK, VK = [KT], [VP]
                pz, PZ = Pz[bi % 2], "Pzs%d" % (bi % 2)
                pa, PA = Pa[bi % 2], "Pas%d" % (bi % 2)
                for s_ in range(NSA if not (blk == "new" and "noz" in cfg.dbg) else 0):
                    for h2 in range(2):
                        c0 = col(s_, h2)
                        P.op("pe", lambda e: e.matmul(
                            pz[:, c0:c0 + TS], lhsT=kt_[h2 * 64:(h2 + 1) * 64, s_, :],
                            rhs=qTs[h2 * 64:(h2 + 1) * 64, s_ * TS:(s_ + 1) * TS],
                            start=True, stop=True, skip_group_check=True),
                            reads=KK + ["qTs"], writes=[PZ])
                spt, S = sp_t[bi % 2], "sps_t%d" % (bi % 2)
                at, A = a_t[bi % 2], "as_t%d" % (bi % 2)
                zsb_, ZS = zsb, "zsb"
                P.op("dve", lambda e: e.tensor_tensor(out=zsb_[:], in0=pz[:],
                                                      in1=bias_bc[:].rearrange("p s h t -> p (s h t)"), op=ALU.add),
                     reads=[PZ, "bias_bc"], writes=[ZS])
                P.op("act", lambda e: e.activation(out=e_t[:], in_=zsb_[:], func=AF.Exp), reads=[ZS], writes=["es_t"])
                P.op("act", lambda e: e.activation(out=spt[:], in_=e_t[:], func=AF.Ln, bias=1.0), reads=["es_t"], writes=[S])
                if blk == NPG:
                    P.op("dve", lambda e: e.tensor_tensor(out=spt[:], in0=spt[:],
                                                          in1=maskn[:].rearrange("p c t -> p (c t)"), op=ALU.mult),
                         reads=[S, "maskn"], writes=[S])
                P.op("pe", lambda e: e.matmul(pa[:], lhsT=negU[:], rhs=spt[:], start=True, stop=first),
                     reads=["negU", S], writes=[PA])
                if not first:
                    P.op("pe", lambda e: e.matmul(pa[:], lhsT=negO[:], rhs=SPbf[:], start=False, stop=True),
                         reads=["negO", "SPsbf"], writes=[PA])
                P.op("dve", lambda e: e.tensor_tensor(out=t1[:], in0=pa[:], in1=zsb[:], op=ALU.add),
                     reads=[PA, "zsb"], writes=["t1s"])
                P.op("act", lambda e: e.activation(out=at[:], in_=t1[:], func=AF.Exp), reads=["t1s"], writes=[A])
                if blk == NPG:
                    P.op("dve", lambda e: e.tensor_tensor(out=at[:], in0=at[:],
                                                          in1=maskn[:].rearrange("p c t -> p (c t)"), op=ALU.mult),
                         reads=[A, "maskn"], writes=[A])
                for s_ in range(NSA if not (blk == "new" and "noav" in cfg.dbg) else 0):
                    for h2 in range(2):
                        c0 = col(s_, h2)
                        P.op("pe", lambda e: e.matmul(
                            Po[h2 * 64:(h2 + 1) * 64, s_ * TS:(s_ + 1) * TS], lhsT=(vt_[:, s_, h2 * 64:(h2 + 1) * 64] if vt_ is not None else Vpg[gb][:, s_, h2 * 64:(h2 + 1) * 64]),
                            rhs=at[:, c0:c0 + TS], start=False, stop=False, skip_group_check=True),
                            reads=VK + [A], writes=["Pos"])
                if not last:
                    P.op("pool", lambda e: e.tensor_tensor(out=SP32[:], in0=SP32[:], in1=spt[:], op=ALU.add),
                         reads=["SPs32", S], writes=["SPs32"])
                    P.op("pool", lambda e: e.tensor_copy(out=SPbf[:], in_=SP32[:]), reads=["SPs32"], writes=["SPsbf"])
            P.op("pe", lambda e: e.matmul(Po[:], lhsT=zero_l[:], rhs=ones_m[:, 0:NTK], start=False, stop=True),
                 reads=["zero_l", "ones_m2"], writes=["Pos"])
            P.op("dve", lambda e: e.tensor_copy(out=o_sb[:], in_=Po[:]), reads=["Pos"], writes=["o_sb"])
            out_deps.append(P.dma("sp", oT_o, o_sb[:], reads=["o_sb"], semkey="oo"))
            P.wait_all("sp", out_deps)
    return nc


_NC = None
_NC1 = None


def _f32(a):
    return np.ascontiguousarray(np.asarray(a, dtype=np.float32))


def _satt_inputs(inputs, c):
    f = _f32
    wq = np.asarray(inputs["sb_w_qkv"], dtype=np.float32)[0]
    w_hs = np.concatenate([wq[:, j * D + 128 * c:j * D + 128 * (c + 1)] for j in range(3)], axis=1)
    ck = np.asarray(inputs["cache_k"], dtype=np.float32)[0]
    cv = np.asarray(inputs["cache_v"], dtype=np.float32)[0]
    nph = ck.shape[0]
    return {
        "xs": f(inputs["x_sample"]).reshape(-1, D),
        "cvec": f(inputs["c_sample"]),
        "g_mix0": f(inputs["norm_mix"][0]),
        "adaw": f(np.asarray(inputs["ada_w_mix"], dtype=np.float32)[0][:, :2 * D]),
        "adab": f(np.asarray(inputs["ada_b_mix"], dtype=np.float32)[0][:2 * D]),
        "w_hs": f(w_hs),
        "bias2": f(np.asarray(inputs["sb_bias"], dtype=np.float32)[0][2 * c:2 * c + 2]),
        "cache_k": f(ck[:, :, 2 * c:2 * c + 2, :]).reshape(nph * 128, 128),
        "cache_v": f(cv[:, :, 2 * c:2 * c + 2, :]).reshape(nph * 128, 128),
        "ptab": np.ascontiguousarray(np.asarray(inputs["page_table"], dtype=np.int32)).reshape(-1),
    }


def _core_inputs(inputs, c, oT_all=None):
    f = _f32
    sq = c % 4
    m = {
        "xp": f(inputs["x_prompt"][sq]),
        "xs": f(inputs["x_sample"][NS * c:NS * (c + 1)]).reshape(NST, D),
        "cvec": np.concatenate([f(inputs["c_prompt"][sq:sq + 1]), f(inputs["c_sample"][NS * c:NS * (c + 1)])], axis=0),
        "norm_mix": f(inputs["norm_mix"]), "norm_ffn": f(inputs["norm_ffn"]),
        "ada_w_mix": f(inputs["ada_w_mix"]), "ada_b_mix": f(inputs["ada_b_mix"]),
        "ada_w_ffn": f(inputs["ada_w_ffn"]), "ada_b_ffn": f(inputs["ada_b_ffn"]),
        "w_up": f(inputs["w_up"]), "w_down": f(inputs["w_down"]),
        "w_qkv": f(inputs["sb_w_qkv"][0]), "w_o": f(inputs["sb_w_o"][0]),
        "sb_bias": f(inputs["sb_bias"][0]),
        "state_wkv": f(inputs["state_wkv"][0][NS * c:NS * (c + 1)]),
        "state_shift": f(inputs["state_shift"][0][NS * c:NS * (c + 1)]),
        "rw_mu": f(inputs["rw_mu"][0]), "rw_w_r": f(inputs["rw_w_r"][0]), "rw_w_k": f(inputs["rw_w_k"][0]),
        "rw_w_v": f(inputs["rw_w_v"][0]), "rw_w_o": f(inputs["rw_w_o"][0]), "rw_w0": f(inputs["rw_w0"][0]),
        "rw_w1": f(inputs["rw_w1"][0]), "rw_w2": f(inputs["rw_w2"][0]), "rw_a0": f(inputs["rw_a0"][0]),
        "rw_a1": f(inputs["rw_a1"][0]), "rw_a2": f(inputs["rw_a2"][0]), "rw_g1": f(inputs["rw_g1"][0]),
        "rw_g2": f(inputs["rw_g2"][0]), "rw_k_k": f(inputs["rw_k_k"][0]), "rw_k_a": f(inputs["rw_k_a"][0]),
        "rw_r_k": f(inputs["rw_r_k"][0]).reshape(-1), "rw_ln_g": f(inputs["rw_ln_g"][0]),
        "rw_ln_b": f(inputs["rw_ln_b"][0]), "final_norm": f(inputs["final_norm"]),
    }
    if oT_all is None:
        m["cache_k"] = np.asarray(inputs["cache_k"], dtype=np.float32).reshape(-1, D)
        m["cache_v"] = np.asarray(inputs["cache_v"], dtype=np.float32).reshape(-1, D)
        m["ptab"] = np.ascontiguousarray(np.asarray(inputs["page_table"], dtype=np.int32)[NS * c:NS * (c + 1)]).reshape(-1)
    else:
        m["oTs_in"] = np.ascontiguousarray(np.transpose(oT_all[:, :, NST * c:NST * (c + 1)], (1, 0, 2)))
    return m


def kernel(**inputs):
    global _NC, _NC1
    SEQ = int(np.shape(inputs["x_prompt"])[1])
    NPG = int(np.shape(inputs["page_table"])[1])
    NPHYS = int(np.shape(inputs["cache_k"])[1])
    if _NC is None:
        _NC1 = build_satt(Cfg(SEQ=SEQ, NPG=NPG, NPHYS=NPHYS))
        _NC = build_nc(Cfg(SEQ=SEQ, NPG=NPG, NPHYS=NPHYS, split=True))
    B, DB = 4, 32
    r1 = run_bass_kernel_spmd(_NC1, [_satt_inputs(inputs, c) for c in range(NCORES)], core_ids=list(range(NCORES))).results
    oT_all = np.stack([np.asarray(r1[g]["oT_o"], dtype=np.float32) for g in range(NCORES)])
    ks_all = np.stack([np.asarray(r1[g]["ks_o"], dtype=np.float32) for g in range(NCORES)])
    vs_all = np.stack([np.asarray(r1[g]["vs_o"], dtype=np.float32) for g in range(NCORES)])
    k_sample = np.transpose(ks_all.reshape(8, DB, TS, 2, 64), (1, 2, 0, 3, 4)).reshape(1, DB, TS, 16, 64)
    v_sample = np.transpose(vs_all.reshape(8, DB, TS, 2, 64), (1, 2, 0, 3, 4)).reshape(1, DB, TS, 16, 64)
    r = run_bass_kernel_spmd(_NC, [_core_inputs(inputs, c, oT_all) for c in range(NCORES)],
                             core_ids=list(range(NCORES))).results
    k_prompt = np.stack([r[s]["kp"] for s in range(B)]).reshape(1, B, SEQ, 16, 64)
    v_prompt = np.stack([r[s]["vp"] for s in range(B)]).reshape(1, B, SEQ, 16, 64)
    y_prompt = np.stack([r[s]["yp"] for s in range(B)]).reshape(B, SEQ, D)
    y_sample = np.concatenate([r[c]["ys"] for c in range(NCORES)]).reshape(DB, TS, D)
    wkv_prompt = np.stack([r[s]["wkvp"] for s in range(B)]).reshape(1, B, 16, 64, 64)
    shift_prompt = np.stack([r[s]["shp"] for s in range(B)]).reshape(1, B, D)
    wkv_sample = np.concatenate([r[c]["wkvs"] for c in range(NCORES)]).reshape(1, DB, 16, 64, 64)
    shift_sample = np.concatenate([r[c]["shs"] for c in range(NCORES)]).reshape(1, DB, D)
    return (_f32(y_prompt), _f32(y_sample), _f32(k_prompt), _f32(v_prompt), _f32(k_sample), _f32(v_sample),
            _f32(wkv_prompt), _f32(shift_prompt), _f32(wkv_sample), _f32(shift_sample))
```

```python
import numpy as np
from contextlib import ExitStack
import concourse.bass as bass
import concourse.mybir as mybir
from concourse.bass_utils import run_bass_kernel_spmd

F32 = mybir.dt.float32
BF16 = mybir.dt.bfloat16
I32 = mybir.dt.int32
AF = mybir.ActivationFunctionType
ALU = mybir.AluOpType

D = 1024
NCORES = 8
NS = 4
TS = 8
NST = NS * TS
NSEQ = 1 + NS
EPS = 1e-6


class Cfg:
    def __init__(self, SEQ=4096, NPG=64, NPHYS=2560, debug=False, upto=99, split=False):
        self.SEQ, self.NPG, self.NPHYS, self.debug, self.upto = SEQ, NPG, NPHYS, debug, upto
        self.split = split
        import os as _os
        self.dbg = set(_os.environ.get('KDBG', '').split(','))
        self.kstop = int(_os.environ.get('KSTOP', '99'))
        self.nsmax = int(_os.environ.get('NSMAX', '99'))


class Prog:
    def __init__(self, nc, stack):
        self.nc = nc
        self.stack = stack
        self.eng = {"pe": nc.tensor, "dve": nc.vector, "act": nc.scalar,
                    "pool": nc.gpsimd, "sp": nc.sync}
        self.sem = {k: stack.enter_context(nc.semaphore("e_" + k)) for k in self.eng}
        self.cnt = {k: 0 for k in self.eng}
        self.seen = {k: {} for k in self.eng}
        self.lastw = {}
        self.readers = {}
        self.dsem = {}
        self.nsem = 0
        self.excl = set()

    def _wait(self, e, dep):
        sem, val, src = dep
        if src == e and e == "pe":
            return
        name = id(sem)
        if self.seen[e].get(name, 0) >= val:
            return
        self.eng[e].wait_ge(sem, val)
        self.seen[e][name] = val

    def _deps(self, e, reads, writes):
        for k in reads:
            if k in self.lastw:
                self._wait(e, self.lastw[k])
        for k in writes:
            if k in self.lastw:
                self._wait(e, self.lastw[k])
            for d in self.readers.get(k, {}).values():
                self._wait(e, d)

    def _record(self, dep, reads, writes):
        for k in writes:
            self.lastw[k] = dep
            self.readers[k] = {}
        for k in reads:
            if k in writes:
                continue
            self.readers.setdefault(k, {})[id(dep[0])] = dep

    def op(self, e, fn, reads=(), writes=()):
        xr = [k for k in reads if k in self.excl]
        if xr:
            reads = [k for k in reads if k not in self.excl]
            writes = list(writes) + [k for k in xr if k not in writes]
        self._deps(e, reads, writes)
        ins = fn(self.eng[e])
        self.cnt[e] += 1
        ins.then_inc(self.sem[e], 1)
        self._record((self.sem[e], self.cnt[e], e), reads, writes)
        return ins

    def dma(self, e, out, in_, reads=(), writes=(), semkey=None, indirect=None, **kw):
        self._deps(e, reads, writes)
        key = semkey if semkey is not None else (writes[0] if writes else reads[0])
        if key not in self.dsem:
            self.dsem[key] = [self.stack.enter_context(self.nc.semaphore("d%d" % self.nsem)), 0]
            self.nsem += 1
        ds = self.dsem[key]
        if indirect is not None:
            ins = self.eng[e].indirect_dma_start(out=out, out_offset=None, in_=in_,
                                                 in_offset=indirect, **kw)
        else:
            ins = self.eng[e].dma_start(out=out, in_=in_, **kw)
        ds[1] += 16
        ins.then_inc(ds[0], 16)
        dep = (ds[0], ds[1], "dma")
        self._record(dep, reads, writes)
        return dep

    def wait_all(self, e, deps):
        for d in deps:
            self._wait(e, d)

    def barrier(self):
        deps = [(self.sem[k], self.cnt[k], k) for k in self.eng if self.cnt[k] > 0]
        deps += [(ds[0], ds[1], "dma") for ds in self.dsem.values()]
        for e in self.eng:
            for d in deps:
                if d[2] == e:
                    continue
                self._wait(e, d)


def build_nc(cfg):
    SEQ, NPG, NPHYS = cfg.SEQ, cfg.NPG, cfg.NPHYS
    NT = SEQ // 128
    NQ = SEQ // 512
    nc = bass.Bass("TRN2", target_bir_lowering=False)

    def din(name, shape, dt=F32):
        return nc.dram_tensor(name, list(shape), dt, kind="ExternalInput").ap()

    def dout(name, shape, dt=F32):
        return nc.dram_tensor(name, list(shape), dt, kind="ExternalOutput").ap()

    def dscr(name, shape, dt=BF16):
        return nc.dram_tensor(name, list(shape), dt, kind="Internal").ap()

    xp = din("xp", [SEQ, D])
    xs = din("xs", [NST, D])
    cvec = din("cvec", [NSEQ, D])
    norm_mix = din("norm_mix", [2, D])
    norm_ffn = din("norm_ffn", [2, D])
    ada_w_mix = din("ada_w_mix", [2, D, 3 * D])
    ada_b_mix = din("ada_b_mix", [2, 3 * D])
    ada_w_ffn = din("ada_w_ffn", [2, D, 3 * D])
    ada_b_ffn = din("ada_b_ffn", [2, 3 * D])
    w_up = din("w_up", [2, D, 4 * D])
    w_down = din("w_down", [2, 4 * D, D])
    w_qkv = din("w_qkv", [D, 3 * D])
    w_o = din("w_o", [D, D])
    sb_bias = din("sb_bias", [16])
    if not cfg.split:
        cache_k = din("cache_k", [NPHYS * 128, D])
        cache_v = din("cache_v", [NPHYS * 128, D])
        ptab = din("ptab", [NS * NPG], I32)
    else:
        oTs_in = din("oTs_in", [128, 8, NST])
    state_wkv = din("state_wkv", [NS, 16, 64, 64])
    state_shift = din("state_shift", [NS, D])
    rw_mu = din("rw_mu", [6, D])
    rw_w_r = din("rw_w_r", [D, D])
    rw_w_k = din("rw_w_k", [D, D])
    rw_w_v = din("rw_w_v", [D, D])
    rw_w_o = din("rw_w_o", [D, D])
    rw_w0 = din("rw_w0", [D])
    rw_w1 = din("rw_w1", [D, 64])
    rw_w2 = din("rw_w2", [64, D])
    rw_a0 = din("rw_a0", [D])
    rw_a1 = din("rw_a1", [D, 64])
    rw_a2 = din("rw_a2", [64, D])
    rw_g1 = din("rw_g1", [D, 160])
    rw_g2 = din("rw_g2", [160, D])
    rw_k_k = din("rw_k_k", [D])
    rw_k_a = din("rw_k_a", [D])
    rw_r_k = din("rw_r_k", [D])
    rw_ln_g = din("rw_ln_g", [D])
    rw_ln_b = din("rw_ln_b", [D])
    final_norm = din("final_norm", [D])

    kp = dout("kp", [SEQ, D])
    vp = dout("vp", [SEQ, D])
    if not cfg.split:
        ks = dout("ks", [NST, D])
        vs = dout("vs", [NST, D])
    if cfg.debug:
        xl0p = dout("xl0p", [SEQ, D])
        xl0s = dout("xl0s", [NST, D])
    else:
        xl0p = dscr("xl0p", [SEQ, D], F32)
        xl0s = dscr("xl0s", [NST, D], F32)
    yp = dout("yp", [SEQ, D])
    ys = dout("ys", [NST, D])
    wkvp = dout("wkvp", [16, 64, 64])
    shp = dout("shp", [D])
    wkvs = dout("wkvs", [NS, 16, 64, 64])
    shs = dout("shs", [NS, D])

    Wqkv_b = dscr("Wqkv_b", [D, 3 * D])
    Wo_b = dscr("Wo_b", [D, D])
    Wup_b = [dscr("Wup_b%d" % i, [D, 4 * D]) for i in range(2)]
    Wdn_b = [dscr("Wdn_b%d" % i, [4 * D, D]) for i in range(2)]
    Wam_b = [dscr("Wam_b%d" % i, [D, 3 * D]) for i in range(2)]
    Waf_b = [dscr("Waf_b%d" % i, [D, 3 * D]) for i in range(2)]
    qT_d = dscr("qT_d", [128, 8, SEQ])
    kT_d = dscr("kT_d", [128, 8, SEQ])
    V_d = dscr("V_d", [SEQ, D])
    oT_d = dscr("oT_d", [128, 8, SEQ])
    ygT_d = dscr("ygT_d", [128, 8, SEQ])
    Wr_b = dscr("Wr_b", [D, D])
    Wk_b = dscr("Wk_b", [D, D])
    Wv_b = dscr("Wv_b", [D, D])
    Wrwo_b = dscr("Wrwo_b", [D, D])

    with ExitStack() as st:
        P = Prog(nc, st)

        uniq = [0]

        def sbt(stack, name, shape, dt=F32):
            uniq[0] += 1
            return stack.enter_context(nc.sbuf_tensor("%s_u%d" % (name, uniq[0]), list(shape), dt))

        def pst(stack, name, shape, dt=F32):
            uniq[0] += 1
            P.excl.add(name)
            esz = 4 if dt == F32 else 2
            n = int(np.prod(shape[1:]))
            per_bank = 2048 // esz
            npad = ((n + per_bank - 1) // per_bank) * per_bank
            t_ = stack.enter_context(nc.psum_tensor("%s_u%d" % (name, uniq[0]), [128, npad], dt))
            v = t_[:, 0:n]
            if len(shape) == 3:
                v = v.rearrange("p (a b) -> p a b", b=shape[2])
            elif len(shape) == 4:
                v = v.rearrange("p (a b c) -> p a b c", b=shape[2], c=shape[3])
            return v

        out_deps = []

        ident = sbt(st, "ident", [128, 128], F32)
        identb = sbt(st, "identb", [128, 128], BF16)
        negU = sbt(st, "negU", [128, 128], BF16)
        negO = sbt(st, "negO", [128, 128], BF16)
        biasb = sbt(st, "biasb", [128, 16], F32)
        cT = sbt(st, "cT", [128, 8, NSEQ], F32)
        scT = sbt(st, "scT", [128, 8, NSEQ], BF16)
        modT = [sbt(st, "modT%d" % i, [128, 16, NSEQ], F32) for i in range(4)]
        Acoef = [sbt(st, "Acoef%d" % i, [128, 8, NSEQ], F32) for i in range(4)]
        gate_bc = [sbt(st, "gate%d" % (i % 2), [128, NSEQ, D], BF16) for i in range(2)] * 2
        gsm = [sbt(st, "gsm%d" % (i % 2), [NST, D], BF16) for i in range(2)] * 2

        qTs = sbt(st, "qTs", [128, 8, NST], BF16)
        oTs = sbt(st, "oTs", [128, 8, NST], BF16)
        ygs = sbt(st, "ygs", [128, 8, NST], BF16)
        P.op("pool", lambda e: e.memset(ident[:], 0.0), writes=["ident"])
        P.op("pool", lambda e: e.affine_select(
            out=ident[:], in_=ident[:], pattern=[[-1, 128]], compare_op=ALU.not_equal,
            fill=1.0, base=0, channel_multiplier=1), reads=["ident"], writes=["ident"])
        P.op("pool", lambda e: e.tensor_copy(out=identb[:], in_=ident[:]), reads=["ident"], writes=["identb"])
        P.op("pool", lambda e: e.memset(negO[:], -1.0), writes=["negO"])
        P.op("pool", lambda e: e.affine_select(
            out=negU[:], in_=negO[:], pattern=[[-1, 128]], compare_op=ALU.is_ge,
            fill=0.0, base=0, channel_multiplier=1), reads=["negO"], writes=["negU"])
        P.dma("sp", biasb[:], sb_bias.partition_broadcast(128), writes=["biasb"])
        with nc.allow_non_contiguous_dma(reason="tiny transposed parameter loads"):
            for s_ in range(NSEQ):
                P.dma("sp", cT[:, :, s_], cvec[s_].rearrange("(k p) -> p k", p=128), writes=["cT"])
        P.op("act", lambda e: e.activation(out=scT[:], in_=cT[:], func=AF.Silu),
             reads=["cT"], writes=["scT"])

        with ExitStack() as ph:
            NB = 4
            bnc = [sbt(ph, "bnc%d" % i, [128, 2048], BF16) for i in range(NB)]
            cnt = [0]

            def cast_w(src, dst, rows, cols):
                cw = 2048 if cols % 2048 == 0 else 1536 if cols % 1536 == 0 else 1024
                for r in range(rows // 128):
                    for c in range(cols // cw):
                        b = cnt[0] % NB
                        cnt[0] += 1
                        key = "bnc%d" % b
                        P.dma("pool", bnc[b][:, 0:cw], src[r * 128:(r + 1) * 128, c * cw:(c + 1) * cw],
                              writes=[key], max_dma_last_dim=4096)
                        P.dma("sp", dst[r * 128:(r + 1) * 128, c * cw:(c + 1) * cw], bnc[b][:, 0:cw],
                              reads=[key], semkey=key + "o")

            cast_w(w_qkv, Wqkv_b, D, 3 * D)
            cast_w(w_o, Wo_b, D, D)
            cast_w(rw_w_r, Wr_b, D, D)
            cast_w(rw_w_k, Wk_b, D, D)
            cast_w(rw_w_v, Wv_b, D, D)
            cast_w(rw_w_o, Wrwo_b, D, D)
            for i in range(2):
                cast_w(ada_w_mix[i], Wam_b[i], D, 3 * D)
                cast_w(ada_w_ffn[i], Waf_b[i], D, 3 * D)
                cast_w(w_up[i], Wup_b[i], D, 4 * D)
                cast_w(w_down[i], Wdn_b[i], 4 * D, D)
            P.barrier()

        def compute_ada(idx, Wb, bvec, gvec):
            with ExitStack() as ph:
                wada = sbt(ph, "wada", [128, 8, 3 * D], BF16)
                screp = sbt(ph, "screp", [128, 8, NSEQ, 128], BF16)
                P.op("dve", lambda e: e.tensor_copy(
                    out=screp[:], in_=scT[:].unsqueeze(3).to_broadcast([128, 8, NSEQ, 128])),
                    reads=["scT"], writes=["screp"])
                adabT = sbt(ph, "adabT", [128, 24], F32)
                gT = sbt(ph, "gT", [128, 8], F32)
                adab_bc = sbt(ph, "adab_bc", [128, D], F32)
                psm = pst(ph, "psm", [128, 16, 8], F32)
                psg = [pst(ph, "psg%d" % i, [128, 512], F32) for i in range(2)]
                with nc.allow_non_contiguous_dma(reason="tiny transposed parameter loads"):
                    P.dma("sp", gT[:], gvec.rearrange("(k p) -> p k", p=128), writes=["gT"])
                    P.dma("sp", adabT[:], bvec.rearrange("(k p) -> p k", p=128), writes=["adabT"])
                P.dma("sp", adab_bc[:], bvec[2 * D:3 * D].partition_broadcast(128), writes=["adab_bc"])
                for k in range(8):
                    P.dma("sp", wada[:, k, :], Wb[k * 128:(k + 1) * 128, :], writes=["wada"])
                M, A, G = "modT%d" % idx, "Acoef%d" % idx, "gate%d" % (idx % 2)
                for oc in range(16):
                    for k in range(8):
                        P.op("pe", lambda e: e.matmul(
                            psm[:, oc, 0:NSEQ], lhsT=wada[:, k, oc * 128:(oc + 1) * 128],
                            rhs=scT[:, k, :], start=(k == 0), stop=(k == 7)),
                            reads=["wada", "scT"], writes=["psm"])
                P.op("dve", lambda e: e.tensor_tensor(
                    out=modT[idx][:], in0=psm[:, :, 0:NSEQ],
                    in1=adabT[:, 0:16].unsqueeze(2).to_broadcast([128, 16, NSEQ]), op=ALU.add),
                    reads=["psm", "adabT"], writes=[M])
                P.op("dve", lambda e: e.tensor_scalar(
                    out=Acoef[idx][:], in0=modT[idx][:, 8:16, :], scalar1=1.0, scalar2=None, op0=ALU.add),
                    reads=[M], writes=[A])
                P.op("dve", lambda e: e.tensor_tensor(
                    out=Acoef[idx][:], in0=Acoef[idx][:],
                    in1=gT[:].unsqueeze(2).to_broadcast([128, 8, NSEQ]), op=ALU.mult),
                    reads=[A, "gT"], writes=[A])
                for s_ in range(NSEQ):
                    for n in range(2):
                        pg = psg[n]
                        PG = "psg%d" % n
                        for k in range(8):
                            P.op("pe", lambda e: e.matmul(
                                pg[:], lhsT=screp[:, k, s_, :],
                                rhs=wada[:, k, 2 * D + n * 512:2 * D + (n + 1) * 512],
                                start=(k == 0), stop=(k == 7)),
                                reads=["wada", "screp"], writes=[PG])
                        P.op("dve", lambda e: e.tensor_tensor(
                            out=gate_bc[idx][:, s_, n * 512:(n + 1) * 512], in0=pg[:],
                            in1=adab_bc[:, n * 512:(n + 1) * 512], op=ALU.add),
                            reads=[PG, "adab_bc"], writes=[G])
                for s_ in range(NS):
                    P.dma("sp", gsm[idx][s_ * TS:(s_ + 1) * TS, :], gate_bc[idx][0:TS, 1 + s_, :],
                          reads=[G], writes=["gsm%d" % (idx % 2)])
                P.barrier()

        if cfg.upto >= 1:
            compute_ada(0, Wam_b[0], ada_b_mix[0], norm_mix[0])
            compute_ada(1, Waf_b[0], ada_b_ffn[0], norm_ffn[0])

        def norm_to_hT(tl, X, np_, idx, groups, hT_ap_fn, HT, pstr):
            b = tl["b"]
            J, SS, RS, XN = "junk%d" % b, "ssq%d" % b, "rstd%d" % b, "xn%d" % b
            P.op("act", lambda e: e.activation(out=tl["junk"][0:np_, :], in_=tl["x"][0:np_, :],
                                               func=AF.Square, accum_out=tl["ssq"][0:np_, :]),
                 reads=[X], writes=[J, SS])
            P.op("act", lambda e: e.activation(out=tl["rstd"][0:np_, :], in_=tl["ssq"][0:np_, :],
                                               func=AF.Sqrt, scale=1.0 / D, bias=EPS),
                 reads=[SS], writes=[RS])
            P.op("dve", lambda e: e.reciprocal(out=tl["rstd"][0:np_, :], in_=tl["rstd"][0:np_, :]),
                 reads=[RS], writes=[RS])
            P.op("dve", lambda e: e.tensor_scalar(out=tl["xn"][0:np_, :], in0=tl["x"][0:np_, :],
                                                  scalar1=tl["rstd"][0:np_, :], scalar2=None,
                                                  op0=ALU.mult),
                 reads=[X, RS], writes=[XN])
            for k in range(8 if cfg.kstop >= 3 else 0):
                pt, PT = pstr[k % 2]
                P.op("pe", lambda e: e.transpose(out=pt[:, 0:np_], in_=tl["xn"][0:np_, k * 128:(k + 1) * 128],
                                                 identity=ident[0:np_, 0:np_]),
                     reads=[XN, "ident"], writes=[PT])
                for (sq, c0, c1) in groups:
                    P.op("dve", lambda e: e.tensor_scalar(
                        out=hT_ap_fn(k, c0, c1), in0=pt[:, c0:c1],
                        scalar1=Acoef[idx][:, k, sq:sq + 1], scalar2=modT[idx][:, k, sq:sq + 1],
                        op0=ALU.mult, op1=ALU.add),
                        reads=[PT, "Acoef%d" % idx, "modT%d" % idx], writes=[HT])

        stiles = [("p", q, 512) for q in range(NQ)] + [("s", 0, NST)]
        sgroups = [(1 + s, s * TS, (s + 1) * TS) for s in range(NS)]

        st_att = ExitStack()
        kTs = sbt(st_att, "kTs", [128, 8, NS, 128], BF16)
        Vs = sbt(st_att, "Vs", [128, NS, D], BF16)
        if cfg.upto >= 2:
            with ExitStack() as ph:
                wqkv = sbt(ph, "wqkv", [128, 8, 3 * D], BF16)
                NBUF = 2
                tls = [dict(b=i, x=sbt(ph, "xt%d" % i, [128, D]), junk=sbt(ph, "junk%d" % i, [128, D], F32 if "junk32" in cfg.dbg else BF16),
                            ssq=sbt(ph, "ssq%d" % i, [128, 1]), rstd=sbt(ph, "rstd%d" % i, [128, 1]),
                            xn=sbt(ph, "xn%d" % i, [128, D])) for i in range(NBUF)]
                hT = [sbt(ph, "hT%d" % i, [128, 8, 512], BF16) for i in range(2)]
                qk_st = [sbt(ph, "qkst%d" % i, [128, 512], BF16) for i in range(4)]
                kvo = [sbt(ph, "kvo%d" % i, [128, 2 * D], F32) for i in range(2)]
                vbf = [sbt(ph, "vbf%d" % i, [128, D], BF16) for i in range(2)]
                ptr_ = [pst(ph, "ptr%d" % i, [128, 512], F32) for i in range(2)]
                pqk = [pst(ph, "pqk%d" % i, [128, 512], F32) for i in range(2)]
                pkv = [pst(ph, "pkv%d" % i, [128, 512], F32) for i in range(4)]
                pstr = [(ptr_[i], "ptr%d" % i) for i in range(2)]
                for k in range(8):
                    P.dma("sp", wqkv[:, k, :], Wqkv_b[k * 128:(k + 1) * 128, :], writes=["wqkv"])
                if "nomemset" not in cfg.dbg:
                    P.op("pool", lambda e: e.memset(kTs[:], 0.0), writes=["kTs"])
                    P.op("pool", lambda e: e.memset(Vs[:], 0.0), writes=["Vs"])
                it = 0
                nq = 0
                for si, (kind, qi, ntok) in enumerate(stiles):
                    if kind == "s" and ("nosample" in cfg.dbg or cfg.split):
                        continue
                    hb = si % 2
                    HT = "hT%d" % hb
                    nsub = (ntok + 127) // 128
                    for sub in range(nsub):
                        tl = tls[it % NBUF]
                        it += 1
                        b = tl["b"]
                        X = "xt%d" % b
                        np_ = min(128, ntok)
                        if kind == "p":
                            t = qi * 4 + sub
                            P.dma("sp", tl["x"][0:np_, :], xp[t * 128:(t + 1) * 128, :], writes=[X])
                            groups = [(0, 0, 128)]
                        else:
                            P.dma("sp", tl["x"][0:np_, :], xs, writes=[X])
                            groups = sgroups
                        if cfg.kstop < 2:
                            continue
                        norm_to_hT(tl, X, np_, 0, groups,
                                   lambda k, c0, c1: hT[hb][:, k, sub * 128 + c0:sub * 128 + c1], HT, pstr)
                        if cfg.kstop < 4:
                            continue
                        kb_ = (it - 1) % 2
                        KV, VB = "kvo%d" % kb_, "vbf%d" % kb_
                        for n in range(4):
                            po, PO = pkv[n], "pkv%d" % n
                            for k in range(8):
                                P.op("pe", lambda e: e.matmul(
                                    po[0:np_, :], lhsT=hT[hb][:, k, sub * 128:sub * 128 + np_],
                                    rhs=wqkv[:, k, D + n * 512:D + (n + 1) * 512],
                                    start=(k == 0), stop=(k == 7)),
                                    reads=[HT, "wqkv"], writes=[PO])
                            P.op("act", lambda e: e.copy(out=kvo[kb_][0:np_, n * 512:(n + 1) * 512], in_=po[0:np_, :]),
                                 reads=[PO], writes=[KV])
                            if n >= 2 and "novbf" not in cfg.dbg:
                                if "vbfpsum" in cfg.dbg:
                                    P.op("dve", lambda e: e.tensor_copy(
                                        out=vbf[kb_][0:np_, (n - 2) * 512:(n - 1) * 512], in_=po[0:np_, :]),
                                        reads=[PO, KV], writes=[VB])
                                else:
                                    P.op("pool", lambda e: e.tensor_copy(
                                        out=vbf[kb_][0:np_, (n - 2) * 512:(n - 1) * 512],
                                        in_=kvo[kb_][0:np_, n * 512:(n + 1) * 512]),
                                        reads=[KV], writes=[VB])
                        if cfg.kstop < 5:
                            continue
                        if kind == "p":
                            out_deps.append(P.dma("sp", kp[t * 128:(t + 1) * 128, :], kvo[kb_][:, 0:D],
                                                  reads=[KV], semkey="ok%d" % kb_))
                            out_deps.append(P.dma("sp", vp[t * 128:(t + 1) * 128, :], kvo[kb_][:, D:2 * D],
                                                  reads=[KV], semkey="ov%d" % kb_))
                            if "novd" not in cfg.dbg:
                                P.dma("sp", V_d[t * 128:(t + 1) * 128, :], vbf[kb_][:], reads=[VB], semkey="ovb%d" % kb_)
                        else:
                            out_deps.append(P.dma("sp", ks, kvo[kb_][0:NST, 0:D], reads=[KV], semkey="ok%d" % kb_))
                            out_deps.append(P.dma("sp", vs, kvo[kb_][0:NST, D:2 * D], reads=[KV], semkey="ov%d" % kb_))
                            for s_ in range(NS):
                                P.dma("sp", Vs[0:TS, s_, :], vbf[kb_][s_ * TS:(s_ + 1) * TS, :],
                                      reads=[VB], writes=["Vs"], semkey="vs_mv")
                    for fc in range(0 if "noqk" in cfg.dbg else 16):
                        pq, PQ = pqk[fc % 2], "pqk%d" % (fc % 2)
                        for k in range(8):
                            P.op("pe", lambda e: e.matmul(
                                pq[:, 0:ntok], lhsT=wqkv[:, k, fc * 128:(fc + 1) * 128],
                                rhs=hT[hb][:, k, 0:ntok], start=(k == 0), stop=(k == 7)),
                                reads=[HT, "wqkv"], writes=[PQ])
                        if kind == "p":
                            sb_ = nq % 4
                            nq += 1
                            QS = "qkst%d" % sb_
                            if fc < 8:
                                P.op("act", lambda e: e.activation(out=qk_st[sb_][:], in_=pq[:], func=AF.Copy, scale=0.125),
                                     reads=[PQ], writes=[QS])
                                P.dma("sp", qT_d[:, fc, qi * 512:(qi + 1) * 512], qk_st[sb_][:], reads=[QS], semkey=QS + "o")
                            else:
                                P.op("dve", lambda e: e.tensor_copy(out=qk_st[sb_][:], in_=pq[:]),
                                     reads=[PQ], writes=[QS])
                                P.dma("sp", kT_d[:, fc - 8, qi * 512:(qi + 1) * 512], qk_st[sb_][:], reads=[QS], semkey=QS + "o")
                        else:
                            if fc < 8:
                                P.op("act", lambda e: e.activation(out=qTs[:, fc, :], in_=pq[:, 0:NST], func=AF.Copy, scale=0.125),
                                     reads=[PQ], writes=["qTs"])
                            else:
                                P.op("dve", lambda e: e.tensor_copy(
                                    out=kTs[:, fc - 8, :, 0:TS],
                                    in_=pq[:, 0:NST].rearrange("p (s t) -> p s t", t=TS)),
                                    reads=[PQ], writes=["kTs"])
                P.barrier()

        def attn_pipeline(tmp, nblocks, emit_z, emit_av, mask_of, bias_ap_of, ncols=512):
            raise NotImplementedError

        if cfg.upto >= 3:
            with ExitStack() as ph:
                kT_all = sbt(ph, "kT_all", [128, 4, SEQ], BF16)
                V_all = sbt(ph, "V_all", [128, NT, 512], BF16)
                qT_t = [sbt(ph, "qT_t%d" % i, [128, 4, 512], BF16) for i in range(2)]
                masks = sbt(ph, "masks", [128, 4, 512], BF16)
                ones_m = sbt(ph, "ones_m", [128, 512], BF16)
                e_t = [sbt(ph, "e_t%d" % i, [128, 512], F32) for i in range(2)]
                sp_t = [[sbt(ph, "sp_t%d_%d" % (i, j), [128, 512], BF16) for j in range(2)] for i in range(2)]
                a_t = [[sbt(ph, "a_t%d_%d" % (i, j), [128, 512], BF16) for j in range(2)] for i in range(2)]
                SP32 = [sbt(ph, "SP32_%d" % i, [128, 512], F32) for i in range(2)]
                SPbf = [sbt(ph, "SPbf_%d" % i, [128, 512], BF16) for i in range(2)]
                o_st = [sbt(ph, "o_st%d" % i, [128, 512], BF16) for i in range(2)]
                Pz = [pst(ph, "Pz%d" % i, [128, 512], F32) for i in range(2)]
                Pa = [pst(ph, "Pa%d" % i, [128, 512], F32) for i in range(2)]
                Po = [pst(ph, "Po%d" % i, [128, 512], F32) for i in range(2)]

                P.op("pool", lambda e: e.memset(ones_m[:], 1.0), writes=["ones_m"])
                for j in range(4):
                    P.op("pool", lambda e: e.affine_select(
                        out=masks[:, j, :], in_=ones_m[:], pattern=[[1, 512]], compare_op=ALU.is_gt,
                        fill=0.0, base=-128 * j, channel_multiplier=-1), reads=["ones_m"], writes=["masks"])

                for gh, qi in [(a_, b_) for a_ in range(2) for b_ in range(NQ)]:
                    if qi == 0:
                        for gl in range(4):
                            P.dma("sp", kT_all[:, gl, :], kT_d[:, gh * 4 + gl, :], writes=["kT_all"])
                        for t in range(NT):
                            P.dma("sp", V_all[:, t, :], V_d[t * 128:(t + 1) * 128, gh * 512:(gh + 1) * 512],
                                  writes=["V_all"])
                    qb = (gh * NQ + qi) % 2
                    QT = "qT_t%d" % qb
                    for gl in range(4):
                        P.dma("sp", qT_t[qb][:, gl, :], qT_d[:, gh * 4 + gl, qi * 512:(qi + 1) * 512], writes=[QT])
                    nblk = 4 * qi + 4
                    for gl in range(4):
                        g = gh * 4 + gl
                        po, PO = Po[g % 2], "Po%d" % (g % 2)
                        for h2 in range(2):
                            P.op("pool", lambda e: e.memset(SP32[h2][:], 0.0), writes=["SP32_%d" % h2])
                        for bi, kb in enumerate(range(nblk - 1, -1, -1)):
                            j = kb - 4 * qi
                            first, last = (bi == 0), (kb == 0)
                            pr = slice(None)
                            for h2 in range(2):
                                pr = slice(h2 * 64, (h2 + 1) * 64)
                                P.op("pe", lambda e: e.matmul(
                                    Pz[h2][:], lhsT=kT_all[pr, gl, kb * 128:(kb + 1) * 128],
                                    rhs=qT_t[qb][pr, gl, :], start=True, stop=True),
                                    reads=["kT_all", QT], writes=["Pz%d" % h2])
                            for h2 in range(2):
                                h = 2 * g + h2
                                S = "sp_t%d_%d" % (h2, bi % 2)
                                spt = sp_t[h2][bi % 2]
                                P.op("act", lambda e: e.activation(out=e_t[h2][:], in_=Pz[h2][:], func=AF.Exp,
                                                                   bias=biasb[:, h:h + 1]),
                                     reads=["Pz%d" % h2, "biasb"], writes=["e_t%d" % h2])
                                P.op("act", lambda e: e.activation(out=spt[:], in_=e_t[h2][:], func=AF.Ln, bias=1.0),
                                     reads=["e_t%d" % h2], writes=[S])
                                if j >= 0:
                                    P.op("dve", lambda e: e.tensor_tensor(out=spt[:], in0=spt[:], in1=masks[:, j, :],
                                                                          op=ALU.mult),
                                         reads=[S, "masks"], writes=[S])
                            for h2 in range(2):
                                pr = slice(h2 * 64, (h2 + 1) * 64)
                                S = "sp_t%d_%d" % (h2, bi % 2)
                                spt = sp_t[h2][bi % 2]
                                PA = "Pa%d" % h2
                                P.op("pe", lambda e: e.matmul(
                                    Pa[h2][:], lhsT=kT_all[pr, gl, kb * 128:(kb + 1) * 128],
                                    rhs=qT_t[qb][pr, gl, :], start=True, stop=False),
                                    reads=["kT_all", QT], writes=[PA])
                                P.op("pe", lambda e: e.matmul(
                                    Pa[h2][:], lhsT=negU[:], rhs=spt[:], start=False, stop=first),
                                    reads=["negU", S], writes=[PA])
                                if not first:
                                    P.op("pe", lambda e: e.matmul(
                                        Pa[h2][:], lhsT=negO[:], rhs=SPbf[h2][:], start=False, stop=True),
                                        reads=["negO", "SPbf_%d" % h2], writes=[PA])
                            for h2 in range(2):
                                h = 2 * g + h2
                                A = "a_t%d_%d" % (h2, bi % 2)
                                at = a_t[h2][bi % 2]
                                P.op("act", lambda e: e.activation(out=at[:], in_=Pa[h2][:], func=AF.Exp,
                                                                   bias=biasb[:, h:h + 1]),
                                     reads=["Pa%d" % h2, "biasb"], writes=[A])
                                if j >= 0:
                                    P.op("dve", lambda e: e.tensor_tensor(out=at[:], in0=at[:], in1=masks[:, j, :],
                                                                          op=ALU.mult),
                                         reads=[A, "masks"], writes=[A])
                            for h2 in range(2):
                                h = 2 * g + h2
                                at = a_t[h2][bi % 2]
                                P.op("pe", lambda e: e.matmul(
                                    po[h2 * 64:(h2 + 1) * 64, :], lhsT=V_all[:, kb, (h - 8 * gh) * 64:(h - 8 * gh + 1) * 64],
                                    rhs=at[:], start=first, stop=last),
                                    reads=["V_all", "a_t%d_%d" % (h2, bi % 2)], writes=[PO])
                            if not last:
                                for h2 in range(2):
                                    S = "sp_t%d_%d" % (h2, bi % 2)
                                    P.op("pool", lambda e: e.tensor_tensor(out=SP32[h2][:], in0=SP32[h2][:],
                                                                           in1=sp_t[h2][bi % 2][:], op=ALU.add),
                                         reads=["SP32_%d" % h2, S], writes=["SP32_%d" % h2])
                                    P.op("pool", lambda e: e.tensor_copy(out=SPbf[h2][:], in_=SP32[h2][:]),
                                         reads=["SP32_%d" % h2], writes=["SPbf_%d" % h2])
                        OS = "o_st%d" % (g % 2)
                        P.op("dve", lambda e: e.tensor_copy(out=o_st[g % 2][:], in_=po[:]), reads=[PO], writes=[OS])
                        P.dma("sp", oT_d[:, g, qi * 512:(qi + 1) * 512], o_st[g % 2][:], reads=[OS], semkey=OS + "o")
                P.barrier()

        if cfg.split:
            P.dma("pool", oTs[:], oTs_in, writes=["oTs"])
        if cfg.upto >= 4 and not cfg.split:
            with ExitStack() as ph:
                NCOL = NS * 16 * TS
                pt_i = sbt(ph, "pt_i", [128, NS * NPG], I32)
                pt_f = sbt(ph, "pt_f", [128, NS * NPG], F32)
                iota_f = sbt(ph, "iota_f", [128, 1], F32)
                iota_i = sbt(ph, "iota_i", [128, 1], I32)
                idx = sbt(ph, "idx", [128, NS * NPG], I32)
                NG = 2
                Kpg = [[sbt(ph, "Kpg%d_%d" % (i, s), [128, D], BF16) for s in range(NS)] for i in range(NG)]
                Vpg = [[sbt(ph, "Vpg%d_%d" % (i, s), [128, D], BF16) for s in range(NS)] for i in range(NG)]
                kTp = [sbt(ph, "kTp%d" % i, [128, NS, 8, 128], BF16) for i in range(NG)]
                maskn = sbt(ph, "maskn", [128, NS * 16, TS], BF16)
                ones_m = sbt(ph, "ones_m2", [128, NCOL], BF16)
                bias_bc = sbt(ph, "bias_bc", [128, NS, 16, TS], F32)
                zsb = sbt(ph, "zsb", [128, NCOL], F32)
                e_t = sbt(ph, "es_t", [128, NCOL], F32)
                sp_t = [sbt(ph, "sps_t%d" % i, [128, NCOL], BF16) for i in range(2)]
                t1 = sbt(ph, "t1s", [128, NCOL], F32)
                a_t = [sbt(ph, "as_t%d" % i, [128, NCOL], BF16) for i in range(2)]
                SP32 = sbt(ph, "SPs32", [128, NCOL], F32)
                SPbf = sbt(ph, "SPsbf", [128, NCOL], BF16)
                zero_l = sbt(ph, "zero_l", [128, 128], BF16)
                Pz = [pst(ph, "Pzs%d" % i, [128, NCOL], F32) for i in range(2)]
                Pa = [pst(ph, "Pas%d" % i, [128, NCOL], F32) for i in range(2)]
                Po = pst(ph, "Pos", [128, NS * 8 * TS], F32)
                Ptr = [pst(ph, "Ptrs%d" % i, [128, 8, 128], BF16) for i in range(2)]

                P.dma("sp", pt_i[:], ptab.partition_broadcast(128), writes=["pt_i"])
                P.op("dve", lambda e: e.tensor_copy(out=pt_f[:], in_=pt_i[:]), reads=["pt_i"], writes=["pt_f"])
                P.op("pool", lambda e: e.iota(iota_i[:], pattern=[[0, 1]], base=0, channel_multiplier=1),
                     writes=["iota_i"])
                P.op("dve", lambda e: e.tensor_copy(out=iota_f[:], in_=iota_i[:]), reads=["iota_i"], writes=["iota_f"])
                P.op("dve", lambda e: e.tensor_scalar(out=pt_f[:], in0=pt_f[:], scalar1=128.0, scalar2=iota_f[:, 0:1],
                                                      op0=ALU.mult, op1=ALU.add),
                     reads=["pt_f", "iota_f"], writes=["pt_f"])
                P.op("dve", lambda e: e.tensor_copy(out=idx[:], in_=pt_f[:]), reads=["pt_f"], writes=["idx"])
                P.op("pool", lambda e: e.memset(ones_m[:], 1.0), writes=["ones_m2"])
                P.op("pool", lambda e: e.memset(zero_l[:], 0.0), writes=["zero_l"])
                P.op("pool", lambda e: e.memset(SP32[:], 0.0), writes=["SPs32"])
                P.op("pool", lambda e: e.affine_select(
                    out=maskn[:], in_=ones_m[:].rearrange("p (c t) -> p c t", t=TS),
                    pattern=[[0, NS * 16], [1, TS]], compare_op=ALU.is_gt,
                    fill=0.0, base=0, channel_multiplier=-1), reads=["ones_m2"], writes=["maskn"])
                P.op("dve", lambda e: e.tensor_copy(
                    out=bias_bc[:, 0, :, :], in_=biasb[:].unsqueeze(2).to_broadcast([128, 16, TS])),
                    reads=["biasb"], writes=["bias_bc"])
                for s_ in range(1, NS):
                    P.op("dve", lambda e: e.tensor_copy(out=bias_bc[:, s_, :, :], in_=bias_bc[:, 0, :, :]),
                         reads=["bias_bc"], writes=["bias_bc"])
                P.op("pe", lambda e: e.matmul(Po[:], lhsT=zero_l[:], rhs=ones_m[:, 0:NS * 8 * TS], start=True, stop=False),
                     reads=["zero_l", "ones_m2"], writes=["Pos"])

                def col(s_, g, h2):
                    return ((s_ * 8 + g) * 2 + h2) * TS

                blocks = ["new"] + list(range(NPG - 1, -1, -1))
                for bi, blk in enumerate(blocks):
                    gb = bi % NG
                    first, last = (bi == 0), (bi == len(blocks) - 1)
                    if blk == "new":
                        def k_lhsT(s_, g, h2):
                            return kTs[h2 * 64:(h2 + 1) * 64, g, s_, :]

                        def v_lhsT(s_, h):
                            return Vs[:, s_, h * 64:(h + 1) * 64]
                        KK, VK = ["kTs"], ["Vs"]
                    else:
                        KT = "kTp%d" % gb
                        for s_ in range(NS):
                            c_ = s_ * NPG + blk
                            P.dma("pool", Kpg[gb][s_][:], cache_k, writes=["Kpg%d_%d" % (gb, s_)],
                                  indirect=bass.IndirectOffsetOnAxis(ap=idx[:, c_:c_ + 1], axis=0), reads=["idx"])
                            P.dma("pool", Vpg[gb][s_][:], cache_v, writes=["Vpg%d_%d" % (gb, s_)],
                                  indirect=bass.IndirectOffsetOnAxis(ap=idx[:, c_:c_ + 1], axis=0), reads=["idx"])
                        for s_ in range(NS):
                            ptb, PTB = Ptr[s_ % 2], "Ptrs%d" % (s_ % 2)
                            for g in range(8):
                                P.op("pe", lambda e: e.transpose(out=ptb[:, g, :], in_=Kpg[gb][s_][:, g * 128:(g + 1) * 128],
                                                                 identity=identb[:]),
                                     reads=["Kpg%d_%d" % (gb, s_), "identb"], writes=[PTB])
                            if s_ % 2 == 0:
                                P.op("dve", lambda e: e.tensor_copy(out=kTp[gb][:, s_, :, :], in_=ptb[:]),
                                     reads=[PTB], writes=[KT])
                            else:
                                P.op("act", lambda e: e.copy(out=kTp[gb][:, s_, :, :], in_=ptb[:]),
                                     reads=[PTB], writes=[KT])

                        def k_lhsT(s_, g, h2):
                            return kTp[gb][h2 * 64:(h2 + 1) * 64, s_, g, :]

                        def v_lhsT(s_, h):
                            return Vpg[gb][s_][:, h * 64:(h + 1) * 64]
                        KK = [KT]
                        VK = ["Vpg%d_%d" % (gb, s_) for s_ in range(NS)]
                    pz, PZ = Pz[bi % 2], "Pzs%d" % (bi % 2)
                    pa, PA = Pa[bi % 2], "Pas%d" % (bi % 2)
                    for s_ in range(NS):
                        for g in range(8):
                            for h2 in range(2):
                                c0 = col(s_, g, h2)
                                P.op("pe", lambda e: e.matmul(
                                    pz[:, c0:c0 + TS], lhsT=k_lhsT(s_, g, h2),
                                    rhs=qTs[h2 * 64:(h2 + 1) * 64, g, s_ * TS:(s_ + 1) * TS],
                                    start=True, stop=True, skip_group_check=True),
                                    reads=KK + ["qTs"], writes=[PZ])
                    spt, S = sp_t[bi % 2], "sps_t%d" % (bi % 2)
                    at, A = a_t[bi % 2], "as_t%d" % (bi % 2)
                    P.op("dve", lambda e: e.tensor_tensor(out=zsb[:], in0=pz[:],
                                                          in1=bias_bc[:].rearrange("p s h t -> p (s h t)"), op=ALU.add),
                         reads=[PZ, "bias_bc"], writes=["zsb"])
                    P.op("act", lambda e: e.activation(out=e_t[:], in_=zsb[:], func=AF.Exp),
                         reads=["zsb"], writes=["es_t"])
                    P.op("act", lambda e: e.activation(out=spt[:], in_=e_t[:], func=AF.Ln, bias=1.0),
                         reads=["es_t"], writes=[S])
                    if blk == "new":
                        P.op("dve", lambda e: e.tensor_tensor(out=spt[:], in0=spt[:],
                                                              in1=maskn[:].rearrange("p c t -> p (c t)"), op=ALU.mult),
                             reads=[S, "maskn"], writes=[S])
                    P.op("pe", lambda e: e.matmul(pa[:], lhsT=negU[:], rhs=spt[:], start=True, stop=first),
                         reads=["negU", S], writes=[PA])
                    if not first:
                        P.op("pe", lambda e: e.matmul(pa[:], lhsT=negO[:], rhs=SPbf[:], start=False, stop=True),
                             reads=["negO", "SPsbf"], writes=[PA])
                    P.op("dve", lambda e: e.tensor_tensor(out=t1[:], in0=pa[:], in1=zsb[:], op=ALU.add),
                         reads=[PA, "zsb"], writes=["t1s"])
                    P.op("act", lambda e: e.activation(out=at[:], in_=t1[:], func=AF.Exp),
                         reads=["t1s"], writes=[A])
                    if blk == "new":
                        P.op("dve", lambda e: e.tensor_tensor(out=at[:], in0=at[:],
                                                              in1=maskn[:].rearrange("p c t -> p (c t)"), op=ALU.mult),
                             reads=[A, "maskn"], writes=[A])
                    for s_ in range(NS):
                        for g in range(8):
                            for h2 in range(2):
                                c0 = col(s_, g, h2)
                                oc = (s_ * 8 + g) * TS
                                P.op("pe", lambda e: e.matmul(
                                    Po[h2 * 64:(h2 + 1) * 64, oc:oc + TS], lhsT=v_lhsT(s_, 2 * g + h2),
                                    rhs=at[:, c0:c0 + TS], start=False, stop=False, skip_group_check=True),
                                    reads=VK + [A], writes=["Pos"])
                    if not last:
                        P.op("pool", lambda e: e.tensor_tensor(out=SP32[:], in0=SP32[:], in1=spt[:], op=ALU.add),
                             reads=["SPs32", S], writes=["SPs32"])
                        P.op("pool", lambda e: e.tensor_copy(out=SPbf[:], in_=SP32[:]),
                             reads=["SPs32"], writes=["SPsbf"])
                P.op("pe", lambda e: e.matmul(Po[:], lhsT=zero_l[:], rhs=ones_m[:, 0:NS * 8 * TS], start=False, stop=True),
                     reads=["zero_l", "ones_m2"], writes=["Pos"])
                P.op("dve", lambda e: e.tensor_copy(
                    out=oTs[:].rearrange("p g (s t) -> p s g t", t=TS),
                    in_=Po[:].rearrange("p (s g t) -> p s g t", g=8, t=TS)),
                    reads=["Pos"], writes=["oTs"])
                P.barrier()

        st_att.close()
        def ffn_phase(li, mix_idx, ffn_idx, Wo_src, get_oT, x_src_fn, x_dst_fn, final_out):
            with ExitStack() as ph:
                wo = sbt(ph, "wo", [128, 8, D], BF16)
                NBUF = 2
                tls = [dict(b=i, x=sbt(ph, "xt%d" % i, [128, D]), junk=sbt(ph, "junk%d" % i, [128, D], F32 if "junk32" in cfg.dbg else BF16),
                            ssq=sbt(ph, "ssq%d" % i, [128, 1]), rstd=sbt(ph, "rstd%d" % i, [128, 1]),
                            xn=sbt(ph, "xn%d" % i, [128, D])) for i in range(NBUF)]
                x1 = [sbt(ph, "x1_%d" % i, [128, D]) for i in range(4)]
                oT = [sbt(ph, "oT0", [128, 8, 512], BF16)] * 2
                h2T = [sbt(ph, "h2T0", [128, 8, 512], BF16)] * 2
                uT = sbt(ph, "uT", [128, 32, 512], BF16)
                rl = [sbt(ph, "rl%d" % i, [128, 512], BF16) for i in range(2)]
                NWB = 2
                wup = [sbt(ph, "wup%d" % i, [128, 8, 1024], BF16) for i in range(NWB)]
                wdn = [sbt(ph, "wdn%d" % i, [128, 8, 512], BF16) for i in range(NWB)]
                jts = [sbt(ph, "jt%d" % i, [128, 512], F32) for i in range(2)]
                ptr_ = [pst(ph, "ptr%d" % i, [128, 512], F32) for i in range(2)]
                pstr = [(ptr_[i], "ptr%d" % i) for i in range(2)]
                pmm = [pst(ph, "pmm%d" % i, [128, 512], F32) for i in range(4)]
                for k in range(8):
                    P.dma("sp", wo[:, k, :], Wo_src[k * 128:(k + 1) * 128, :], writes=["wo"])
                wcount = [0, 0]
                it = 0
                for si, (kind, qi, ntok) in enumerate(stiles):
                    hb = si % 2
                    OT, H2 = "oT0", "h2T0"
                    get_oT(kind, qi, ntok, oT[hb], OT)
                    nsub = (ntok + 127) // 128
                    np_ = min(128, ntok)
                    for sub in range(nsub):
                        tl = tls[it % NBUF]
                        it += 1
                        b = tl["b"]
                        X = "xt%d" % b
                        X1 = "x1_%d" % sub
                        x_src_fn(kind, qi, sub, tl["x"][0:np_, :], X)
                        for n in range(2):
                            pm, PM = pmm[n], "pmm%d" % n
                            for k in range(8):
                                P.op("pe", lambda e: e.matmul(
                                    pm[0:np_, :], lhsT=oT[hb][:, k, sub * 128:sub * 128 + np_],
                                    rhs=wo[:, k, n * 512:(n + 1) * 512], start=(k == 0), stop=(k == 7)),
                                    reads=[OT, "wo"], writes=[PM])
                            if kind == "p":
                                grp = [(0, 0, 128)]
                            else:
                                grp = sgroups
                            gmix = (gate_bc[mix_idx][0:np_, 0, n * 512:(n + 1) * 512] if kind == "p"
                                    else gsm[mix_idx][0:np_, n * 512:(n + 1) * 512])
                            P.op("dve", lambda e: e.tensor_tensor(
                                out=x1[sub][0:np_, n * 512:(n + 1) * 512], in0=pm[0:np_, :],
                                in1=gmix, op=ALU.mult),
                                reads=[PM, "gate%d" % (mix_idx % 2), "gsm%d" % (mix_idx % 2)], writes=[X1])
                            P.op("pool", lambda e: e.tensor_tensor(
                                out=x1[sub][0:np_, n * 512:(n + 1) * 512], in0=x1[sub][0:np_, n * 512:(n + 1) * 512],
                                in1=tl["x"][0:np_, n * 512:(n + 1) * 512], op=ALU.add),
                                reads=[X1, X], writes=[X1])
                        tl2 = dict(tl)
                        tl2["x"] = x1[sub]
                        norm_to_hT(tl2, X1, np_, ffn_idx, grp,
                                   lambda k, c0, c1: h2T[hb][:, k, sub * 128 + c0:sub * 128 + c1], H2, pstr)
                    for fg in range(4):
                        wb = wcount[0] % NWB
                        wcount[0] += 1
                        WU = "wup%d" % wb
                        for k in range(8):
                            P.dma("sp", wup[wb][:, k, :], Wup_b[li][k * 128:(k + 1) * 128, fg * 1024:(fg + 1) * 1024],
                                  writes=[WU])
                        for fl in range(8):
                            fc = fg * 8 + fl
                            pm, PM = pmm[fc % 4], "pmm%d" % (fc % 4)
                            for k in range(8):
                                P.op("pe", lambda e: e.matmul(
                                    pm[:, 0:ntok], lhsT=wup[wb][:, k, fl * 128:(fl + 1) * 128],
                                    rhs=h2T[hb][:, k, 0:ntok], start=(k == 0), stop=(k == 7)),
                                    reads=[WU, H2], writes=[PM])
                            R = "rl%d" % (fc % 2)
                            P.op("act", lambda e: e.activation(out=rl[fc % 2][:, 0:ntok], in_=pm[:, 0:ntok], func=AF.Relu),
                                 reads=[PM], writes=[R])
                            P.op("pool", lambda e: e.tensor_tensor(out=uT[:, fc, 0:ntok], in0=rl[fc % 2][:, 0:ntok],
                                                                   in1=rl[fc % 2][:, 0:ntok], op=ALU.mult),
                                 reads=[R], writes=["uT"])
                    for n in range(2):
                        for sub in range(nsub):
                            pass
                    for sub in range(nsub):
                        pass
                    for n in range(2):
                        accs = [(pmm[s4], "pmm%d" % s4) for s4 in range(nsub)]
                        for fg in range(4):
                            wb = wcount[1] % NWB
                            wcount[1] += 1
                            WD = "wdn%d" % wb
                            for fl in range(8):
                                fc = fg * 8 + fl
                                P.dma("sp", wdn[wb][:, fl, 0:512],
                                      Wdn_b[li][fc * 128:(fc + 1) * 128, n * 512:(n + 1) * 512], writes=[WD])
                            for sub in range(nsub):
                                pm, PM = accs[sub]
                                for fl in range(8):
                                    fc = fg * 8 + fl
                                    P.op("pe", lambda e: e.matmul(
                                        pm[0:np_, :], lhsT=uT[:, fc, sub * 128:sub * 128 + np_],
                                        rhs=wdn[wb][:, fl, 0:512], start=(fc == 0), stop=(fc == 31)),
                                        reads=["uT", WD], writes=[PM])
                        for sub in range(nsub):
                            pm, PM = accs[sub]
                            X1 = "x1_%d" % sub
                            J = "jt%d" % (sub % 2)
                            jt = jts[sub % 2]
                            gffn = (gate_bc[ffn_idx][0:np_, 0, n * 512:(n + 1) * 512] if kind == "p"
                                    else gsm[ffn_idx][0:np_, n * 512:(n + 1) * 512])
                            P.op("dve", lambda e: e.tensor_tensor(
                                out=jt[0:np_, 0:512], in0=pm[0:np_, :], in1=gffn, op=ALU.mult),
                                reads=[PM, "gate%d" % (ffn_idx % 2), "gsm%d" % (ffn_idx % 2)], writes=[J])
                            P.op("pool", lambda e: e.tensor_tensor(
                                out=x1[sub][0:np_, n * 512:(n + 1) * 512], in0=x1[sub][0:np_, n * 512:(n + 1) * 512],
                                in1=jt[0:np_, 0:512], op=ALU.add),
                                reads=[X1, J], writes=[X1])
                    for sub in range(nsub):
                        x_dst_fn(kind, qi, sub, x1[sub][0:np_, :], "x1_%d" % sub)
                P.barrier()

        def get_oT0(kind, qi, ntok, oT_tile, OT):
            if kind == "p":
                for g in range(8):
                    P.dma("sp", oT_tile[:, g, :], oT_d[:, g, qi * 512:(qi + 1) * 512], writes=[OT])
            else:
                P.op("dve", lambda e: e.tensor_copy(out=oT_tile[:, :, 0:NST], in_=oTs[:]), reads=["oTs"], writes=[OT])

        def x_src0(kind, qi, sub, dst, X):
            if kind == "p":
                t = qi * 4 + sub
                P.dma("sp", dst, xp[t * 128:(t + 1) * 128, :], writes=[X])
            else:
                P.dma("sp", dst, xs, writes=[X])

        def x_dst0(kind, qi, sub, src, X1):
            if kind == "p":
                t = qi * 4 + sub
                out_deps.append(P.dma("sp", xl0p[t * 128:(t + 1) * 128, :], src, reads=[X1], semkey="xo%d" % sub))
            else:
                out_deps.append(P.dma("sp", xl0s, src, reads=[X1], semkey="xo%d" % sub))

        if cfg.upto >= 5:
            ffn_phase(0, 0, 1, Wo_b, get_oT0, x_src0, x_dst0, False)

        if cfg.upto >= 6:
            compute_ada(2, Wam_b[1], ada_b_mix[1], norm_mix[1])
            compute_ada(3, Waf_b[1], ada_b_ffn[1], norm_ffn[1])
        if cfg.upto >= 6:
            with ExitStack() as ph:
                TT = 256
                CL = 64
                GN_EPS = 1e-5 * 64
                dtiles = [("p", q, TT) for q in range(SEQ // TT)] + [("s", 0, NST)]
                Wst = sbt(ph, "Wst", [128, 8, D], BF16)
                Wr = Wk = Wv = Wst

                def load_W(src):
                    for k in range(8):
                        P.dma("sp", Wst[:, k, :], src[k * 128:(k + 1) * 128, :], writes=["Wst"])
                w1b = sbt(ph, "w1b", [128, 8, 64], BF16)
                a1b = sbt(ph, "a1b", [128, 8, 64], BF16)
                g1b = sbt(ph, "g1b", [128, 8, 160], BF16)
                w2b = sbt(ph, "w2b", [64, D], BF16)
                a2b = sbt(ph, "a2b", [64, D], BF16)
                g2b0 = sbt(ph, "g2b0", [128, D], BF16)
                g2b1 = sbt(ph, "g2b1", [32, D], BF16)
                muT = sbt(ph, "muT", [128, 6, 8], F32)
                w0n = sbt(ph, "w0n", [128, 8], F32)
                a0T = sbt(ph, "a0T", [128, 8], F32)
                kkT_ = sbt(ph, "kkT_", [128, 8], F32)
                kaT_ = sbt(ph, "kaT_", [128, 8], F32)
                rkT_ = sbt(ph, "rkT_", [128, 8], F32)
                lngT = sbt(ph, "lngT", [128, 8], F32)
                lnbT = sbt(ph, "lnbT", [128, 8], F32)
                sshT = sbt(ph, "sshT", [128, 8, NS], F32)
                blkb = sbt(ph, "blkb", [128, 128], BF16)
                blk32 = sbt(ph, "blk32", [128, 128], F32)
                MK = sbt(ph, "MK", [128, 128], BF16)
                MB = sbt(ph, "MB", [128, 128], BF16)
                MN = sbt(ph, "MN", [128, 64], BF16)
                I64 = sbt(ph, "I64", [128, 64], BF16)
                onesb = sbt(ph, "onesb", [128, 128], BF16)
                rmask = [sbt(ph, "rmask0", [128, TT], F32), sbt(ph, "rmask1", [128, NST], F32)]
                hTe = sbt(ph, "hTe", [128, 8, TT + 1], F32)
                lastcol = sbt(ph, "lastcol", [128, 8, 1], F32)
                prevS = sbt(ph, "prevS", [128, 8, NST], F32)
                xx = sbt(ph, "xx", [128, 8, TT], F32)
                xnb = [sbt(ph, "xnb%d" % i, [128, 8, TT], BF16) for i in range(2)]
                rT = sbt(ph, "rT", [128, 8, TT], BF16)
                kT = sbt(ph, "kT", [128, 8, TT], BF16)
                vT = sbt(ph, "vT", [128, 8, TT], BF16)
                aT = sbt(ph, "aT", [128, 8, TT], BF16)
                gT_ = sbt(ph, "gT_", [128, 8, TT], BF16)
                ld = sbt(ph, "ld", [128, 8, TT], F32)
                lw_t = sbt(ph, "lw_t", [128, TT], BF16)
                lg2_t = sbt(ph, "lg2_t", [32, TT], BF16)
                NCH = TT // CL
                Vtok = sbt(ph, "Vtok", [128, NCH, 512], BF16)
                KR = sbt(ph, "KR", [128, 8, NCH, 2, CL], BF16)
                BK = sbt(ph, "BK", [128, 8, NCH, 2, CL], BF16)
                gam = sbt(ph, "gam", [128, 8, NCH], F32)
                bon = sbt(ph, "bon", [128, 8, TT], BF16)
                Ysb = sbt(ph, "Ysb", [128, 8, NCH, CL], F32)
                H32 = sbt(ph, "H32", [128, 8, 64], F32)
                Hbf = sbt(ph, "Hbf", [128, 8, 64], BF16)
                Htmp = sbt(ph, "Htmp", [128, 8, 64], F32)
                tA = [sbt(ph, "tA%d" % i, [128, TT], F32) for i in range(6)]
                tB = [sbt(ph, "tB%d" % i, [128, TT], BF16) for i in range(3)]
                GBm = sbt(ph, "GBm", [128, 8, 128], BF16)
                GKm = sbt(ph, "GKm", [128, 8, 128], BF16)
                Xl = [sbt(ph, "Xl%d" % i, [128, 8, 64], F32) for i in range(2)]
                Yl = [sbt(ph, "Yl%d" % i, [128, 8, 64], F32) for i in range(2)]
                Pl = [sbt(ph, "Pl%d" % i, [128, 8, 64], F32) for i in range(2)]
                Wsb = sbt(ph, "Wsb", [128, 8, 64], F32)
                Usb = sbt(ph, "Usb", [128, 8, 64], BF16)
                Ktok = sbt(ph, "Ktok", [128, 8, 64], BF16)
                nBtok = sbt(ph, "nBtok", [128, 8, 64], BF16)
                stT = sbt(ph, "stT", [64, 8, 128], F32)
                NBUF = 1
                tls = [dict(b=i, x=sbt(ph, "xt%d" % i, [128, D]), junk=sbt(ph, "junk%d" % i, [128, D], BF16),
                            ssq=sbt(ph, "ssq%d" % i, [128, 1]), rstd=sbt(ph, "rstd%d" % i, [128, 1]),
                            xn=sbt(ph, "xn%d" % i, [128, D])) for i in range(NBUF)]
                ptr_ = [pst(ph, "ptr0", [128, 512], F32)] * 2
                pstr = [(ptr_[i], "ptr0") for i in range(2)]
                pm_ = [pst(ph, "pmd%d" % i, [128, 512], F32) for i in range(2)]
                Pg = pst(ph, "Pg", [128, 8, 128], F32)
                Pn = [pst(ph, "Pn%d" % i, [128, 8, 64], F32) for i in range(2)]
                Ptk = pst(ph, "Ptk", [128, 2, 8, 64], BF16)

                for k in range(8):
                    P.dma("pool", w1b[:, k, :], rw_w1[k * 128:(k + 1) * 128, :], writes=["w1b"])
                    P.dma("pool", a1b[:, k, :], rw_a1[k * 128:(k + 1) * 128, :], writes=["a1b"])
                    P.dma("pool", g1b[:, k, :], rw_g1[k * 128:(k + 1) * 128, :], writes=["g1b"])
                P.dma("pool", w2b[:], rw_w2, writes=["w2b"], max_dma_last_dim=4096)
                P.dma("pool", a2b[:], rw_a2, writes=["a2b"], max_dma_last_dim=4096)
                P.dma("pool", g2b0[:], rw_g2[0:128, :], writes=["g2b0"], max_dma_last_dim=4096)
                P.dma("pool", g2b1[:], rw_g2[128:160, :], writes=["g2b1"], max_dma_last_dim=4096)
                with nc.allow_non_contiguous_dma(reason="tiny transposed parameter loads"):
                    for n in range(6):
                        P.dma("sp", muT[:, n, :], rw_mu[n].rearrange("(k p) -> p k", p=128), writes=["muT"])
                    for (tile_, src_, key_) in [(w0n, rw_w0, "w0n"), (a0T, rw_a0, "a0T"), (kkT_, rw_k_k, "kkT_"),
                                                (kaT_, rw_k_a, "kaT_"), (rkT_, rw_r_k, "rkT_"),
                                                (lngT, rw_ln_g, "lngT"), (lnbT, rw_ln_b, "lnbT")]:
                        P.dma("sp", tile_[:], src_.rearrange("(k p) -> p k", p=128), writes=[key_])
                    for s_ in range(NS):
                        P.dma("sp", sshT[:, :, s_], state_shift[s_].rearrange("(k p) -> p k", p=128), writes=["sshT"])
                P.op("dve", lambda e: e.tensor_scalar(out=w0n[:], in0=w0n[:], scalar1=-1.0, scalar2=None, op0=ALU.mult),
                     reads=["w0n"], writes=["w0n"])
                P.op("pool", lambda e: e.memset(onesb[:], 1.0), writes=["onesb"])
                P.op("pool", lambda e: e.memset(blkb[:], 0.0), writes=["blkb"])
                for h2 in range(2):
                    pr = slice(h2 * 64, (h2 + 1) * 64)
                    P.op("pool", lambda e: e.memset(blkb[pr, h2 * 64:(h2 + 1) * 64], 1.0), reads=["blkb"], writes=["blkb"])
                    P.op("pool", lambda e: e.affine_select(
                        out=MK[pr, 0:64], in_=onesb[pr, 0:64], pattern=[[1, 64]], compare_op=ALU.is_gt,
                        fill=0.0, base=0, channel_multiplier=-1), reads=["onesb"], writes=["MK"])
                    P.op("pool", lambda e: e.affine_select(
                        out=MK[pr, 64:128], in_=onesb[pr, 0:64], pattern=[[1, 64]], compare_op=ALU.is_ge,
                        fill=0.0, base=0, channel_multiplier=-1), reads=["onesb"], writes=["MK"])
                    P.op("pool", lambda e: e.affine_select(
                        out=MN[pr, :], in_=negO[pr, 0:64], pattern=[[-1, 64]], compare_op=ALU.is_gt,
                        fill=0.0, base=0, channel_multiplier=1), reads=["negO"], writes=["MN"])
                    P.op("pool", lambda e: e.tensor_copy(out=I64[pr, :], in_=identb[pr, h2 * 64:(h2 + 1) * 64]),
                         reads=["identb"], writes=["I64"])
                P.op("pool", lambda e: e.tensor_copy(out=blk32[:], in_=blkb[:]), reads=["blkb"], writes=["blk32"])
                P.op("dve", lambda e: e.tensor_scalar(out=MB[:], in0=MK[:], scalar1=-1.0, scalar2=None, op0=ALU.mult),
                     reads=["MK"], writes=["MB"])
                P.op("pool", lambda e: e.memset(rmask[0][:], 1.0), writes=["rmask0"])
                P.op("pool", lambda e: e.memset(rmask[0][:].rearrange("p (c t) -> p c t", t=CL)[:, :, 0:1], 0.0),
                     reads=["rmask0"], writes=["rmask0"])
                P.op("pool", lambda e: e.memset(rmask[1][:], 1.0), writes=["rmask1"])
                P.op("pool", lambda e: e.memset(rmask[1][:].rearrange("p (c t) -> p c t", t=TS)[:, :, 0:1], 0.0),
                     reads=["rmask1"], writes=["rmask1"])
                P.op("pool", lambda e: e.memset(H32[:], 0.0), writes=["H32"])
                P.op("pool", lambda e: e.memset(Hbf[:], 0.0), writes=["Hbf"])

                def hd(t, g, h2):
                    return t[h2 * 64:(h2 + 1) * 64, g, :]

                def store_state(dst):
                    for g in range(8):
                        P.op("pe", lambda e: e.transpose(out=Pg[0:64, g, :], in_=H32[:, g, :], identity=ident[:]),
                             reads=["H32", "ident"], writes=["Pg"])
                    P.op("dve", lambda e: e.tensor_copy(out=stT[:], in_=Pg[0:64, :, :]), reads=["Pg"], writes=["stT"])
                    return P.dma("sp", dst.rearrange("(g h) i j -> i g h j", h=2),
                                 stT[:].rearrange("i g (h j) -> i g h j", h=2), reads=["stT"], semkey="stTo")

                def load_state(src):
                    P.dma("sp", stT[:].rearrange("i g (h j) -> i g h j", h=2),
                          src.rearrange("(g h) i j -> i g h j", h=2), writes=["stT"])
                    for g in range(8):
                        P.op("pe", lambda e: e.transpose(out=Pn[0][:, g, :], in_=stT[:, g, :], identity=ident[0:64, 0:64]),
                             reads=["stT", "ident"], writes=["Pn0"])
                    P.op("dve", lambda e: e.tensor_copy(out=H32[:], in_=Pn[0][:]), reads=["Pn0"], writes=["H32"])
                    P.op("pool", lambda e: e.tensor_copy(out=Hbf[:], in_=H32[:]), reads=["H32"], writes=["Hbf"])

                it = 0
                for di, (kind, qi, ntok) in enumerate(dtiles):
                    cl = CL if kind == "p" else TS
                    nch = ntok // cl
                    rm = rmask[0] if kind == "p" else rmask[1]
                    RM = "rmask0" if kind == "p" else "rmask1"
                    nsub = (ntok + 127) // 128
                    np_ = min(128, ntok)
                    if kind == "p":
                        if qi == 0:
                            P.op("pool", lambda e: e.memset(hTe[:, :, 0:1], 0.0), writes=["hTe"])
                        else:
                            P.op("pool", lambda e: e.tensor_copy(out=hTe[:, :, 0:1], in_=lastcol[:]),
                                 reads=["lastcol"], writes=["hTe"])
                    for sub in range(nsub):
                        tl = tls[it % NBUF]
                        it += 1
                        X = "xt%d" % tl["b"]
                        if kind == "p":
                            t = qi * (TT // 128) + sub
                            P.dma("sp", tl["x"][0:np_, :], xl0p[t * 128:(t + 1) * 128, :], writes=[X])
                            groups = [(0, 0, 128)]
                        else:
                            P.dma("sp", tl["x"][0:np_, :], xl0s, writes=[X])
                            groups = sgroups
                        norm_to_hT(tl, X, np_, 2, groups,
                                   lambda k, c0, c1: hTe[:, k, 1 + sub * 128 + c0:1 + sub * 128 + c1], "hTe", pstr)
                    hcur = hTe[:, :, 1:1 + ntok]
                    if kind == "p":
                        hprev = hTe[:, :, 0:ntok]
                        PV = "hTe"
                    else:
                        P.op("pool", lambda e: e.tensor_copy(
                            out=prevS[:].rearrange("p k (s t) -> p k s t", t=TS)[:, :, :, 1:TS],
                            in_=hTe[:, :, 1:1 + NST].rearrange("p k (s t) -> p k s t", t=TS)[:, :, :, 0:TS - 1]),
                            reads=["hTe"], writes=["prevS"])
                        P.op("pool", lambda e: e.tensor_copy(
                            out=prevS[:].rearrange("p k (s t) -> p k s t", t=TS)[:, :, :, 0],
                            in_=sshT[:]), reads=["sshT", "prevS"], writes=["prevS"])
                        hprev = prevS[:]
                        PV = "prevS"
                    P.op("dve", lambda e: e.tensor_tensor(out=xx[:, :, 0:ntok], in0=hprev, in1=hcur, op=ALU.subtract),
                         reads=["hTe", PV], writes=["xx"])
                    with nc.allow_non_contiguous_dma(reason="tiny transposed outputs"):
                        if kind == "p" and qi == SEQ // TT - 1:
                            out_deps.append(P.dma("sp", shp.rearrange("(k p) -> p k", p=128), hTe[:, :, TT],
                                                  reads=["hTe"], semkey="sho"))
                        if kind == "s":
                            for s_ in range(NS):
                                out_deps.append(P.dma("sp", shs[s_].rearrange("(k p) -> p k", p=128),
                                                      hTe[:, :, 1 + s_ * TS + TS - 1], reads=["hTe"], semkey="sho"))

                    def make_xn(n, nb):
                        XN = "xnb%d" % nb
                        for k in range(8):
                            P.op("dve", lambda e: e.scalar_tensor_tensor(
                                out=xnb[nb][:, k, 0:ntok], in0=xx[:, k, 0:ntok], scalar=muT[:, n, k:k + 1],
                                in1=hTe[:, k, 1:1 + ntok], op0=ALU.mult, op1=ALU.add),
                                reads=["xx", "muT", "hTe"], writes=[XN])
                        return xnb[nb], XN

                    def proj_fm(W, WK, xn_, XN, fc, pm, PM, m0=0, m1=128, wcols=None):
                        for k in range(8):
                            P.op("pe", lambda e: e.matmul(
                                pm[0:m1 - m0, 0:ntok],
                                lhsT=(W[:, k, fc * 128 + m0:fc * 128 + m1] if wcols is None else W[:, k, wcols[0]:wcols[1]]),
                                rhs=xn_[:, k, 0:ntok], start=(k == 0), stop=(k == 7)),
                                reads=[WK, XN], writes=[PM])

                    xn_, XN = make_xn(0, 0)
                    load_W(Wr_b)
                    for fc in range(8):
                        pm, PM = pm_[fc % 2], "pmd%d" % (fc % 2)
                        proj_fm(Wr, "Wst", xn_, XN, fc, pm, PM)
                        P.op("act", lambda e: e.copy(out=rT[:, fc, 0:ntok], in_=pm[:, 0:ntok]), reads=[PM], writes=["rT"])
                    xn_, XN = make_xn(1, 1)
                    pm, PM = pm_[0], "pmd0"
                    proj_fm(w1b, "w1b", xn_, XN, 0, pm, PM, wcols=(0, 64), m1=64)
                    P.op("act", lambda e: e.activation(out=lw_t[0:64, 0:ntok], in_=pm[0:64, 0:ntok], func=AF.Tanh),
                         reads=[PM], writes=["lw_t"])
                    for fc in range(8):
                        pm, PM = pm_[(fc + 1) % 2], "pmd%d" % ((fc + 1) % 2)
                        P.op("pe", lambda e: e.matmul(pm[:, 0:ntok], lhsT=w2b[0:64, fc * 128:(fc + 1) * 128],
                                                      rhs=lw_t[0:64, 0:ntok], start=True, stop=True),
                             reads=["w2b", "lw_t"], writes=[PM])
                        P.op("act", lambda e: e.activation(out=tA[0][:, 0:ntok], in_=pm[:, 0:ntok], func=AF.Exp,
                                                           scale=-1.0, bias=w0n[:, fc:fc + 1]),
                             reads=[PM, "w0n"], writes=["tA0"])
                        P.op("act", lambda e: e.activation(out=tA[0][:, 0:ntok], in_=tA[0][:, 0:ntok], func=AF.Ln, bias=1.0),
                             reads=["tA0"], writes=["tA0"])
                        P.op("act", lambda e: e.activation(out=tA[0][:, 0:ntok], in_=tA[0][:, 0:ntok], func=AF.Exp,
                                                           scale=-1.0, bias=-0.5),
                             reads=["tA0"], writes=["tA0"])
                        P.op("dve", lambda e: e.tensor_scalar(out=ld[:, fc, 0:ntok], in0=tA[0][:, 0:ntok], scalar1=-1.0,
                                                              scalar2=None, op0=ALU.mult),
                             reads=["tA0"], writes=["ld"])
                    xn_, XN = make_xn(2, 0)
                    load_W(Wk_b)
                    for fc in range(8):
                        pm, PM = pm_[fc % 2], "pmd%d" % (fc % 2)
                        proj_fm(Wk, "Wst", xn_, XN, fc, pm, PM)
                        P.op("act", lambda e: e.copy(out=kT[:, fc, 0:ntok], in_=pm[:, 0:ntok]), reads=[PM], writes=["kT"])
                    xn_, XN = make_xn(3, 1)
                    load_W(Wv_b)
                    for fc in range(8):
                        pm, PM = pm_[fc % 2], "pmd%d" % (fc % 2)
                        proj_fm(Wv, "Wst", xn_, XN, fc, pm, PM)
                        P.op("act", lambda e: e.copy(out=vT[:, fc, 0:ntok], in_=pm[:, 0:ntok]), reads=[PM], writes=["vT"])
                    if kind == "s":
                        P.op("pool", lambda e: e.memset(Vtok[:], 0.0), writes=["Vtok"])
                    Wv4 = Wv[:].rearrange("p k (g h d) -> p k g h d", h=2, d=64)
                    for c in range(nch):
                        pm, PM = pm_[c % 2], "pmd%d" % (c % 2)
                        for h2 in range(2):
                            for k in range(8):
                                P.op("pe", lambda e: e.matmul(
                                    pm[h2 * 64:h2 * 64 + cl, :], lhsT=xn_[:, k, c * cl:(c + 1) * cl],
                                    rhs=Wv4[:, k, :, h2, :], start=(k == 0), stop=(k == 7)),
                                    reads=["Wst", XN], writes=[PM])
                        if cl == 64:
                            P.op("dve", lambda e: e.tensor_copy(out=Vtok[:, c, :], in_=pm[:]), reads=[PM], writes=["Vtok"])
                        else:
                            for h2 in range(2):
                                P.op("dve", lambda e: e.tensor_copy(out=Vtok[h2 * 64:h2 * 64 + cl, c, :],
                                                                    in_=pm[h2 * 64:h2 * 64 + cl, :]),
                                     reads=[PM], writes=["Vtok"])
                    xn_, XN = make_xn(4, 0)
                    pm, PM = pm_[0], "pmd0"
                    proj_fm(a1b, "a1b", xn_, XN, 0, pm, PM, wcols=(0, 64), m1=64)
                    P.op("act", lambda e: e.copy(out=lw_t[0:64, 0:ntok], in_=pm[0:64, 0:ntok]), reads=[PM], writes=["lw_t"])
                    for fc in range(8):
                        pm, PM = pm_[(fc + 1) % 2], "pmd%d" % ((fc + 1) % 2)
                        P.op("pe", lambda e: e.matmul(pm[:, 0:ntok], lhsT=a2b[0:64, fc * 128:(fc + 1) * 128],
                                                      rhs=lw_t[0:64, 0:ntok], start=True, stop=True),
                             reads=["a2b", "lw_t"], writes=[PM])
                        P.op("act", lambda e: e.activation(out=aT[:, fc, 0:ntok], in_=pm[:, 0:ntok], func=AF.Sigmoid,
                                                           bias=a0T[:, fc:fc + 1]),
                             reads=[PM, "a0T"], writes=["aT"])
                    xn_, XN = make_xn(5, 1)
                    pm, PM = pm_[0], "pmd0"
                    proj_fm(g1b, "g1b", xn_, XN, 0, pm, PM, wcols=(0, 128))
                    P.op("act", lambda e: e.activation(out=lw_t[:, 0:ntok], in_=pm[:, 0:ntok], func=AF.Sigmoid),
                         reads=[PM], writes=["lw_t"])
                    pm, PM = pm_[1], "pmd1"
                    proj_fm(g1b, "g1b", xn_, XN, 0, pm, PM, wcols=(128, 160), m1=32)
                    P.op("act", lambda e: e.activation(out=lg2_t[0:32, 0:ntok], in_=pm[0:32, 0:ntok], func=AF.Sigmoid),
                         reads=[PM], writes=["lg2_t"])
                    for fc in range(8):
                        pm, PM = pm_[fc % 2], "pmd%d" % (fc % 2)
                        P.op("pe", lambda e: e.matmul(pm[:, 0:ntok], lhsT=g2b0[:, fc * 128:(fc + 1) * 128],
                                                      rhs=lw_t[:, 0:ntok], start=True, stop=False),
                             reads=["g2b0", "lw_t"], writes=[PM])
                        P.op("pe", lambda e: e.matmul(pm[:, 0:ntok], lhsT=g2b1[0:32, fc * 128:(fc + 1) * 128],
                                                      rhs=lg2_t[0:32, 0:ntok], start=False, stop=True),
                             reads=["g2b1", "lg2_t"], writes=[PM])
                        P.op("act", lambda e: e.copy(out=gT_[:, fc, 0:ntok], in_=pm[:, 0:ntok]), reads=[PM], writes=["gT_"])

                    if kind == "s":
                        P.op("pool", lambda e: e.memset(KR[:], 0.0), writes=["KR"])
                        P.op("pool", lambda e: e.memset(BK[:], 0.0), writes=["BK"])
                    nt = slice(0, ntok)
                    for g in range(8):
                        pm, PM = pm_[g % 2], "pmd%d" % (g % 2)
                        kk, kk2, rn, Linc, Lexc, E1 = tA[0], tB[0], tA[1], tA[2], tA[3], tA[4]
                        P.op("dve", lambda e: e.tensor_scalar(out=kk[:, nt], in0=kT[:, g, nt], scalar1=kkT_[:, g:g + 1],
                                                              scalar2=None, op0=ALU.mult),
                             reads=["kT", "kkT_"], writes=["tA0"])
                        P.op("act", lambda e: e.activation(out=kk2[:, nt], in_=kk[:, nt], func=AF.Square),
                             reads=["tA0"], writes=["tB0"])
                        P.op("pe", lambda e: e.matmul(pm[:, nt], lhsT=blkb[:], rhs=kk2[:, nt], start=True, stop=True),
                             reads=["blkb", "tB0"], writes=[PM])
                        P.op("act", lambda e: e.activation(out=rn[:, nt], in_=pm[:, nt], func=AF.Sqrt),
                             reads=[PM], writes=["tA1"])
                        P.op("dve", lambda e: e.tensor_scalar(out=rn[:, nt], in0=rn[:, nt], scalar1=1e-12, scalar2=None,
                                                              op0=ALU.max), reads=["tA1"], writes=["tA1"])
                        P.op("dve", lambda e: e.reciprocal(out=rn[:, nt], in_=rn[:, nt]), reads=["tA1"], writes=["tA1"])
                        P.op("dve", lambda e: e.tensor_tensor(out=kk[:, nt], in0=kk[:, nt], in1=rn[:, nt], op=ALU.mult),
                             reads=["tA0", "tA1"], writes=["tA0"])
                        kh = tA[5]
                        P.op("dve", lambda e: e.tensor_scalar(out=kh[:, nt], in0=aT[:, g, nt], scalar1=kaT_[:, g:g + 1],
                                                              scalar2=kaT_[:, g:g + 1], op0=ALU.mult, op1=ALU.subtract),
                             reads=["aT", "kaT_"], writes=["tA5"])
                        P.op("dve", lambda e: e.scalar_tensor_tensor(out=kh[:, nt], in0=kh[:, nt], scalar=1.0,
                                                                     in1=kT[:, g, nt], op0=ALU.add, op1=ALU.mult),
                             reads=["tA5", "kT"], writes=["tA5"])
                        P.op("dve", lambda e: e.tensor_tensor(out=rn[:, nt], in0=kk[:, nt], in1=aT[:, g, nt], op=ALU.mult),
                             reads=["tA0", "aT"], writes=["tA1"])
                        P.op("dve", lambda e: e.scalar_tensor_tensor(out=tB[1][:, nt], in0=rT[:, g, nt],
                                                                     scalar=rkT_[:, g:g + 1], in1=kh[:, nt],
                                                                     op0=ALU.mult, op1=ALU.mult),
                             reads=["rT", "rkT_", "tA5"], writes=["tB1"])
                        pm2, PM2 = pm_[(g + 1) % 2], "pmd%d" % ((g + 1) % 2)
                        P.op("pe", lambda e: e.matmul(pm2[:, nt], lhsT=blkb[:], rhs=tB[1][:, nt], start=True, stop=True),
                             reads=["blkb", "tB1"], writes=[PM2])
                        P.op("dve", lambda e: e.tensor_tensor(out=bon[:, g, nt], in0=pm2[:, nt], in1=vT[:, g, nt], op=ALU.mult),
                             reads=[PM2, "vT"], writes=["bon"])
                        P.op("dve", lambda e: e.tensor_tensor_scan(out=Linc[:, nt], data0=rm[:, nt], data1=ld[:, g, nt],
                                                                   initial=0.0, op0=ALU.mult, op1=ALU.add),
                             reads=[RM, "ld"], writes=["tA2"])
                        P.op("pool", lambda e: e.tensor_tensor(out=Lexc[:, nt], in0=Linc[:, nt], in1=ld[:, g, nt], op=ALU.subtract),
                             reads=["tA2", "ld"], writes=["tA3"])
                        c3 = lambda ap: ap.rearrange("p (c t) -> p c t", t=cl)
                        P.op("act", lambda e: e.activation(out=E1[:, nt], in_=Linc[:, nt], func=AF.Exp),
                             reads=["tA2"], writes=["tA4"])
                        P.op("dve", lambda e: e.tensor_tensor(out=KR[:, g, 0:nch, 1, 0:cl], in0=c3(rT[:, g, nt]),
                                                              in1=c3(E1[:, nt]), op=ALU.mult),
                             reads=["rT", "tA4"], writes=["KR"])
                        P.op("pool", lambda e: e.tensor_copy(out=gam[:, g, 0:nch], in_=c3(E1[:, nt])[:, :, cl - 1]),
                             reads=["tA4"], writes=["gam"])
                        P.op("act", lambda e: e.activation(out=E1[:, nt], in_=Lexc[:, nt], func=AF.Exp),
                             reads=["tA3", "gam"], writes=["tA4"])
                        P.op("dve", lambda e: e.tensor_tensor(out=KR[:, g, 0:nch, 0, 0:cl], in0=c3(kk[:, nt]),
                                                              in1=c3(E1[:, nt]), op=ALU.mult),
                             reads=["tA0", "tA4"], writes=["KR"])
                        P.op("act", lambda e: e.activation(out=E1[:, nt], in_=Linc[:, nt], func=AF.Exp, scale=-1.0),
                             reads=["tA2", "KR"], writes=["tA4"])
                        P.op("dve", lambda e: e.tensor_tensor(out=BK[:, g, 0:nch, 0, 0:cl], in0=c3(rn[:, nt]),
                                                              in1=c3(E1[:, nt]), op=ALU.mult),
                             reads=["tA1", "tA4"], writes=["BK"])
                        P.op("dve", lambda e: e.tensor_tensor(out=BK[:, g, 0:nch, 1, 0:cl], in0=c3(kh[:, nt]),
                                                              in1=c3(E1[:, nt]), op=ALU.mult),
                             reads=["tA5", "tA4"], writes=["BK"])

                    for c in range(nch):
                        if kind == "s":
                            load_state(state_wkv[c])
                        for which, (li_, Gm_, GM, Mm, MM) in enumerate([(0, GBm, "GBm", MB, "MB"), (1, GKm, "GKm", MK, "MK")]):
                            for g in range(8):
                                for h2 in range(2):
                                    pr = slice(h2 * 64, (h2 + 1) * 64)
                                    P.op("pe", lambda e: e.matmul(
                                        Pg[pr, g, :], lhsT=BK[pr, g, c, li_, :],
                                        rhs=KR[pr, g, c, :, :], start=True, stop=True, skip_group_check=True),
                                        reads=["BK", "KR"], writes=["Pg"])
                            P.op("dve", lambda e: e.tensor_tensor(
                                out=Gm_[:], in0=Pg[:], in1=Mm[:].unsqueeze(1).to_broadcast([128, 8, 128]), op=ALU.mult),
                                reads=["Pg", MM], writes=[GM])
                            if which == 0:
                                P.op("dve", lambda e: e.tensor_tensor(
                                    out=Xl[0][:], in0=Pg[:, :, 0:64],
                                    in1=MB[:, 0:64].unsqueeze(1).to_broadcast([128, 8, 64]), op=ALU.mult),
                                    reads=["Pg", "MB"], writes=["Xl0"])
                        for g in range(8):
                            for h2 in range(2):
                                pr = slice(h2 * 64, (h2 + 1) * 64)
                                P.op("pe", lambda e: e.matmul(
                                    Pn[0][pr, g, :], lhsT=KR[pr, g, c, 0, :], rhs=BK[pr, g, c, 0, :],
                                    start=True, stop=True, skip_group_check=True),
                                    reads=["BK", "KR"], writes=["Pn0"])
                        P.op("dve", lambda e: e.tensor_tensor(
                            out=Yl[0][:], in0=Pn[0][:], in1=MN[:].unsqueeze(1).to_broadcast([128, 8, 64]), op=ALU.mult),
                            reads=["Pn0", "MN"], writes=["Yl0"])
                        P.op("pool", lambda e: e.tensor_tensor(
                            out=Pl[0][:], in0=Xl[0][:], in1=I64[:].unsqueeze(1).to_broadcast([128, 8, 64]), op=ALU.add),
                            reads=["Xl0", "I64"], writes=["Pl0"])
                        cur = 0
                        for lev in range(5):
                            nx = 1 - cur
                            XC, YC, PC = "Xl%d" % cur, "Yl%d" % cur, "Pl%d" % cur
                            XN_, YN_, PN_ = "Xl%d" % nx, "Yl%d" % nx, "Pl%d" % nx
                            for g in range(8):
                                for h2 in range(2):
                                    pr = slice(h2 * 64, (h2 + 1) * 64)
                                    P.op("pe", lambda e: e.matmul(Pn[1][pr, g, :], lhsT=Xl[cur][pr, g, :], rhs=Yl[cur][pr, g, :],
                                                                  start=True, stop=True, skip_group_check=True),
                                         reads=[XC, YC], writes=["Pn1"])
                            P.op("act", lambda e: e.copy(out=Yl[nx][:], in_=Pn[1][:]), reads=["Pn1"], writes=[YN_])
                            if lev < 4:
                                for g in range(8):
                                    for h2 in range(2):
                                        pr = slice(h2 * 64, (h2 + 1) * 64)
                                        P.op("pe", lambda e: e.matmul(Pn[0][pr, g, :], lhsT=Yl[cur][pr, g, :], rhs=Xl[cur][pr, g, :],
                                                                      start=True, stop=True, skip_group_check=True),
                                             reads=[XC, YC], writes=["Pn0"])
                                P.op("dve", lambda e: e.tensor_copy(out=Xl[nx][:], in_=Pn[0][:]), reads=["Pn0"], writes=[XN_])
                            for g in range(8):
                                for h2 in range(2):
                                    pr = slice(h2 * 64, (h2 + 1) * 64)
                                    P.op("pe", lambda e: e.matmul(Pg[pr, g, 0:64], lhsT=Yl[nx][pr, g, :], rhs=Pl[cur][pr, g, :],
                                                                  start=True, stop=True, skip_group_check=True),
                                         reads=[YN_, PC], writes=["Pg"])
                            P.op("dve", lambda e: e.tensor_tensor(out=Pl[nx][:], in0=Pg[:, :, 0:64], in1=Pl[cur][:], op=ALU.add),
                                 reads=["Pg", PC], writes=[PN_])
                            cur = nx
                        Tt, TT_ = Pl[cur], "Pl%d" % cur
                        for g in range(8):
                            for h2 in range(2):
                                pr = slice(h2 * 64, (h2 + 1) * 64)
                                P.op("pe", lambda e: e.matmul(Pn[0][pr, g, :], lhsT=KR[pr, g, c, 0, :], rhs=Hbf[pr, g, :],
                                                              start=True, stop=False, skip_group_check=True),
                                     reads=["KR", "Hbf"], writes=["Pn0"])
                                P.op("pe", lambda e: e.matmul(Pn[0][pr, g, :], lhsT=GKm[pr, g, 0:64],
                                                              rhs=Vtok[pr, c, g * 64:(g + 1) * 64],
                                                              start=False, stop=True, skip_group_check=True),
                                     reads=["GKm", "Vtok"], writes=["Pn0"])
                        P.op("act", lambda e: e.copy(out=Wsb[:], in_=Pn[0][:]), reads=["Pn0"], writes=["Wsb"])
                        for g in range(8):
                            for h2 in range(2):
                                pr = slice(h2 * 64, (h2 + 1) * 64)
                                P.op("pe", lambda e: e.matmul(Pn[1][pr, g, :], lhsT=Tt[pr, g, :], rhs=Wsb[pr, g, :],
                                                              start=True, stop=True, skip_group_check=True),
                                     reads=[TT_, "Wsb"], writes=["Pn1"])
                        P.op("act", lambda e: e.copy(out=Usb[:], in_=Pn[1][:]), reads=["Pn1"], writes=["Usb"])
                        for g in range(8):
                            for h2 in range(2):
                                pr = slice(h2 * 64, (h2 + 1) * 64)
                                P.op("pe", lambda e: e.transpose(out=Ptk[pr, 0, g, :], in_=BK[pr, g, c, 1, :],
                                                                 identity=identb[pr, h2 * 64:(h2 + 1) * 64]),
                                     reads=["BK", "identb"], writes=["Ptk"])
                                P.op("pe", lambda e: e.transpose(out=Ptk[pr, 1, g, :], in_=BK[pr, g, c, 0, :],
                                                                 identity=identb[pr, h2 * 64:(h2 + 1) * 64]),
                                     reads=["BK", "identb"], writes=["Ptk"])
                        P.op("dve", lambda e: e.tensor_copy(out=Ktok[:], in_=Ptk[:, 0, :, :]), reads=["Ptk"], writes=["Ktok"])
                        P.op("dve", lambda e: e.tensor_scalar(out=nBtok[:], in0=Ptk[:, 1, :, :], scalar1=-1.0, scalar2=None,
                                                              op0=ALU.mult), reads=["Ptk"], writes=["nBtok"])
                        for g in range(8):
                            for h2 in range(2):
                                pr = slice(h2 * 64, (h2 + 1) * 64)
                                P.op("pe", lambda e: e.matmul(Pg[pr, g, 0:64], lhsT=Hbf[pr, g, :], rhs=KR[pr, g, c, 1, :],
                                                              start=True, stop=False, skip_group_check=True),
                                     reads=["Hbf", "KR"], writes=["Pg"])
                                P.op("pe", lambda e: e.matmul(Pg[pr, g, 0:64], lhsT=Vtok[pr, c, g * 64:(g + 1) * 64],
                                                              rhs=GKm[pr, g, 64:128], start=False, stop=False, skip_group_check=True),
                                     reads=["Vtok", "GKm"], writes=["Pg"])
                                P.op("pe", lambda e: e.matmul(Pg[pr, g, 0:64], lhsT=Usb[pr, g, :], rhs=GBm[pr, g, 64:128],
                                                              start=False, stop=True, skip_group_check=True),
                                     reads=["Usb", "GBm"], writes=["Pg"])
                        P.op("act", lambda e: e.copy(out=Ysb[:, :, c, :], in_=Pg[:, :, 0:64]), reads=["Pg"], writes=["Ysb"])
                        for g in range(8):
                            for h2 in range(2):
                                pr = slice(h2 * 64, (h2 + 1) * 64)
                                P.op("pe", lambda e: e.matmul(Pn[0][pr, g, :], lhsT=Ktok[pr, g, :],
                                                              rhs=Vtok[pr, c, g * 64:(g + 1) * 64],
                                                              start=True, stop=False, skip_group_check=True),
                                     reads=["Ktok", "Vtok"], writes=["Pn0"])
                                P.op("pe", lambda e: e.matmul(Pn[0][pr, g, :], lhsT=nBtok[pr, g, :], rhs=Usb[pr, g, :],
                                                              start=False, stop=True, skip_group_check=True),
                                     reads=["nBtok", "Usb"], writes=["Pn0"])
                        P.op("dve", lambda e: e.tensor_tensor(out=Htmp[:], in0=Pn[0][:], in1=H32[:], op=ALU.add),
                             reads=["Pn0", "H32"], writes=["Htmp"])
                        P.op("pool", lambda e: e.tensor_tensor(
                            out=H32[:], in0=Htmp[:], in1=gam[:, :, c:c + 1].to_broadcast([128, 8, 64]), op=ALU.mult),
                            reads=["Htmp", "gam"], writes=["H32"])
                        P.op("act", lambda e: e.copy(out=Hbf[:], in_=H32[:]), reads=["H32"], writes=["Hbf"])
                        if kind == "s":
                            out_deps.append(store_state(wkvs[c]))
                    if kind == "p" and qi == SEQ // TT - 1:
                        out_deps.append(store_state(wkvp))

                    for g in range(8):
                        pm, PM = pm_[0], "pmd0"
                        pm2, PM2 = pm_[1], "pmd1"
                        Yg = Ysb[:, g, 0:nch, 0:cl]
                        m_, ysq, var_, yn = tA[0], tA[1], tA[2], tA[3]
                        c3 = lambda ap: ap.rearrange("p (c t) -> p c t", t=cl)
                        P.op("pool", lambda e: e.tensor_copy(out=c3(yn[:, nt]), in_=Yg), reads=["Ysb"], writes=["tA3"])
                        P.op("pe", lambda e: e.matmul(pm[:, nt], lhsT=blk32[:], rhs=yn[:, nt], start=True, stop=True),
                             reads=["blk32", "tA3"], writes=[PM])
                        P.op("pool", lambda e: e.tensor_tensor(out=ysq[:, nt], in0=yn[:, nt], in1=yn[:, nt], op=ALU.mult),
                             reads=["tA3"], writes=["tA1"])
                        P.op("pe", lambda e: e.matmul(pm2[:, nt], lhsT=blk32[:], rhs=ysq[:, nt], start=True, stop=True),
                             reads=["blk32", "tA1"], writes=[PM2])
                        P.op("act", lambda e: e.activation(out=m_[:, nt], in_=pm[:, nt], func=AF.Copy, scale=1.0 / 64),
                             reads=[PM], writes=["tA0"])
                        P.op("pool", lambda e: e.tensor_tensor(out=ysq[:, nt], in0=m_[:, nt], in1=m_[:, nt], op=ALU.mult),
                             reads=["tA0", "tA1"], writes=["tA1"])
                        P.op("dve", lambda e: e.scalar_tensor_tensor(out=var_[:, nt], in0=pm2[:, nt], scalar=1.0 / 64,
                                                                     in1=ysq[:, nt], op0=ALU.mult, op1=ALU.subtract),
                             reads=[PM2, "tA1"], writes=["tA2"])
                        P.op("act", lambda e: e.activation(out=var_[:, nt], in_=var_[:, nt], func=AF.Sqrt, bias=GN_EPS),
                             reads=["tA2"], writes=["tA2"])
                        P.op("dve", lambda e: e.reciprocal(out=var_[:, nt], in_=var_[:, nt]), reads=["tA2"], writes=["tA2"])
                        P.op("dve", lambda e: e.tensor_tensor(out=yn[:, nt], in0=yn[:, nt], in1=m_[:, nt], op=ALU.subtract),
                             reads=["tA3", "tA0"], writes=["tA3"])
                        P.op("dve", lambda e: e.tensor_tensor(out=yn[:, nt], in0=yn[:, nt], in1=var_[:, nt], op=ALU.mult),
                             reads=["tA3", "tA2"], writes=["tA3"])
                        P.op("dve", lambda e: e.tensor_scalar(out=yn[:, nt], in0=yn[:, nt], scalar1=lngT[:, g:g + 1],
                                                              scalar2=lnbT[:, g:g + 1], op0=ALU.mult, op1=ALU.add),
                             reads=["tA3", "lngT", "lnbT"], writes=["tA3"])
                        P.op("pool", lambda e: e.tensor_tensor(out=yn[:, nt], in0=yn[:, nt], in1=bon[:, g, nt], op=ALU.add),
                             reads=["tA3", "bon"], writes=["tA3"])
                        if kind == "p":
                            yb, YB = tB[2], "tB2"
                            P.op("dve", lambda e: e.tensor_tensor(out=yb[:, nt], in0=yn[:, nt], in1=gT_[:, g, nt], op=ALU.mult),
                                 reads=["tA3", "gT_"], writes=[YB])
                            P.dma("sp", ygT_d[:, g, qi * TT:(qi + 1) * TT], yb[:, nt], reads=[YB], semkey="ygo")
                        else:
                            P.op("dve", lambda e: e.tensor_tensor(out=ygs[:, g, :], in0=yn[:, nt], in1=gT_[:, g, nt], op=ALU.mult),
                                 reads=["tA3", "gT_"], writes=["ygs"])
                    if kind == "p":
                        P.op("pool", lambda e: e.tensor_copy(out=lastcol[:], in_=hTe[:, :, TT:TT + 1]),
                             reads=["hTe"], writes=["lastcol"])
                P.barrier()

        if cfg.upto >= 7:
            gfin = sbt(st, "gfin", [128, D], F32)
            P.dma("sp", gfin[:], final_norm.partition_broadcast(128), writes=["gfin"])
            fin = dict(ssq=sbt(st, "fssq", [128, 1]), rstd=sbt(st, "frstd", [128, 1]),
                       junk=sbt(st, "fjunk", [128, D], BF16), y=[sbt(st, "fy%d" % i, [128, D]) for i in range(2)])
            fcnt = [0]

            def get_oT1(kind, qi, ntok, oT_tile, OT):
                if kind == "p":
                    for g in range(8):
                        P.dma("sp", oT_tile[:, g, :], ygT_d[:, g, qi * 512:(qi + 1) * 512], writes=[OT])
                else:
                    P.op("dve", lambda e: e.tensor_copy(out=oT_tile[:, :, 0:NST], in_=ygs[:]), reads=["ygs"], writes=[OT])

            def x_src1(kind, qi, sub, dst, X):
                if kind == "p":
                    t = qi * 4 + sub
                    P.dma("sp", dst, xl0p[t * 128:(t + 1) * 128, :], writes=[X])
                else:
                    P.dma("sp", dst, xl0s, writes=[X])

            def x_dst1(kind, qi, sub, src, X1):
                np_ = src.shape[0]
                fb = fcnt[0] % 2
                fcnt[0] += 1
                FY = "fy%d" % fb
                P.op("act", lambda e: e.activation(out=fin["junk"][0:np_, :], in_=src, func=AF.Square,
                                                   accum_out=fin["ssq"][0:np_, :]),
                     reads=[X1], writes=["fjunk", "fssq"])
                P.op("act", lambda e: e.activation(out=fin["rstd"][0:np_, :], in_=fin["ssq"][0:np_, :], func=AF.Sqrt,
                                                   scale=1.0 / D, bias=EPS), reads=["fssq"], writes=["frstd"])
                P.op("dve", lambda e: e.reciprocal(out=fin["rstd"][0:np_, :], in_=fin["rstd"][0:np_, :]),
                     reads=["frstd"], writes=["frstd"])
                P.op("dve", lambda e: e.scalar_tensor_tensor(out=fin["y"][fb][0:np_, :], in0=src, scalar=fin["rstd"][0:np_, :],
                                                             in1=gfin[0:np_, :], op0=ALU.mult, op1=ALU.mult),
                     reads=[X1, "frstd", "gfin"], writes=[FY])
                if kind == "p":
                    t = qi * 4 + sub
                    out_deps.append(P.dma("sp", yp[t * 128:(t + 1) * 128, :], fin["y"][fb][0:np_, :], reads=[FY], semkey=FY + "o"))
                else:
                    out_deps.append(P.dma("sp", ys, fin["y"][fb][0:np_, :], reads=[FY], semkey=FY + "o"))

            ffn_phase(1, 2, 3, Wrwo_b, get_oT1, x_src1, x_dst1, True)

        P.wait_all("sp", out_deps)
        build_nc.last_counts = dict(P.cnt)
    return nc


def build_satt(cfg):
    NPG, NPHYS = cfg.NPG, cfg.NPHYS
    NSA = 32
    NTK = NSA * TS
    NCOL = NSA * 2 * TS
    nc = bass.Bass("TRN2", target_bir_lowering=False)

    def din(name, shape, dt=F32):
        return nc.dram_tensor(name, list(shape), dt, kind="ExternalInput").ap()

    def dout(name, shape, dt=F32):
        return nc.dram_tensor(name, list(shape), dt, kind="ExternalOutput").ap()

    xs = din("xs", [NTK, D])
    cvec = din("cvec", [NSA, D])
    g_mix0 = din("g_mix0", [D])
    adaw = din("adaw", [D, 2 * D])
    adab = din("adab", [2 * D])
    w_hs = din("w_hs", [D, 384])
    bias2 = din("bias2", [2])
    cache_kv = din("cache_kv", [NPHYS * 128, 256])
    ptab = din("ptab", [NSA * NPG], I32)
    KVg_d = [nc.dram_tensor("KVg_d%d" % b_, [128, NSA, 256], F32, kind="Internal").ap() for b_ in range(NPG + 1)]
    oT_o = dout("oT_o", [128, NTK])
    ks_o = dout("ks_o", [NTK, 128])
    vs_o = dout("vs_o", [NTK, 128])
    idx_o = dout("idx_o", [128, NSA * NPG], I32) if cfg.debug else None

    with ExitStack() as st:
        P = Prog(nc, st)
        uniq = [0]

        def sbt(stack, name, shape, dt=F32):
            uniq[0] += 1
            return stack.enter_context(nc.sbuf_tensor("%s_u%d" % (name, uniq[0]), list(shape), dt))

        def pst(stack, name, shape, dt=F32):
            uniq[0] += 1
            P.excl.add(name)
            esz = 4 if dt == F32 else 2
            n = int(np.prod(shape[1:]))
            per_bank = 2048 // esz
            npad = ((n + per_bank - 1) // per_bank) * per_bank
            t_ = stack.enter_context(nc.psum_tensor("%s_u%d" % (name, uniq[0]), [128, npad], dt))
            v = t_[:, 0:n]
            if len(shape) == 3:
                v = v.rearrange("p (a b) -> p a b", b=shape[2])
            elif len(shape) == 4:
                v = v.rearrange("p (a b c) -> p a b c", b=shape[2], c=shape[3])
            return v

        out_deps = []
        if "noguard" not in cfg.dbg:
            guard = sbt(st, "guard", [128, 4096], F32)
        ident = sbt(st, "ident", [128, 128], F32)
        identb = sbt(st, "identb", [128, 128], BF16)
        negU = sbt(st, "negU", [128, 128], BF16)
        negO = sbt(st, "negO", [128, 128], BF16)
        P.op("pool", lambda e: e.memset(ident[:], 0.0), writes=["ident"])
        P.op("pool", lambda e: e.affine_select(
            out=ident[:], in_=ident[:], pattern=[[-1, 128]], compare_op=ALU.not_equal,
            fill=1.0, base=0, channel_multiplier=1), reads=["ident"], writes=["ident"])
        P.op("pool", lambda e: e.tensor_copy(out=identb[:], in_=ident[:]), reads=["ident"], writes=["identb"])
        P.op("pool", lambda e: e.memset(negO[:], -1.0), writes=["negO"])
        P.op("pool", lambda e: e.affine_select(
            out=negU[:], in_=negO[:], pattern=[[-1, 128]], compare_op=ALU.is_ge,
            fill=0.0, base=0, channel_multiplier=1), reads=["negO"], writes=["negU"])

        with ExitStack() as ph:
            pt_i = sbt(ph, "pt_i", [128, NSA * NPG], I32)
            pt_f = sbt(ph, "pt_f", [128, NSA * NPG], F32)
            iota_f = sbt(ph, "iota_f", [128, 1], F32)
            iota_i = sbt(ph, "iota_i", [128, 1], I32)
            idx = sbt(ph, "idx", [128, NSA * NPG], I32)
            NGB = 8
            G32 = [sbt(ph, "G32_%d" % i, [128, 256], F32) for i in range(NGB)]
            P.dma("sp", pt_i[:], ptab.partition_broadcast(128), writes=["pt_i"])
            P.op("dve", lambda e: e.tensor_copy(out=pt_f[:], in_=pt_i[:]), reads=["pt_i"], writes=["pt_f"])
            P.op("pool", lambda e: e.iota(iota_i[:], pattern=[[0, 1]], base=0, channel_multiplier=1), writes=["iota_i"])
            P.op("dve", lambda e: e.tensor_copy(out=iota_f[:], in_=iota_i[:]), reads=["iota_i"], writes=["iota_f"])
            P.op("dve", lambda e: e.tensor_scalar(out=pt_f[:], in0=pt_f[:], scalar1=128.0, scalar2=iota_f[:, 0:1],
                                                  op0=ALU.mult, op1=ALU.add), reads=["pt_f", "iota_f"], writes=["pt_f"])
            P.op("dve", lambda e: e.tensor_copy(out=idx[:], in_=pt_f[:]), reads=["pt_f"], writes=["idx"])
            if cfg.debug:
                out_deps.append(P.dma("sp", idx_o, idx[:], reads=["idx"], semkey="idxo"))
            gc = 0
            for blk in range(NPG):
                for s_ in range(NSA):
                    c_ = s_ * NPG + blk
                    for (src_, dst_) in ((cache_kv, KVg_d),):
                        gi = gc % NGB
                        gc += 1
                        GK_ = "G32_%d" % gi
                        P.dma("pool", G32[gi][:], src_, writes=[GK_],
                              indirect=bass.IndirectOffsetOnAxis(ap=idx[:, c_:c_ + 1], axis=0), reads=["idx"])
                        P.dma("sp", dst_[blk][:, s_, :], G32[gi][:], reads=[GK_], semkey=GK_ + "o")
            P.barrier()

        qTs = sbt(st, "qTs", [128, NTK], BF16)
        kTs = sbt(st, "kTs", [128, NSA, 128], BF16)
        Vs = sbt(st, "Vs", [128, NSA, 128], BF16)
        P.op("pool", lambda e: e.memset(kTs[:], 0.0), writes=["kTs"])
        P.op("pool", lambda e: e.memset(Vs[:], 0.0), writes=["Vs"])

        for _once in ([] if "nophase1" in cfg.dbg else [0]):
          with ExitStack() as ph:
              cT = sbt(ph, "cT", [128, 8, NSA], F32)
              scT = sbt(ph, "scT", [128, 8, NSA], BF16)
              wada = sbt(ph, "wada", [128, 8, 2 * D], BF16)
              whs = sbt(ph, "whs", [128, 8, 384], BF16)
              adabT = sbt(ph, "adabT", [128, 16], F32)
              gT = sbt(ph, "gT", [128, 8], F32)
              modT = sbt(ph, "modT", [128, 16, NSA], F32)
              Acoef = sbt(ph, "Acoef", [128, 8, NSA], F32)
              A_full = sbt(ph, "A_full", [128, 8, NTK], F32)
              B_full = sbt(ph, "B_full", [128, 8, NTK], F32)
              hT = sbt(ph, "hT", [128, 8, NTK], BF16)
              xt = [sbt(ph, "xt%d" % i, [128, D]) for i in range(2)]
              junk = sbt(ph, "junk", [128, D], BF16)
              ssq = [sbt(ph, "ssq%d" % i, [128, 1]) for i in range(2)]
              rstd = [sbt(ph, "rstd%d" % i, [128, 1]) for i in range(2)]
              xn = [sbt(ph, "xn%d" % i, [128, D]) for i in range(2)]
              tmpf = sbt(ph, "tmpf", [128, 128], F32)
              kvo = [sbt(ph, "kvo%d" % i, [128, 256], F32) for i in range(2)]
              vbf = [sbt(ph, "vbf%d" % i, [128, 128], BF16) for i in range(2)]
              psm = pst(ph, "psm", [128, 16, NSA], F32)
              ptr_ = [pst(ph, "ptr%d" % i, [128, 512], F32) for i in range(2)]
              pq = [pst(ph, "pq%d" % i, [128, 512], F32) for i in range(2)]
              with nc.allow_non_contiguous_dma(reason="tiny transposed parameter loads"):
                  for s_ in range(NSA):
                      P.dma("sp", cT[:, :, s_], cvec[s_].rearrange("(k p) -> p k", p=128), writes=["cT"])
                  P.dma("sp", gT[:], g_mix0.rearrange("(k p) -> p k", p=128), writes=["gT"])
                  P.dma("sp", adabT[:], adab.rearrange("(k p) -> p k", p=128), writes=["adabT"])
              for k in range(8):
                  for hc in range(2):
                      P.dma("pool", wada[:, k, hc * D:(hc + 1) * D], adaw[k * 128:(k + 1) * 128, hc * D:(hc + 1) * D],
                            writes=["wada"], max_dma_last_dim=4096)
                  P.dma("pool", whs[:, k, :], w_hs[k * 128:(k + 1) * 128, :], writes=["whs"])
              P.op("act", lambda e: e.activation(out=scT[:], in_=cT[:], func=AF.Silu), reads=["cT"], writes=["scT"])
              for oc in range(16):
                  for k in range(8):
                      P.op("pe", lambda e: e.matmul(psm[:, oc, :], lhsT=wada[:, k, oc * 128:(oc + 1) * 128],
                                                    rhs=scT[:, k, :], start=(k == 0), stop=(k == 7)),
                           reads=["wada", "scT"], writes=["psm"])
              P.op("dve", lambda e: e.tensor_tensor(
                  out=modT[:], in0=psm[:], in1=adabT[:].unsqueeze(2).to_broadcast([128, 16, NSA]), op=ALU.add),
                  reads=["psm", "adabT"], writes=["modT"])
              P.op("dve", lambda e: e.tensor_scalar(out=Acoef[:], in0=modT[:, 8:16, :], scalar1=1.0, scalar2=None, op0=ALU.add),
                   reads=["modT"], writes=["Acoef"])
              P.op("dve", lambda e: e.tensor_tensor(
                  out=Acoef[:], in0=Acoef[:], in1=gT[:].unsqueeze(2).to_broadcast([128, 8, NSA]), op=ALU.mult),
                  reads=["Acoef", "gT"], writes=["Acoef"])
              for k in range(8):
                  P.op("dve", lambda e: e.tensor_copy(
                      out=A_full[:, k, :].rearrange("p (s t) -> p s t", t=TS),
                      in_=Acoef[:, k, :].unsqueeze(2).to_broadcast([128, NSA, TS])),
                      reads=["Acoef"], writes=["A_full"])
                  P.op("pool", lambda e: e.tensor_copy(
                      out=B_full[:, k, :].rearrange("p (s t) -> p s t", t=TS),
                      in_=modT[:, k, :].unsqueeze(2).to_broadcast([128, NSA, TS])),
                      reads=["modT"], writes=["B_full"])
              for ti in range(2):
                  X, SS, RS, XN = "xt%d" % ti, "ssq%d" % ti, "rstd%d" % ti, "xn%d" % ti
                  P.dma("sp", xt[ti][:], xs[ti * 128:(ti + 1) * 128, :], writes=[X])
                  P.op("act", lambda e: e.activation(out=junk[:], in_=xt[ti][:], func=AF.Square, accum_out=ssq[ti][:]),
                       reads=[X], writes=["junk", SS])
                  P.op("act", lambda e: e.activation(out=rstd[ti][:], in_=ssq[ti][:], func=AF.Sqrt, scale=1.0 / D, bias=EPS),
                       reads=[SS], writes=[RS])
                  P.op("dve", lambda e: e.reciprocal(out=rstd[ti][:], in_=rstd[ti][:]), reads=[RS], writes=[RS])
                  P.op("dve", lambda e: e.tensor_scalar(out=xn[ti][:], in0=xt[ti][:], scalar1=rstd[ti][:], scalar2=None,
                                                        op0=ALU.mult), reads=[X, RS], writes=[XN])
                  for k in range(8):
                      pt, PT = ptr_[k % 2], "ptr%d" % (k % 2)
                      P.op("pe", lambda e: e.transpose(out=pt[:, 0:128], in_=xn[ti][:, k * 128:(k + 1) * 128], identity=ident[:]),
                           reads=[XN, "ident"], writes=[PT])
                      P.op("dve", lambda e: e.tensor_tensor(out=tmpf[:], in0=pt[:, 0:128],
                                                            in1=A_full[:, k, ti * 128:(ti + 1) * 128], op=ALU.mult),
                           reads=[PT, "A_full"], writes=["tmpf"])
                      P.op("pool", lambda e: e.tensor_tensor(out=hT[:, k, ti * 128:(ti + 1) * 128], in0=tmpf[:],
                                                             in1=B_full[:, k, ti * 128:(ti + 1) * 128], op=ALU.add),
                           reads=["tmpf", "B_full"], writes=["hT"])
              for which in range(1):
                  p_, PQ = pq[which], "pq%d" % which
                  for k in range(8):
                      P.op("pe", lambda e: e.matmul(p_[:, 0:NTK], lhsT=whs[:, k, which * 128:(which + 1) * 128],
                                                    rhs=hT[:, k, :], start=(k == 0), stop=(k == 7)),
                           reads=["whs", "hT"], writes=[PQ])
                  if which == 0:
                      P.op("act", lambda e: e.activation(out=qTs[:], in_=p_[:, 0:NTK], func=AF.Copy, scale=0.125),
                           reads=[PQ], writes=["qTs"])
                  else:
                      pass
              zt = sbt(ph, "zt", [128, NSA * 256], F32)
              P.op("pool", lambda e: e.memset(zt[:], 0.0), writes=["zt"])
              P.dma("sp", KVg_d[NPG].rearrange("p s d -> p (s d)"), zt[:], reads=["zt"], writes=["KgN"], semkey="KgN")
              for ti in range(2):
                  p_, PQ = pq[ti], "pq%d" % ti
                  KV, VB = "kvo%d" % ti, "vbf%d" % ti
                  for k in range(8):
                      P.op("pe", lambda e: e.matmul(p_[:, 0:256], lhsT=hT[:, k, ti * 128:(ti + 1) * 128],
                                                    rhs=whs[:, k, 128:384], start=(k == 0), stop=(k == 7)),
                           reads=["whs", "hT"], writes=[PQ])
                  P.op("act", lambda e: e.copy(out=kvo[ti][:], in_=p_[:, 0:256]), reads=[PQ], writes=[KV])
                  P.op("pool", lambda e: e.tensor_copy(out=vbf[ti][:], in_=kvo[ti][:, 128:256]), reads=[KV], writes=[VB])
                  out_deps.append(P.dma("sp", ks_o[ti * 128:(ti + 1) * 128, :], kvo[ti][:, 0:128], reads=[KV], semkey="ok%d" % ti))
                  out_deps.append(P.dma("sp", vs_o[ti * 128:(ti + 1) * 128, :], kvo[ti][:, 128:256], reads=[KV], semkey="ov%d" % ti))
                  for sl in range(16):
                      s_ = ti * 16 + sl
                      P.dma("sp", KVg_d[NPG][0:TS, s_, :], kvo[ti][sl * TS:(sl + 1) * TS, :], reads=[KV], writes=["KgN"], semkey="KgN")
              P.barrier()

        with ExitStack() as ph:
            NG = 2
            KV32 = [sbt(ph, "KV32_%d" % i, [128, NSA, 256], F32) for i in range(NG)]
            Kpg = [sbt(ph, "Kpg%d" % i, [128, NSA, 128], BF16) for i in range(NG)]
            Vpg = [sbt(ph, "Vpg%d" % i, [128, NSA, 128], BF16) for i in range(NG)]
            kTp = [sbt(ph, "kTp%d" % i, [128, NSA, 128], BF16) for i in range(NG)]
            maskn = sbt(ph, "maskn", [128, NSA * 2, TS], BF16)
            ones_m = sbt(ph, "ones_m2", [128, NCOL], BF16)
            bias2b = sbt(ph, "bias2b", [128, 2], F32)
            bias_bc = sbt(ph, "bias_bc", [128, NSA, 2, TS], F32)
            zsb = sbt(ph, "zsb", [128, NCOL], F32)
            e_t = sbt(ph, "es_t", [128, NCOL], F32)
            sp_t = [sbt(ph, "sps_t%d" % i, [128, NCOL], BF16) for i in range(2)]
            t1 = sbt(ph, "t1s", [128, NCOL], F32)
            a_t = [sbt(ph, "as_t%d" % i, [128, NCOL], BF16) for i in range(2)]
            SP32 = sbt(ph, "SPs32", [128, NCOL], F32)
            SPbf = sbt(ph, "SPsbf", [128, NCOL], BF16)
            zero_l = sbt(ph, "zero_l", [128, 128], BF16)
            zsb_n = sbt(ph, "zsb_n", [128, NCOL], F32)
            sp_n = sbt(ph, "sp_n", [128, NCOL], BF16)
            o_sb = sbt(ph, "o_sb", [128, NTK], F32)
            Pz = [pst(ph, "Pzs%d" % i, [128, NCOL], F32) for i in range(2)]
            Pa = [pst(ph, "Pas%d" % i, [128, NCOL], F32) for i in range(2)]
            Po = pst(ph, "Pos", [128, NTK], F32)
            Ptr = [pst(ph, "Ptrs%d" % i, [128, 8, 128], BF16) for i in range(2)]

            P.dma("sp", bias2b[:], bias2.partition_broadcast(128), writes=["bias2b"])
            P.op("pool", lambda e: e.memset(ones_m[:], 1.0), writes=["ones_m2"])
            P.op("pool", lambda e: e.memset(zero_l[:], 0.0), writes=["zero_l"])
            P.op("pool", lambda e: e.memset(SP32[:], 0.0), writes=["SPs32"])
            P.op("pool", lambda e: e.affine_select(
                out=maskn[:], in_=ones_m[:].rearrange("p (c t) -> p c t", t=TS),
                pattern=[[0, NSA * 2], [1, TS]], compare_op=ALU.is_gt,
                fill=0.0, base=0, channel_multiplier=-1), reads=["ones_m2"], writes=["maskn"])
            for s_ in range(NSA):
                P.op("dve" if s_ % 2 == 0 else "pool", lambda e: e.tensor_copy(
                    out=bias_bc[:, s_, :, :], in_=bias2b[:].unsqueeze(2).to_broadcast([128, 2, TS])),
                    reads=["bias2b"], writes=["bias_bc"])
            P.op("pe", lambda e: e.matmul(Po[:], lhsT=zero_l[:], rhs=ones_m[:, 0:NTK], start=True, stop=False),
                 reads=["zero_l", "ones_m2"], writes=["Pos"])

            def col(s_, h2):
                return (s_ * 2 + h2) * TS

            blocks = list(range(NPG, -1, -1))
            if cfg.kstop <= 4:
                blocks = blocks[:max(0, cfg.kstop - 2)]

            gdeps = []
            for bi, blk in enumerate(blocks):
                gb = bi % NG
                first, last = (bi == 0), (bi == len(blocks) - 1)
                if True:
                    KT, KP, VP = "kTp%d" % gb, "Kpg%d" % gb, "Vpg%d" % gb
                    P.dma("sp", KV32[gb][:], KVg_d[blk], reads=["KgN"], writes=["KV32_%d" % gb])
                    P.op("pool", lambda e: e.tensor_copy(out=Kpg[gb][:], in_=KV32[gb][:, :, 0:128]), reads=["KV32_%d" % gb], writes=[KP])
                    P.op("pool", lambda e: e.tensor_copy(out=Vpg[gb][:], in_=KV32[gb][:, :, 128:256]), reads=["KV32_%d" % gb], writes=[VP])
                    if "gatheronly" in cfg.dbg:
                        continue
                    for q4 in range(NSA // 8):
                        ptb, PTB = Ptr[q4 % 2], "Ptrs%d" % (q4 % 2)
                        for sl in range(8):
                            s_ = q4 * 8 + sl
                            P.op("pe", lambda e: e.transpose(out=ptb[:, sl, :], in_=Kpg[gb][:, s_, :], identity=identb[:]),
                                 reads=[KP, "identb"], writes=[PTB])
                        P.op("dve" if q4 % 2 == 0 else "act",
                             (lambda e: e.tensor_copy(out=kTp[gb][:, q4 * 8:(q4 + 1) * 8, :], in_=ptb[:])) if q4 % 2 == 0
                             else (lambda e: e.copy(out=kTp[gb][:, q4 * 8:(q4 + 1) * 8, :], in_=ptb[:])),
                             reads=[PTB], writes=[KT])
                    kt_, vt_ = kTp[gb], None
                    KK, VK = [KT], [VP]
                pz, PZ = Pz[bi % 2], "Pzs%d" % (bi % 2)
                pa, PA = Pa[bi % 2], "Pas%d" % (bi % 2)
                for s_ in range(NSA if not (blk == "new" and "noz" in cfg.dbg) else 0):
                    for h2 in range(2):
                        c0 = col(s_, h2)
                        P.op("pe", lambda e: e.matmul(
                            pz[:, c0:c0 + TS], lhsT=kt_[h2 * 64:(h2 + 1) * 64, s_, :],
                            rhs=qTs[h2 * 64:(h2 + 1) * 64, s_ * TS:(s_ + 1) * TS],
                            start=True, stop=True, skip_group_check=True),
                            reads=KK + ["qTs"], writes=[PZ])
                spt, S = sp_t[bi % 2], "sps_t%d" % (bi % 2)
                at, A = a_t[bi % 2], "as_t%d" % (bi % 2)
                zsb_, ZS = zsb, "zsb"
                P.op("dve", lambda e: e.tensor_tensor(out=zsb_[:], in0=pz[:],
                                                      in1=bias_bc[:].rearrange("p s h t -> p (s h t)"), op=ALU.add),
                     reads=[PZ, "bias_bc"], writes=[ZS])
                P.op("act", lambda e: e.activation(out=e_t[:], in_=zsb_[:], func=AF.Exp), reads=[ZS], writes=["es_t"])
                P.op("act", lambda e: e.activation(out=spt[:], in_=e_t[:], func=AF.Ln, bias=1.0), reads=["es_t"], writes=[S])
                if blk == NPG:
                    P.op("dve", lambda e: e.tensor_tensor(out=spt[:], in0=spt[:],
                                                          in1=maskn[:].rearrange("p c t -> p (c t)"), op=ALU.mult),
                         reads=[S, "maskn"], writes=[S])
                P.op("pe", lambda e: e.matmul(pa[:], lhsT=negU[:], rhs=spt[:], start=True, stop=first),
                     reads=["negU", S], writes=[PA])
                if not first:
                    P.op("pe", lambda e: e.matmul(pa[:], lhsT=negO[:], rhs=SPbf[:], start=False, stop=True),
                         reads=["negO", "SPsbf"], writes=[PA])
                P.op("dve", lambda e: e.tensor_tensor(out=t1[:], in0=pa[:], in1=zsb[:], op=ALU.add),
                     reads=[PA, "zsb"], writes=["t1s"])
                P.op("act", lambda e: e.activation(out=at[:], in_=t1[:], func=AF.Exp), reads=["t1s"], writes=[A])
                if blk == NPG:
                    P.op("dve", lambda e: e.tensor_tensor(out=at[:], in0=at[:],
                                                          in1=maskn[:].rearrange("p c t -> p (c t)"), op=ALU.mult),
                         reads=[A, "maskn"], writes=[A])
                for s_ in range(NSA if not (blk == "new" and "noav" in cfg.dbg) else 0):
                    for h2 in range(2):
                        c0 = col(s_, h2)
                        P.op("pe", lambda e: e.matmul(
                            Po[h2 * 64:(h2 + 1) * 64, s_ * TS:(s_ + 1) * TS], lhsT=(vt_[:, s_, h2 * 64:(h2 + 1) * 64] if vt_ is not None else Vpg[gb][:, s_, h2 * 64:(h2 + 1) * 64]),
                            rhs=at[:, c0:c0 + TS], start=False, stop=False, skip_group_check=True),
                            reads=VK + [A], writes=["Pos"])
                if not last:
                    P.op("pool", lambda e: e.tensor_tensor(out=SP32[:], in0=SP32[:], in1=spt[:], op=ALU.add),
                         reads=["SPs32", S], writes=["SPs32"])
                    P.op("pool", lambda e: e.tensor_copy(out=SPbf[:], in_=SP32[:]), reads=["SPs32"], writes=["SPsbf"])
            P.op("pe", lambda e: e.matmul(Po[:], lhsT=zero_l[:], rhs=ones_m[:, 0:NTK], start=False, stop=True),
                 reads=["zero_l", "ones_m2"], writes=["Pos"])
            P.op("dve", lambda e: e.tensor_copy(out=o_sb[:], in_=Po[:]), reads=["Pos"], writes=["o_sb"])
            out_deps.append(P.dma("sp", oT_o, o_sb[:], reads=["o_sb"], semkey="oo"))
            P.wait_all("sp", out_deps)
    return nc


_NC = None
_NC1 = None


def _f32(a):
    return np.ascontiguousarray(np.asarray(a, dtype=np.float32))


def _satt_inputs(inputs, c):
    f = _f32
    wq = np.asarray(inputs["sb_w_qkv"], dtype=np.float32)[0]
    w_hs = np.concatenate([wq[:, j * D + 128 * c:j * D + 128 * (c + 1)] for j in range(3)], axis=1)
    ck = np.asarray(inputs["cache_k"], dtype=np.float32)[0]
    cv = np.asarray(inputs["cache_v"], dtype=np.float32)[0]
    nph = ck.shape[0]
    return {
        "xs": f(inputs["x_sample"]).reshape(-1, D),
        "cvec": f(inputs["c_sample"]),
        "g_mix0": f(inputs["norm_mix"][0]),
        "adaw": f(np.asarray(inputs["ada_w_mix"], dtype=np.float32)[0][:, :2 * D]),
        "adab": f(np.asarray(inputs["ada_b_mix"], dtype=np.float32)[0][:2 * D]),
        "w_hs": f(w_hs),
        "bias2": f(np.asarray(inputs["sb_bias"], dtype=np.float32)[0][2 * c:2 * c + 2]),
        "cache_kv": np.concatenate([ck[:, :, 2 * c:2 * c + 2, :].reshape(nph * 128, 128),
                                    cv[:, :, 2 * c:2 * c + 2, :].reshape(nph * 128, 128)], axis=1),
        "ptab": np.ascontiguousarray(np.asarray(inputs["page_table"], dtype=np.int32)).reshape(-1),
    }


def _core_inputs(inputs, c, oT_all=None):
    f = _f32
    sq = c % 4
    m = {
        "xp": f(inputs["x_prompt"][sq]),
        "xs": f(inputs["x_sample"][NS * c:NS * (c + 1)]).reshape(NST, D),
        "cvec": np.concatenate([f(inputs["c_prompt"][sq:sq + 1]), f(inputs["c_sample"][NS * c:NS * (c + 1)])], axis=0),
        "norm_mix": f(inputs["norm_mix"]), "norm_ffn": f(inputs["norm_ffn"]),
        "ada_w_mix": f(inputs["ada_w_mix"]), "ada_b_mix": f(inputs["ada_b_mix"]),
        "ada_w_ffn": f(inputs["ada_w_ffn"]), "ada_b_ffn": f(inputs["ada_b_ffn"]),
        "w_up": f(inputs["w_up"]), "w_down": f(inputs["w_down"]),
        "w_qkv": f(inputs["sb_w_qkv"][0]), "w_o": f(inputs["sb_w_o"][0]),
        "sb_bias": f(inputs["sb_bias"][0]),
        "state_wkv": f(inputs["state_wkv"][0][NS * c:NS * (c + 1)]),
        "state_shift": f(inputs["state_shift"][0][NS * c:NS * (c + 1)]),
        "rw_mu": f(inputs["rw_mu"][0]), "rw_w_r": f(inputs["rw_w_r"][0]), "rw_w_k": f(inputs["rw_w_k"][0]),
        "rw_w_v": f(inputs["rw_w_v"][0]), "rw_w_o": f(inputs["rw_w_o"][0]), "rw_w0": f(inputs["rw_w0"][0]),
        "rw_w1": f(inputs["rw_w1"][0]), "rw_w2": f(inputs["rw_w2"][0]), "rw_a0": f(inputs["rw_a0"][0]),
        "rw_a1": f(inputs["rw_a1"][0]), "rw_a2": f(inputs["rw_a2"][0]), "rw_g1": f(inputs["rw_g1"][0]),
        "rw_g2": f(inputs["rw_g2"][0]), "rw_k_k": f(inputs["rw_k_k"][0]), "rw_k_a": f(inputs["rw_k_a"][0]),
        "rw_r_k": f(inputs["rw_r_k"][0]).reshape(-1), "rw_ln_g": f(inputs["rw_ln_g"][0]),
        "rw_ln_b": f(inputs["rw_ln_b"][0]), "final_norm": f(inputs["final_norm"]),
    }
    if oT_all is None:
        m["cache_k"] = np.asarray(inputs["cache_k"], dtype=np.float32).reshape(-1, D)
        m["cache_v"] = np.asarray(inputs["cache_v"], dtype=np.float32).reshape(-1, D)
        m["ptab"] = np.ascontiguousarray(np.asarray(inputs["page_table"], dtype=np.int32)[NS * c:NS * (c + 1)]).reshape(-1)
    else:
        m["oTs_in"] = np.ascontiguousarray(np.transpose(oT_all[:, :, NST * c:NST * (c + 1)], (1, 0, 2)))
    return m


def kernel(**inputs):
    global _NC, _NC1
    SEQ = int(np.shape(inputs["x_prompt"])[1])
    NPG = int(np.shape(inputs["page_table"])[1])
    NPHYS = int(np.shape(inputs["cache_k"])[1])
    if _NC is None:
        _NC1 = build_satt(Cfg(SEQ=SEQ, NPG=NPG, NPHYS=NPHYS))
        _NC = build_nc(Cfg(SEQ=SEQ, NPG=NPG, NPHYS=NPHYS, split=True))
    B, DB = 4, 32
    r1 = run_bass_kernel_spmd(_NC1, [_satt_inputs(inputs, c) for c in range(NCORES)], core_ids=list(range(NCORES))).results
    oT_all = np.stack([np.asarray(r1[g]["oT_o"], dtype=np.float32) for g in range(NCORES)])
    ks_all = np.stack([np.asarray(r1[g]["ks_o"], dtype=np.float32) for g in range(NCORES)])
    vs_all = np.stack([np.asarray(r1[g]["vs_o"], dtype=np.float32) for g in range(NCORES)])
    k_sample = np.transpose(ks_all.reshape(8, DB, TS, 2, 64), (1, 2, 0, 3, 4)).reshape(1, DB, TS, 16, 64)
    v_sample = np.transpose(vs_all.reshape(8, DB, TS, 2, 64), (1, 2, 0, 3, 4)).reshape(1, DB, TS, 16, 64)
    r = run_bass_kernel_spmd(_NC, [_core_inputs(inputs, c, oT_all) for c in range(NCORES)],
                             core_ids=list(range(NCORES))).results
    k_prompt = np.stack([r[s]["kp"] for s in range(B)]).reshape(1, B, SEQ, 16, 64)
    v_prompt = np.stack([r[s]["vp"] for s in range(B)]).reshape(1, B, SEQ, 16, 64)
    y_prompt = np.stack([r[s]["yp"] for s in range(B)]).reshape(B, SEQ, D)
    y_sample = np.concatenate([r[c]["ys"] for c in range(NCORES)]).reshape(DB, TS, D)
    wkv_prompt = np.stack([r[s]["wkvp"] for s in range(B)]).reshape(1, B, 16, 64, 64)
    shift_prompt = np.stack([r[s]["shp"] for s in range(B)]).reshape(1, B, D)
    wkv_sample = np.concatenate([r[c]["wkvs"] for c in range(NCORES)]).reshape(1, DB, 16, 64, 64)
    shift_sample = np.concatenate([r[c]["shs"] for c in range(NCORES)]).reshape(1, DB, D)
    return (_f32(y_prompt), _f32(y_sample), _f32(k_prompt), _f32(v_prompt), _f32(k_sample), _f32(v_sample),
            _f32(wkv_prompt), _f32(shift_prompt), _f32(wkv_sample), _f32(shift_sample))
```

```python
import numpy as np
from contextlib import ExitStack
import concourse.bass as bass
import concourse.mybir as mybir
from concourse.bass_utils import run_bass_kernel_spmd

F32 = mybir.dt.float32
BF16 = mybir.dt.bfloat16
I32 = mybir.dt.int32
AF = mybir.ActivationFunctionType
ALU = mybir.AluOpType

D = 1024
NCORES = 8
NS = 4
TS = 8
NST = NS * TS
NSEQ = 1 + NS
EPS = 1e-6


class Cfg:
    def __init__(self, SEQ=4096, NPG=64, NPHYS=2560, debug=False, upto=99, split=False):
        self.SEQ, self.NPG, self.NPHYS, self.debug, self.upto = SEQ, NPG, NPHYS, debug, upto
        self.split = split
        import os as _os
        self.dbg = set(_os.environ.get('KDBG', '').split(','))
        self.kstop = int(_os.environ.get('KSTOP', '99'))
        self.nsmax = int(_os.environ.get('NSMAX', '99'))


class Prog:
    def __init__(self, nc, stack):
        self.nc = nc
        self.stack = stack
        self.eng = {"pe": nc.tensor, "dve": nc.vector, "act": nc.scalar,
                    "pool": nc.gpsimd, "sp": nc.sync}
        self.sem = {k: stack.enter_context(nc.semaphore("e_" + k)) for k in self.eng}
        self.cnt = {k: 0 for k in self.eng}
        self.seen = {k: {} for k in self.eng}
        self.lastw = {}
        self.readers = {}
        self.dsem = {}
        self.nsem = 0
        self.excl = set()

    def _wait(self, e, dep):
        sem, val, src = dep
        if src == e and e == "pe":
            return
        name = id(sem)
        if self.seen[e].get(name, 0) >= val:
            return
        self.eng[e].wait_ge(sem, val)
        self.seen[e][name] = val

    def _deps(self, e, reads, writes):
        for k in reads:
            if k in self.lastw:
                self._wait(e, self.lastw[k])
        for k in writes:
            if k in self.lastw:
                self._wait(e, self.lastw[k])
            for d in self.readers.get(k, {}).values():
                self._wait(e, d)

    def _record(self, dep, reads, writes):
        for k in writes:
            self.lastw[k] = dep
            self.readers[k] = {}
        for k in reads:
            if k in writes:
                continue
            self.readers.setdefault(k, {})[id(dep[0])] = dep

    def op(self, e, fn, reads=(), writes=()):
        xr = [k for k in reads if k in self.excl]
        if xr:
            reads = [k for k in reads if k not in self.excl]
            writes = list(writes) + [k for k in xr if k not in writes]
        self._deps(e, reads, writes)
        ins = fn(self.eng[e])
        self.cnt[e] += 1
        ins.then_inc(self.sem[e], 1)
        self._record((self.sem[e], self.cnt[e], e), reads, writes)
        return ins

    def dma(self, e, out, in_, reads=(), writes=(), semkey=None, indirect=None, **kw):
        self._deps(e, reads, writes)
        key = semkey if semkey is not None else (writes[0] if writes else reads[0])
        if key not in self.dsem:
            self.dsem[key] = [self.stack.enter_context(self.nc.semaphore("d%d" % self.nsem)), 0]
            self.nsem += 1
        ds = self.dsem[key]
        if indirect is not None:
            ins = self.eng[e].indirect_dma_start(out=out, out_offset=None, in_=in_,
                                                 in_offset=indirect, **kw)
        else:
            ins = self.eng[e].dma_start(out=out, in_=in_, **kw)
        ds[1] += 16
        ins.then_inc(ds[0], 16)
        dep = (ds[0], ds[1], "dma")
        self._record(dep, reads, writes)
        return dep

    def wait_all(self, e, deps):
        for d in deps:
            self._wait(e, d)

    def barrier(self):
        deps = [(self.sem[k], self.cnt[k], k) for k in self.eng if self.cnt[k] > 0]
        deps += [(ds[0], ds[1], "dma") for ds in self.dsem.values()]
        for e in self.eng:
            for d in deps:
                if d[2] == e:
                    continue
                self._wait(e, d)


def build_nc(cfg):
    SEQ, NPG, NPHYS = cfg.SEQ, cfg.NPG, cfg.NPHYS
    NT = SEQ // 128
    NQ = SEQ // 512
    nc = bass.Bass("TRN2", target_bir_lowering=False)

    def din(name, shape, dt=F32):
        return nc.dram_tensor(name, list(shape), dt, kind="ExternalInput").ap()

    def dout(name, shape, dt=F32):
        return nc.dram_tensor(name, list(shape), dt, kind="ExternalOutput").ap()

    def dscr(name, shape, dt=BF16):
        return nc.dram_tensor(name, list(shape), dt, kind="Internal").ap()

    xp = din("xp", [SEQ, D])
    xs = din("xs", [NST, D])
    cvec = din("cvec", [NSEQ, D])
    norm_mix = din("norm_mix", [2, D])
    norm_ffn = din("norm_ffn", [2, D])
    ada_w_mix = din("ada_w_mix", [2, D, 3 * D])
    ada_b_mix = din("ada_b_mix", [2, 3 * D])
    ada_w_ffn = din("ada_w_ffn", [2, D, 3 * D])
    ada_b_ffn = din("ada_b_ffn", [2, 3 * D])
    w_up = din("w_up", [2, D, 4 * D])
    w_down = din("w_down", [2, 4 * D, D])
    w_qkv = din("w_qkv", [D, 3 * D])
    w_o = din("w_o", [D, D])
    sb_bias = din("sb_bias", [16])
    if not cfg.split:
        cache_k = din("cache_k", [NPHYS * 128, D])
        cache_v = din("cache_v", [NPHYS * 128, D])
        ptab = din("ptab", [NS * NPG], I32)
    else:
        oTs_in = din("oTs_in", [128, 8, NST])
    state_wkv = din("state_wkv", [NS, 16, 64, 64])
    state_shift = din("state_shift", [NS, D])
    rw_mu = din("rw_mu", [6, D])
    rw_w_r = din("rw_w_r", [D, D])
    rw_w_k = din("rw_w_k", [D, D])
    rw_w_v = din("rw_w_v", [D, D])
    rw_w_o = din("rw_w_o", [D, D])
    rw_w0 = din("rw_w0", [D])
    rw_w1 = din("rw_w1", [D, 64])
    rw_w2 = din("rw_w2", [64, D])
    rw_a0 = din("rw_a0", [D])
    rw_a1 = din("rw_a1", [D, 64])
    rw_a2 = din("rw_a2", [64, D])
    rw_g1 = din("rw_g1", [D, 160])
    rw_g2 = din("rw_g2", [160, D])
    rw_k_k = din("rw_k_k", [D])
    rw_k_a = din("rw_k_a", [D])
    rw_r_k = din("rw_r_k", [D])
    rw_ln_g = din("rw_ln_g", [D])
    rw_ln_b = din("rw_ln_b", [D])
    final_norm = din("final_norm", [D])

    kp = dout("kp", [SEQ, D])
    vp = dout("vp", [SEQ, D])
    if not cfg.split:
        ks = dout("ks", [NST, D])
        vs = dout("vs", [NST, D])
    if cfg.debug:
        xl0p = dout("xl0p", [SEQ, D])
        xl0s = dout("xl0s", [NST, D])
    else:
        xl0p = dscr("xl0p", [SEQ, D], F32)
        xl0s = dscr("xl0s", [NST, D], F32)
    yp = dout("yp", [SEQ, D])
    ys = dout("ys", [NST, D])
    wkvp = dout("wkvp", [16, 64, 64])
    shp = dout("shp", [D])
    wkvs = dout("wkvs", [NS, 16, 64, 64])
    shs = dout("shs", [NS, D])

    Wqkv_b = dscr("Wqkv_b", [D, 3 * D])
    Wo_b = dscr("Wo_b", [D, D])
    Wup_b = [dscr("Wup_b%d" % i, [D, 4 * D]) for i in range(2)]
    Wdn_b = [dscr("Wdn_b%d" % i, [4 * D, D]) for i in range(2)]
    Wam_b = [dscr("Wam_b%d" % i, [D, 3 * D]) for i in range(2)]
    Waf_b = [dscr("Waf_b%d" % i, [D, 3 * D]) for i in range(2)]
    qT_d = dscr("qT_d", [128, 8, SEQ])
    kT_d = dscr("kT_d", [128, 8, SEQ])
    V_d = dscr("V_d", [SEQ, D])
    oT_d = dscr("oT_d", [128, 8, SEQ])
    ygT_d = dscr("ygT_d", [128, 8, SEQ])
    Wr_b = dscr("Wr_b", [D, D])
    Wk_b = dscr("Wk_b", [D, D])
    Wv_b = dscr("Wv_b", [D, D])
    Wrwo_b = dscr("Wrwo_b", [D, D])

    with ExitStack() as st:
        P = Prog(nc, st)

        uniq = [0]

        def sbt(stack, name, shape, dt=F32):
            uniq[0] += 1
            return stack.enter_context(nc.sbuf_tensor("%s_u%d" % (name, uniq[0]), list(shape), dt))

        def pst(stack, name, shape, dt=F32):
            uniq[0] += 1
            P.excl.add(name)
            esz = 4 if dt == F32 else 2
            n = int(np.prod(shape[1:]))
            per_bank = 2048 // esz
            npad = ((n + per_bank - 1) // per_bank) * per_bank
            t_ = stack.enter_context(nc.psum_tensor("%s_u%d" % (name, uniq[0]), [128, npad], dt))
            v = t_[:, 0:n]
            if len(shape) == 3:
                v = v.rearrange("p (a b) -> p a b", b=shape[2])
            elif len(shape) == 4:
                v = v.rearrange("p (a b c) -> p a b c", b=shape[2], c=shape[3])
            return v

        out_deps = []

        ident = sbt(st, "ident", [128, 128], F32)
        identb = sbt(st, "identb", [128, 128], BF16)
        negU = sbt(st, "negU", [128, 128], BF16)
        negO = sbt(st, "negO", [128, 128], BF16)
        biasb = sbt(st, "biasb", [128, 16], F32)
        cT = sbt(st, "cT", [128, 8, NSEQ], F32)
        scT = sbt(st, "scT", [128, 8, NSEQ], BF16)
        modT = [sbt(st, "modT%d" % i, [128, 16, NSEQ], F32) for i in range(4)]
        Acoef = [sbt(st, "Acoef%d" % i, [128, 8, NSEQ], F32) for i in range(4)]
        gate_bc = [sbt(st, "gate%d" % (i % 2), [128, NSEQ, D], BF16) for i in range(2)] * 2
        gsm = [sbt(st, "gsm%d" % (i % 2), [NST, D], BF16) for i in range(2)] * 2

        qTs = sbt(st, "qTs", [128, 8, NST], BF16)
        oTs = sbt(st, "oTs", [128, 8, NST], BF16)
        ygs = sbt(st, "ygs", [128, 8, NST], BF16)
        P.op("pool", lambda e: e.memset(ident[:], 0.0), writes=["ident"])
        P.op("pool", lambda e: e.affine_select(
            out=ident[:], in_=ident[:], pattern=[[-1, 128]], compare_op=ALU.not_equal,
            fill=1.0, base=0, channel_multiplier=1), reads=["ident"], writes=["ident"])
        P.op("pool", lambda e: e.tensor_copy(out=identb[:], in_=ident[:]), reads=["ident"], writes=["identb"])
        P.op("pool", lambda e: e.memset(negO[:], -1.0), writes=["negO"])
        P.op("pool", lambda e: e.affine_select(
            out=negU[:], in_=negO[:], pattern=[[-1, 128]], compare_op=ALU.is_ge,
            fill=0.0, base=0, channel_multiplier=1), reads=["negO"], writes=["negU"])
        P.dma("sp", biasb[:], sb_bias.partition_broadcast(128), writes=["biasb"])
        with nc.allow_non_contiguous_dma(reason="tiny transposed parameter loads"):
            for s_ in range(NSEQ):
                P.dma("sp", cT[:, :, s_], cvec[s_].rearrange("(k p) -> p k", p=128), writes=["cT"])
        P.op("act", lambda e: e.activation(out=scT[:], in_=cT[:], func=AF.Silu),
             reads=["cT"], writes=["scT"])

        with ExitStack() as ph:
            NB = 8
            bnc = [sbt(ph, "bnc%d" % i, [128, 2048], BF16) for i in range(NB)]
            cnt = [0]

            def cast_w(src, dst, rows, cols):
                cw = 2048 if cols % 2048 == 0 else 1536 if cols % 1536 == 0 else 1024
                for r in range(rows // 128):
                    for c in range(cols // cw):
                        b = cnt[0] % NB
                        cnt[0] += 1
                        key = "bnc%d" % b
                        P.dma("pool", bnc[b][:, 0:cw], src[r * 128:(r + 1) * 128, c * cw:(c + 1) * cw],
                              writes=[key], max_dma_last_dim=4096)
                        P.dma("sp", dst[r * 128:(r + 1) * 128, c * cw:(c + 1) * cw], bnc[b][:, 0:cw],
                              reads=[key], semkey=key + "o")

            cast_w(w_qkv, Wqkv_b, D, 3 * D)
            cast_w(w_o, Wo_b, D, D)
            cast_w(rw_w_r, Wr_b, D, D)
            cast_w(rw_w_k, Wk_b, D, D)
            cast_w(rw_w_v, Wv_b, D, D)
            cast_w(rw_w_o, Wrwo_b, D, D)
            for i in range(2):
                cast_w(ada_w_mix[i], Wam_b[i], D, 3 * D)
                cast_w(ada_w_ffn[i], Waf_b[i], D, 3 * D)
                cast_w(w_up[i], Wup_b[i], D, 4 * D)
                cast_w(w_down[i], Wdn_b[i], 4 * D, D)
            P.barrier()

        def compute_ada(idx, Wb, bvec, gvec):
            with ExitStack() as ph:
                wada = sbt(ph, "wada", [128, 8, 3 * D], BF16)
                screp = sbt(ph, "screp", [128, 8, NSEQ, 128], BF16)
                P.op("dve", lambda e: e.tensor_copy(
                    out=screp[:], in_=scT[:].unsqueeze(3).to_broadcast([128, 8, NSEQ, 128])),
                    reads=["scT"], writes=["screp"])
                adabT = sbt(ph, "adabT", [128, 24], F32)
                gT = sbt(ph, "gT", [128, 8], F32)
                adab_bc = sbt(ph, "adab_bc", [128, D], F32)
                psm = pst(ph, "psm", [128, 16, 8], F32)
                psg = [pst(ph, "psg%d" % i, [128, 512], F32) for i in range(2)]
                with nc.allow_non_contiguous_dma(reason="tiny transposed parameter loads"):
                    P.dma("sp", gT[:], gvec.rearrange("(k p) -> p k", p=128), writes=["gT"])
                    P.dma("sp", adabT[:], bvec.rearrange("(k p) -> p k", p=128), writes=["adabT"])
                P.dma("sp", adab_bc[:], bvec[2 * D:3 * D].partition_broadcast(128), writes=["adab_bc"])
                for k in range(8):
                    P.dma("sp", wada[:, k, :], Wb[k * 128:(k + 1) * 128, :], writes=["wada"])
                M, A, G = "modT%d" % idx, "Acoef%d" % idx, "gate%d" % (idx % 2)
                for oc in range(16):
                    for k in range(8):
                        P.op("pe", lambda e: e.matmul(
                            psm[:, oc, 0:NSEQ], lhsT=wada[:, k, oc * 128:(oc + 1) * 128],
                            rhs=scT[:, k, :], start=(k == 0), stop=(k == 7)),
                            reads=["wada", "scT"], writes=["psm"])
                P.op("dve", lambda e: e.tensor_tensor(
                    out=modT[idx][:], in0=psm[:, :, 0:NSEQ],
                    in1=adabT[:, 0:16].unsqueeze(2).to_broadcast([128, 16, NSEQ]), op=ALU.add),
                    reads=["psm", "adabT"], writes=[M])
                P.op("dve", lambda e: e.tensor_scalar(
                    out=Acoef[idx][:], in0=modT[idx][:, 8:16, :], scalar1=1.0, scalar2=None, op0=ALU.add),
                    reads=[M], writes=[A])
                P.op("dve", lambda e: e.tensor_tensor(
                    out=Acoef[idx][:], in0=Acoef[idx][:],
                    in1=gT[:].unsqueeze(2).to_broadcast([128, 8, NSEQ]), op=ALU.mult),
                    reads=[A, "gT"], writes=[A])
                for s_ in range(NSEQ):
                    for n in range(2):
                        pg = psg[n]
                        PG = "psg%d" % n
                        for k in range(8):
                            P.op("pe", lambda e: e.matmul(
                                pg[:], lhsT=screp[:, k, s_, :],
                                rhs=wada[:, k, 2 * D + n * 512:2 * D + (n + 1) * 512],
                                start=(k == 0), stop=(k == 7)),
                                reads=["wada", "screp"], writes=[PG])
                        P.op("dve", lambda e: e.tensor_tensor(
                            out=gate_bc[idx][:, s_, n * 512:(n + 1) * 512], in0=pg[:],
                            in1=adab_bc[:, n * 512:(n + 1) * 512], op=ALU.add),
                            reads=[PG, "adab_bc"], writes=[G])
                for s_ in range(NS):
                    P.dma("sp", gsm[idx][s_ * TS:(s_ + 1) * TS, :], gate_bc[idx][0:TS, 1 + s_, :],
                          reads=[G], writes=["gsm%d" % (idx % 2)])
                P.barrier()

        if cfg.upto >= 1:
            compute_ada(0, Wam_b[0], ada_b_mix[0], norm_mix[0])
            compute_ada(1, Waf_b[0], ada_b_ffn[0], norm_ffn[0])

        def norm_to_hT(tl, X, np_, idx, groups, hT_ap_fn, HT, pstr):
            b = tl["b"]
            J, SS, RS, XN = "junk%d" % b, "ssq%d" % b, "rstd%d" % b, "xn%d" % b
            P.op("act", lambda e: e.activation(out=tl["junk"][0:np_, :], in_=tl["x"][0:np_, :],
                                               func=AF.Square, accum_out=tl["ssq"][0:np_, :]),
                 reads=[X], writes=[J, SS])
            P.op("act", lambda e: e.activation(out=tl["rstd"][0:np_, :], in_=tl["ssq"][0:np_, :],
                                               func=AF.Sqrt, scale=1.0 / D, bias=EPS),
                 reads=[SS], writes=[RS])
            P.op("dve", lambda e: e.reciprocal(out=tl["rstd"][0:np_, :], in_=tl["rstd"][0:np_, :]),
                 reads=[RS], writes=[RS])
            P.op("dve", lambda e: e.tensor_scalar(out=tl["xn"][0:np_, :], in0=tl["x"][0:np_, :],
                                                  scalar1=tl["rstd"][0:np_, :], scalar2=None,
                                                  op0=ALU.mult),
                 reads=[X, RS], writes=[XN])
            for k in range(8 if cfg.kstop >= 3 else 0):
                pt, PT = pstr[k % 2]
                P.op("pe", lambda e: e.transpose(out=pt[:, 0:np_], in_=tl["xn"][0:np_, k * 128:(k + 1) * 128],
                                                 identity=ident[0:np_, 0:np_]),
                     reads=[XN, "ident"], writes=[PT])
                for (sq, c0, c1) in groups:
                    P.op("dve", lambda e: e.tensor_scalar(
                        out=hT_ap_fn(k, c0, c1), in0=pt[:, c0:c1],
                        scalar1=Acoef[idx][:, k, sq:sq + 1], scalar2=modT[idx][:, k, sq:sq + 1],
                        op0=ALU.mult, op1=ALU.add),
                        reads=[PT, "Acoef%d" % idx, "modT%d" % idx], writes=[HT])

        stiles = [("p", q, 512) for q in range(NQ)] + [("s", 0, NST)]
        sgroups = [(1 + s, s * TS, (s + 1) * TS) for s in range(NS)]

        st_att = ExitStack()
        kTs = sbt(st_att, "kTs", [128, 8, NS, 128], BF16)
        Vs = sbt(st_att, "Vs", [128, NS, D], BF16)
        if cfg.upto >= 2:
            with ExitStack() as ph:
                wqkv = sbt(ph, "wqkv", [128, 8, 3 * D], BF16)
                NBUF = 2
                tls = [dict(b=i, x=sbt(ph, "xt%d" % i, [128, D]), junk=sbt(ph, "junk%d" % i, [128, D], F32 if "junk32" in cfg.dbg else BF16),
                            ssq=sbt(ph, "ssq%d" % i, [128, 1]), rstd=sbt(ph, "rstd%d" % i, [128, 1]),
                            xn=sbt(ph, "xn%d" % i, [128, D])) for i in range(NBUF)]
                hT = [sbt(ph, "hT%d" % i, [128, 8, 512], BF16) for i in range(2)]
                qk_st = [sbt(ph, "qkst%d" % i, [128, 512], BF16) for i in range(4)]
                kvo = [sbt(ph, "kvo%d" % i, [128, 2 * D], F32) for i in range(2)]
                vbf = [sbt(ph, "vbf%d" % i, [128, D], BF16) for i in range(2)]
                ptr_ = [pst(ph, "ptr%d" % i, [128, 512], F32) for i in range(2)]
                pqk = [pst(ph, "pqk%d" % i, [128, 512], F32) for i in range(2)]
                pkv = [pst(ph, "pkv%d" % i, [128, 512], F32) for i in range(4)]
                pstr = [(ptr_[i], "ptr%d" % i) for i in range(2)]
                for k in range(8):
                    P.dma("sp", wqkv[:, k, :], Wqkv_b[k * 128:(k + 1) * 128, :], writes=["wqkv"])
                if "nomemset" not in cfg.dbg:
                    P.op("pool", lambda e: e.memset(kTs[:], 0.0), writes=["kTs"])
                    P.op("pool", lambda e: e.memset(Vs[:], 0.0), writes=["Vs"])
                it = 0
                nq = 0
                for si, (kind, qi, ntok) in enumerate(stiles):
                    if kind == "s" and ("nosample" in cfg.dbg or cfg.split):
                        continue
                    hb = si % 2
                    HT = "hT%d" % hb
                    nsub = (ntok + 127) // 128
                    for sub in range(nsub):
                        tl = tls[it % NBUF]
                        it += 1
                        b = tl["b"]
                        X = "xt%d" % b
                        np_ = min(128, ntok)
                        if kind == "p":
                            t = qi * 4 + sub
                            P.dma("sp", tl["x"][0:np_, :], xp[t * 128:(t + 1) * 128, :], writes=[X])
                            groups = [(0, 0, 128)]
                        else:
                            P.dma("sp", tl["x"][0:np_, :], xs, writes=[X])
                            groups = sgroups
                        if cfg.kstop < 2:
                            continue
                        norm_to_hT(tl, X, np_, 0, groups,
                                   lambda k, c0, c1: hT[hb][:, k, sub * 128 + c0:sub * 128 + c1], HT, pstr)
                        if cfg.kstop < 4:
                            continue
                        kb_ = (it - 1) % 2
                        KV, VB = "kvo%d" % kb_, "vbf%d" % kb_
                        for n in range(4):
                            po, PO = pkv[n], "pkv%d" % n
                            for k in range(8):
                                P.op("pe", lambda e: e.matmul(
                                    po[0:np_, :], lhsT=hT[hb][:, k, sub * 128:sub * 128 + np_],
                                    rhs=wqkv[:, k, D + n * 512:D + (n + 1) * 512],
                                    start=(k == 0), stop=(k == 7)),
                                    reads=[HT, "wqkv"], writes=[PO])
                            P.op("act", lambda e: e.copy(out=kvo[kb_][0:np_, n * 512:(n + 1) * 512], in_=po[0:np_, :]),
                                 reads=[PO], writes=[KV])
                            if n >= 2 and "novbf" not in cfg.dbg:
                                if "vbfpsum" in cfg.dbg:
                                    P.op("dve", lambda e: e.tensor_copy(
                                        out=vbf[kb_][0:np_, (n - 2) * 512:(n - 1) * 512], in_=po[0:np_, :]),
                                        reads=[PO, KV], writes=[VB])
                                else:
                                    P.op("pool", lambda e: e.tensor_copy(
                                        out=vbf[kb_][0:np_, (n - 2) * 512:(n - 1) * 512],
                                        in_=kvo[kb_][0:np_, n * 512:(n + 1) * 512]),
                                        reads=[KV], writes=[VB])
                        if cfg.kstop < 5:
                            continue
                        if kind == "p":
                            out_deps.append(P.dma("sp", kp[t * 128:(t + 1) * 128, :], kvo[kb_][:, 0:D],
                                                  reads=[KV], semkey="ok%d" % kb_))
                            out_deps.append(P.dma("sp", vp[t * 128:(t + 1) * 128, :], kvo[kb_][:, D:2 * D],
                                                  reads=[KV], semkey="ov%d" % kb_))
                            if "novd" not in cfg.dbg:
                                P.dma("sp", V_d[t * 128:(t + 1) * 128, :], vbf[kb_][:], reads=[VB], semkey="ovb%d" % kb_)
                        else:
                            out_deps.append(P.dma("sp", ks, kvo[kb_][0:NST, 0:D], reads=[KV], semkey="ok%d" % kb_))
                            out_deps.append(P.dma("sp", vs, kvo[kb_][0:NST, D:2 * D], reads=[KV], semkey="ov%d" % kb_))
                            for s_ in range(NS):
                                P.dma("sp", Vs[0:TS, s_, :], vbf[kb_][s_ * TS:(s_ + 1) * TS, :],
                                      reads=[VB], writes=["Vs"], semkey="vs_mv")
                    for fc in range(0 if "noqk" in cfg.dbg else 16):
                        pq, PQ = pqk[fc % 2], "pqk%d" % (fc % 2)
                        for k in range(8):
                            P.op("pe", lambda e: e.matmul(
                                pq[:, 0:ntok], lhsT=wqkv[:, k, fc * 128:(fc + 1) * 128],
                                rhs=hT[hb][:, k, 0:ntok], start=(k == 0), stop=(k == 7)),
                                reads=[HT, "wqkv"], writes=[PQ])
                        if kind == "p":
                            sb_ = nq % 4
                            nq += 1
                            QS = "qkst%d" % sb_
                            if fc < 8:
                                P.op("act", lambda e: e.activation(out=qk_st[sb_][:], in_=pq[:], func=AF.Copy, scale=0.125),
                                     reads=[PQ], writes=[QS])
                                P.dma("sp", qT_d[:, fc, qi * 512:(qi + 1) * 512], qk_st[sb_][:], reads=[QS], semkey=QS + "o")
                            else:
                                P.op("dve", lambda e: e.tensor_copy(out=qk_st[sb_][:], in_=pq[:]),
                                     reads=[PQ], writes=[QS])
                                P.dma("sp", kT_d[:, fc - 8, qi * 512:(qi + 1) * 512], qk_st[sb_][:], reads=[QS], semkey=QS + "o")
                        else:
                            if fc < 8:
                                P.op("act", lambda e: e.activation(out=qTs[:, fc, :], in_=pq[:, 0:NST], func=AF.Copy, scale=0.125),
                                     reads=[PQ], writes=["qTs"])
                            else:
                                P.op("dve", lambda e: e.tensor_copy(
                                    out=kTs[:, fc - 8, :, 0:TS],
                                    in_=pq[:, 0:NST].rearrange("p (s t) -> p s t", t=TS)),
                                    reads=[PQ], writes=["kTs"])
                P.barrier()

        def attn_pipeline(tmp, nblocks, emit_z, emit_av, mask_of, bias_ap_of, ncols=512):
            raise NotImplementedError

        if cfg.upto >= 3:
            with ExitStack() as ph:
                kT_all = sbt(ph, "kT_all", [128, 4, SEQ], BF16)
                V_all = sbt(ph, "V_all", [128, NT, 512], BF16)
                qT_t = [sbt(ph, "qT_t%d" % i, [128, 4, 512], BF16) for i in range(2)]
                masks = sbt(ph, "masks", [128, 4, 512], BF16)
                ones_m = sbt(ph, "ones_m", [128, 512], BF16)
                e_t = [sbt(ph, "e_t%d" % i, [128, 512], F32) for i in range(2)]
                sp_t = [[sbt(ph, "sp_t%d_%d" % (i, j), [128, 512], BF16) for j in range(2)] for i in range(2)]
                a_t = [[sbt(ph, "a_t%d_%d" % (i, j), [128, 512], BF16) for j in range(2)] for i in range(2)]
                SP32 = [sbt(ph, "SP32_%d" % i, [128, 512], F32) for i in range(2)]
                SPbf = [sbt(ph, "SPbf_%d" % i, [128, 512], BF16) for i in range(2)]
                o_st = [sbt(ph, "o_st%d" % i, [128, 512], BF16) for i in range(2)]
                Pz = [pst(ph, "Pz%d" % i, [128, 512], F32) for i in range(2)]
                Pa = [pst(ph, "Pa%d" % i, [128, 512], F32) for i in range(2)]
                Po = [pst(ph, "Po%d" % i, [128, 512], F32) for i in range(2)]

                P.op("pool", lambda e: e.memset(ones_m[:], 1.0), writes=["ones_m"])
                for j in range(4):
                    P.op("pool", lambda e: e.affine_select(
                        out=masks[:, j, :], in_=ones_m[:], pattern=[[1, 512]], compare_op=ALU.is_gt,
                        fill=0.0, base=-128 * j, channel_multiplier=-1), reads=["ones_m"], writes=["masks"])

                for gh, qi in [(a_, b_) for a_ in range(2) for b_ in range(NQ)]:
                    if qi == 0:
                        for gl in range(4):
                            P.dma("sp", kT_all[:, gl, :], kT_d[:, gh * 4 + gl, :], writes=["kT_all"])
                        for t in range(NT):
                            P.dma("sp", V_all[:, t, :], V_d[t * 128:(t + 1) * 128, gh * 512:(gh + 1) * 512],
                                  writes=["V_all"])
                    qb = (gh * NQ + qi) % 2
                    QT = "qT_t%d" % qb
                    for gl in range(4):
                        P.dma("sp", qT_t[qb][:, gl, :], qT_d[:, gh * 4 + gl, qi * 512:(qi + 1) * 512], writes=[QT])
                    nblk = 4 * qi + 4
                    for gl in range(4):
                        g = gh * 4 + gl
                        po, PO = Po[g % 2], "Po%d" % (g % 2)
                        for h2 in range(2):
                            P.op("pool", lambda e: e.memset(SP32[h2][:], 0.0), writes=["SP32_%d" % h2])
                        for bi, kb in enumerate(range(nblk - 1, -1, -1)):
                            j = kb - 4 * qi
                            first, last = (bi == 0), (kb == 0)
                            pr = slice(None)
                            for h2 in range(2):
                                pr = slice(h2 * 64, (h2 + 1) * 64)
                                P.op("pe", lambda e: e.matmul(
                                    Pz[h2][:], lhsT=kT_all[pr, gl, kb * 128:(kb + 1) * 128],
                                    rhs=qT_t[qb][pr, gl, :], start=True, stop=True),
                                    reads=["kT_all", QT], writes=["Pz%d" % h2])
                            for h2 in range(2):
                                h = 2 * g + h2
                                S = "sp_t%d_%d" % (h2, bi % 2)
                                spt = sp_t[h2][bi % 2]
                                P.op("act", lambda e: e.activation(out=e_t[h2][:], in_=Pz[h2][:], func=AF.Exp,
                                                                   bias=biasb[:, h:h + 1]),
                                     reads=["Pz%d" % h2, "biasb"], writes=["e_t%d" % h2])
                                P.op("act", lambda e: e.activation(out=spt[:], in_=e_t[h2][:], func=AF.Ln, bias=1.0),
                                     reads=["e_t%d" % h2], writes=[S])
                                if j >= 0:
                                    P.op("dve", lambda e: e.tensor_tensor(out=spt[:], in0=spt[:], in1=masks[:, j, :],
                                                                          op=ALU.mult),
                                         reads=[S, "masks"], writes=[S])
                            for h2 in range(2):
                                pr = slice(h2 * 64, (h2 + 1) * 64)
                                S = "sp_t%d_%d" % (h2, bi % 2)
                                spt = sp_t[h2][bi % 2]
                                PA = "Pa%d" % h2
                                P.op("pe", lambda e: e.matmul(
                                    Pa[h2][:], lhsT=kT_all[pr, gl, kb * 128:(kb + 1) * 128],
                                    rhs=qT_t[qb][pr, gl, :], start=True, stop=False),
                                    reads=["kT_all", QT], writes=[PA])
                                P.op("pe", lambda e: e.matmul(
                                    Pa[h2][:], lhsT=negU[:], rhs=spt[:], start=False, stop=first),
                                    reads=["negU", S], writes=[PA])
                                if not first:
                                    P.op("pe", lambda e: e.matmul(
                                        Pa[h2][:], lhsT=negO[:], rhs=SPbf[h2][:], start=False, stop=True),
                                        reads=["negO", "SPbf_%d" % h2], writes=[PA])
                            for h2 in range(2):
                                h = 2 * g + h2
                                A = "a_t%d_%d" % (h2, bi % 2)
                                at = a_t[h2][bi % 2]
                                P.op("act", lambda e: e.activation(out=at[:], in_=Pa[h2][:], func=AF.Exp,
                                                                   bias=biasb[:, h:h + 1]),
                                     reads=["Pa%d" % h2, "biasb"], writes=[A])
                                if j >= 0:
                                    P.op("dve", lambda e: e.tensor_tensor(out=at[:], in0=at[:], in1=masks[:, j, :],
                                                                          op=ALU.mult),
                                         reads=[A, "masks"], writes=[A])
                            for h2 in range(2):
                                h = 2 * g + h2
                                at = a_t[h2][bi % 2]
                                P.op("pe", lambda e: e.matmul(
                                    po[h2 * 64:(h2 + 1) * 64, :], lhsT=V_all[:, kb, (h - 8 * gh) * 64:(h - 8 * gh + 1) * 64],
                                    rhs=at[:], start=first, stop=last),
                                    reads=["V_all", "a_t%d_%d" % (h2, bi % 2)], writes=[PO])
                            if not last:
                                for h2 in range(2):
                                    S = "sp_t%d_%d" % (h2, bi % 2)
                                    P.op("pool", lambda e: e.tensor_tensor(out=SP32[h2][:], in0=SP32[h2][:],
                                                                           in1=sp_t[h2][bi % 2][:], op=ALU.add),
                                         reads=["SP32_%d" % h2, S], writes=["SP32_%d" % h2])
                                    P.op("pool", lambda e: e.tensor_copy(out=SPbf[h2][:], in_=SP32[h2][:]),
                                         reads=["SP32_%d" % h2], writes=["SPbf_%d" % h2])
                        OS = "o_st%d" % (g % 2)
                        P.op("dve", lambda e: e.tensor_copy(out=o_st[g % 2][:], in_=po[:]), reads=[PO], writes=[OS])
                        P.dma("sp", oT_d[:, g, qi * 512:(qi + 1) * 512], o_st[g % 2][:], reads=[OS], semkey=OS + "o")
                P.barrier()

        if cfg.split:
            P.dma("pool", oTs[:], oTs_in, writes=["oTs"])
        if cfg.upto >= 4 and not cfg.split:
            with ExitStack() as ph:
                NCOL = NS * 16 * TS
                pt_i = sbt(ph, "pt_i", [128, NS * NPG], I32)
                pt_f = sbt(ph, "pt_f", [128, NS * NPG], F32)
                iota_f = sbt(ph, "iota_f", [128, 1], F32)
                iota_i = sbt(ph, "iota_i", [128, 1], I32)
                idx = sbt(ph, "idx", [128, NS * NPG], I32)
                NG = 2
                Kpg = [[sbt(ph, "Kpg%d_%d" % (i, s), [128, D], BF16) for s in range(NS)] for i in range(NG)]
                Vpg = [[sbt(ph, "Vpg%d_%d" % (i, s), [128, D], BF16) for s in range(NS)] for i in range(NG)]
                kTp = [sbt(ph, "kTp%d" % i, [128, NS, 8, 128], BF16) for i in range(NG)]
                maskn = sbt(ph, "maskn", [128, NS * 16, TS], BF16)
                ones_m = sbt(ph, "ones_m2", [128, NCOL], BF16)
                bias_bc = sbt(ph, "bias_bc", [128, NS, 16, TS], F32)
                zsb = sbt(ph, "zsb", [128, NCOL], F32)
                e_t = sbt(ph, "es_t", [128, NCOL], F32)
                sp_t = [sbt(ph, "sps_t%d" % i, [128, NCOL], BF16) for i in range(2)]
                t1 = sbt(ph, "t1s", [128, NCOL], F32)
                a_t = [sbt(ph, "as_t%d" % i, [128, NCOL], BF16) for i in range(2)]
                SP32 = sbt(ph, "SPs32", [128, NCOL], F32)
                SPbf = sbt(ph, "SPsbf", [128, NCOL], BF16)
                zero_l = sbt(ph, "zero_l", [128, 128], BF16)
                Pz = [pst(ph, "Pzs%d" % i, [128, NCOL], F32) for i in range(2)]
                Pa = [pst(ph, "Pas%d" % i, [128, NCOL], F32) for i in range(2)]
                Po = pst(ph, "Pos", [128, NS * 8 * TS], F32)
                Ptr = [pst(ph, "Ptrs%d" % i, [128, 8, 128], BF16) for i in range(2)]

                P.dma("sp", pt_i[:], ptab.partition_broadcast(128), writes=["pt_i"])
                P.op("dve", lambda e: e.tensor_copy(out=pt_f[:], in_=pt_i[:]), reads=["pt_i"], writes=["pt_f"])
                P.op("pool", lambda e: e.iota(iota_i[:], pattern=[[0, 1]], base=0, channel_multiplier=1),
                     writes=["iota_i"])
                P.op("dve", lambda e: e.tensor_copy(out=iota_f[:], in_=iota_i[:]), reads=["iota_i"], writes=["iota_f"])
                P.op("dve", lambda e: e.tensor_scalar(out=pt_f[:], in0=pt_f[:], scalar1=128.0, scalar2=iota_f[:, 0:1],
                                                      op0=ALU.mult, op1=ALU.add),
                     reads=["pt_f", "iota_f"], writes=["pt_f"])
                P.op("dve", lambda e: e.tensor_copy(out=idx[:], in_=pt_f[:]), reads=["pt_f"], writes=["idx"])
                P.op("pool", lambda e: e.memset(ones_m[:], 1.0), writes=["ones_m2"])
                P.op("pool", lambda e: e.memset(zero_l[:], 0.0), writes=["zero_l"])
                P.op("pool", lambda e: e.memset(SP32[:], 0.0), writes=["SPs32"])
                P.op("pool", lambda e: e.affine_select(
                    out=maskn[:], in_=ones_m[:].rearrange("p (c t) -> p c t", t=TS),
                    pattern=[[0, NS * 16], [1, TS]], compare_op=ALU.is_gt,
                    fill=0.0, base=0, channel_multiplier=-1), reads=["ones_m2"], writes=["maskn"])
                P.op("dve", lambda e: e.tensor_copy(
                    out=bias_bc[:, 0, :, :], in_=biasb[:].unsqueeze(2).to_broadcast([128, 16, TS])),
                    reads=["biasb"], writes=["bias_bc"])
                for s_ in range(1, NS):
                    P.op("dve", lambda e: e.tensor_copy(out=bias_bc[:, s_, :, :], in_=bias_bc[:, 0, :, :]),
                         reads=["bias_bc"], writes=["bias_bc"])
                P.op("pe", lambda e: e.matmul(Po[:], lhsT=zero_l[:], rhs=ones_m[:, 0:NS * 8 * TS], start=True, stop=False),
                     reads=["zero_l", "ones_m2"], writes=["Pos"])

                def col(s_, g, h2):
                    return ((s_ * 8 + g) * 2 + h2) * TS

                blocks = ["new"] + list(range(NPG - 1, -1, -1))
                for bi, blk in enumerate(blocks):
                    gb = bi % NG
                    first, last = (bi == 0), (bi == len(blocks) - 1)
                    if blk == "new":
                        def k_lhsT(s_, g, h2):
                            return kTs[h2 * 64:(h2 + 1) * 64, g, s_, :]

                        def v_lhsT(s_, h):
                            return Vs[:, s_, h * 64:(h + 1) * 64]
                        KK, VK = ["kTs"], ["Vs"]
                    else:
                        KT = "kTp%d" % gb
                        for s_ in range(NS):
                            c_ = s_ * NPG + blk
                            P.dma("pool", Kpg[gb][s_][:], cache_k, writes=["Kpg%d_%d" % (gb, s_)],
                                  indirect=bass.IndirectOffsetOnAxis(ap=idx[:, c_:c_ + 1], axis=0), reads=["idx"])
                            P.dma("pool", Vpg[gb][s_][:], cache_v, writes=["Vpg%d_%d" % (gb, s_)],
                                  indirect=bass.IndirectOffsetOnAxis(ap=idx[:, c_:c_ + 1], axis=0), reads=["idx"])
                        for s_ in range(NS):
                            ptb, PTB = Ptr[s_ % 2], "Ptrs%d" % (s_ % 2)
                            for g in range(8):
                                P.op("pe", lambda e: e.transpose(out=ptb[:, g, :], in_=Kpg[gb][s_][:, g * 128:(g + 1) * 128],
                                                                 identity=identb[:]),
                                     reads=["Kpg%d_%d" % (gb, s_), "identb"], writes=[PTB])
                            if s_ % 2 == 0:
                                P.op("dve", lambda e: e.tensor_copy(out=kTp[gb][:, s_, :, :], in_=ptb[:]),
                                     reads=[PTB], writes=[KT])
                            else:
                                P.op("act", lambda e: e.copy(out=kTp[gb][:, s_, :, :], in_=ptb[:]),
                                     reads=[PTB], writes=[KT])

                        def k_lhsT(s_, g, h2):
                            return kTp[gb][h2 * 64:(h2 + 1) * 64, s_, g, :]

                        def v_lhsT(s_, h):
                            return Vpg[gb][s_][:, h * 64:(h + 1) * 64]
                        KK = [KT]
                        VK = ["Vpg%d_%d" % (gb, s_) for s_ in range(NS)]
                    pz, PZ = Pz[bi % 2], "Pzs%d" % (bi % 2)
                    pa, PA = Pa[bi % 2], "Pas%d" % (bi % 2)
                    for s_ in range(NS):
                        for g in range(8):
                            for h2 in range(2):
                                c0 = col(s_, g, h2)
                                P.op("pe", lambda e: e.matmul(
                                    pz[:, c0:c0 + TS], lhsT=k_lhsT(s_, g, h2),
                                    rhs=qTs[h2 * 64:(h2 + 1) * 64, g, s_ * TS:(s_ + 1) * TS],
                                    start=True, stop=True, skip_group_check=True),
                                    reads=KK + ["qTs"], writes=[PZ])
                    spt, S = sp_t[bi % 2], "sps_t%d" % (bi % 2)
                    at, A = a_t[bi % 2], "as_t%d" % (bi % 2)
                    P.op("dve", lambda e: e.tensor_tensor(out=zsb[:], in0=pz[:],
                                                          in1=bias_bc[:].rearrange("p s h t -> p (s h t)"), op=ALU.add),
                         reads=[PZ, "bias_bc"], writes=["zsb"])
                    P.op("act", lambda e: e.activation(out=e_t[:], in_=zsb[:], func=AF.Exp),
                         reads=["zsb"], writes=["es_t"])
                    P.op("act", lambda e: e.activation(out=spt[:], in_=e_t[:], func=AF.Ln, bias=1.0),
                         reads=["es_t"], writes=[S])
                    if blk == "new":
                        P.op("dve", lambda e: e.tensor_tensor(out=spt[:], in0=spt[:],
                                                              in1=maskn[:].rearrange("p c t -> p (c t)"), op=ALU.mult),
                             reads=[S, "maskn"], writes=[S])
                    P.op("pe", lambda e: e.matmul(pa[:], lhsT=negU[:], rhs=spt[:], start=True, stop=first),
                         reads=["negU", S], writes=[PA])
                    if not first:
                        P.op("pe", lambda e: e.matmul(pa[:], lhsT=negO[:], rhs=SPbf[:], start=False, stop=True),
                             reads=["negO", "SPsbf"], writes=[PA])
                    P.op("dve", lambda e: e.tensor_tensor(out=t1[:], in0=pa[:], in1=zsb[:], op=ALU.add),
                         reads=[PA, "zsb"], writes=["t1s"])
                    P.op("act", lambda e: e.activation(out=at[:], in_=t1[:], func=AF.Exp),
                         reads=["t1s"], writes=[A])
                    if blk == "new":
                        P.op("dve", lambda e: e.tensor_tensor(out=at[:], in0=at[:],
                                                              in1=maskn[:].rearrange("p c t -> p (c t)"), op=ALU.mult),
                             reads=[A, "maskn"], writes=[A])
                    for s_ in range(NS):
                        for g in range(8):
                            for h2 in range(2):
                                c0 = col(s_, g, h2)
                                oc = (s_ * 8 + g) * TS
                                P.op("pe", lambda e: e.matmul(
                                    Po[h2 * 64:(h2 + 1) * 64, oc:oc + TS], lhsT=v_lhsT(s_, 2 * g + h2),
                                    rhs=at[:, c0:c0 + TS], start=False, stop=False, skip_group_check=True),
                                    reads=VK + [A], writes=["Pos"])
                    if not last:
                        P.op("pool", lambda e: e.tensor_tensor(out=SP32[:], in0=SP32[:], in1=spt[:], op=ALU.add),
                             reads=["SPs32", S], writes=["SPs32"])
                        P.op("pool", lambda e: e.tensor_copy(out=SPbf[:], in_=SP32[:]),
                             reads=["SPs32"], writes=["SPsbf"])
                P.op("pe", lambda e: e.matmul(Po[:], lhsT=zero_l[:], rhs=ones_m[:, 0:NS * 8 * TS], start=False, stop=True),
                     reads=["zero_l", "ones_m2"], writes=["Pos"])
                P.op("dve", lambda e: e.tensor_copy(
                    out=oTs[:].rearrange("p g (s t) -> p s g t", t=TS),
                    in_=Po[:].rearrange("p (s g t) -> p s g t", g=8, t=TS)),
                    reads=["Pos"], writes=["oTs"])
                P.barrier()

        st_att.close()
        def ffn_phase(li, mix_idx, ffn_idx, Wo_src, get_oT, x_src_fn, x_dst_fn, final_out):
            with ExitStack() as ph:
                wo = sbt(ph, "wo", [128, 8, D], BF16)
                NBUF = 2
                tls = [dict(b=i, x=sbt(ph, "xt%d" % i, [128, D]), junk=sbt(ph, "junk%d" % i, [128, D], F32 if "junk32" in cfg.dbg else BF16),
                            ssq=sbt(ph, "ssq%d" % i, [128, 1]), rstd=sbt(ph, "rstd%d" % i, [128, 1]),
                            xn=sbt(ph, "xn%d" % i, [128, D])) for i in range(NBUF)]
                x1 = [sbt(ph, "x1_%d" % i, [128, D]) for i in range(4)]
                oT = [sbt(ph, "oT0", [128, 8, 512], BF16)] * 2
                h2T = [sbt(ph, "h2T0", [128, 8, 512], BF16)] * 2
                uT = sbt(ph, "uT", [128, 32, 512], BF16)
                rl = [sbt(ph, "rl%d" % i, [128, 512], BF16) for i in range(2)]
                NWB = 2
                wup = [sbt(ph, "wup%d" % i, [128, 8, 1024], BF16) for i in range(NWB)]
                wdn = [sbt(ph, "wdn%d" % i, [128, 8, 512], BF16) for i in range(NWB)]
                jts = [sbt(ph, "jt%d" % i, [128, 512], F32) for i in range(2)]
                ptr_ = [pst(ph, "ptr%d" % i, [128, 512], F32) for i in range(2)]
                pstr = [(ptr_[i], "ptr%d" % i) for i in range(2)]
                pmm = [pst(ph, "pmm%d" % i, [128, 512], F32) for i in range(4)]
                for k in range(8):
                    P.dma("sp", wo[:, k, :], Wo_src[k * 128:(k + 1) * 128, :], writes=["wo"])
                wcount = [0, 0]
                it = 0
                for si, (kind, qi, ntok) in enumerate(stiles):
                    hb = si % 2
                    OT, H2 = "oT0", "h2T0"
                    get_oT(kind, qi, ntok, oT[hb], OT)
                    nsub = (ntok + 127) // 128
                    np_ = min(128, ntok)
                    for sub in range(nsub):
                        tl = tls[it % NBUF]
                        it += 1
                        b = tl["b"]
                        X = "xt%d" % b
                        X1 = "x1_%d" % sub
                        x_src_fn(kind, qi, sub, tl["x"][0:np_, :], X)
                        for n in range(2):
                            pm, PM = pmm[n], "pmm%d" % n
                            for k in range(8):
                                P.op("pe", lambda e: e.matmul(
                                    pm[0:np_, :], lhsT=oT[hb][:, k, sub * 128:sub * 128 + np_],
                                    rhs=wo[:, k, n * 512:(n + 1) * 512], start=(k == 0), stop=(k == 7)),
                                    reads=[OT, "wo"], writes=[PM])
                            if kind == "p":
                                grp = [(0, 0, 128)]
                            else:
                                grp = sgroups
                            gmix = (gate_bc[mix_idx][0:np_, 0, n * 512:(n + 1) * 512] if kind == "p"
                                    else gsm[mix_idx][0:np_, n * 512:(n + 1) * 512])
                            P.op("dve", lambda e: e.tensor_tensor(
                                out=x1[sub][0:np_, n * 512:(n + 1) * 512], in0=pm[0:np_, :],
                                in1=gmix, op=ALU.mult),
                                reads=[PM, "gate%d" % (mix_idx % 2), "gsm%d" % (mix_idx % 2)], writes=[X1])
                            P.op("pool", lambda e: e.tensor_tensor(
                                out=x1[sub][0:np_, n * 512:(n + 1) * 512], in0=x1[sub][0:np_, n * 512:(n + 1) * 512],
                                in1=tl["x"][0:np_, n * 512:(n + 1) * 512], op=ALU.add),
                                reads=[X1, X], writes=[X1])
                        tl2 = dict(tl)
                        tl2["x"] = x1[sub]
                        norm_to_hT(tl2, X1, np_, ffn_idx, grp,
                                   lambda k, c0, c1: h2T[hb][:, k, sub * 128 + c0:sub * 128 + c1], H2, pstr)
                    for fg in range(4):
                        wb = wcount[0] % NWB
                        wcount[0] += 1
                        WU = "wup%d" % wb
                        for k in range(8):
                            P.dma("sp", wup[wb][:, k, :], Wup_b[li][k * 128:(k + 1) * 128, fg * 1024:(fg + 1) * 1024],
                                  writes=[WU])
                        for fl in range(8):
                            fc = fg * 8 + fl
                            pm, PM = pmm[fc % 4], "pmm%d" % (fc % 4)
                            for k in range(8):
                                P.op("pe", lambda e: e.matmul(
                                    pm[:, 0:ntok], lhsT=wup[wb][:, k, fl * 128:(fl + 1) * 128],
                                    rhs=h2T[hb][:, k, 0:ntok], start=(k == 0), stop=(k == 7)),
                                    reads=[WU, H2], writes=[PM])
                            R = "rl%d" % (fc % 2)
                            P.op("act", lambda e: e.activation(out=rl[fc % 2][:, 0:ntok], in_=pm[:, 0:ntok], func=AF.Relu),
                                 reads=[PM], writes=[R])
                            P.op("pool", lambda e: e.tensor_tensor(out=uT[:, fc, 0:ntok], in0=rl[fc % 2][:, 0:ntok],
                                                                   in1=rl[fc % 2][:, 0:ntok], op=ALU.mult),
                                 reads=[R], writes=["uT"])
                    for n in range(2):
                        for sub in range(nsub):
                            pass
                    for sub in range(nsub):
                        pass
                    for n in range(2):
                        accs = [(pmm[s4], "pmm%d" % s4) for s4 in range(nsub)]
                        for fg in range(4):
                            wb = wcount[1] % NWB
                            wcount[1] += 1
                            WD = "wdn%d" % wb
                            for fl in range(8):
                                fc = fg * 8 + fl
                                P.dma("sp", wdn[wb][:, fl, 0:512],
                                      Wdn_b[li][fc * 128:(fc + 1) * 128, n * 512:(n + 1) * 512], writes=[WD])
                            for sub in range(nsub):
                                pm, PM = accs[sub]
                                for fl in range(8):
                                    fc = fg * 8 + fl
                                    P.op("pe", lambda e: e.matmul(
                                        pm[0:np_, :], lhsT=uT[:, fc, sub * 128:sub * 128 + np_],
                                        rhs=wdn[wb][:, fl, 0:512], start=(fc == 0), stop=(fc == 31)),
                                        reads=["uT", WD], writes=[PM])
                        for sub in range(nsub):
                            pm, PM = accs[sub]
                            X1 = "x1_%d" % sub
                            J = "jt%d" % (sub % 2)
                            jt = jts[sub % 2]
                            gffn = (gate_bc[ffn_idx][0:np_, 0, n * 512:(n + 1) * 512] if kind == "p"
                                    else gsm[ffn_idx][0:np_, n * 512:(n + 1) * 512])
                            P.op("dve", lambda e: e.tensor_tensor(
                                out=jt[0:np_, 0:512], in0=pm[0:np_, :], in1=gffn, op=ALU.mult),
                                reads=[PM, "gate%d" % (ffn_idx % 2), "gsm%d" % (ffn_idx % 2)], writes=[J])
                            P.op("pool", lambda e: e.tensor_tensor(
                                out=x1[sub][0:np_, n * 512:(n + 1) * 512], in0=x1[sub][0:np_, n * 512:(n + 1) * 512],
                                in1=jt[0:np_, 0:512], op=ALU.add),
                                reads=[X1, J], writes=[X1])
                    for sub in range(nsub):
                        x_dst_fn(kind, qi, sub, x1[sub][0:np_, :], "x1_%d" % sub)
                P.barrier()

        def get_oT0(kind, qi, ntok, oT_tile, OT):
            if kind == "p":
                for g in range(8):
                    P.dma("sp", oT_tile[:, g, :], oT_d[:, g, qi * 512:(qi + 1) * 512], writes=[OT])
            else:
                P.op("dve", lambda e: e.tensor_copy(out=oT_tile[:, :, 0:NST], in_=oTs[:]), reads=["oTs"], writes=[OT])

        def x_src0(kind, qi, sub, dst, X):
            if kind == "p":
                t = qi * 4 + sub
                P.dma("sp", dst, xp[t * 128:(t + 1) * 128, :], writes=[X])
            else:
                P.dma("sp", dst, xs, writes=[X])

        def x_dst0(kind, qi, sub, src, X1):
            if kind == "p":
                t = qi * 4 + sub
                out_deps.append(P.dma("sp", xl0p[t * 128:(t + 1) * 128, :], src, reads=[X1], semkey="xo%d" % sub))
            else:
                out_deps.append(P.dma("sp", xl0s, src, reads=[X1], semkey="xo%d" % sub))

        if cfg.upto >= 5:
            ffn_phase(0, 0, 1, Wo_b, get_oT0, x_src0, x_dst0, False)

        if cfg.upto >= 6:
            compute_ada(2, Wam_b[1], ada_b_mix[1], norm_mix[1])
            compute_ada(3, Waf_b[1], ada_b_ffn[1], norm_ffn[1])
        if cfg.upto >= 6:
            with ExitStack() as ph:
                TT = 256
                CL = 64
                GN_EPS = 1e-5 * 64
                dtiles = [("p", q, TT) for q in range(SEQ // TT)] + [("s", 0, NST)]
                Wst = sbt(ph, "Wst", [128, 8, D], BF16)
                Wr = Wk = Wv = Wst

                def load_W(src):
                    for k in range(8):
                        P.dma("sp", Wst[:, k, :], src[k * 128:(k + 1) * 128, :], writes=["Wst"])
                w1b = sbt(ph, "w1b", [128, 8, 64], BF16)
                a1b = sbt(ph, "a1b", [128, 8, 64], BF16)
                g1b = sbt(ph, "g1b", [128, 8, 160], BF16)
                w2b = sbt(ph, "w2b", [64, D], BF16)
                a2b = sbt(ph, "a2b", [64, D], BF16)
                g2b0 = sbt(ph, "g2b0", [128, D], BF16)
                g2b1 = sbt(ph, "g2b1", [32, D], BF16)
                muT = sbt(ph, "muT", [128, 6, 8], F32)
                w0n = sbt(ph, "w0n", [128, 8], F32)
                a0T = sbt(ph, "a0T", [128, 8], F32)
                kkT_ = sbt(ph, "kkT_", [128, 8], F32)
                kaT_ = sbt(ph, "kaT_", [128, 8], F32)
                rkT_ = sbt(ph, "rkT_", [128, 8], F32)
                lngT = sbt(ph, "lngT", [128, 8], F32)
                lnbT = sbt(ph, "lnbT", [128, 8], F32)
                sshT = sbt(ph, "sshT", [128, 8, NS], F32)
                blkb = sbt(ph, "blkb", [128, 128], BF16)
                blk32 = sbt(ph, "blk32", [128, 128], F32)
                MK = sbt(ph, "MK", [128, 128], BF16)
                MB = sbt(ph, "MB", [128, 128], BF16)
                MN = sbt(ph, "MN", [128, 64], BF16)
                I64 = sbt(ph, "I64", [128, 64], BF16)
                onesb = sbt(ph, "onesb", [128, 128], BF16)
                rmask = [sbt(ph, "rmask0", [128, TT], F32), sbt(ph, "rmask1", [128, NST], F32)]
                hTe = sbt(ph, "hTe", [128, 8, TT + 1], F32)
                lastcol = sbt(ph, "lastcol", [128, 8, 1], F32)
                prevS = sbt(ph, "prevS", [128, 8, NST], F32)
                xx = sbt(ph, "xx", [128, 8, TT], F32)
                xnb = [sbt(ph, "xnb%d" % i, [128, 8, TT], BF16) for i in range(2)]
                rT = sbt(ph, "rT", [128, 8, TT], BF16)
                kT = sbt(ph, "kT", [128, 8, TT], BF16)
                vT = sbt(ph, "vT", [128, 8, TT], BF16)
                aT = sbt(ph, "aT", [128, 8, TT], BF16)
                gT_ = sbt(ph, "gT_", [128, 8, TT], BF16)
                ld = sbt(ph, "ld", [128, 8, TT], F32)
                lw_t = sbt(ph, "lw_t", [128, TT], BF16)
                lg2_t = sbt(ph, "lg2_t", [32, TT], BF16)
                NCH = TT // CL
                Vtok = sbt(ph, "Vtok", [128, NCH, 512], BF16)
                KR = sbt(ph, "KR", [128, 8, NCH, 2, CL], BF16)
                BK = sbt(ph, "BK", [128, 8, NCH, 2, CL], BF16)
                gam = sbt(ph, "gam", [128, 8, NCH], F32)
                bon = sbt(ph, "bon", [128, 8, TT], BF16)
                Ysb = sbt(ph, "Ysb", [128, 8, NCH, CL], F32)
                H32 = sbt(ph, "H32", [128, 8, 64], F32)
                Hbf = sbt(ph, "Hbf", [128, 8, 64], BF16)
                Htmp = sbt(ph, "Htmp", [128, 8, 64], F32)
                tA = [sbt(ph, "tA%d" % i, [128, TT], F32) for i in range(6)]
                tB = [sbt(ph, "tB%d" % i, [128, TT], BF16) for i in range(3)]
                GBm = sbt(ph, "GBm", [128, 8, 128], BF16)
                GKm = sbt(ph, "GKm", [128, 8, 128], BF16)
                Xl = [sbt(ph, "Xl%d" % i, [128, 8, 64], F32) for i in range(2)]
                Yl = [sbt(ph, "Yl%d" % i, [128, 8, 64], F32) for i in range(2)]
                Pl = [sbt(ph, "Pl%d" % i, [128, 8, 64], F32) for i in range(2)]
                Wsb = sbt(ph, "Wsb", [128, 8, 64], F32)
                Usb = sbt(ph, "Usb", [128, 8, 64], BF16)
                Ktok = sbt(ph, "Ktok", [128, 8, 64], BF16)
                nBtok = sbt(ph, "nBtok", [128, 8, 64], BF16)
                stT = sbt(ph, "stT", [64, 8, 128], F32)
                NBUF = 1
                tls = [dict(b=i, x=sbt(ph, "xt%d" % i, [128, D]), junk=sbt(ph, "junk%d" % i, [128, D], BF16),
                            ssq=sbt(ph, "ssq%d" % i, [128, 1]), rstd=sbt(ph, "rstd%d" % i, [128, 1]),
                            xn=sbt(ph, "xn%d" % i, [128, D])) for i in range(NBUF)]
                ptr_ = [pst(ph, "ptr0", [128, 512], F32)] * 2
                pstr = [(ptr_[i], "ptr0") for i in range(2)]
                pm_ = [pst(ph, "pmd%d" % i, [128, 512], F32) for i in range(2)]
                Pg = pst(ph, "Pg", [128, 8, 128], F32)
                Pn = [pst(ph, "Pn%d" % i, [128, 8, 64], F32) for i in range(2)]
                Ptk = pst(ph, "Ptk", [128, 2, 8, 64], BF16)

                for k in range(8):
                    P.dma("pool", w1b[:, k, :], rw_w1[k * 128:(k + 1) * 128, :], writes=["w1b"])
                    P.dma("pool", a1b[:, k, :], rw_a1[k * 128:(k + 1) * 128, :], writes=["a1b"])
                    P.dma("pool", g1b[:, k, :], rw_g1[k * 128:(k + 1) * 128, :], writes=["g1b"])
                P.dma("pool", w2b[:], rw_w2, writes=["w2b"], max_dma_last_dim=4096)
                P.dma("pool", a2b[:], rw_a2, writes=["a2b"], max_dma_last_dim=4096)
                P.dma("pool", g2b0[:], rw_g2[0:128, :], writes=["g2b0"], max_dma_last_dim=4096)
                P.dma("pool", g2b1[:], rw_g2[128:160, :], writes=["g2b1"], max_dma_last_dim=4096)
                with nc.allow_non_contiguous_dma(reason="tiny transposed parameter loads"):
                    for n in range(6):
                        P.dma("sp", muT[:, n, :], rw_mu[n].rearrange("(k p) -> p k", p=128), writes=["muT"])
                    for (tile_, src_, key_) in [(w0n, rw_w0, "w0n"), (a0T, rw_a0, "a0T"), (kkT_, rw_k_k, "kkT_"),
                                                (kaT_, rw_k_a, "kaT_"), (rkT_, rw_r_k, "rkT_"),
                                                (lngT, rw_ln_g, "lngT"), (lnbT, rw_ln_b, "lnbT")]:
                        P.dma("sp", tile_[:], src_.rearrange("(k p) -> p k", p=128), writes=[key_])
                    for s_ in range(NS):
                        P.dma("sp", sshT[:, :, s_], state_shift[s_].rearrange("(k p) -> p k", p=128), writes=["sshT"])
                P.op("dve", lambda e: e.tensor_scalar(out=w0n[:], in0=w0n[:], scalar1=-1.0, scalar2=None, op0=ALU.mult),
                     reads=["w0n"], writes=["w0n"])
                P.op("pool", lambda e: e.memset(onesb[:], 1.0), writes=["onesb"])
                P.op("pool", lambda e: e.memset(blkb[:], 0.0), writes=["blkb"])
                for h2 in range(2):
                    pr = slice(h2 * 64, (h2 + 1) * 64)
                    P.op("pool", lambda e: e.memset(blkb[pr, h2 * 64:(h2 + 1) * 64], 1.0), reads=["blkb"], writes=["blkb"])
                    P.op("pool", lambda e: e.affine_select(
                        out=MK[pr, 0:64], in_=onesb[pr, 0:64], pattern=[[1, 64]], compare_op=ALU.is_gt,
                        fill=0.0, base=0, channel_multiplier=-1), reads=["onesb"], writes=["MK"])
                    P.op("pool", lambda e: e.affine_select(
                        out=MK[pr, 64:128], in_=onesb[pr, 0:64], pattern=[[1, 64]], compare_op=ALU.is_ge,
                        fill=0.0, base=0, channel_multiplier=-1), reads=["onesb"], writes=["MK"])
                    P.op("pool", lambda e: e.affine_select(
                        out=MN[pr, :], in_=negO[pr, 0:64], pattern=[[-1, 64]], compare_op=ALU.is_gt,
                        fill=0.0, base=0, channel_multiplier=1), reads=["negO"], writes=["MN"])
                    P.op("pool", lambda e: e.tensor_copy(out=I64[pr, :], in_=identb[pr, h2 * 64:(h2 + 1) * 64]),
                         reads=["identb"], writes=["I64"])
                P.op("pool", lambda e: e.tensor_copy(out=blk32[:], in_=blkb[:]), reads=["blkb"], writes=["blk32"])
                P.op("dve", lambda e: e.tensor_scalar(out=MB[:], in0=MK[:], scalar1=-1.0, scalar2=None, op0=ALU.mult),
                     reads=["MK"], writes=["MB"])
                P.op("pool", lambda e: e.memset(rmask[0][:], 1.0), writes=["rmask0"])
                P.op("pool", lambda e: e.memset(rmask[0][:].rearrange("p (c t) -> p c t", t=CL)[:, :, 0:1], 0.0),
                     reads=["rmask0"], writes=["rmask0"])
                P.op("pool", lambda e: e.memset(rmask[1][:], 1.0), writes=["rmask1"])
                P.op("pool", lambda e: e.memset(rmask[1][:].rearrange("p (c t) -> p c t", t=TS)[:, :, 0:1], 0.0),
                     reads=["rmask1"], writes=["rmask1"])
                P.op("pool", lambda e: e.memset(H32[:], 0.0), writes=["H32"])
                P.op("pool", lambda e: e.memset(Hbf[:], 0.0), writes=["Hbf"])

                def hd(t, g, h2):
                    return t[h2 * 64:(h2 + 1) * 64, g, :]

                def store_state(dst):
                    for g in range(8):
                        P.op("pe", lambda e: e.transpose(out=Pg[0:64, g, :], in_=H32[:, g, :], identity=ident[:]),
                             reads=["H32", "ident"], writes=["Pg"])
                    P.op("dve", lambda e: e.tensor_copy(out=stT[:], in_=Pg[0:64, :, :]), reads=["Pg"], writes=["stT"])
                    return P.dma("sp", dst.rearrange("(g h) i j -> i g h j", h=2),
                                 stT[:].rearrange("i g (h j) -> i g h j", h=2), reads=["stT"], semkey="stTo")

                def load_state(src):
                    P.dma("sp", stT[:].rearrange("i g (h j) -> i g h j", h=2),
                          src.rearrange("(g h) i j -> i g h j", h=2), writes=["stT"])
                    for g in range(8):
                        P.op("pe", lambda e: e.transpose(out=Pn[0][:, g, :], in_=stT[:, g, :], identity=ident[0:64, 0:64]),
                             reads=["stT", "ident"], writes=["Pn0"])
                    P.op("dve", lambda e: e.tensor_copy(out=H32[:], in_=Pn[0][:]), reads=["Pn0"], writes=["H32"])
                    P.op("pool", lambda e: e.tensor_copy(out=Hbf[:], in_=H32[:]), reads=["H32"], writes=["Hbf"])

                it = 0
                for di, (kind, qi, ntok) in enumerate(dtiles):
                    cl = CL if kind == "p" else TS
                    nch = ntok // cl
                    rm = rmask[0] if kind == "p" else rmask[1]
                    RM = "rmask0" if kind == "p" else "rmask1"
                    nsub = (ntok + 127) // 128
                    np_ = min(128, ntok)
                    if kind == "p":
                        if qi == 0:
                            P.op("pool", lambda e: e.memset(hTe[:, :, 0:1], 0.0), writes=["hTe"])
                        else:
                            P.op("pool", lambda e: e.tensor_copy(out=hTe[:, :, 0:1], in_=lastcol[:]),
                                 reads=["lastcol"], writes=["hTe"])
                    for sub in range(nsub):
                        tl = tls[it % NBUF]
                        it += 1
                        X = "xt%d" % tl["b"]
                        if kind == "p":
                            t = qi * (TT // 128) + sub
                            P.dma("sp", tl["x"][0:np_, :], xl0p[t * 128:(t + 1) * 128, :], writes=[X])
                            groups = [(0, 0, 128)]
                        else:
                            P.dma("sp", tl["x"][0:np_, :], xl0s, writes=[X])
                            groups = sgroups
                        norm_to_hT(tl, X, np_, 2, groups,
                                   lambda k, c0, c1: hTe[:, k, 1 + sub * 128 + c0:1 + sub * 128 + c1], "hTe", pstr)
                    hcur = hTe[:, :, 1:1 + ntok]
                    if kind == "p":
                        hprev = hTe[:, :, 0:ntok]
                        PV = "hTe"
                    else:
                        P.op("pool", lambda e: e.tensor_copy(
                            out=prevS[:].rearrange("p k (s t) -> p k s t", t=TS)[:, :, :, 1:TS],
                            in_=hTe[:, :, 1:1 + NST].rearrange("p k (s t) -> p k s t", t=TS)[:, :, :, 0:TS - 1]),
                            reads=["hTe"], writes=["prevS"])
                        P.op("pool", lambda e: e.tensor_copy(
                            out=prevS[:].rearrange("p k (s t) -> p k s t", t=TS)[:, :, :, 0],
                            in_=sshT[:]), reads=["sshT", "prevS"], writes=["prevS"])
                        hprev = prevS[:]
                        PV = "prevS"
                    P.op("dve", lambda e: e.tensor_tensor(out=xx[:, :, 0:ntok], in0=hprev, in1=hcur, op=ALU.subtract),
                         reads=["hTe", PV], writes=["xx"])
                    with nc.allow_non_contiguous_dma(reason="tiny transposed outputs"):
                        if kind == "p" and qi == SEQ // TT - 1:
                            out_deps.append(P.dma("sp", shp.rearrange("(k p) -> p k", p=128), hTe[:, :, TT],
                                                  reads=["hTe"], semkey="sho"))
                        if kind == "s":
                            for s_ in range(NS):
                                out_deps.append(P.dma("sp", shs[s_].rearrange("(k p) -> p k", p=128),
                                                      hTe[:, :, 1 + s_ * TS + TS - 1], reads=["hTe"], semkey="sho"))

                    def make_xn(n, nb):
                        XN = "xnb%d" % nb
                        for k in range(8):
                            P.op("dve", lambda e: e.scalar_tensor_tensor(
                                out=xnb[nb][:, k, 0:ntok], in0=xx[:, k, 0:ntok], scalar=muT[:, n, k:k + 1],
                                in1=hTe[:, k, 1:1 + ntok], op0=ALU.mult, op1=ALU.add),
                                reads=["xx", "muT", "hTe"], writes=[XN])
                        return xnb[nb], XN

                    def proj_fm(W, WK, xn_, XN, fc, pm, PM, m0=0, m1=128, wcols=None):
                        for k in range(8):
                            P.op("pe", lambda e: e.matmul(
                                pm[0:m1 - m0, 0:ntok],
                                lhsT=(W[:, k, fc * 128 + m0:fc * 128 + m1] if wcols is None else W[:, k, wcols[0]:wcols[1]]),
                                rhs=xn_[:, k, 0:ntok], start=(k == 0), stop=(k == 7)),
                                reads=[WK, XN], writes=[PM])

                    xn_, XN = make_xn(0, 0)
                    load_W(Wr_b)
                    for fc in range(8):
                        pm, PM = pm_[fc % 2], "pmd%d" % (fc % 2)
                        proj_fm(Wr, "Wst", xn_, XN, fc, pm, PM)
                        P.op("act", lambda e: e.copy(out=rT[:, fc, 0:ntok], in_=pm[:, 0:ntok]), reads=[PM], writes=["rT"])
                    xn_, XN = make_xn(1, 1)
                    pm, PM = pm_[0], "pmd0"
                    proj_fm(w1b, "w1b", xn_, XN, 0, pm, PM, wcols=(0, 64), m1=64)
                    P.op("act", lambda e: e.activation(out=lw_t[0:64, 0:ntok], in_=pm[0:64, 0:ntok], func=AF.Tanh),
                         reads=[PM], writes=["lw_t"])
                    for fc in range(8):
                        pm, PM = pm_[(fc + 1) % 2], "pmd%d" % ((fc + 1) % 2)
                        P.op("pe", lambda e: e.matmul(pm[:, 0:ntok], lhsT=w2b[0:64, fc * 128:(fc + 1) * 128],
                                                      rhs=lw_t[0:64, 0:ntok], start=True, stop=True),
                             reads=["w2b", "lw_t"], writes=[PM])
                        P.op("act", lambda e: e.activation(out=tA[0][:, 0:ntok], in_=pm[:, 0:ntok], func=AF.Exp,
                                                           scale=-1.0, bias=w0n[:, fc:fc + 1]),
                             reads=[PM, "w0n"], writes=["tA0"])
                        P.op("act", lambda e: e.activation(out=tA[0][:, 0:ntok], in_=tA[0][:, 0:ntok], func=AF.Ln, bias=1.0),
                             reads=["tA0"], writes=["tA0"])
                        P.op("act", lambda e: e.activation(out=tA[0][:, 0:ntok], in_=tA[0][:, 0:ntok], func=AF.Exp,
                                                           scale=-1.0, bias=-0.5),
                             reads=["tA0"], writes=["tA0"])
                        P.op("dve", lambda e: e.tensor_scalar(out=ld[:, fc, 0:ntok], in0=tA[0][:, 0:ntok], scalar1=-1.0,
                                                              scalar2=None, op0=ALU.mult),
                             reads=["tA0"], writes=["ld"])
                    xn_, XN = make_xn(2, 0)
                    load_W(Wk_b)
                    for fc in range(8):
                        pm, PM = pm_[fc % 2], "pmd%d" % (fc % 2)
                        proj_fm(Wk, "Wst", xn_, XN, fc, pm, PM)
                        P.op("act", lambda e: e.copy(out=kT[:, fc, 0:ntok], in_=pm[:, 0:ntok]), reads=[PM], writes=["kT"])
                    xn_, XN = make_xn(3, 1)
                    load_W(Wv_b)
                    for fc in range(8):
                        pm, PM = pm_[fc % 2], "pmd%d" % (fc % 2)
                        proj_fm(Wv, "Wst", xn_, XN, fc, pm, PM)
                        P.op("act", lambda e: e.copy(out=vT[:, fc, 0:ntok], in_=pm[:, 0:ntok]), reads=[PM], writes=["vT"])
                    if kind == "s":
                        P.op("pool", lambda e: e.memset(Vtok[:], 0.0), writes=["Vtok"])
                    Wv4 = Wv[:].rearrange("p k (g h d) -> p k g h d", h=2, d=64)
                    for c in range(nch):
                        pm, PM = pm_[c % 2], "pmd%d" % (c % 2)
                        for h2 in range(2):
                            for k in range(8):
                                P.op("pe", lambda e: e.matmul(
                                    pm[h2 * 64:h2 * 64 + cl, :], lhsT=xn_[:, k, c * cl:(c + 1) * cl],
                                    rhs=Wv4[:, k, :, h2, :], start=(k == 0), stop=(k == 7)),
                                    reads=["Wst", XN], writes=[PM])
                        if cl == 64:
                            P.op("dve", lambda e: e.tensor_copy(out=Vtok[:, c, :], in_=pm[:]), reads=[PM], writes=["Vtok"])
                        else:
                            for h2 in range(2):
                                P.op("dve", lambda e: e.tensor_copy(out=Vtok[h2 * 64:h2 * 64 + cl, c, :],
                                                                    in_=pm[h2 * 64:h2 * 64 + cl, :]),
                                     reads=[PM], writes=["Vtok"])
                    xn_, XN = make_xn(4, 0)
                    pm, PM = pm_[0], "pmd0"
                    proj_fm(a1b, "a1b", xn_, XN, 0, pm, PM, wcols=(0, 64), m1=64)
                    P.op("act", lambda e: e.copy(out=lw_t[0:64, 0:ntok], in_=pm[0:64, 0:ntok]), reads=[PM], writes=["lw_t"])
                    for fc in range(8):
                        pm, PM = pm_[(fc + 1) % 2], "pmd%d" % ((fc + 1) % 2)
                        P.op("pe", lambda e: e.matmul(pm[:, 0:ntok], lhsT=a2b[0:64, fc * 128:(fc + 1) * 128],
                                                      rhs=lw_t[0:64, 0:ntok], start=True, stop=True),
                             reads=["a2b", "lw_t"], writes=[PM])
                        P.op("act", lambda e: e.activation(out=aT[:, fc, 0:ntok], in_=pm[:, 0:ntok], func=AF.Sigmoid,
                                                           bias=a0T[:, fc:fc + 1]),
                             reads=[PM, "a0T"], writes=["aT"])
                    xn_, XN = make_xn(5, 1)
                    pm, PM = pm_[0], "pmd0"
                    proj_fm(g1b, "g1b", xn_, XN, 0, pm, PM, wcols=(0, 128))
                    P.op("act", lambda e: e.activation(out=lw_t[:, 0:ntok], in_=pm[:, 0:ntok], func=AF.Sigmoid),
                         reads=[PM], writes=["lw_t"])
                    pm, PM = pm_[1], "pmd1"
                    proj_fm(g1b, "g1b", xn_, XN, 0, pm, PM, wcols=(128, 160), m1=32)
                    P.op("act", lambda e: e.activation(out=lg2_t[0:32, 0:ntok], in_=pm[0:32, 0:ntok], func=AF.Sigmoid),
                         reads=[PM], writes=["lg2_t"])
                    for fc in range(8):
                        pm, PM = pm_[fc % 2], "pmd%d" % (fc % 2)
                        P.op("pe", lambda e: e.matmul(pm[:, 0:ntok], lhsT=g2b0[:, fc * 128:(fc + 1) * 128],
                                                      rhs=lw_t[:, 0:ntok], start=True, stop=False),
                             reads=["g2b0", "lw_t"], writes=[PM])
                        P.op("pe", lambda e: e.matmul(pm[:, 0:ntok], lhsT=g2b1[0:32, fc * 128:(fc + 1) * 128],
                                                      rhs=lg2_t[0:32, 0:ntok], start=False, stop=True),
                             reads=["g2b1", "lg2_t"], writes=[PM])
                        P.op("act", lambda e: e.copy(out=gT_[:, fc, 0:ntok], in_=pm[:, 0:ntok]), reads=[PM], writes=["gT_"])

                    if kind == "s":
                        P.op("pool", lambda e: e.memset(KR[:], 0.0), writes=["KR"])
                        P.op("pool", lambda e: e.memset(BK[:], 0.0), writes=["BK"])
                    nt = slice(0, ntok)
                    for g in range(8):
                        pm, PM = pm_[g % 2], "pmd%d" % (g % 2)
                        kk, kk2, rn, Linc, Lexc, E1 = tA[0], tB[0], tA[1], tA[2], tA[3], tA[4]
                        P.op("dve", lambda e: e.tensor_scalar(out=kk[:, nt], in0=kT[:, g, nt], scalar1=kkT_[:, g:g + 1],
                                                              scalar2=None, op0=ALU.mult),
                             reads=["kT", "kkT_"], writes=["tA0"])
                        P.op("act", lambda e: e.activation(out=kk2[:, nt], in_=kk[:, nt], func=AF.Square),
                             reads=["tA0"], writes=["tB0"])
                        P.op("pe", lambda e: e.matmul(pm[:, nt], lhsT=blkb[:], rhs=kk2[:, nt], start=True, stop=True),
                             reads=["blkb", "tB0"], writes=[PM])
                        P.op("act", lambda e: e.activation(out=rn[:, nt], in_=pm[:, nt], func=AF.Sqrt),
                             reads=[PM], writes=["tA1"])
                        P.op("dve", lambda e: e.tensor_scalar(out=rn[:, nt], in0=rn[:, nt], scalar1=1e-12, scalar2=None,
                                                              op0=ALU.max), reads=["tA1"], writes=["tA1"])
                        P.op("dve", lambda e: e.reciprocal(out=rn[:, nt], in_=rn[:, nt]), reads=["tA1"], writes=["tA1"])
                        P.op("dve", lambda e: e.tensor_tensor(out=kk[:, nt], in0=kk[:, nt], in1=rn[:, nt], op=ALU.mult),
                             reads=["tA0", "tA1"], writes=["tA0"])
                        kh = tA[5]
                        P.op("dve", lambda e: e.tensor_scalar(out=kh[:, nt], in0=aT[:, g, nt], scalar1=kaT_[:, g:g + 1],
                                                              scalar2=kaT_[:, g:g + 1], op0=ALU.mult, op1=ALU.subtract),
                             reads=["aT", "kaT_"], writes=["tA5"])
                        P.op("dve", lambda e: e.scalar_tensor_tensor(out=kh[:, nt], in0=kh[:, nt], scalar=1.0,
                                                                     in1=kT[:, g, nt], op0=ALU.add, op1=ALU.mult),
                             reads=["tA5", "kT"], writes=["tA5"])
                        P.op("dve", lambda e: e.tensor_tensor(out=rn[:, nt], in0=kk[:, nt], in1=aT[:, g, nt], op=ALU.mult),
                             reads=["tA0", "aT"], writes=["tA1"])
                        P.op("dve", lambda e: e.scalar_tensor_tensor(out=tB[1][:, nt], in0=rT[:, g, nt],
                                                                     scalar=rkT_[:, g:g + 1], in1=kh[:, nt],
                                                                     op0=ALU.mult, op1=ALU.mult),
                             reads=["rT", "rkT_", "tA5"], writes=["tB1"])
                        pm2, PM2 = pm_[(g + 1) % 2], "pmd%d" % ((g + 1) % 2)
                        P.op("pe", lambda e: e.matmul(pm2[:, nt], lhsT=blkb[:], rhs=tB[1][:, nt], start=True, stop=True),
                             reads=["blkb", "tB1"], writes=[PM2])
                        P.op("dve", lambda e: e.tensor_tensor(out=bon[:, g, nt], in0=pm2[:, nt], in1=vT[:, g, nt], op=ALU.mult),
                             reads=[PM2, "vT"], writes=["bon"])
                        P.op("dve", lambda e: e.tensor_tensor_scan(out=Linc[:, nt], data0=rm[:, nt], data1=ld[:, g, nt],
                                                                   initial=0.0, op0=ALU.mult, op1=ALU.add),
                             reads=[RM, "ld"], writes=["tA2"])
                        P.op("pool", lambda e: e.tensor_tensor(out=Lexc[:, nt], in0=Linc[:, nt], in1=ld[:, g, nt], op=ALU.subtract),
                             reads=["tA2", "ld"], writes=["tA3"])
                        c3 = lambda ap: ap.rearrange("p (c t) -> p c t", t=cl)
                        P.op("act", lambda e: e.activation(out=E1[:, nt], in_=Linc[:, nt], func=AF.Exp),
                             reads=["tA2"], writes=["tA4"])
                        P.op("dve", lambda e: e.tensor_tensor(out=KR[:, g, 0:nch, 1, 0:cl], in0=c3(rT[:, g, nt]),
                                                              in1=c3(E1[:, nt]), op=ALU.mult),
                             reads=["rT", "tA4"], writes=["KR"])
                        P.op("pool", lambda e: e.tensor_copy(out=gam[:, g, 0:nch], in_=c3(E1[:, nt])[:, :, cl - 1]),
                             reads=["tA4"], writes=["gam"])
                        P.op("act", lambda e: e.activation(out=E1[:, nt], in_=Lexc[:, nt], func=AF.Exp),
                             reads=["tA3", "gam"], writes=["tA4"])
                        P.op("dve", lambda e: e.tensor_tensor(out=KR[:, g, 0:nch, 0, 0:cl], in0=c3(kk[:, nt]),
                                                              in1=c3(E1[:, nt]), op=ALU.mult),
                             reads=["tA0", "tA4"], writes=["KR"])
                        P.op("act", lambda e: e.activation(out=E1[:, nt], in_=Linc[:, nt], func=AF.Exp, scale=-1.0),
                             reads=["tA2", "KR"], writes=["tA4"])
                        P.op("dve", lambda e: e.tensor_tensor(out=BK[:, g, 0:nch, 0, 0:cl], in0=c3(rn[:, nt]),
                                                              in1=c3(E1[:, nt]), op=ALU.mult),
                             reads=["tA1", "tA4"], writes=["BK"])
                        P.op("dve", lambda e: e.tensor_tensor(out=BK[:, g, 0:nch, 1, 0:cl], in0=c3(kh[:, nt]),
                                                              in1=c3(E1[:, nt]), op=ALU.mult),
                             reads=["tA5", "tA4"], writes=["BK"])

                    for c in range(nch):
                        if kind == "s":
                            load_state(state_wkv[c])
                        for which, (li_, Gm_, GM, Mm, MM) in enumerate([(0, GBm, "GBm", MB, "MB"), (1, GKm, "GKm", MK, "MK")]):
                            for g in range(8):
                                for h2 in range(2):
                                    pr = slice(h2 * 64, (h2 + 1) * 64)
                                    P.op("pe", lambda e: e.matmul(
                                        Pg[pr, g, :], lhsT=BK[pr, g, c, li_, :],
                                        rhs=KR[pr, g, c, :, :], start=True, stop=True, skip_group_check=True),
                                        reads=["BK", "KR"], writes=["Pg"])
                            P.op("dve", lambda e: e.tensor_tensor(
                                out=Gm_[:], in0=Pg[:], in1=Mm[:].unsqueeze(1).to_broadcast([128, 8, 128]), op=ALU.mult),
                                reads=["Pg", MM], writes=[GM])
                            if which == 0:
                                P.op("dve", lambda e: e.tensor_tensor(
                                    out=Xl[0][:], in0=Pg[:, :, 0:64],
                                    in1=MB[:, 0:64].unsqueeze(1).to_broadcast([128, 8, 64]), op=ALU.mult),
                                    reads=["Pg", "MB"], writes=["Xl0"])
                        for g in range(8):
                            for h2 in range(2):
                                pr = slice(h2 * 64, (h2 + 1) * 64)
                                P.op("pe", lambda e: e.matmul(
                                    Pn[0][pr, g, :], lhsT=KR[pr, g, c, 0, :], rhs=BK[pr, g, c, 0, :],
                                    start=True, stop=True, skip_group_check=True),
                                    reads=["BK", "KR"], writes=["Pn0"])
                        P.op("dve", lambda e: e.tensor_tensor(
                            out=Yl[0][:], in0=Pn[0][:], in1=MN[:].unsqueeze(1).to_broadcast([128, 8, 64]), op=ALU.mult),
                            reads=["Pn0", "MN"], writes=["Yl0"])
                        P.op("pool", lambda e: e.tensor_tensor(
                            out=Pl[0][:], in0=Xl[0][:], in1=I64[:].unsqueeze(1).to_broadcast([128, 8, 64]), op=ALU.add),
                            reads=["Xl0", "I64"], writes=["Pl0"])
                        cur = 0
                        for lev in range(5):
                            nx = 1 - cur
                            XC, YC, PC = "Xl%d" % cur, "Yl%d" % cur, "Pl%d" % cur
                            XN_, YN_, PN_ = "Xl%d" % nx, "Yl%d" % nx, "Pl%d" % nx
                            for g in range(8):
                                for h2 in range(2):
                                    pr = slice(h2 * 64, (h2 + 1) * 64)
                                    P.op("pe", lambda e: e.matmul(Pn[1][pr, g, :], lhsT=Xl[cur][pr, g, :], rhs=Yl[cur][pr, g, :],
                                                                  start=True, stop=True, skip_group_check=True),
                                         reads=[XC, YC], writes=["Pn1"])
                            P.op("act", lambda e: e.copy(out=Yl[nx][:], in_=Pn[1][:]), reads=["Pn1"], writes=[YN_])
                            if lev < 4:
                                for g in range(8):
                                    for h2 in range(2):
                                        pr = slice(h2 * 64, (h2 + 1) * 64)
                                        P.op("pe", lambda e: e.matmul(Pn[0][pr, g, :], lhsT=Yl[cur][pr, g, :], rhs=Xl[cur][pr, g, :],
                                                                      start=True, stop=True, skip_group_check=True),
                                             reads=[XC, YC], writes=["Pn0"])
                                P.op("dve", lambda e: e.tensor_copy(out=Xl[nx][:], in_=Pn[0][:]), reads=["Pn0"], writes=[XN_])
                            for g in range(8):
                                for h2 in range(2):
                                    pr = slice(h2 * 64, (h2 + 1) * 64)
                                    P.op("pe", lambda e: e.matmul(Pg[pr, g, 0:64], lhsT=Yl[nx][pr, g, :], rhs=Pl[cur][pr, g, :],
                                                                  start=True, stop=True, skip_group_check=True),
                                         reads=[YN_, PC], writes=["Pg"])
                            P.op("dve", lambda e: e.tensor_tensor(out=Pl[nx][:], in0=Pg[:, :, 0:64], in1=Pl[cur][:], op=ALU.add),
                                 reads=["Pg", PC], writes=[PN_])
                            cur = nx
                        Tt, TT_ = Pl[cur], "Pl%d" % cur
                        for g in range(8):
                            for h2 in range(2):
                                pr = slice(h2 * 64, (h2 + 1) * 64)
                                P.op("pe", lambda e: e.matmul(Pn[0][pr, g, :], lhsT=KR[pr, g, c, 0, :], rhs=Hbf[pr, g, :],
                                                              start=True, stop=False, skip_group_check=True),
                                     reads=["KR", "Hbf"], writes=["Pn0"])
                                P.op("pe", lambda e: e.matmul(Pn[0][pr, g, :], lhsT=GKm[pr, g, 0:64],
                                                              rhs=Vtok[pr, c, g * 64:(g + 1) * 64],
                                                              start=False, stop=True, skip_group_check=True),
                                     reads=["GKm", "Vtok"], writes=["Pn0"])
                        P.op("act", lambda e: e.copy(out=Wsb[:], in_=Pn[0][:]), reads=["Pn0"], writes=["Wsb"])
                        for g in range(8):
                            for h2 in range(2):
                                pr = slice(h2 * 64, (h2 + 1) * 64)
                                P.op("pe", lambda e: e.matmul(Pn[1][pr, g, :], lhsT=Tt[pr, g, :], rhs=Wsb[pr, g, :],
                                                              start=True, stop=True, skip_group_check=True),
                                     reads=[TT_, "Wsb"], writes=["Pn1"])
                        P.op("act", lambda e: e.copy(out=Usb[:], in_=Pn[1][:]), reads=["Pn1"], writes=["Usb"])
                        for g in range(8):
                            for h2 in range(2):
                                pr = slice(h2 * 64, (h2 + 1) * 64)
                                P.op("pe", lambda e: e.transpose(out=Ptk[pr, 0, g, :], in_=BK[pr, g, c, 1, :],
                                                                 identity=identb[pr, h2 * 64:(h2 + 1) * 64]),
                                     reads=["BK", "identb"], writes=["Ptk"])
                                P.op("pe", lambda e: e.transpose(out=Ptk[pr, 1, g, :], in_=BK[pr, g, c, 0, :],
                                                                 identity=identb[pr, h2 * 64:(h2 + 1) * 64]),
                                     reads=["BK", "identb"], writes=["Ptk"])
                        P.op("dve", lambda e: e.tensor_copy(out=Ktok[:], in_=Ptk[:, 0, :, :]), reads=["Ptk"], writes=["Ktok"])
                        P.op("dve", lambda e: e.tensor_scalar(out=nBtok[:], in0=Ptk[:, 1, :, :], scalar1=-1.0, scalar2=None,
                                                              op0=ALU.mult), reads=["Ptk"], writes=["nBtok"])
                        for g in range(8):
                            for h2 in range(2):
                                pr = slice(h2 * 64, (h2 + 1) * 64)
                                P.op("pe", lambda e: e.matmul(Pg[pr, g, 0:64], lhsT=Hbf[pr, g, :], rhs=KR[pr, g, c, 1, :],
                                                              start=True, stop=False, skip_group_check=True),
                                     reads=["Hbf", "KR"], writes=["Pg"])
                                P.op("pe", lambda e: e.matmul(Pg[pr, g, 0:64], lhsT=Vtok[pr, c, g * 64:(g + 1) * 64],
                                                              rhs=GKm[pr, g, 64:128], start=False, stop=False, skip_group_check=True),
                                     reads=["Vtok", "GKm"], writes=["Pg"])
                                P.op("pe", lambda e: e.matmul(Pg[pr, g, 0:64], lhsT=Usb[pr, g, :], rhs=GBm[pr, g, 64:128],
                                                              start=False, stop=True, skip_group_check=True),
                                     reads=["Usb", "GBm"], writes=["Pg"])
                        P.op("act", lambda e: e.copy(out=Ysb[:, :, c, :], in_=Pg[:, :, 0:64]), reads=["Pg"], writes=["Ysb"])
                        for g in range(8):
                            for h2 in range(2):
                                pr = slice(h2 * 64, (h2 + 1) * 64)
                                P.op("pe", lambda e: e.matmul(Pn[0][pr, g, :], lhsT=Ktok[pr, g, :],
                                                              rhs=Vtok[pr, c, g * 64:(g + 1) * 64],
                                                              start=True, stop=False, skip_group_check=True),
                                     reads=["Ktok", "Vtok"], writes=["Pn0"])
                                P.op("pe", lambda e: e.matmul(Pn[0][pr, g, :], lhsT=nBtok[pr, g, :], rhs=Usb[pr, g, :],
                                                              start=False, stop=True, skip_group_check=True),
                                     reads=["nBtok", "Usb"], writes=["Pn0"])
                        P.op("dve", lambda e: e.tensor_tensor(out=Htmp[:], in0=Pn[0][:], in1=H32[:], op=ALU.add),
                             reads=["Pn0", "H32"], writes=["Htmp"])
                        P.op("pool", lambda e: e.tensor_tensor(
                            out=H32[:], in0=Htmp[:], in1=gam[:, :, c:c + 1].to_broadcast([128, 8, 64]), op=ALU.mult),
                            reads=["Htmp", "gam"], writes=["H32"])
                        P.op("act", lambda e: e.copy(out=Hbf[:], in_=H32[:]), reads=["H32"], writes=["Hbf"])
                        if kind == "s":
                            out_deps.append(store_state(wkvs[c]))
                    if kind == "p" and qi == SEQ // TT - 1:
                        out_deps.append(store_state(wkvp))

                    for g in range(8):
                        pm, PM = pm_[0], "pmd0"
                        pm2, PM2 = pm_[1], "pmd1"
                        Yg = Ysb[:, g, 0:nch, 0:cl]
                        m_, ysq, var_, yn = tA[0], tA[1], tA[2], tA[3]
                        c3 = lambda ap: ap.rearrange("p (c t) -> p c t", t=cl)
                        P.op("pool", lambda e: e.tensor_copy(out=c3(yn[:, nt]), in_=Yg), reads=["Ysb"], writes=["tA3"])
                        P.op("pe", lambda e: e.matmul(pm[:, nt], lhsT=blk32[:], rhs=yn[:, nt], start=True, stop=True),
                             reads=["blk32", "tA3"], writes=[PM])
                        P.op("pool", lambda e: e.tensor_tensor(out=ysq[:, nt], in0=yn[:, nt], in1=yn[:, nt], op=ALU.mult),
                             reads=["tA3"], writes=["tA1"])
                        P.op("pe", lambda e: e.matmul(pm2[:, nt], lhsT=blk32[:], rhs=ysq[:, nt], start=True, stop=True),
                             reads=["blk32", "tA1"], writes=[PM2])
                        P.op("act", lambda e: e.activation(out=m_[:, nt], in_=pm[:, nt], func=AF.Copy, scale=1.0 / 64),
                             reads=[PM], writes=["tA0"])
                        P.op("pool", lambda e: e.tensor_tensor(out=ysq[:, nt], in0=m_[:, nt], in1=m_[:, nt], op=ALU.mult),
                             reads=["tA0", "tA1"], writes=["tA1"])
                        P.op("dve", lambda e: e.scalar_tensor_tensor(out=var_[:, nt], in0=pm2[:, nt], scalar=1.0 / 64,
                                                                     in1=ysq[:, nt], op0=ALU.mult, op1=ALU.subtract),
                             reads=[PM2, "tA1"], writes=["tA2"])
                        P.op("act", lambda e: e.activation(out=var_[:, nt], in_=var_[:, nt], func=AF.Sqrt, bias=GN_EPS),
                             reads=["tA2"], writes=["tA2"])
                        P.op("dve", lambda e: e.reciprocal(out=var_[:, nt], in_=var_[:, nt]), reads=["tA2"], writes=["tA2"])
                        P.op("dve", lambda e: e.tensor_tensor(out=yn[:, nt], in0=yn[:, nt], in1=m_[:, nt], op=ALU.subtract),
                             reads=["tA3", "tA0"], writes=["tA3"])
                        P.op("dve", lambda e: e.tensor_tensor(out=yn[:, nt], in0=yn[:, nt], in1=var_[:, nt], op=ALU.mult),
                             reads=["tA3", "tA2"], writes=["tA3"])
                        P.op("dve", lambda e: e.tensor_scalar(out=yn[:, nt], in0=yn[:, nt], scalar1=lngT[:, g:g + 1],
                                                              scalar2=lnbT[:, g:g + 1], op0=ALU.mult, op1=ALU.add),
                             reads=["tA3", "lngT", "lnbT"], writes=["tA3"])
                        P.op("pool", lambda e: e.tensor_tensor(out=yn[:, nt], in0=yn[:, nt], in1=bon[:, g, nt], op=ALU.add),
                             reads=["tA3", "bon"], writes=["tA3"])
                        if kind == "p":
                            yb, YB = tB[2], "tB2"
                            P.op("dve", lambda e: e.tensor_tensor(out=yb[:, nt], in0=yn[:, nt], in1=gT_[:, g, nt], op=ALU.mult),
                                 reads=["tA3", "gT_"], writes=[YB])
                            P.dma("sp", ygT_d[:, g, qi * TT:(qi + 1) * TT], yb[:, nt], reads=[YB], semkey="ygo")
                        else:
                            P.op("dve", lambda e: e.tensor_tensor(out=ygs[:, g, :], in0=yn[:, nt], in1=gT_[:, g, nt], op=ALU.mult),
                                 reads=["tA3", "gT_"], writes=["ygs"])
                    if kind == "p":
                        P.op("pool", lambda e: e.tensor_copy(out=lastcol[:], in_=hTe[:, :, TT:TT + 1]),
                             reads=["hTe"], writes=["lastcol"])
                P.barrier()

        if cfg.upto >= 7:
            gfin = sbt(st, "gfin", [128, D], F32)
            P.dma("sp", gfin[:], final_norm.partition_broadcast(128), writes=["gfin"])
            fin = dict(ssq=sbt(st, "fssq", [128, 1]), rstd=sbt(st, "frstd", [128, 1]),
                       junk=sbt(st, "fjunk", [128, D], BF16), y=[sbt(st, "fy%d" % i, [128, D]) for i in range(2)])
            fcnt = [0]

            def get_oT1(kind, qi, ntok, oT_tile, OT):
                if kind == "p":
                    for g in range(8):
                        P.dma("sp", oT_tile[:, g, :], ygT_d[:, g, qi * 512:(qi + 1) * 512], writes=[OT])
                else:
                    P.op("dve", lambda e: e.tensor_copy(out=oT_tile[:, :, 0:NST], in_=ygs[:]), reads=["ygs"], writes=[OT])

            def x_src1(kind, qi, sub, dst, X):
                if kind == "p":
                    t = qi * 4 + sub
                    P.dma("sp", dst, xl0p[t * 128:(t + 1) * 128, :], writes=[X])
                else:
                    P.dma("sp", dst, xl0s, writes=[X])

            def x_dst1(kind, qi, sub, src, X1):
                np_ = src.shape[0]
                fb = fcnt[0] % 2
                fcnt[0] += 1
                FY = "fy%d" % fb
                P.op("act", lambda e: e.activation(out=fin["junk"][0:np_, :], in_=src, func=AF.Square,
                                                   accum_out=fin["ssq"][0:np_, :]),
                     reads=[X1], writes=["fjunk", "fssq"])
                P.op("act", lambda e: e.activation(out=fin["rstd"][0:np_, :], in_=fin["ssq"][0:np_, :], func=AF.Sqrt,
                                                   scale=1.0 / D, bias=EPS), reads=["fssq"], writes=["frstd"])
                P.op("dve", lambda e: e.reciprocal(out=fin["rstd"][0:np_, :], in_=fin["rstd"][0:np_, :]),
                     reads=["frstd"], writes=["frstd"])
                P.op("dve", lambda e: e.scalar_tensor_tensor(out=fin["y"][fb][0:np_, :], in0=src, scalar=fin["rstd"][0:np_, :],
                                                             in1=gfin[0:np_, :], op0=ALU.mult, op1=ALU.mult),
                     reads=[X1, "frstd", "gfin"], writes=[FY])
                if kind == "p":
                    t = qi * 4 + sub
                    out_deps.append(P.dma("sp", yp[t * 128:(t + 1) * 128, :], fin["y"][fb][0:np_, :], reads=[FY], semkey=FY + "o"))
                else:
                    out_deps.append(P.dma("sp", ys, fin["y"][fb][0:np_, :], reads=[FY], semkey=FY + "o"))

            ffn_phase(1, 2, 3, Wrwo_b, get_oT1, x_src1, x_dst1, True)

        P.wait_all("sp", out_deps)
        build_nc.last_counts = dict(P.cnt)
    return nc


def build_satt(cfg):
    NPG, NPHYS = cfg.NPG, cfg.NPHYS
    NSA = 32
    NTK = NSA * TS
    NCOL = NSA * 2 * TS
    nc = bass.Bass("TRN2", target_bir_lowering=False)

    def din(name, shape, dt=F32):
        return nc.dram_tensor(name, list(shape), dt, kind="ExternalInput").ap()

    def dout(name, shape, dt=F32):
        return nc.dram_tensor(name, list(shape), dt, kind="ExternalOutput").ap()

    xs = din("xs", [NTK, D])
    cvec = din("cvec", [NSA, D])
    g_mix0 = din("g_mix0", [D])
    adaw = din("adaw", [D, 2 * D])
    adab = din("adab", [2 * D])
    w_hs = din("w_hs", [D, 384])
    bias2 = din("bias2", [2])
    cache_kv = din("cache_kv", [NPHYS * 128, 256])
    ptab = din("ptab", [NSA * NPG], I32)
    KVg_d = [nc.dram_tensor("KVg_d%d" % b_, [128, NSA, 256], F32, kind="Internal").ap() for b_ in range(NPG + 1)]
    oT_o = dout("oT_o", [128, NTK])
    ks_o = dout("ks_o", [NTK, 128])
    vs_o = dout("vs_o", [NTK, 128])
    idx_o = dout("idx_o", [128, NSA * NPG], I32) if cfg.debug else None

    with ExitStack() as st:
        P = Prog(nc, st)
        uniq = [0]

        def sbt(stack, name, shape, dt=F32):
            uniq[0] += 1
            return stack.enter_context(nc.sbuf_tensor("%s_u%d" % (name, uniq[0]), list(shape), dt))

        def pst(stack, name, shape, dt=F32):
            uniq[0] += 1
            P.excl.add(name)
            esz = 4 if dt == F32 else 2
            n = int(np.prod(shape[1:]))
            per_bank = 2048 // esz
            npad = ((n + per_bank - 1) // per_bank) * per_bank
            t_ = stack.enter_context(nc.psum_tensor("%s_u%d" % (name, uniq[0]), [128, npad], dt))
            v = t_[:, 0:n]
            if len(shape) == 3:
                v = v.rearrange("p (a b) -> p a b", b=shape[2])
            elif len(shape) == 4:
                v = v.rearrange("p (a b c) -> p a b c", b=shape[2], c=shape[3])
            return v

        out_deps = []
        if "noguard" not in cfg.dbg:
            guard = sbt(st, "guard", [128, 4096], F32)
        ident = sbt(st, "ident", [128, 128], F32)
        identb = sbt(st, "identb", [128, 128], BF16)
        negU = sbt(st, "negU", [128, 128], BF16)
        negO = sbt(st, "negO", [128, 128], BF16)
        P.op("pool", lambda e: e.memset(ident[:], 0.0), writes=["ident"])
        P.op("pool", lambda e: e.affine_select(
            out=ident[:], in_=ident[:], pattern=[[-1, 128]], compare_op=ALU.not_equal,
            fill=1.0, base=0, channel_multiplier=1), reads=["ident"], writes=["ident"])
        P.op("pool", lambda e: e.tensor_copy(out=identb[:], in_=ident[:]), reads=["ident"], writes=["identb"])
        P.op("pool", lambda e: e.memset(negO[:], -1.0), writes=["negO"])
        P.op("pool", lambda e: e.affine_select(
            out=negU[:], in_=negO[:], pattern=[[-1, 128]], compare_op=ALU.is_ge,
            fill=0.0, base=0, channel_multiplier=1), reads=["negO"], writes=["negU"])

        with ExitStack() as ph:
            pt_i = sbt(ph, "pt_i", [128, NSA * NPG], I32)
            pt_f = sbt(ph, "pt_f", [128, NSA * NPG], F32)
            iota_f = sbt(ph, "iota_f", [128, 1], F32)
            iota_i = sbt(ph, "iota_i", [128, 1], I32)
            idx = sbt(ph, "idx", [128, NSA * NPG], I32)
            NGB = 16
            G32 = [sbt(ph, "G32_%d" % i, [128, 256], F32) for i in range(NGB)]
            P.dma("sp", pt_i[:], ptab.partition_broadcast(128), writes=["pt_i"])
            P.op("dve", lambda e: e.tensor_copy(out=pt_f[:], in_=pt_i[:]), reads=["pt_i"], writes=["pt_f"])
            P.op("pool", lambda e: e.iota(iota_i[:], pattern=[[0, 1]], base=0, channel_multiplier=1), writes=["iota_i"])
            P.op("dve", lambda e: e.tensor_copy(out=iota_f[:], in_=iota_i[:]), reads=["iota_i"], writes=["iota_f"])
            P.op("dve", lambda e: e.tensor_scalar(out=pt_f[:], in0=pt_f[:], scalar1=128.0, scalar2=iota_f[:, 0:1],
                                                  op0=ALU.mult, op1=ALU.add), reads=["pt_f", "iota_f"], writes=["pt_f"])
            P.op("dve", lambda e: e.tensor_copy(out=idx[:], in_=pt_f[:]), reads=["pt_f"], writes=["idx"])
            if cfg.debug:
                out_deps.append(P.dma("sp", idx_o, idx[:], reads=["idx"], semkey="idxo"))
            gc = 0
            for blk in range(NPG):
                for s_ in range(NSA):
                    c_ = s_ * NPG + blk
                    for (src_, dst_) in ((cache_kv, KVg_d),):
                        gi = gc % NGB
                        gc += 1
                        GK_ = "G32_%d" % gi
                        P.dma("pool", G32[gi][:], src_, writes=[GK_],
                              indirect=bass.IndirectOffsetOnAxis(ap=idx[:, c_:c_ + 1], axis=0), reads=["idx"])
                        P.dma("sp", dst_[blk][:, s_, :], G32[gi][:], reads=[GK_], semkey=GK_ + "o")
            P.barrier()

        qTs = sbt(st, "qTs", [128, NTK], BF16)
        kTs = sbt(st, "kTs", [128, NSA, 128], BF16)
        Vs = sbt(st, "Vs", [128, NSA, 128], BF16)
        P.op("pool", lambda e: e.memset(kTs[:], 0.0), writes=["kTs"])
        P.op("pool", lambda e: e.memset(Vs[:], 0.0), writes=["Vs"])

        for _once in ([] if "nophase1" in cfg.dbg else [0]):
          with ExitStack() as ph:
              cT = sbt(ph, "cT", [128, 8, NSA], F32)
              scT = sbt(ph, "scT", [128, 8, NSA], BF16)
              wada = sbt(ph, "wada", [128, 8, 2 * D], BF16)
              whs = sbt(ph, "whs", [128, 8, 384], BF16)
              adabT = sbt(ph, "adabT", [128, 16], F32)
              gT = sbt(ph, "gT", [128, 8], F32)
              modT = sbt(ph, "modT", [128, 16, NSA], F32)
              Acoef = sbt(ph, "Acoef", [128, 8, NSA], F32)
              A_full = sbt(ph, "A_full", [128, 8, NTK], F32)
              B_full = sbt(ph, "B_full", [128, 8, NTK], F32)
              hT = sbt(ph, "hT", [128, 8, NTK], BF16)
              xt = [sbt(ph, "xt%d" % i, [128, D]) for i in range(2)]
              junk = sbt(ph, "junk", [128, D], BF16)
              ssq = [sbt(ph, "ssq%d" % i, [128, 1]) for i in range(2)]
              rstd = [sbt(ph, "rstd%d" % i, [128, 1]) for i in range(2)]
              xn = [sbt(ph, "xn%d" % i, [128, D]) for i in range(2)]
              tmpf = sbt(ph, "tmpf", [128, 128], F32)
              kvo = [sbt(ph, "kvo%d" % i, [128, 256], F32) for i in range(2)]
              vbf = [sbt(ph, "vbf%d" % i, [128, 128], BF16) for i in range(2)]
              psm = pst(ph, "psm", [128, 16, NSA], F32)
              ptr_ = [pst(ph, "ptr%d" % i, [128, 512], F32) for i in range(2)]
              pq = [pst(ph, "pq%d" % i, [128, 512], F32) for i in range(2)]
              with nc.allow_non_contiguous_dma(reason="tiny transposed parameter loads"):
                  for s_ in range(NSA):
                      P.dma("sp", cT[:, :, s_], cvec[s_].rearrange("(k p) -> p k", p=128), writes=["cT"])
                  P.dma("sp", gT[:], g_mix0.rearrange("(k p) -> p k", p=128), writes=["gT"])
                  P.dma("sp", adabT[:], adab.rearrange("(k p) -> p k", p=128), writes=["adabT"])
              for k in range(8):
                  for hc in range(2):
                      P.dma("pool", wada[:, k, hc * D:(hc + 1) * D], adaw[k * 128:(k + 1) * 128, hc * D:(hc + 1) * D],
                            writes=["wada"], max_dma_last_dim=4096)
                  P.dma("pool", whs[:, k, :], w_hs[k * 128:(k + 1) * 128, :], writes=["whs"])
              P.op("act", lambda e: e.activation(out=scT[:], in_=cT[:], func=AF.Silu), reads=["cT"], writes=["scT"])
              for oc in range(16):
                  for k in range(8):
                      P.op("pe", lambda e: e.matmul(psm[:, oc, :], lhsT=wada[:, k, oc * 128:(oc + 1) * 128],
                                                    rhs=scT[:, k, :], start=(k == 0), stop=(k == 7)),
                           reads=["wada", "scT"], writes=["psm"])
              P.op("dve", lambda e: e.tensor_tensor(
                  out=modT[:], in0=psm[:], in1=adabT[:].unsqueeze(2).to_broadcast([128, 16, NSA]), op=ALU.add),
                  reads=["psm", "adabT"], writes=["modT"])
              P.op("dve", lambda e: e.tensor_scalar(out=Acoef[:], in0=modT[:, 8:16, :], scalar1=1.0, scalar2=None, op0=ALU.add),
                   reads=["modT"], writes=["Acoef"])
              P.op("dve", lambda e: e.tensor_tensor(
                  out=Acoef[:], in0=Acoef[:], in1=gT[:].unsqueeze(2).to_broadcast([128, 8, NSA]), op=ALU.mult),
                  reads=["Acoef", "gT"], writes=["Acoef"])
              for k in range(8):
                  P.op("dve", lambda e: e.tensor_copy(
                      out=A_full[:, k, :].rearrange("p (s t) -> p s t", t=TS),
                      in_=Acoef[:, k, :].unsqueeze(2).to_broadcast([128, NSA, TS])),
                      reads=["Acoef"], writes=["A_full"])
                  P.op("pool", lambda e: e.tensor_copy(
                      out=B_full[:, k, :].rearrange("p (s t) -> p s t", t=TS),
                      in_=modT[:, k, :].unsqueeze(2).to_broadcast([128, NSA, TS])),
                      reads=["modT"], writes=["B_full"])
              for ti in range(2):
                  X, SS, RS, XN = "xt%d" % ti, "ssq%d" % ti, "rstd%d" % ti, "xn%d" % ti
                  P.dma("sp", xt[ti][:], xs[ti * 128:(ti + 1) * 128, :], writes=[X])
                  P.op("act", lambda e: e.activation(out=junk[:], in_=xt[ti][:], func=AF.Square, accum_out=ssq[ti][:]),
                       reads=[X], writes=["junk", SS])
                  P.op("act", lambda e: e.activation(out=rstd[ti][:], in_=ssq[ti][:], func=AF.Sqrt, scale=1.0 / D, bias=EPS),
                       reads=[SS], writes=[RS])
                  P.op("dve", lambda e: e.reciprocal(out=rstd[ti][:], in_=rstd[ti][:]), reads=[RS], writes=[RS])
                  P.op("dve", lambda e: e.tensor_scalar(out=xn[ti][:], in0=xt[ti][:], scalar1=rstd[ti][:], scalar2=None,
                                                        op0=ALU.mult), reads=[X, RS], writes=[XN])
                  for k in range(8):
                      pt, PT = ptr_[k % 2], "ptr%d" % (k % 2)
                      P.op("pe", lambda e: e.transpose(out=pt[:, 0:128], in_=xn[ti][:, k * 128:(k + 1) * 128], identity=ident[:]),
                           reads=[XN, "ident"], writes=[PT])
                      P.op("dve", lambda e: e.tensor_tensor(out=tmpf[:], in0=pt[:, 0:128],
                                                            in1=A_full[:, k, ti * 128:(ti + 1) * 128], op=ALU.mult),
                           reads=[PT, "A_full"], writes=["tmpf"])
                      P.op("pool", lambda e: e.tensor_tensor(out=hT[:, k, ti * 128:(ti + 1) * 128], in0=tmpf[:],
                                                             in1=B_full[:, k, ti * 128:(ti + 1) * 128], op=ALU.add),
                           reads=["tmpf", "B_full"], writes=["hT"])
              for which in range(1):
                  p_, PQ = pq[which], "pq%d" % which
                  for k in range(8):
                      P.op("pe", lambda e: e.matmul(p_[:, 0:NTK], lhsT=whs[:, k, which * 128:(which + 1) * 128],
                                                    rhs=hT[:, k, :], start=(k == 0), stop=(k == 7)),
                           reads=["whs", "hT"], writes=[PQ])
                  if which == 0:
                      P.op("act", lambda e: e.activation(out=qTs[:], in_=p_[:, 0:NTK], func=AF.Copy, scale=0.125),
                           reads=[PQ], writes=["qTs"])
                  else:
                      pass
              zt = sbt(ph, "zt", [128, NSA * 256], F32)
              P.op("pool", lambda e: e.memset(zt[:], 0.0), writes=["zt"])
              P.dma("sp", KVg_d[NPG].rearrange("p s d -> p (s d)"), zt[:], reads=["zt"], writes=["KgN"], semkey="KgN")
              for ti in range(2):
                  p_, PQ = pq[ti], "pq%d" % ti
                  KV, VB = "kvo%d" % ti, "vbf%d" % ti
                  for k in range(8):
                      P.op("pe", lambda e: e.matmul(p_[:, 0:256], lhsT=hT[:, k, ti * 128:(ti + 1) * 128],
                                                    rhs=whs[:, k, 128:384], start=(k == 0), stop=(k == 7)),
                           reads=["whs", "hT"], writes=[PQ])
                  P.op("act", lambda e: e.copy(out=kvo[ti][:], in_=p_[:, 0:256]), reads=[PQ], writes=[KV])
                  P.op("pool", lambda e: e.tensor_copy(out=vbf[ti][:], in_=kvo[ti][:, 128:256]), reads=[KV], writes=[VB])
                  out_deps.append(P.dma("sp", ks_o[ti * 128:(ti + 1) * 128, :], kvo[ti][:, 0:128], reads=[KV], semkey="ok%d" % ti))
                  out_deps.append(P.dma("sp", vs_o[ti * 128:(ti + 1) * 128, :], kvo[ti][:, 128:256], reads=[KV], semkey="ov%d" % ti))
                  for sl in range(16):
                      s_ = ti * 16 + sl
                      P.dma("sp", KVg_d[NPG][0:TS, s_, :], kvo[ti][sl * TS:(sl + 1) * TS, :], reads=[KV], writes=["KgN"], semkey="KgN")
              P.barrier()

        with ExitStack() as ph:
            NG = 2
            KV32 = [sbt(ph, "KV32_%d" % i, [128, NSA, 256], F32) for i in range(NG)]
            Kpg = [sbt(ph, "Kpg%d" % i, [128, NSA, 128], BF16) for i in range(NG)]
            Vpg = [sbt(ph, "Vpg%d" % i, [128, NSA, 128], BF16) for i in range(NG)]
            kTp = [sbt(ph, "kTp%d" % i, [128, NSA, 128], BF16) for i in range(NG)]
            maskn = sbt(ph, "maskn", [128, NSA * 2, TS], BF16)
            ones_m = sbt(ph, "ones_m2", [128, NCOL], BF16)
            bias2b = sbt(ph, "bias2b", [128, 2], F32)
            bias_bc = sbt(ph, "bias_bc", [128, NSA, 2, TS], F32)
            zsb = sbt(ph, "zsb", [128, NCOL], F32)
            e_t = sbt(ph, "es_t", [128, NCOL], F32)
            sp_t = [sbt(ph, "sps_t%d" % i, [128, NCOL], BF16) for i in range(2)]
            t1 = sbt(ph, "t1s", [128, NCOL], F32)
            a_t = [sbt(ph, "as_t%d" % i, [128, NCOL], BF16) for i in range(2)]
            SP32 = sbt(ph, "SPs32", [128, NCOL], F32)
            SPbf = sbt(ph, "SPsbf", [128, NCOL], BF16)
            zero_l = sbt(ph, "zero_l", [128, 128], BF16)
            zsb_n = sbt(ph, "zsb_n", [128, NCOL], F32)
            sp_n = sbt(ph, "sp_n", [128, NCOL], BF16)
            o_sb = sbt(ph, "o_sb", [128, NTK], F32)
            Pz = [pst(ph, "Pzs%d" % i, [128, NCOL], F32) for i in range(2)]
            Pa = [pst(ph, "Pas%d" % i, [128, NCOL], F32) for i in range(2)]
            Po = pst(ph, "Pos", [128, NTK], F32)
            Ptr = [pst(ph, "Ptrs%d" % i, [128, 8, 128], BF16) for i in range(2)]

            P.dma("sp", bias2b[:], bias2.partition_broadcast(128), writes=["bias2b"])
            P.op("pool", lambda e: e.memset(ones_m[:], 1.0), writes=["ones_m2"])
            P.op("pool", lambda e: e.memset(zero_l[:], 0.0), writes=["zero_l"])
            P.op("pool", lambda e: e.memset(SP32[:], 0.0), writes=["SPs32"])
            P.op("pool", lambda e: e.affine_select(
                out=maskn[:], in_=ones_m[:].rearrange("p (c t) -> p c t", t=TS),
                pattern=[[0, NSA * 2], [1, TS]], compare_op=ALU.is_gt,
                fill=0.0, base=0, channel_multiplier=-1), reads=["ones_m2"], writes=["maskn"])
            for s_ in range(NSA):
                P.op("dve" if s_ % 2 == 0 else "pool", lambda e: e.tensor_copy(
                    out=bias_bc[:, s_, :, :], in_=bias2b[:].unsqueeze(2).to_broadcast([128, 2, TS])),
                    reads=["bias2b"], writes=["bias_bc"])
            P.op("pe", lambda e: e.matmul(Po[:], lhsT=zero_l[:], rhs=ones_m[:, 0:NTK], start=True, stop=False),
                 reads=["zero_l", "ones_m2"], writes=["Pos"])

            def col(s_, h2):
                return (s_ * 2 + h2) * TS

            blocks = list(range(NPG, -1, -1))
            if cfg.kstop <= 4:
                blocks = blocks[:max(0, cfg.kstop - 2)]

            gdeps = []
            for bi, blk in enumerate(blocks):
                gb = bi % NG
                first, last = (bi == 0), (bi == len(blocks) - 1)
                if True:
                    KT, KP, VP = "kTp%d" % gb, "Kpg%d" % gb, "Vpg%d" % gb
                    P.dma("sp", KV32[gb][:], KVg_d[blk], reads=["KgN"], writes=["KV32_%d" % gb])
                    P.op("pool", lambda e: e.tensor_copy(out=Kpg[gb][:], in_=KV32[gb][:, :, 0:128]), reads=["KV32_%d" % gb], writes=[KP])
                    P.op("pool", lambda e: e.tensor_copy(out=Vpg[gb][:], in_=KV32[gb][:, :, 128:256]), reads=["KV32_%d" % gb], writes=[VP])
                    if "gatheronly" in cfg.dbg:
                        continue
                    for q4 in range(NSA // 8):
                        ptb, PTB = Ptr[q4 % 2], "Ptrs%d" % (q4 % 2)
                        for sl in range(8):
                            s_ = q4 * 8 + sl
                            P.op("pe", lambda e: e.transpose(out=ptb[:, sl, :], in_=Kpg[gb][:, s_, :], identity=identb[:]),
                                 reads=[KP, "identb"], writes=[PTB])
                        P.op("dve" if q4 % 2 == 0 else "act",
                             (lambda e: e.tensor_copy(out=kTp[gb][:, q4 * 8:(q4 + 1) * 8, :], in_=ptb[:])) if q4 % 2 == 0
                             else (lambda e: e.copy(out=kTp[gb][:, q4 * 8:(q4 + 1) * 8, :], in_=ptb[:])),
                             reads=[PTB], writes=[KT])
                    kt_, vt_ = kTp[gb], None
                    KK, VK = [KT], [VP]
                pz, PZ = Pz[bi % 2], "Pzs%d" % (bi % 2)
                pa, PA = Pa[bi % 2], "Pas%d" % (bi % 2)
                for s_ in range(NSA if not (blk == "new" and "noz" in cfg.dbg) else 0):
                    for h2 in range(2):
                        c0 = col(s_, h2)
                        P.op("pe", lambda e: e.matmul(
                            pz[:, c0:c0 + TS], lhsT=kt_[h2 * 64:(h2 + 1) * 64, s_, :],
                            rhs=qTs[h2 * 64:(h2 + 1) * 64, s_ * TS:(s_ + 1) * TS],
                            start=True, stop=True, skip_group_check=True),
                            reads=KK + ["qTs"], writes=[PZ])
                spt, S = sp_t[bi % 2], "sps_t%d" % (bi % 2)
                at, A = a_t[bi % 2], "as_t%d" % (bi % 2)
                zsb_, ZS = zsb, "zsb"
                P.op("dve", lambda e: e.tensor_tensor(out=zsb_[:], in0=pz[:],
                                                      in1=bias_bc[:].rearrange("p s h t -> p (s h t)"), op=ALU.add),
                     reads=[PZ, "bias_bc"], writes=[ZS])
                P.op("act", lambda e: e.activation(out=e_t[:], in_=zsb_[:], func=AF.Exp), reads=[ZS], writes=["es_t"])
                P.op("act", lambda e: e.activation(out=spt[:], in_=e_t[:], func=AF.Ln, bias=1.0), reads=["es_t"], writes=[S])
                if blk == NPG:
                    P.op("dve", lambda e: e.tensor_tensor(out=spt[:], in0=spt[:],
                                                          in1=maskn[:].rearrange("p c t -> p (c t)"), op=ALU.mult),
                         reads=[S, "maskn"], writes=[S])
                P.op("pe", lambda e: e.matmul(pa[:], lhsT=negU[:], rhs=spt[:], start=True, stop=first),
                     reads=["negU", S], writes=[PA])
                if not first:
                    P.op("pe", lambda e: e.matmul(pa[:], lhsT=negO[:], rhs=SPbf[:], start=False, stop=True),
                         reads=["negO", "SPsbf"], writes=[PA])
                P.op("dve", lambda e: e.tensor_tensor(out=t1[:], in0=pa[:], in1=zsb[:], op=ALU.add),
                     reads=[PA, "zsb"], writes=["t1s"])
                P.op("act", lambda e: e.activation(out=at[:], in_=t1[:], func=AF.Exp), reads=["t1s"], writes=[A])
                if blk == NPG:
                    P.op("dve", lambda e: e.tensor_tensor(out=at[:], in0=at[:],
                                                          in1=maskn[:].rearrange("p c t -> p (c t)"), op=ALU.mult),
                         reads=[A, "maskn"], writes=[A])
                for s_ in range(NSA if not (blk == "new" and "noav" in cfg.dbg) else 0):
                    for h2 in range(2):
                        c0 = col(s_, h2)
                        P.op("pe", lambda e: e.matmul(
                            Po[h2 * 64:(h2 + 1) * 64, s_ * TS:(s_ + 1) * TS], lhsT=(vt_[:, s_, h2 * 64:(h2 + 1) * 64] if vt_ is not None else Vpg[gb][:, s_, h2 * 64:(h2 + 1) * 64]),
                            rhs=at[:, c0:c0 + TS], start=False, stop=False, skip_group_check=True),
                            reads=VK + [A], writes=["Pos"])
                if not last:
                    P.op("pool", lambda e: e.tensor_tensor(out=SP32[:], in0=SP32[:], in1=spt[:], op=ALU.add),
                         reads=["SPs32", S], writes=["SPs32"])
                    P.op("pool", lambda e: e.tensor_copy(out=SPbf[:], in_=SP32[:]), reads=["SPs32"], writes=["SPsbf"])
            P.op("pe", lambda e: e.matmul(Po[:], lhsT=zero_l[:], rhs=ones_m[:, 0:NTK], start=False, stop=True),
                 reads=["zero_l", "ones_m2"], writes=["Pos"])
            P.op("dve", lambda e: e.tensor_copy(out=o_sb[:], in_=Po[:]), reads=["Pos"], writes=["o_sb"])
            out_deps.append(P.dma("sp", oT_o, o_sb[:], reads=["o_sb"], semkey="oo"))
            P.wait_all("sp", out_deps)
    return nc


_NC = None
_NC1 = None


def _f32(a):
    return np.ascontiguousarray(np.asarray(a, dtype=np.float32))


def _satt_inputs(inputs, c):
    f = _f32
    wq = np.asarray(inputs["sb_w_qkv"], dtype=np.float32)[0]
    w_hs = np.concatenate([wq[:, j * D + 128 * c:j * D + 128 * (c + 1)] for j in range(3)], axis=1)
    ck = np.asarray(inputs["cache_k"], dtype=np.float32)[0]
    cv = np.asarray(inputs["cache_v"], dtype=np.float32)[0]
    nph = ck.shape[0]
    return {
        "xs": f(inputs["x_sample"]).reshape(-1, D),
        "cvec": f(inputs["c_sample"]),
        "g_mix0": f(inputs["norm_mix"][0]),
        "adaw": f(np.asarray(inputs["ada_w_mix"], dtype=np.float32)[0][:, :2 * D]),
        "adab": f(np.asarray(inputs["ada_b_mix"], dtype=np.float32)[0][:2 * D]),
        "w_hs": f(w_hs),
        "bias2": f(np.asarray(inputs["sb_bias"], dtype=np.float32)[0][2 * c:2 * c + 2]),
        "cache_kv": np.concatenate([ck[:, :, 2 * c:2 * c + 2, :].reshape(nph * 128, 128),
                                    cv[:, :, 2 * c:2 * c + 2, :].reshape(nph * 128, 128)], axis=1),
        "ptab": np.ascontiguousarray(np.asarray(inputs["page_table"], dtype=np.int32)).reshape(-1),
    }


def _core_inputs(inputs, c, oT_all=None):
    f = _f32
    sq = c % 4
    m = {
        "xp": f(inputs["x_prompt"][sq]),
        "xs": f(inputs["x_sample"][NS * c:NS * (c + 1)]).reshape(NST, D),
        "cvec": np.concatenate([f(inputs["c_prompt"][sq:sq + 1]), f(inputs["c_sample"][NS * c:NS * (c + 1)])], axis=0),
        "norm_mix": f(inputs["norm_mix"]), "norm_ffn": f(inputs["norm_ffn"]),
        "ada_w_mix": f(inputs["ada_w_mix"]), "ada_b_mix": f(inputs["ada_b_mix"]),
        "ada_w_ffn": f(inputs["ada_w_ffn"]), "ada_b_ffn": f(inputs["ada_b_ffn"]),
        "w_up": f(inputs["w_up"]), "w_down": f(inputs["w_down"]),
        "w_qkv": f(inputs["sb_w_qkv"][0]), "w_o": f(inputs["sb_w_o"][0]),
        "sb_bias": f(inputs["sb_bias"][0]),
        "state_wkv": f(inputs["state_wkv"][0][NS * c:NS * (c + 1)]),
        "state_shift": f(inputs["state_shift"][0][NS * c:NS * (c + 1)]),
        "rw_mu": f(inputs["rw_mu"][0]), "rw_w_r": f(inputs["rw_w_r"][0]), "rw_w_k": f(inputs["rw_w_k"][0]),
        "rw_w_v": f(inputs["rw_w_v"][0]), "rw_w_o": f(inputs["rw_w_o"][0]), "rw_w0": f(inputs["rw_w0"][0]),
        "rw_w1": f(inputs["rw_w1"][0]), "rw_w2": f(inputs["rw_w2"][0]), "rw_a0": f(inputs["rw_a0"][0]),
        "rw_a1": f(inputs["rw_a1"][0]), "rw_a2": f(inputs["rw_a2"][0]), "rw_g1": f(inputs["rw_g1"][0]),
        "rw_g2": f(inputs["rw_g2"][0]), "rw_k_k": f(inputs["rw_k_k"][0]), "rw_k_a": f(inputs["rw_k_a"][0]),
        "rw_r_k": f(inputs["rw_r_k"][0]).reshape(-1), "rw_ln_g": f(inputs["rw_ln_g"][0]),
        "rw_ln_b": f(inputs["rw_ln_b"][0]), "final_norm": f(inputs["final_norm"]),
    }
    if oT_all is None:
        m["cache_k"] = np.asarray(inputs["cache_k"], dtype=np.float32).reshape(-1, D)
        m["cache_v"] = np.asarray(inputs["cache_v"], dtype=np.float32).reshape(-1, D)
        m["ptab"] = np.ascontiguousarray(np.asarray(inputs["page_table"], dtype=np.int32)[NS * c:NS * (c + 1)]).reshape(-1)
    else:
        m["oTs_in"] = np.ascontiguousarray(np.transpose(oT_all[:, :, NST * c:NST * (c + 1)], (1, 0, 2)))
    return m


def kernel(**inputs):
    global _NC, _NC1
    SEQ = int(np.shape(inputs["x_prompt"])[1])
    NPG = int(np.shape(inputs["page_table"])[1])
    NPHYS = int(np.shape(inputs["cache_k"])[1])
    if _NC is None:
        _NC1 = build_satt(Cfg(SEQ=SEQ, NPG=NPG, NPHYS=NPHYS))
        _NC = build_nc(Cfg(SEQ=SEQ, NPG=NPG, NPHYS=NPHYS, split=True))
    B, DB = 4, 32
    r1 = run_bass_kernel_spmd(_NC1, [_satt_inputs(inputs, c) for c in range(NCORES)], core_ids=list(range(NCORES))).results
    oT_all = np.stack([np.asarray(r1[g]["oT_o"], dtype=np.float32) for g in range(NCORES)])
    ks_all = np.stack([np.asarray(r1[g]["ks_o"], dtype=np.float32) for g in range(NCORES)])
    vs_all = np.stack([np.asarray(r1[g]["vs_o"], dtype=np.float32) for g in range(NCORES)])
    k_sample = np.transpose(ks_all.reshape(8, DB, TS, 2, 64), (1, 2, 0, 3, 4)).reshape(1, DB, TS, 16, 64)
    v_sample = np.transpose(vs_all.reshape(8, DB, TS, 2, 64), (1, 2, 0, 3, 4)).reshape(1, DB, TS, 16, 64)
    r = run_bass_kernel_spmd(_NC, [_core_inputs(inputs, c, oT_all) for c in range(NCORES)],
                             core_ids=list(range(NCORES))).results
    k_prompt = np.stack([r[s]["kp"] for s in range(B)]).reshape(1, B, SEQ, 16, 64)
    v_prompt = np.stack([r[s]["vp"] for s in range(B)]).reshape(1, B, SEQ, 16, 64)
    y_prompt = np.stack([r[s]["yp"] for s in range(B)]).reshape(B, SEQ, D)
    y_sample = np.concatenate([r[c]["ys"] for c in range(NCORES)]).reshape(DB, TS, D)
    wkv_prompt = np.stack([r[s]["wkvp"] for s in range(B)]).reshape(1, B, 16, 64, 64)
    shift_prompt = np.stack([r[s]["shp"] for s in range(B)]).reshape(1, B, D)
    wkv_sample = np.concatenate([r[c]["wkvs"] for c in range(NCORES)]).reshape(1, DB, 16, 64, 64)
    shift_sample = np.concatenate([r[c]["shs"] for c in range(NCORES)]).reshape(1, DB, D)
    return (_f32(y_prompt), _f32(y_sample), _f32(k_prompt), _f32(v_prompt), _f32(k_sample), _f32(v_sample),
            _f32(wkv_prompt), _f32(shift_prompt), _f32(wkv_sample), _f32(shift_sample))
```
